# Optimizing a Trainium2 kernel written in Bass

```python
import math
import jax, jax.numpy as jnp
from jax import lax
import numpy as np

D_MODEL = 1024
BATCH = 4
SEQ = 8192
DEPTH = 2

D_MIX = D_MODEL
HEAD_DIM = 64
D_POOL = D_MIX // 4
D_CONV = D_MIX // 4
D_NSA = D_MIX // 2
N_POOL_GROUPS = 4
POOL_GROUP = D_POOL // N_POOL_GROUPS
POOL_WINDOWS = (2, 4, 8, 16)
CONV_WIDTH = 3
N_HEADS = D_NSA // HEAD_DIM
N_KV = 2
GROUP = N_HEADS // N_KV
D_KV = N_KV * HEAD_DIM
CMP_BLOCK = 32
CMP_STRIDE = 16
CMP_HIDDEN = 128
SLC_BLOCK = 64
N_SELECT = 16
N_LOCAL = 2
WINDOW = 512
Q_BLOCK = 128
N_BUCKETS = 32
MAX_DISTANCE = 128
ALPHA = (2 * DEPTH) ** 0.25
BETA = (8 * DEPTH) ** -0.25
LN_EPS = 1e-5
NEG = -1e30
FORCED = 1e9
PROJ_SIZES = (D_POOL, D_POOL,
              D_CONV, D_CONV, D_CONV, D_CONV,
              D_NSA, D_KV, D_KV, D_KV, D_KV, D_KV, D_KV, 3 * N_HEADS, D_NSA)
D_PROJ = sum(PROJ_SIZES)

kernel_name = 'hymba_pool_conv_nsa_deepnorm'


def layer_norm(x, g, b):
    xf = x.astype(jnp.float32)
    mu = jnp.mean(xf, axis=-1, keepdims=True)
    var = jnp.mean(jnp.square(xf - mu), axis=-1, keepdims=True)
    return ((xf - mu) * lax.rsqrt(var + LN_EPS) * g + b).astype(x.dtype)


def rel_bucket(dist):
    n = jnp.maximum(dist, 0)
    max_exact = N_BUCKETS // 2
    nf = jnp.maximum(n, 1).astype(jnp.float32)
    large = max_exact + (jnp.log(nf / max_exact) / math.log(MAX_DISTANCE / max_exact)
                         * (N_BUCKETS - max_exact)).astype(jnp.int32)
    large = jnp.minimum(large, N_BUCKETS - 1)
    return jnp.where(n < max_exact, n, large)


def pool_mixer(v, w_grp, scale):
    B_, T, _ = v.shape
    vf = v.astype(jnp.float32)
    cs = jnp.pad(jnp.cumsum(vf, axis=1), ((0, 0), (1, 0), (0, 0)))
    pos = jnp.arange(T)
    outs = []
    for gi, w in enumerate(POOL_WINDOWS):
        sl = slice(gi * POOL_GROUP, (gi + 1) * POOL_GROUP)
        start = jnp.maximum(pos + 1 - w, 0)
        cnt = (pos + 1 - start).astype(jnp.float32)[None, :, None]
        outs.append((cs[:, 1:, sl] - cs[:, start, sl]) / cnt - vf[..., sl])
    p = jnp.stack(outs, axis=2)
    y = jnp.einsum('btgc,gcd->btgd', p, w_grp).reshape(B_, T, D_POOL)
    return (y * scale).astype(v.dtype)


def short_conv_mixer(b, c, xin, conv_w):
    u = c * xin
    T = u.shape[1]
    up = jnp.pad(u, ((0, 0), (CONV_WIDTH - 1, 0), (0, 0)))
    y = sum(conv_w[k] * up[:, k:k + T] for k in range(CONV_WIDTH))
    return b * y


def compress(kv, pe, w1, w2):
    B_, T = kv.shape[:2]
    n_cmp = (T - CMP_BLOCK) // CMP_STRIDE + 1
    idx = jnp.arange(n_cmp)[:, None] * CMP_STRIDE + jnp.arange(CMP_BLOCK)[None, :]
    blk = kv[:, idx] + pe[None, None, :, None, :]
    blk = jnp.moveaxis(blk, 3, 2).reshape(B_, n_cmp, N_KV, CMP_BLOCK * HEAD_DIM)
    out = jax.nn.silu(blk @ w1) @ w2
    return out.transpose(0, 2, 1, 3)


def nsa_mixer(q, k_cmp, v_cmp, k_slc, v_slc, k_win, v_win, gates,
              pe_k, w1_k, w2_k, pe_v, w1_v, w2_v, rel_bias):
    B_, T = q.shape[:2]
    f32 = jnp.float32
    n_cmp = (T - CMP_BLOCK) // CMP_STRIDE + 1
    n_slc = T // SLC_BLOCK
    n_qb = T // Q_BLOCK
    k_sel = min(N_SELECT, n_slc)
    heads = lambda a: a.reshape(B_, T, N_KV, HEAD_DIM)
    kc = compress(heads(k_cmp), pe_k, w1_k, w2_k).astype(f32)
    vc = compress(heads(v_cmp), pe_v, w1_v, w2_v).astype(f32)
    ksb = heads(k_slc).transpose(0, 2, 1, 3).reshape(B_, N_KV, n_slc, SLC_BLOCK, HEAD_DIM)
    vsb = heads(v_slc).transpose(0, 2, 1, 3).reshape(B_, N_KV, n_slc, SLC_BLOCK, HEAD_DIM)
    pad = ((0, 0), (0, 0), (WINDOW, 0), (0, 0))
    kwp = jnp.pad(heads(k_win).transpose(0, 2, 1, 3), pad)
    vwp = jnp.pad(heads(v_win).transpose(0, 2, 1, 3), pad)
    qbl = q.reshape(B_, n_qb, Q_BLOCK, N_KV, GROUP, HEAD_DIM).transpose(1, 0, 3, 4, 2, 5)
    gbl = gates.reshape(B_, n_qb, Q_BLOCK, N_KV, GROUP, 3).transpose(1, 0, 3, 4, 2, 5)
    rel_gr = rel_bias.T.reshape(N_KV, GROUP, N_BUCKETS).astype(f32)
    cstart = np.arange(n_cmp)[:, None] * CMP_STRIDE
    sstart = np.arange(n_slc)[None, :] * SLC_BLOCK
    overlap = np.clip(np.minimum(cstart + CMP_BLOCK, sstart + SLC_BLOCK) - np.maximum(cstart, sstart), 0, None)
    ov = jnp.asarray(overlap / CMP_STRIDE, dtype=f32)
    cmp_end = jnp.arange(n_cmp) * CMP_STRIDE + CMP_BLOCK - 1
    slc_idx = jnp.arange(n_slc)
    bi = jnp.arange(B_)[:, None, None, None]
    gi = jnp.arange(N_KV)[None, :, None, None]
    g5 = jnp.arange(N_KV)[None, :, None, None, None]
    r5 = jnp.arange(GROUP)[None, None, :, None, None]
    scale = HEAD_DIM ** -0.5

    def masked_softmax(s, m):
        return jnp.where(m, jax.nn.softmax(jnp.where(m, s, NEG), axis=-1), 0.0)

    def block(args):
        qb, gb, qi = args
        t = qi * Q_BLOCK + jnp.arange(Q_BLOCK)
        qf = qb.astype(f32) * scale
        d1 = t[:, None] - cmp_end[None, :]
        m1 = d1 >= 0
        s1 = jnp.einsum('bgrqd,bgnd->bgrqn', qf, kc) + rel_gr[:, :, rel_bucket(d1)]
        p1 = masked_softmax(s1, m1)
        o1 = jnp.einsum('bgrqn,bgnd->bgrqd', p1, vc)
        imp = jnp.einsum('bgrqn,nj->bgqj', p1, ov)
        back = (t // SLC_BLOCK)[:, None] - slc_idx[None, :]
        forced = (slc_idx[None, :] == 0) | ((back >= 0) & (back < N_LOCAL))
        score = jnp.where(forced, FORCED, jnp.where(back >= 0, imp, NEG))
        _, sel = lax.top_k(score, k_sel)
        kg = ksb[bi, gi, sel].reshape(B_, N_KV, Q_BLOCK, k_sel * SLC_BLOCK, HEAD_DIM).astype(f32)
        vg = vsb[bi, gi, sel].reshape(B_, N_KV, Q_BLOCK, k_sel * SLC_BLOCK, HEAD_DIM).astype(f32)
        kpos = (sel[..., None] * SLC_BLOCK + jnp.arange(SLC_BLOCK)).reshape(B_, N_KV, Q_BLOCK, -1)
        d2 = t[None, None, :, None] - kpos
        m2 = (d2 >= 0)[:, :, None]
        s2 = jnp.einsum('bgrqd,bgqkd->bgrqk', qf, kg) + rel_gr[g5, r5, rel_bucket(d2)[:, :, None]]
        o2 = jnp.einsum('bgrqk,bgqkd->bgrqd', masked_softmax(s2, m2), vg)
        start = qi * Q_BLOCK
        kw = lax.dynamic_slice_in_dim(kwp, start, Q_BLOCK + WINDOW, axis=2).astype(f32)
        vw = lax.dynamic_slice_in_dim(vwp, start, Q_BLOCK + WINDOW, axis=2).astype(f32)
        kpos3 = start - WINDOW + jnp.arange(Q_BLOCK + WINDOW)
        d3 = t[:, None] - kpos3[None, :]
        m3 = (d3 >= 0) & (d3 < WINDOW) & (kpos3[None, :] >= 0)
        s3 = jnp.einsum('bgrqd,bgkd->bgrqk', qf, kw) + rel_gr[:, :, rel_bucket(d3)]
        o3 = jnp.einsum('bgrqk,bgkd->bgrqd', masked_softmax(s3, m3), vw)
        g = jax.nn.sigmoid(gb.astype(f32))
        return g[..., 0:1] * o1 + g[..., 1:2] * o2 + g[..., 2:3] * o3

    out = lax.map(block, (qbl, gbl, jnp.arange(n_qb)))
    return out.transpose(1, 0, 4, 2, 3, 5).reshape(B_, T, D_NSA).astype(q.dtype)


def hybrid_layer(x, w_in, w_out, pool_w, pool_scale, conv_w,
                 pe_k, w1_k, w2_k, pe_v, w1_v, w2_v, rel_bias, ln_g, ln_b):
    proj = x @ w_in
    split_points = np.cumsum(PROJ_SIZES)[:-1].tolist()
    (pool_v, pool_z, conv_b, conv_c, conv_x, conv_z,
     q, k_cmp, v_cmp, k_slc, v_slc, k_win, v_win, gates, nsa_z) = jnp.split(proj, split_points, axis=-1)
    y_pool = jax.nn.silu(pool_z) * pool_mixer(pool_v, pool_w, pool_scale)
    y_conv = jax.nn.silu(conv_z) * short_conv_mixer(conv_b, conv_c, conv_x, conv_w)
    y_nsa = jax.nn.silu(nsa_z) * nsa_mixer(q, k_cmp, v_cmp, k_slc, v_slc, k_win, v_win, gates,
                                           pe_k, w1_k, w2_k, pe_v, w1_v, w2_v, rel_bias)
    y = jnp.concatenate([y_pool, y_conv, y_nsa], axis=-1) @ w_out
    return layer_norm(ALPHA * x + y, ln_g, ln_b)


def setup_inputs(seed: int = 0) -> dict:
    key = jax.random.key(seed)
    ks = jax.random.split(key, 16)
    f = jnp.float32
    nrm = lambda k, s: jax.random.normal(k, s, f)
    return {
        'x': nrm(ks[0], (BATCH, SEQ, D_MODEL)),
        'w_in': nrm(ks[1], (DEPTH, D_MODEL, D_PROJ)) * D_MODEL ** -0.5,
        'w_out': nrm(ks[2], (DEPTH, D_MIX, D_MODEL)) * (D_MIX ** -0.5 * BETA),
        'pool_w': nrm(ks[3], (DEPTH, N_POOL_GROUPS, POOL_GROUP, POOL_GROUP)) * POOL_GROUP ** -0.5,
        'pool_scale': 1.0 + 0.1 * nrm(ks[4], (DEPTH, D_POOL)),
        'conv_w': nrm(ks[5], (DEPTH, CONV_WIDTH, D_CONV)) * CONV_WIDTH ** -0.5,
        'cmp_pe_k': 0.1 * nrm(ks[6], (DEPTH, CMP_BLOCK, HEAD_DIM)),
        'cmp_w1_k': nrm(ks[7], (DEPTH, CMP_BLOCK * HEAD_DIM, CMP_HIDDEN)) * (CMP_BLOCK * HEAD_DIM) ** -0.5,
        'cmp_w2_k': nrm(ks[8], (DEPTH, CMP_HIDDEN, HEAD_DIM)) * (2.0 * CMP_HIDDEN ** -0.5),
        'cmp_pe_v': 0.1 * nrm(ks[9], (DEPTH, CMP_BLOCK, HEAD_DIM)),
        'cmp_w1_v': nrm(ks[10], (DEPTH, CMP_BLOCK * HEAD_DIM, CMP_HIDDEN)) * (CMP_BLOCK * HEAD_DIM) ** -0.5,
        'cmp_w2_v': nrm(ks[11], (DEPTH, CMP_HIDDEN, HEAD_DIM)) * (2.0 * CMP_HIDDEN ** -0.5),
        'rel_bias': 0.5 * nrm(ks[12], (N_BUCKETS, N_HEADS)),
        'ln_g': 1.0 + 0.02 * nrm(ks[13], (DEPTH, D_MODEL)),
        'ln_b': 0.02 * nrm(ks[14], (DEPTH, D_MODEL)),
    }


def reference(x, w_in, w_out, pool_w, pool_scale, conv_w, cmp_pe_k, cmp_w1_k, cmp_w2_k,
              cmp_pe_v, cmp_w1_v, cmp_w2_v, rel_bias, ln_g, ln_b):
    h = x
    for l in range(DEPTH):
        h = hybrid_layer(h, w_in[l], w_out[l], pool_w[l], pool_scale[l], conv_w[l],
                         cmp_pe_k[l], cmp_w1_k[l], cmp_w2_k[l],
                         cmp_pe_v[l], cmp_w1_v[l], cmp_w2_v[l],
                         rel_bias, ln_g[l], ln_b[l])
    return h
```

```python
import math
import ml_dtypes
import numpy as np
import concourse.bass as bass
import concourse.mybir as mybir

F32 = mybir.dt.float32
BF16 = mybir.dt.bfloat16
AF = mybir.ActivationFunctionType
ALU = mybir.AluOpType
AX = mybir.AxisListType

COMPUTE = ("pe", "act", "dve", "pool")


class Res:
    __slots__ = ("name", "writers", "readers", "sem", "dma_cnt", "gen_deps", "excl")

    def __init__(self, name):
        self.name = name
        self.writers = []
        self.readers = []
        self.gen_deps = []
        self.excl = False
        self.sem = {}
        self.dma_cnt = {}


class Op:
    __slots__ = ("eng", "fn", "deps", "is_dma", "dres", "dval", "idx", "cnt", "signal")

    def __init__(self, eng, fn, is_dma):
        self.eng = eng
        self.fn = fn
        self.deps = []
        self.is_dma = is_dma
        self.dres = None
        self.dval = 0
        self.cnt = 0
        self.signal = False


class Emitter:
    def __init__(self, nc):
        self.nc = nc
        self.ops = []
        self.res = {}
        self.nres = 0
        self.dma_res = {}
        self.phase = Res("phase")
        self.bar_tile = None

    def R(self, name=None):
        self.nres += 1
        r = Res(name or f"r{self.nres}")
        return r

    def sb(self, name, shape, dtype):
        t = self.nc.alloc_sbuf_tensor(name, list(shape), dtype)
        t.res = None
        return t

    def op(self, eng, fn, reads=(), writes=(), pwrites=(), dma=False, key=None):
        o = Op(eng, fn, dma)
        o.idx = len(self.ops)
        if self.phase not in writes:
            reads = list(reads) + [self.phase]
        xr = [r for r in reads if r.excl]
        if xr:
            reads = [r for r in reads if not r.excl]
            writes = list(writes) + [r for r in xr if r not in writes]
        deps = []
        for r in reads:
            deps.extend((p, "raw") for p in r.writers)
        for r in writes:
            deps.extend((p, "war") for p in r.readers)
            deps.extend((p, "waw") for p in r.writers)
        for r in pwrites:
            if r.readers:
                deps.extend((p, "war") for p in r.readers)
                deps.extend((p, "waw") for p in r.writers if not (p.is_dma and dma))
            else:
                deps.extend((p, "war") for p in r.gen_deps)
                deps.extend((p, "waw") for p in r.writers if not (p.is_dma and dma))
        for r in reads:
            r.readers.append(o)
        for r in writes:
            r.gen_deps = r.readers + r.writers
            r.writers = [o]
            r.readers = []
        for r in pwrites:
            if r.readers:
                r.gen_deps = r.readers + r.writers
                r.writers = [o]
                r.readers = []
            else:
                r.writers.append(o)
        if dma:
            allw = list(writes) + list(pwrites)
            d = key if key is not None else allw[0]
            self.dma_res[id(d)] = d
            d.dma_cnt[eng] = d.dma_cnt.get(eng, 0) + 1
            o.dres = d
            o.dval = 16 * d.dma_cnt[eng]
        best = {}
        for p, kind in deps:
            if p is o:
                continue
            if p.is_dma:
                key = ("d", id(p.dres), p.eng)
                if key not in best or best[key].dval < p.dval:
                    best[key] = p
            else:
                if p.eng == o.eng and not dma:
                    if p.eng == "pe":
                        continue
                key = ("c", p.eng)
                if key not in best or best[key].idx < p.idx:
                    best[key] = p
        o.deps = list(best.values())
        for p in o.deps:
            p.signal = True
        self.ops.append(o)
        return o

    def barrier(self):
        if self.bar_tile is None:
            self.bar_tile = self.nc.alloc_sbuf_tensor("s_bar_tile", [128, 2], F32)
        t = self.bar_tile
        self.op("dve", lambda e: e.memset(t[:], 0.0), writes=[self.phase])

    def emit(self, final_res=()):
        nc = self.nc
        esem = {e: nc.alloc_semaphore(f"s_{e}") for e in COMPUTE}
        cnt = {e: 0 for e in COMPUTE}
        for o in self.ops:
            if o.is_dma:
                if o.eng not in o.dres.sem:
                    o.dres.sem[o.eng] = nc.alloc_semaphore(f"d_{o.dres.name}_{o.eng}")
            elif o.signal:
                cnt[o.eng] += 1
                o.cnt = cnt[o.eng]
        engs = ["pe", "act", "dve", "pool", "sp"]
        per = {e: [o for o in self.ops if o.eng == e] for e in engs}
        final_waits = [(r.sem[e], 16 * r.dma_cnt[e]) for r in self.dma_res.values() for e in r.sem]

        def run(e, eng):
            waited = {}
            for o in per[e]:
                for p in o.deps:
                    if p.is_dma:
                        s, v = p.dres.sem[p.eng], p.dval
                    else:
                        s, v = esem[p.eng], p.cnt
                    if waited.get(s.num, -1) >= v:
                        continue
                    waited[s.num] = v
                    eng.wait_ge(s, v)
                ins = o.fn(eng)
                if o.is_dma:
                    ins.then_inc(o.dres.sem[o.eng], 16)
                elif o.signal:
                    ins.then_inc(esem[o.eng], 1)
            if e == "sp":
                for s, v in final_waits:
                    eng.wait_ge(s, v)

        with nc.Block() as block:
            @block.tensor
            def _(eng):
                run("pe", eng)

            @block.scalar
            def _(eng):
                run("act", eng)

            @block.vector
            def _(eng):
                run("dve", eng)

            @block.gpsimd
            def _(eng):
                run("pool", eng)

            @block.sync
            def _(eng):
                run("sp", eng)
        return {e: len(per[e]) for e in engs}


T = 8192
D = 1024
NCHUNK = 16
LG = 4208
LG3 = 768
LGT = LG + LG3
NEGB = -30000.0
NFM = 1280
NTM = 396


def bucket_np(n):
    n = np.maximum(n, 0)
    nf = np.maximum(n, 1).astype(np.float32)
    large = 16 + (np.log(nf / np.float32(16)) / np.float32(math.log(8.0)) * np.float32(16)).astype(np.int32)
    large = np.minimum(large, 31)
    return np.where(n < 16, n, large)


def host_consts():
    c = {}
    c["cf"] = np.eye(128, dtype=np.float32)
    J = np.eye(128, dtype=np.float32)[::-1].copy()
    I4 = np.tile(np.eye(128, dtype=np.float32), (1, 4))
    ov = np.zeros((512, 128), np.float32)
    for i in range(511):
        for j in range(128):
            o = min(16 * i + 32, 64 * j + 64) - max(16 * i, 64 * j)
            if o > 0:
                ov[i, j] = o / 16.0
    ovc = ov.reshape(4, 128, 128).transpose(1, 0, 2).reshape(128, 512)
    E = np.zeros((128, 8192), np.float32)
    for jt in range(64):
        for kk in range(128):
            j = 2 * jt + (1 if kk >= 64 else 0)
            E[64 + (j % 64), jt * 128 + kk] = 1.0
    c["cb"] = np.concatenate([J, I4, ovc, E], axis=1).astype(ml_dtypes.bfloat16)
    oh = np.zeros((33, LGT), np.float32)
    i = np.arange(LG)
    n = i - 2063
    b = bucket_np(n)
    oh[np.where(n < 0, 32, b), i] = 1.0
    i3 = np.arange(LG3)
    n3 = i3 - 127
    b3 = bucket_np(n3)
    oh[np.where((n3 < 0) | (n3 >= 512), 32, b3), LG + i3] = 1.0
    c["oh"] = oh
    hc = np.zeros((128, 4), np.float32)
    hc[64:, 0] = 1.0
    hc[:64, 1] = 3e9
    hc[:64, 2] = -1e30
    hc[64:, 2] = 4e9
    c["halfc"] = hc
    return c


POOL_WINDOWS = (2, 4, 8, 16)


def host_layer_inputs(inp, l, g):
    w_in = inp["w_in"][l]
    offs = np.cumsum([0, 256, 256, 256, 256, 256, 256, 512, 128, 128, 128, 128, 128, 128, 24, 512])
    (o_pv, o_pz, o_cb, o_cc, o_cx, o_cz, o_q, o_kc, o_vc, o_ks, o_vs, o_kw, o_vw, o_g, o_z) = offs[:15]
    s = slice
    cols = []
    for o in (o_pv, o_pz, o_cb, o_cc, o_cx, o_cz):
        cols.append(np.arange(o + 128 * g, o + 128 * g + 128))
    for r in range(4):
        cols.append(np.arange(o_q + 64 * (4 * g + r), o_q + 64 * (4 * g + r) + 64))
    cols.append(np.arange(o_kc + 64 * g, o_kc + 64 * g + 64))
    cols.append(np.arange(o_vc + 64 * g, o_vc + 64 * g + 64))
    cols.append(np.arange(o_ks + 64 * g, o_ks + 64 * g + 64))
    cols.append(np.arange(o_kw + 64 * g, o_kw + 64 * g + 64))
    cf = np.concatenate(cols)
    assert cf.size == NFM
    ct = np.concatenate([
        np.arange(o_vs + 64 * g, o_vs + 64 * g + 64),
        np.arange(o_vw + 64 * g, o_vw + 64 * g + 64),
        np.arange(o_g + 12 * g, o_g + 12 * g + 12),
        np.arange(o_z + 256 * g, o_z + 256 * g + 256),
    ])
    assert ct.size == NTM
    d = {}
    d["wf"] = np.ascontiguousarray(w_in[:, cf])
    d["wt"] = np.ascontiguousarray(w_in[:, ct])
    pw = np.zeros((128, 128), np.float32)
    pw[0:64, 0:64] = inp["pool_w"][l, 2 * g]
    pw[64:128, 64:128] = inp["pool_w"][l, 2 * g + 1]
    d["poolw"] = pw
    pc = np.zeros((128, 8), np.float32)
    for h in range(2):
        w = POOL_WINDOWS[2 * g + h]
        pc[64 * h:64 * h + 64, POOL_WINDOWS.index(w)] = 1.0 / w
    pc[:, 4] = inp["pool_scale"][l, 128 * g:128 * g + 128]
    for k in range(3):
        pc[:, 5 + k] = inp["conv_w"][l, k, 128 * g:128 * g + 128]
    d["pcoef"] = pc
    pcorr = np.ones((128, 16), np.float32)
    for h in range(2):
        w = POOL_WINDOWS[2 * g + h]
        for t in range(16):
            pcorr[64 * h:64 * h + 64, t] = w / min(t + 1, w)
    d["pcorr"] = pcorr
    w1k = inp["cmp_w1_k"][l].reshape(32, 64, 128).transpose(1, 0, 2).reshape(64, 32 * 128)
    w1v = inp["cmp_w1_v"][l].reshape(32, 64, 128).transpose(1, 0, 2).reshape(64, 32 * 128)
    d["w1kv"] = np.ascontiguousarray(np.concatenate([w1k, w1v], axis=0))
    d["pekv"] = np.ascontiguousarray(np.concatenate([inp["cmp_pe_k"][l].T, inp["cmp_pe_v"][l].T], axis=0))
    d["w2kv"] = np.ascontiguousarray(np.concatenate([inp["cmp_w2_k"][l], inp["cmp_w2_v"][l]], axis=1))
    rb = np.full((33, 4), NEGB, np.float32)
    rb[:32] = inp["rel_bias"][:, 4 * g:4 * g + 4]
    d["relb"] = rb
    return d


def build_F(L=2, G=2, nchunks=NCHUNK):
    nc = bass.Bass("TRN2", target_bir_lowering=False)
    k = Emitter(nc)

    def din(name, shape, dt=F32):
        return nc.dram_tensor(name, shape, dt, kind="ExternalInput").ap()

    def dout(name, shape, dt=F32):
        return nc.dram_tensor(name, shape, dt, kind="ExternalOutput").ap()

    TT = 512 * nchunks
    x_d = din("x", [TT, D])
    wf_all = din("wf", [L * G * D, NFM])
    wt_all = din("wt", [L * G * D, NTM])
    poolw_all = din("poolw", [L * G * 128, 128])
    pcoef_all = din("pcoef", [L * G * 128, 8])
    pcorr_all = din("pcorr", [G * 128, 16])
    w1kv_all = din("w1kv", [L * 128, 4096])
    pekv_all = din("pekv", [L * 128, 32])
    w2kv_all = din("w2kv", [L * 128, 128])
    relb_all = din("relb", [G * 33, 4])
    oh_d = din("oh", [33, LGT])
    cf_d = din("cf", [128, 128])
    cb_d = din("cb", [128, 128 + 512 + 512 + 8192], BF16)
    halfc_d = din("halfc", [128, 4])
    wo_all = din("wo", [L * D, D])
    lng_t = nc.dram_tensor("lng", [L, D], F32, kind="ExternalInput")
    lnb_t = nc.dram_tensor("lnb", [L, D], F32, kind="ExternalInput")
    out_d = dout("out", [TT, D])
    gd_t = nc.dram_tensor("gd", [4, LGT], BF16)
    gd = gd_t.ap()
    ys_d = nc.dram_tensor("ys", [D, TT], BF16).ap()
    x1_d = nc.dram_tensor("x1", [TT, D], F32).ap()
    r_gd = k.R("gd")
    r_ys = k.R("ys")
    r_x1 = k.R("x1")
    r_out = k.R("out")

    def sb(name, shape, dt):
        return nc.alloc_sbuf_tensor("s_" + name, list(shape), dt)

    def E(eng, name, kw, reads=(), writes=(), pwrites=()):
        k.op(eng, lambda e: getattr(e, name)(**kw), reads=reads, writes=writes, pwrites=pwrites)

    Wf = sb("Wf", [128, 8, NFM], BF16); r_Wf = k.R("Wf")
    Wt = sb("Wt", [128, 8, NTM], BF16); r_Wt = k.R("Wt")
    ident = sb("ident", [128, 128], F32); r_ident = k.R("ident")
    identb = sb("identb", [128, 128], BF16); r_identb = k.R("identb")
    cbt = sb("cbt", [128, 128 + 512 + 512], BF16); r_cbt = k.R("cbt")
    Jm = cbt[:, 0:128]
    I4 = cbt[:, 128:640]
    ovt = cbt[:, 640:1152]
    KE = sb("KE", [128, T], BF16); r_KEe = k.R("KEe")
    r_KE = [k.R(f"KE{i}") for i in range(NCHUNK)]
    KW = sb("KW", [64, T], BF16); r_KW = [k.R(f"KW{i}") for i in range(NCHUNK)]
    kvT = sb("kvT", [128, T + 32], BF16); r_kvT = [k.R(f"kvT{i}") for i in range(NCHUNK + 1)]
    VS = sb("VS", [128, 64, 65], BF16); r_VS = [k.R(f"VS{i}") for i in range(NCHUNK)]; r_VSo = k.R("VSo")
    VW = sb("VW", [128, 64, 65], BF16); r_VW = [k.R(f"VW{i}") for i in range(NCHUNK)]; r_VWo = k.R("VWo")
    kcT = sb("kcT", [64, 544], BF16); r_kcT = k.R("kcT")
    vcT = sb("vcT", [64, 544], BF16); r_vcT = k.R("vcT")
    vcp = sb("vcp", [128, 4, 65], BF16); r_vcp = k.R("vcp")
    Rres = sb("Rres", [128, 7, 512], BF16); r_Rres = k.R("Rres")
    NR1 = 3
    R1t = [sb(f"R1t{i}", [128, 512], BF16) for i in range(NR1)]; r_R1t = [k.R(f"R1t{i}") for i in range(NR1)]
    poolw = sb("poolw", [128, 128], F32); r_poolw = k.R("poolw")
    poolwb = sb("poolwb", [128, 128], BF16); r_poolwb = k.R("poolwb")
    pcoef = sb("pcoef", [128, 8], F32); r_pcoef = k.R("pcoef")
    pcorr = sb("pcorr", [128, 16], F32); r_pcorr = k.R("pcorr")
    halfc = sb("halfc", [128, 4], F32); r_halfc = k.R("halfc")
    w1kv = sb("w1kv", [128, 32, 128], BF16); r_w1kv = k.R("w1kv")
    pekv = sb("pekv", [128, 32], F32); r_pekv = k.R("pekv")
    pekvb = sb("pekvb", [128, 32], BF16); r_pekvb = k.R("pekvb")
    w2kv = sb("w2kv", [128, 128], F32); r_w2kv = k.R("w2kv")
    w2kvb = sb("w2kvb", [128, 128], BF16); r_w2kvb = k.R("w2kvb")
    hbias = sb("hbias", [128, 2], F32); r_hbias = k.R("hbias")
    relb = sb("relb", [33, 4], F32); r_relb = k.R("relb")
    crt = sb("crt", [4, 1], F32); r_crt = k.R("crt")
    Gc = [sb(f"Gc{i}", [4, 512], BF16) for i in range(2)]; r_Gc = [k.R(f"Gc{i}") for i in range(2)]
    stg = [sb(f"stg{i}", [128, 4096], F32) for i in range(2)]
    r_stg = [k.R(f"stg{i}") for i in range(2)]
    xT = sb("xT", [128, 8, 512], BF16); r_xT = k.R("xT")
    RQA = sb("RQA", [128, 4, 512], BF16); r_RQAq = [k.R(f"RQAq{i}") for i in range(4)]; r_RQAs = [k.R(f"RQAs{i}") for i in range(4)]
    RQB = sb("RQB", [128, 4, 512], BF16); r_RQBq = [k.R(f"RQBq{i}") for i in range(4)]; r_RQBs = [k.R(f"RQBs{i}") for i in range(4)]
    gsig = sb("gsig", [128, 4, 12], F32); r_gsig = [k.R(f"gsig{i}") for i in range(4)]
    zs = sb("zs", [128, 4, 256], F32); r_zs = [k.R(f"zs{i}") for i in range(4)]
    pv = sb("pv", [128, 528], F32); r_pv = k.R("pv")
    sA = sb("sA", [128, 528], F32); r_sA = k.R("sA")
    sB = sb("sB", [128, 528], F32); r_sB = k.R("sB")
    pm = sb("pm", [128, 512], F32); r_pm = k.R("pm")
    pmb = sb("pmb", [128, 512], BF16); r_pmb = k.R("pmb")
    pz = sb("pz", [128, 512], F32); r_pz = k.R("pz")
    cbb = sb("cbb", [128, 512], F32); r_cbb = k.R("cbb")
    ccc = sb("ccc", [128, 512], F32); r_ccc = k.R("ccc")
    uu = sb("uu", [128, 514], F32); r_uu = k.R("uu")
    cz = sb("cz", [128, 512], F32); r_cz = k.R("cz")
    ypool = sb("ypool", [128, 512], BF16); r_ypool = k.R("ypool")
    yconvb = sb("yconvb", [128, 512], BF16); r_yconvb = k.R("yconvb")
    yTn = sb("yTn", [128, 2, 128], BF16); r_yTn = k.R("yTn")
    yconv = sb("yconv", [128, 512], F32); r_yconv = k.R("yconv")
    hkv = sb("hkv", [128, 128], BF16); r_hkv = k.R("hkv")
    Xc = sb("Xc", [128, 16, 66], BF16); r_Xc = k.R("Xc")
    NE = 4
    Et = [sb(f"E{i}", [128, 512], BF16) for i in range(NE)]; r_E = [k.R(f"E{i}") for i in range(NE)]
    OTs = sb("OTs", [65, 512], F32); r_OTs = k.R("OTs")
    sc = sb("sc", [128, 128], F32); r_sc = k.R("sc")
    sc2 = sb("sc2", [128, 128], F32); r_sc2 = k.R("sc2")
    mx = sb("mx", [128, 16], F32); r_mx = k.R("mx")
    th = sb("th", [128, 1], F32); r_th = k.R("th")
    seln = sb("seln", [128, 192], BF16); r_seln = k.R("seln")
    den = sb("den", [128, 12], F32); r_den = k.R("den")
    coef = sb("coef", [128, 12], F32); r_coef = k.R("coef")
    yacc = sb("yacc", [128, 2, 256], F32); r_yacc = [k.R("yacc0"), k.R("yacc1")]

    pb = [nc.alloc_psum_tensor(f"pb{i}", [128, 512], F32) for i in range(8)]
    r_pb = [k.R(f"pb{i}") for i in range(8)]
    for r_ in r_pb:
        r_.excl = True
    nbs = [0]

    def nb():
        i = nbs[0]
        nbs[0] = (i + 1) % 3
        return i

    cnt2 = [0]

    def evac_eng():
        cnt2[0] += 1
        return "act" if cnt2[0] % 2 else "dve"

    def copy_op(eng, out, in_, reads, writes=(), pwrites=(), scale=None):
        if eng == "act":
            kw = dict(out=out, in_=in_, func=AF.Copy)
            if scale is not None:
                kw["scale"] = scale
            E("act", "activation", kw, reads, writes, pwrites)
        else:
            if scale is None:
                E(eng, "tensor_copy", dict(out=out, in_=in_), reads, writes, pwrites)
            else:
                E(eng, "tensor_scalar_mul", dict(out=out, in0=in_, scalar1=scale), reads, writes, pwrites)

    def dma(eng, out, in_, reads, writes=(), pwrites=(), key=None):
        k.op(eng, lambda e: e.dma_start(out=out, in_=in_), reads=reads, writes=writes, pwrites=pwrites, dma=True, key=key)

    def mm(out, lhsT, rhs, start, stop, reads, writes=(), pwrites=()):
        k.op("pe", lambda e: e.matmul(out, lhsT, rhs, start=start, stop=stop), reads=reads, writes=writes, pwrites=pwrites)

    def tr(out, in_, reads, writes=(), pwrites=()):
        k.op("pe", lambda e: e.transpose(out, in_, ident[:]), reads=list(reads) + [r_ident], writes=writes, pwrites=pwrites)

    si = [0]

    def next_stg():
        s = si[0] % 2
        si[0] += 1
        return s

    def A_body(l, g, xsrc, xreads):
        lg = l * G + g
        dma("sp", ident[:], cf_d[:, :], [], [r_ident])
        dma("sp", cbt[:], cb_d[:, 0:1152], [], [r_cbt])
        dma("sp", KE[64:128, :], cb_d[64:128, 1152:1152 + 8192], [], [r_KEe])
        dma("pool", poolw[:], poolw_all[128 * lg:128 * lg + 128, :], [], [r_poolw])
        dma("pool", pcoef[:], pcoef_all[128 * lg:128 * lg + 128, :], [], [r_pcoef])
        dma("pool", pcorr[:], pcorr_all[128 * g:128 * g + 128, :], [], [r_pcorr])
        dma("pool", halfc[:], halfc_d[:, :], [], [r_halfc])
        dma("pool", pekv[:], pekv_all[128 * l:128 * l + 128, :], [], [r_pekv])
        dma("pool", w2kv[:], w2kv_all[128 * l:128 * l + 128, :], [], [r_w2kv])
        dma("pool", relb[:], relb_all[33 * g:33 * g + 33, :], [], [r_relb])
        copy_op("dve", identb[:], ident[:], [r_ident], [r_identb])
        copy_op("dve", poolwb[:], poolw[:], [r_poolw], [r_poolwb])
        copy_op("dve", pekvb[:], pekv[:], [r_pekv], [r_pekvb])
        copy_op("dve", w2kvb[:], w2kv[:], [r_w2kv], [r_w2kvb])
        E("pool", "memset", dict(ap=kvT[:], constant=0.0), writes=r_kvT)
        E("pool", "memset", dict(ap=kcT[:], constant=0.0), writes=[r_kcT])
        E("pool", "memset", dict(ap=vcT[:], constant=0.0), writes=[r_vcT])
        E("pool", "memset", dict(ap=vcp[:], constant=0.0), writes=[r_vcp])
        E("pool", "memset", dict(ap=vcp[:, :, 64:65], constant=1.0), writes=[r_vcp])
        E("pool", "memset", dict(ap=VS[:, :, 64:65], constant=1.0), writes=[r_VSo])
        E("pool", "memset", dict(ap=VW[:, :, 64:65], constant=1.0), writes=[r_VWo])
        E("pool", "memset", dict(ap=seln[:], constant=0.0), writes=[r_seln])
        E("pool", "memset", dict(ap=pv[:, 0:16], constant=0.0), pwrites=[r_pv])
        E("pool", "memset", dict(ap=uu[:, 0:2], constant=0.0), pwrites=[r_uu])
        E("pool", "memset", dict(ap=KE[0:64, :], constant=0.0), writes=r_KE)
        E("pool", "memset", dict(ap=KW[:], constant=0.0), writes=r_KW)
        E("pool", "memset", dict(ap=VS[:, :, 0:64], constant=0.0), writes=r_VS)
        E("pool", "memset", dict(ap=VW[:, :, 0:64], constant=0.0), writes=r_VW)


        def load_cast(dst_fn, src_fn, ncols, r_dst):
            for j in range(8):
                s = next_stg()
                dma("sp" if j % 2 == 0 else "pool", stg[s][:, 0:ncols], src_fn(j), [], [r_stg[s]])
                copy_op(evac_eng(), dst_fn(j), stg[s][:, 0:ncols], [r_stg[s]], pwrites=[r_dst])

        load_cast(lambda j: Wf[:, j, :], lambda j: wf_all[D * lg + 128 * j:D * lg + 128 * j + 128, :], NFM, r_Wf)
        load_cast(lambda j: Wt[:, j, :], lambda j: wt_all[D * lg + 128 * j:D * lg + 128 * j + 128, :], NTM, r_Wt)
        s = next_stg()
        dma("sp", stg[s][:, 0:4096], w1kv_all[128 * l:128 * l + 128, :], [], [r_stg[s]])
        copy_op("act", w1kv[:].rearrange("p a b -> p (a b)"), stg[s][:, 0:4096], [r_stg[s]], [r_w1kv])

        bk = 7
        for l_ in range(32):
            mm(pb[bk][:, 0:1], w1kv[0:64, l_, :], pekvb[0:64, l_:l_ + 1], l_ == 0, l_ == 31, [r_w1kv, r_pekvb], pwrites=[r_pb[bk]])
        for l_ in range(32):
            mm(pb[6][:, 0:1], w1kv[64:128, l_, :], pekvb[64:128, l_:l_ + 1], l_ == 0, l_ == 31, [r_w1kv, r_pekvb], pwrites=[r_pb[6]])
        copy_op("dve", hbias[:, 0:1], pb[bk][:, 0:1], [r_pb[bk]], pwrites=[r_hbias])
        copy_op("dve", hbias[:, 1:2], pb[6][:, 0:1], [r_pb[6]], pwrites=[r_hbias])

        nchg = (LGT + 511) // 512
        order = [4063 // 512] + [c for c in range(nchg) if c != 4063 // 512]
        for ii, cch in enumerate(order):
            c0 = cch * 512
            w = min(512, LGT - c0)
            s = next_stg()
            dma("sp", stg[s][0:33, 0:w], oh_d[:, c0:c0 + w], [], [r_stg[s]])
            b = nb()
            mm(pb[b][0:4, 0:w], relb[:, :], stg[s][0:33, 0:w], True, True, [r_relb, r_stg[s]], [r_pb[b]])
            if ii == 0:
                copy_op("dve", crt[:], pb[b][0:4, 4063 - c0:4063 - c0 + 1], [r_pb[b]], [r_crt])
            gi = ii % 2
            E("dve", "tensor_scalar", dict(out=Gc[gi][:, 0:w], in0=pb[b][0:4, 0:w], scalar1=crt[:, 0:1], scalar2=None, op0=ALU.subtract),
              [r_pb[b], r_crt], [r_Gc[gi]])
            dma("sp", gd[:, c0:c0 + w], Gc[gi][:, 0:w], [r_Gc[gi]], pwrites=[r_gd], key=r_Gc[gi])

        def hankel(base, step):
            return bass.AP(gd_t, base, [[step, 128], [LGT, 4], [1, 128]])

        bases = [(1936 + 128 * dl, 1) for dl in range(2)] + [(LG + 128 * dl, 1) for dl in range(5)]
        for i, (base, step) in enumerate(bases):
            dma("sp" if i % 2 == 0 else "pool", Rres[:, i, :].rearrange("p (r t) -> p r t", r=4), hankel(base, step), [r_gd], pwrites=[r_Rres])
        r1c = [0]

        for ci in range(nchunks):
            t0 = 512 * ci
            s = next_stg()
            xt = stg[s]
            for a in range(4):
                dma("sp" if a % 2 == 0 else "pool", xt[:, 1024 * a:1024 * a + 1024], xsrc[t0 + 128 * a:t0 + 128 * a + 128, :], xreads, pwrites=[r_stg[s]])
            for j in range(8):
                b = nb()
                for a in range(4):
                    tr(pb[b][:, 128 * a:128 * a + 128], xt[:, 1024 * a + 128 * j:1024 * a + 128 * j + 128], [r_stg[s]], pwrites=[r_pb[b]])
                copy_op(evac_eng(), xT[:, j, :], pb[b][:, :], [r_pb[b]], pwrites=[r_xT])


            def fm_block(col0, M):
                b = nb()
                for j in range(8):
                    mm(pb[b][0:M, :], Wf[:, j, col0:col0 + M], xT[:, j, :], j == 0, j == 7, [r_Wf, r_xT], pwrites=[r_pb[b]])
                return b

            b = fm_block(0, 128)
            copy_op("dve", pv[:, 16:528], pb[b][:, :], [r_pb[b]], pwrites=[r_pv])
            b = fm_block(128, 128)
            E("act", "activation", dict(out=pz[:], in_=pb[b][:, :], func=AF.Silu), [r_pb[b]], [r_pz])
            b = fm_block(256, 128)
            copy_op("dve", cbb[:], pb[b][:, :], [r_pb[b]], [r_cbb])
            b = fm_block(384, 128)
            copy_op("act", ccc[:], pb[b][:, :], [r_pb[b]], [r_ccc])
            b = fm_block(512, 128)
            E("dve", "tensor_tensor", dict(out=uu[:, 2:514], in0=ccc[:], in1=pb[b][:, :], op=ALU.mult), [r_ccc, r_pb[b]], pwrites=[r_uu])
            b = fm_block(640, 128)
            E("act", "activation", dict(out=cz[:], in_=pb[b][:, :], func=AF.Silu), [r_pb[b]], [r_cz])
            for r in range(4):
                b = fm_block(768 + 64 * r, 64)
                src = pb[b][0:64, :].rearrange("p (q t) -> p q t", q=4)
                copy_op("act", RQA[0:64, :, 128 * r:128 * r + 128], src, [r_pb[b]], pwrites=r_RQAq, scale=0.125)
                copy_op("dve", RQB[0:64, :, 128 * r:128 * r + 128], src, [r_pb[b]], pwrites=r_RQBq, scale=0.125)
            b = fm_block(1024, 128)
            copy_op("act", kvT[:, t0:t0 + 512], pb[b][:, :], [r_pb[b]], [r_kvT[ci]])
            b = fm_block(1152, 64)
            copy_op("dve", KE[0:64, t0:t0 + 512], pb[b][0:64, :], [r_pb[b]], [r_KE[ci]])
            b = fm_block(1216, 64)
            copy_op("act", KW[0:64, t0:t0 + 512], pb[b][0:64, :], [r_pb[b]], [r_KW[ci]])
            for a in range(4):
                b = nb()
                jt = 4 * ci + a
                for j in range(8):
                    mm(pb[b][:, 0:NTM], xT[:, j, 128 * a:128 * a + 128], Wt[:, j, :], j == 0, j == 7, [r_Wt, r_xT], pwrites=[r_pb[b]])
                copy_op("dve", VS[:, jt, 0:64], pb[b][:, 0:64], [r_pb[b]], pwrites=[r_VS[ci]])
                copy_op("dve", VW[:, jt, 0:64], pb[b][:, 64:128], [r_pb[b]], pwrites=[r_VW[ci]])
                E("act", "activation", dict(out=gsig[:, a, :], in_=pb[b][:, 128:140], func=AF.Sigmoid), [r_pb[b]], [r_gsig[a]])
                E("act", "activation", dict(out=zs[:, a, :], in_=pb[b][:, 140:396], func=AF.Silu), [r_pb[b]], [r_zs[a]])

            E("dve", "tensor_tensor", dict(out=sA[:, 1:528], in0=pv[:, 1:528], in1=pv[:, 0:527], op=ALU.add), [r_pv], [r_sA])
            E("dve", "tensor_scalar", dict(out=pm[:], in0=sA[:, 16:528], scalar1=pcoef[:, 0:1], scalar2=None, op0=ALU.mult), [r_sA, r_pcoef], [r_pm])
            E("dve", "tensor_tensor", dict(out=sB[:, 3:528], in0=sA[:, 3:528], in1=sA[:, 1:526], op=ALU.add), [r_sA], [r_sB])
            E("dve", "scalar_tensor_tensor", dict(out=pm[:], in0=sB[:, 16:528], scalar=pcoef[:, 1:2], in1=pm[:], op0=ALU.mult, op1=ALU.add), [r_sB, r_pcoef, r_pm], [r_pm])
            E("dve", "tensor_tensor", dict(out=sA[:, 7:528], in0=sB[:, 7:528], in1=sB[:, 3:524], op=ALU.add), [r_sB], [r_sA])
            E("dve", "scalar_tensor_tensor", dict(out=pm[:], in0=sA[:, 16:528], scalar=pcoef[:, 2:3], in1=pm[:], op0=ALU.mult, op1=ALU.add), [r_sA, r_pcoef, r_pm], [r_pm])
            E("dve", "tensor_tensor", dict(out=sB[:, 15:528], in0=sA[:, 15:528], in1=sA[:, 7:520], op=ALU.add), [r_sA], [r_sB])
            E("dve", "scalar_tensor_tensor", dict(out=pm[:], in0=sB[:, 16:528], scalar=pcoef[:, 3:4], in1=pm[:], op0=ALU.mult, op1=ALU.add), [r_sB, r_pcoef, r_pm], [r_pm])
            if ci == 0:
                E("dve", "tensor_tensor", dict(out=pm[:, 0:16], in0=pm[:, 0:16], in1=pcorr[:], op=ALU.mult), [r_pm, r_pcorr], [r_pm])
            E("dve", "tensor_tensor", dict(out=pmb[:], in0=pm[:], in1=pv[:, 16:528], op=ALU.subtract), [r_pm, r_pv], [r_pmb])
            b = nb()
            mm(pb[b][:, :], poolwb[:], pmb[:], True, True, [r_poolwb, r_pmb], [r_pb[b]])
            E("dve", "scalar_tensor_tensor", dict(out=ypool[:], in0=pb[b][:, :], scalar=pcoef[:, 4:5], in1=pz[:], op0=ALU.mult, op1=ALU.mult),
              [r_pb[b], r_pcoef, r_pz], [r_ypool])
            dma("sp", ys_d[512 * g:512 * g + 128, t0:t0 + 512], ypool[:], [r_ypool], pwrites=[r_ys], key=r_ypool)
            E("pool", "tensor_copy", dict(out=pv[:, 0:16], in_=pv[:, 512:528]), [r_pv], [r_pv])
            E("dve", "tensor_scalar", dict(out=yconv[:], in0=uu[:, 0:512], scalar1=pcoef[:, 5:6], scalar2=None, op0=ALU.mult), [r_uu, r_pcoef], [r_yconv])
            E("dve", "scalar_tensor_tensor", dict(out=yconv[:], in0=uu[:, 1:513], scalar=pcoef[:, 6:7], in1=yconv[:], op0=ALU.mult, op1=ALU.add), [r_uu, r_pcoef, r_yconv], [r_yconv])
            E("dve", "scalar_tensor_tensor", dict(out=yconv[:], in0=uu[:, 2:514], scalar=pcoef[:, 7:8], in1=yconv[:], op0=ALU.mult, op1=ALU.add), [r_uu, r_pcoef, r_yconv], [r_yconv])
            E("dve", "tensor_tensor", dict(out=yconv[:], in0=yconv[:], in1=cbb[:], op=ALU.mult), [r_yconv, r_cbb], [r_yconv])
            E("dve", "tensor_tensor", dict(out=yconvb[:], in0=yconv[:], in1=cz[:], op=ALU.mult), [r_yconv, r_cz], [r_yconvb])
            dma("sp", ys_d[512 * g + 128:512 * g + 256, t0:t0 + 512], yconvb[:], [r_yconvb], pwrites=[r_ys], key=r_yconvb)
            E("pool", "tensor_copy", dict(out=uu[:, 0:2], in_=uu[:, 512:514]), [r_uu], [r_uu])

            n0 = max(0, 32 * ci - 32)
            n1 = 32 * ci + 31
            cn = n1 - n0 + 1
            b = nb()
            rk = [r_kvT[max(ci - 1, 0)], r_kvT[ci], r_kvT[min(ci + 1, NCHUNK)]]
            E("dve", "tensor_copy", dict(out=Xc[:, :, 0:cn + 1], in_=kvT[:, 16 * n0:16 * n0 + 16 * (cn + 1)].rearrange("p (m l) -> p l m", l=16)), rk, [r_Xc])
            bv = nb()
            hb = (b, bv)
            for half, p0 in enumerate((0, 64)):
                for l_ in range(32):
                    rhs = Xc[p0:p0 + 64, l_ % 16, (l_ // 16):(l_ // 16) + cn]
                    mm(pb[hb[half]][:, 0:cn], w1kv[p0:p0 + 64, l_, :], rhs, l_ == 0, l_ == 31, [r_w1kv, r_Xc], pwrites=[r_pb[hb[half]]])
            for half in range(2):
                E("act", "activation", dict(out=hkv[:, 64 * half:64 * half + cn], in_=pb[hb[half]][:, 0:cn], func=AF.Silu, bias=hbias[:, half:half + 1]),
                  [r_pb[hb[half]], r_hbias], pwrites=[r_hkv])
            b2 = nb()
            mm(pb[b2][0:64, 0:cn], w2kvb[:, 0:64], hkv[:, 0:cn], True, True, [r_w2kvb, r_hkv], pwrites=[r_pb[b2]])
            mm(pb[b2][0:64, 64:64 + cn], w2kvb[:, 64:128], hkv[:, 64:64 + cn], True, True, [r_w2kvb, r_hkv], pwrites=[r_pb[b2]])
            copy_op("dve", kcT[:, n0:n0 + cn], pb[b2][0:64, 0:cn], [r_pb[b2]], [r_kcT])
            copy_op("dve", vcT[:, n0:n0 + cn], pb[b2][0:64, 64:64 + cn], [r_pb[b2]], [r_vcT])
            for c in sorted(set((n0 // 128, n1 // 128))):
                b3 = nb()
                mm(pb[b3][:, 0:64], vcT[:, 128 * c:128 * c + 128], identb[0:64, 0:64], True, True, [r_vcT, r_identb], [r_pb[b3]])
                copy_op("dve", vcp[:, c, 0:64], pb[b3][:, 0:64], [r_pb[b3]], [r_vcp])

            ecnt = [0]
            pend = []
            LOOKAHEAD = 2

            def tile_attn(lhsT, lhs_reads, rhs, rhs_reads, bias_ap, bias_reads, Obank, vrhs, v_reads, first, last, extra=None):
                b = nb()
                mm(pb[b][:, :], lhsT, rhs, True, bias_ap is None, list(lhs_reads) + list(rhs_reads), [r_pb[b]])
                if bias_ap is not None:
                    mm(pb[b][:, :], Jm, bias_ap, False, True, [r_cbt] + list(bias_reads), pwrites=[r_pb[b]])
                ei = ecnt[0] % NE
                ecnt[0] += 1
                E("act", "activation", dict(out=Et[ei][:], in_=pb[b][:, :], func=AF.Exp), [r_pb[b]], [r_E[ei]])

                def stage2():
                    mm(pb[Obank][0:65, :], vrhs, Et[ei][:, :], first, last, [r_E[ei]] + list(v_reads), pwrites=[r_pb[Obank]])
                    if extra is not None:
                        extra(ei)

                pend.append(stage2)
                while len(pend) > LOOKAHEAD:
                    pend.pop(0)()

            def flush():
                while pend:
                    pend.pop(0)()

            def conv(bank):
                copy_op("act", OTs[:], pb[bank][0:65, :], [r_pb[bank]], [r_OTs])
                for r in range(4):
                    k.op("pe", lambda e, r=r: e.transpose(pb[3][:, 65 * r:65 * r + 65], OTs[0:65, 128 * r:128 * r + 128], ident[0:65, 0:65]),
                         reads=[r_OTs, r_ident], pwrites=[r_pb[3]])

            def fold(bi, bank, ya, ry, qb, first, do_conv=True):
                if do_conv:
                    conv(bank)
                bank = 3
                Ob = pb[bank][:, 0:260].rearrange("p (r c) -> p r c", c=65)
                gs = gsig[:, qb, :].rearrange("p (r b) -> p r b", b=3)
                if bi > 0:
                    E("dve", "tensor_scalar_max", dict(out=den[:, 4 * bi:4 * bi + 4], in0=Ob[:, :, 64], scalar1=1e-30), [r_pb[bank]], pwrites=[r_den])
                    E("dve", "reciprocal", dict(out=den[:, 4 * bi:4 * bi + 4], in_=den[:, 4 * bi:4 * bi + 4]), [r_den], pwrites=[r_den])
                E("dve", "tensor_tensor", dict(out=coef[:, 4 * bi:4 * bi + 4], in0=den[:, 4 * bi:4 * bi + 4], in1=gs[:, :, bi], op=ALU.mult),
                  [r_den, r_gsig[qb]], pwrites=[r_coef])
                for r in range(4):
                    if first:
                        E("dve", "tensor_scalar", dict(out=ya[:, 64 * r:64 * r + 64], in0=Ob[:, r, 0:64], scalar1=coef[:, 4 * bi + r:4 * bi + r + 1], scalar2=None, op0=ALU.mult),
                          [r_pb[bank], r_coef], pwrites=[ry])
                    else:
                        E("dve", "scalar_tensor_tensor", dict(out=ya[:, 64 * r:64 * r + 64], in0=Ob[:, r, 0:64], scalar=coef[:, 4 * bi + r:4 * bi + r + 1], in1=ya[:, 64 * r:64 * r + 64], op0=ALU.mult, op1=ALU.add),
                          [r_pb[bank], r_coef, ry], pwrites=[ry])

            def make_deferred(qi, qb, RA, RB, ya, ry):
                def run():
                    for jt in range(qi + 1):
                        if jt < 32:
                            rhs, rr = RA, [r_RQAq[qb], r_RQAs[qb]]
                        else:
                            rhs, rr = RB, [r_RQBq[qb], r_RQBs[qb]]
                        bias_ap = Rres[:, qi - jt, :] if qi - jt <= 1 else None
                        tile_attn(KE[:, 128 * jt:128 * jt + 128], [r_KE[jt // 4], r_KEe], rhs, rr, bias_ap, [r_Rres], 5,
                                  VS[:, jt, :], [r_VS[jt // 4], r_VSo], jt == 0, jt == qi)
                    flush()
                    fold(1, 5, ya, ry, qb, False)
                    E("dve", "tensor_tensor", dict(out=ya, in0=ya, in1=zs[:, qb, :], op=ALU.mult), [ry, r_zs[qb]], [ry])
                    bt_ = nb()
                    tr(pb[bt_][:, 0:128], ya[:, 0:128], [ry], pwrites=[r_pb[bt_]])
                    tr(pb[bt_][:, 128:256], ya[:, 128:256], [ry], pwrites=[r_pb[bt_]])
                    copy_op("act", yTn[:].rearrange("p a b -> p (a b)"), pb[bt_][:, 0:256], [r_pb[bt_]], [r_yTn])
                    q0 = 128 * qi
                    dma("sp", ys_d[512 * g + 256:512 * g + 512, q0:q0 + 128].rearrange("(i c) t -> c i t", c=128), yTn[:], [r_yTn], pwrites=[r_ys], key=r_yTn)
                return run

            deferred = None
            for qb in range(4):
                qi = 4 * ci + qb
                RA = RQA[:, qb, :]
                RB = RQB[:, qb, :]
                ya = yacc[:, qi % 2, :]
                ry = r_yacc[qi % 2]
                cmax = qi // 16
                for c in range(cmax + 1):
                    m = qi - 16 * c
                    bias_ap, bias_reads = None, []
                    if m <= 16:
                        ri = r1c[0] % NR1
                        r1c[0] += 1
                        dma("pool", R1t[ri][:].rearrange("p (r t) -> p r t", r=4), hankel(128 * m, 16), [r_gd], [r_R1t[ri]])
                        bias_ap, bias_reads = R1t[ri][:], [r_R1t[ri]]

                    def extra(ei, c=c, cmax=cmax):
                        for r in range(4):
                            mm(pb[7][:, 128 * r:128 * r + 128], Et[ei][:, 128 * r:128 * r + 128], ovt[:, 128 * c:128 * c + 128], c == 0 and r == 0, c == cmax and r == 3,
                               [r_E[ei], r_cbt], pwrites=[r_pb[7]])

                    tile_attn(kcT[:, 128 * c:128 * c + 128], [r_kcT], RA[0:64, :], [r_RQAq[qb]], bias_ap, bias_reads, 4, vcp[:, c, :], [r_vcp], c == 0, c == cmax, extra)
                flush()
                conv(4)
                O1 = pb[3][:, 0:260].rearrange("p (r c) -> p r c", c=65)
                E("dve", "tensor_scalar_max", dict(out=den[:, 0:4], in0=O1[:, :, 64], scalar1=1e-30), [r_pb[3]], pwrites=[r_den])
                E("dve", "reciprocal", dict(out=den[:, 0:4], in_=den[:, 0:4]), [r_den], pwrites=[r_den])
                E("pool", "memset", dict(ap=sc[:], constant=-1e30), writes=[r_sc])
                ncol = 2 * qi + 2
                E("dve", "tensor_scalar", dict(out=sc[:, 0:ncol], in0=pb[7][:, 0:ncol], scalar1=den[:, 0:1], scalar2=None, op0=ALU.mult),
                  [r_pb[7], r_den], pwrites=[r_sc])
                for r in range(1, 4):
                    E("dve", "scalar_tensor_tensor", dict(out=sc[:, 0:ncol], in0=pb[7][:, 128 * r:128 * r + ncol], scalar=den[:, r:r + 1], in1=sc[:, 0:ncol], op0=ALU.mult, op1=ALU.add),
                      [r_pb[7], r_den, r_sc], pwrites=[r_sc])
                if qi >= 1:
                    ja = 2 * qi - 1
                    E("dve", "tensor_scalar", dict(out=sc[:, ja:ja + 1], in0=sc[:, ja:ja + 1], scalar1=halfc[:, 0:1], scalar2=halfc[:, 1:2], op0=ALU.mult, op1=ALU.add),
                      [r_sc, r_halfc], pwrites=[r_sc])
                E("dve", "memset", dict(ap=sc[:, 0:1], constant=1e9), [r_sc], pwrites=[r_sc])
                E("dve", "memset", dict(ap=sc[:, 2 * qi:2 * qi + 1], constant=2e9), [r_sc], pwrites=[r_sc])
                E("dve", "tensor_copy", dict(out=sc[:, 2 * qi + 1:2 * qi + 2], in_=halfc[:, 2:3]), [r_sc, r_halfc], pwrites=[r_sc])
                E("dve", "max", dict(out=mx[:, 0:8], in_=sc[:]), [r_sc], pwrites=[r_mx])
                E("dve", "match_replace", dict(out=sc2[:], in_to_replace=mx[:, 0:8], in_values=sc[:], imm_value=-3e38), [r_sc, r_mx], [r_sc2])
                E("dve", "max", dict(out=mx[:, 8:16], in_=sc2[:]), [r_sc2, r_mx], pwrites=[r_mx])
                E("dve", "tensor_reduce", dict(out=th[:], in_=mx[:, 8:16], axis=AX.X, op=ALU.min), [r_mx], [r_th])
                E("dve", "tensor_scalar", dict(out=seln[:, 64:192], in0=sc[:], scalar1=th[:, 0:1], scalar2=NEGB, op0=ALU.is_lt, op1=ALU.mult),
                  [r_sc, r_th], [r_seln])
                fold(0, 4, ya, ry, qb, True, do_conv=False)
                dmax = min(4, qi)
                for dl in range(dmax + 1):
                    jt = qi - dl
                    tile_attn(KW[0:64, 128 * jt:128 * jt + 128], [r_KW[jt // 4]], RA[0:64, :], [r_RQAq[qb]], Rres[:, 2 + dl, :], [r_Rres], 6,
                              VW[:, jt, :], [r_VW[jt // 4], r_VWo], dl == 0, dl == dmax)
                flush()
                fold(2, 6, ya, ry, qb, False)
                if deferred is not None:
                    deferred()
                b = nb()
                mm(pb[b][:, :], seln[:, 0:128], I4, True, True, [r_seln, r_cbt], [r_pb[b]])
                copy_op("dve", RA[64:128, :], pb[b][64:128, :], [r_pb[b]], [r_RQAs[qb]])
                if qi >= 32:
                    b = nb()
                    mm(pb[b][:, :], seln[:, 64:192], I4, True, True, [r_seln, r_cbt], [r_pb[b]])
                    copy_op("dve", RB[64:128, :], pb[b][64:128, :], [r_pb[b]], [r_RQBs[qb]])
                deferred = make_deferred(qi, qb, RA, RB, ya, ry)
            deferred()


    WoV = KE[:].rearrange("p (a b) -> p a b", a=8)
    yTbV = [kvT[:, 4096 * s_:4096 * s_ + 4096].rearrange("p (a b) -> p a b", a=8) for s_ in range(2)]
    WfF = bass.AP(Wf, 0, [[8 * NFM, 128], [1, 8 * NFM]]).bitcast(F32)
    xtV = [WfF[:, 0:1024], WfF[:, 1024:2048]]
    ztV = [WfF[:, 2048:3072], WfF[:, 3072:4096]]
    WtF = bass.AP(Wt, 0, [[8 * NTM, 128], [1, 8 * NTM]]).bitcast(F32)
    sqV2 = [WfF[:, 4096:5120], WtF[:, 0:1024]]
    otV = [bass.AP(Rres, 0, [[3584, 128], [1, 3584]]).bitcast(F32)[:, 0:1024], bass.AP(xT, 0, [[4096, 128], [1, 4096]]).bitcast(F32)[:, 0:1024]]
    w1F = bass.AP(w1kv, 0, [[4096, 128], [1, 4096]]).bitcast(F32)
    GtV = w1F[:, 0:1024]
    BtV = w1F[:, 1024:2048]
    stt2 = sb("stt", [128, 8], F32); r_stt2 = [k.R("stt0"), k.R("stt1")]
    r_WoV = k.R("WoV"); r_yTbV = [k.R("yTbV0"), k.R("yTbV1")]; r_xtV = [k.R("xtV0"), k.R("xtV1")]
    r_ztV = [k.R("ztV0"), k.R("ztV1")]; r_sqV2 = [k.R("sqV0"), k.R("sqV1")]; r_otV = [k.R("otV0"), k.R("otV1")]
    r_GtV = k.R("GtV"); r_BtV = k.R("BtV")
    ALPHA_ = (2 * 2) ** 0.25

    def B_body(l, xsrc, xreads, dst, r_dst):
        dma("sp", GtV, bass.AP(lng_t, D * l, [[0, 128], [1, D]]), [], [r_GtV])
        dma("sp", BtV, bass.AP(lnb_t, D * l, [[0, 128], [1, D]]), [], [r_BtV])
        for j in range(8):
            s = next_stg()
            dma("sp" if j % 2 == 0 else "pool", stg[s][:, 0:1024], wo_all[D * l + 128 * j:D * l + 128 * j + 128, :], [], [r_stg[s]])
            copy_op(evac_eng(), WoV[:, j, :], stg[s][:, 0:1024], [r_stg[s]], pwrites=[r_WoV])
        ti = 0
        for ch in range(TT // 512):
            s = ch % 2
            t0 = 512 * ch
            dma("sp" if ch % 2 == 0 else "pool", yTbV[s], ys_d[:, t0:t0 + 512].rearrange("(j p) t -> p j t", p=128), [r_ys], [r_yTbV[s]])
            for a in range(4):
                u = ti % 2
                ti += 1
                r0 = t0 + 128 * a
                dma("sp", xtV[u], xsrc[r0:r0 + 128, :], xreads, [r_xtV[u]])
                for h in range(2):
                    b = nb()
                    for j in range(8):
                        mm(pb[b][:, :], yTbV[s][:, j, 128 * a:128 * a + 128], WoV[:, j, 512 * h:512 * h + 512], j == 0, j == 7, [r_yTbV[s], r_WoV], pwrites=[r_pb[b]])
                    E("dve", "scalar_tensor_tensor", dict(out=ztV[u][:, 512 * h:512 * h + 512], in0=xtV[u][:, 512 * h:512 * h + 512], scalar=ALPHA_, in1=pb[b][:, :], op0=ALU.mult, op1=ALU.add),
                      [r_xtV[u], r_pb[b]], pwrites=[r_ztV[u]])
                stt = stt2[:, 4 * u:4 * u + 4]
                r_stt = r_stt2[u]
                sqV = sqV2[u]
                r_sqV = r_sqV2[u]
                E("dve", "tensor_reduce", dict(out=stt[:, 0:1], in_=ztV[u], axis=AX.X, op=ALU.add), [r_ztV[u]], pwrites=[r_stt])
                E("dve", "tensor_scalar_mul", dict(out=stt[:, 0:1], in0=stt[:, 0:1], scalar1=1.0 / 1024), [r_stt], pwrites=[r_stt])
                E("dve", "tensor_scalar", dict(out=ztV[u], in0=ztV[u], scalar1=stt[:, 0:1], scalar2=None, op0=ALU.subtract), [r_ztV[u], r_stt], [r_ztV[u]])
                E("pool", "tensor_tensor", dict(out=sqV, in0=ztV[u], in1=ztV[u], op=ALU.mult), [r_ztV[u]], [r_sqV])
                E("dve", "tensor_reduce", dict(out=stt[:, 1:2], in_=sqV, axis=AX.X, op=ALU.add), [r_sqV], pwrites=[r_stt])
                E("dve", "tensor_scalar", dict(out=stt[:, 1:2], in0=stt[:, 1:2], scalar1=1.0 / 1024, scalar2=1e-5, op0=ALU.mult, op1=ALU.add), [r_stt], pwrites=[r_stt])
                E("act", "activation", dict(out=stt[:, 2:3], in_=stt[:, 1:2], func=AF.Sqrt), [r_stt], pwrites=[r_stt])
                E("dve", "reciprocal", dict(out=stt[:, 1:2], in_=stt[:, 2:3]), [r_stt], pwrites=[r_stt])
                E("dve", "scalar_tensor_tensor", dict(out=otV[u], in0=ztV[u], scalar=stt[:, 1:2], in1=GtV, op0=ALU.mult, op1=ALU.mult), [r_ztV[u], r_stt, r_GtV], [r_otV[u]])
                E("pool", "tensor_tensor", dict(out=otV[u], in0=otV[u], in1=BtV, op=ALU.add), [r_otV[u], r_BtV], [r_otV[u]])
                dma("sp", dst[r0:r0 + 128, :], otV[u], [r_otV[u]], pwrites=[r_dst], key=r_otV[u])

    xsrc, xreads = x_d, []
    for l in range(L):
        for g in range(G):
            A_body(l, g, xsrc, xreads)
        k.barrier()
        last = l == L - 1
        dst, r_dst = (out_d, r_out) if last else (x1_d, r_x1)
        B_body(l, xsrc, xreads, dst, r_dst)
        k.barrier()
        xsrc, xreads = x1_d, [r_x1]
    stats = k.emit()
    return nc, stats


def host_fused_inputs(inp, L=2, G=2):
    d = dict(host_consts())
    per = [[host_layer_inputs(inp, l, g) for g in range(G)] for l in range(L)]
    d["wf"] = np.concatenate([per[l][g]["wf"] for l in range(L) for g in range(G)], axis=0)
    d["wt"] = np.concatenate([per[l][g]["wt"] for l in range(L) for g in range(G)], axis=0)
    d["poolw"] = np.concatenate([per[l][g]["poolw"] for l in range(L) for g in range(G)], axis=0)
    d["pcoef"] = np.concatenate([per[l][g]["pcoef"] for l in range(L) for g in range(G)], axis=0)
    d["pcorr"] = np.concatenate([per[0][g]["pcorr"] for g in range(G)], axis=0)
    d["w1kv"] = np.concatenate([per[l][0]["w1kv"] for l in range(L)], axis=0)
    d["pekv"] = np.concatenate([per[l][0]["pekv"] for l in range(L)], axis=0)
    d["w2kv"] = np.concatenate([per[l][0]["w2kv"] for l in range(L)], axis=0)
    d["relb"] = np.concatenate([per[0][g]["relb"] for g in range(G)], axis=0)
    perm = []
    for g in range(G):
        perm += list(range(128 * g, 128 * g + 128)) + list(range(256 + 128 * g, 256 + 128 * g + 128)) + list(range(512 + 256 * g, 512 + 256 * g + 256))
    perm = np.asarray(perm)
    d["wo"] = np.concatenate([inp["w_out"][l][perm] for l in range(L)], axis=0)
    d["lng"] = np.ascontiguousarray(inp["ln_g"][:L])
    d["lnb"] = np.ascontiguousarray(inp["ln_b"][:L])
    return {k_: np.ascontiguousarray(v) for k_, v in d.items()}


_CACHE = {}


def kernel(x, w_in, w_out, pool_w, pool_scale, conv_w, cmp_pe_k, cmp_w1_k, cmp_w2_k,
           cmp_pe_v, cmp_w1_v, cmp_w2_v, rel_bias, ln_g, ln_b):
    from concourse.bass_utils import run_bass_kernel_spmd
    inp = dict(x=x, w_in=w_in, w_out=w_out, pool_w=pool_w, pool_scale=pool_scale, conv_w=conv_w,
               cmp_pe_k=cmp_pe_k, cmp_w1_k=cmp_w1_k, cmp_w2_k=cmp_w2_k, cmp_pe_v=cmp_pe_v,
               cmp_w1_v=cmp_w1_v, cmp_w2_v=cmp_w2_v, rel_bias=rel_bias, ln_g=ln_g, ln_b=ln_b)
    inp = {k_: np.asarray(v, dtype=np.float32) for k_, v in inp.items()}
    if "F" not in _CACHE:
        _CACHE["F"] = build_F(2, 2, NCHUNK)[0]
    nc = _CACHE["F"]
    base = host_fused_inputs(inp, 2, 2)
    in_maps = []
    for core in range(8):
        m = dict(base)
        m["x"] = np.ascontiguousarray(inp["x"][core // 2])
        in_maps.append(m)
    res = run_bass_kernel_spmd(nc, in_maps, core_ids=list(range(8))).results
    return np.stack([res[2 * b]["out"] for b in range(inp["x"].shape[0])], axis=0)
```

```python
import math
import ml_dtypes
import numpy as np
import concourse.bass as bass
import concourse.mybir as mybir

F32 = mybir.dt.float32
BF16 = mybir.dt.bfloat16
AF = mybir.ActivationFunctionType
ALU = mybir.AluOpType
AX = mybir.AxisListType

COMPUTE = ("pe", "act", "dve", "pool")


class Res:
    __slots__ = ("name", "writers", "readers", "sem", "dma_cnt", "gen_deps", "excl")

    def __init__(self, name):
        self.name = name
        self.writers = []
        self.readers = []
        self.gen_deps = []
        self.excl = False
        self.sem = {}
        self.dma_cnt = {}


class Op:
    __slots__ = ("eng", "fn", "deps", "is_dma", "dres", "dval", "idx", "cnt", "signal")

    def __init__(self, eng, fn, is_dma):
        self.eng = eng
        self.fn = fn
        self.deps = []
        self.is_dma = is_dma
        self.dres = None
        self.dval = 0
        self.cnt = 0
        self.signal = False


class Emitter:
    def __init__(self, nc):
        self.nc = nc
        self.ops = []
        self.res = {}
        self.nres = 0
        self.dma_res = {}
        self.phase = Res("phase")
        self.bar_tile = None

    def R(self, name=None):
        self.nres += 1
        r = Res(name or f"r{self.nres}")
        return r

    def sb(self, name, shape, dtype):
        t = self.nc.alloc_sbuf_tensor(name, list(shape), dtype)
        t.res = None
        return t

    def op(self, eng, fn, reads=(), writes=(), pwrites=(), dma=False, key=None):
        o = Op(eng, fn, dma)
        o.idx = len(self.ops)
        if self.phase not in writes:
            reads = list(reads) + [self.phase]
        xr = [r for r in reads if r.excl]
        if xr:
            reads = [r for r in reads if not r.excl]
            writes = list(writes) + [r for r in xr if r not in writes]
        deps = []
        for r in reads:
            deps.extend((p, "raw") for p in r.writers)
        for r in writes:
            deps.extend((p, "war") for p in r.readers)
            deps.extend((p, "waw") for p in r.writers)
        for r in pwrites:
            if r.readers:
                deps.extend((p, "war") for p in r.readers)
                deps.extend((p, "waw") for p in r.writers if not (p.is_dma and dma))
            else:
                deps.extend((p, "war") for p in r.gen_deps)
                deps.extend((p, "waw") for p in r.writers if not (p.is_dma and dma))
        for r in reads:
            r.readers.append(o)
        for r in writes:
            r.gen_deps = r.readers + r.writers
            r.writers = [o]
            r.readers = []
        for r in pwrites:
            if r.readers:
                r.gen_deps = r.readers + r.writers
                r.writers = [o]
                r.readers = []
            else:
                r.writers.append(o)
        if dma:
            allw = list(writes) + list(pwrites)
            d = key if key is not None else allw[0]
            self.dma_res[id(d)] = d
            d.dma_cnt[eng] = d.dma_cnt.get(eng, 0) + 1
            o.dres = d
            o.dval = 16 * d.dma_cnt[eng]
        best = {}
        for p, kind in deps:
            if p is o:
                continue
            if p.is_dma:
                key = ("d", id(p.dres), p.eng)
                if key not in best or best[key].dval < p.dval:
                    best[key] = p
            else:
                if p.eng == o.eng and not dma:
                    if p.eng == "pe":
                        continue
                key = ("c", p.eng)
                if key not in best or best[key].idx < p.idx:
                    best[key] = p
        o.deps = list(best.values())
        for p in o.deps:
            p.signal = True
        self.ops.append(o)
        return o

    def barrier(self):
        if self.bar_tile is None:
            self.bar_tile = self.nc.alloc_sbuf_tensor("s_bar_tile", [128, 2], F32)
        t = self.bar_tile
        self.op("dve", lambda e: e.memset(t[:], 0.0), writes=[self.phase])

    def emit(self, final_res=()):
        nc = self.nc
        esem = {e: nc.alloc_semaphore(f"s_{e}") for e in COMPUTE}
        cnt = {e: 0 for e in COMPUTE}
        for o in self.ops:
            if o.is_dma:
                if o.eng not in o.dres.sem:
                    o.dres.sem[o.eng] = nc.alloc_semaphore(f"d_{o.dres.name}_{o.eng}")
            elif o.signal:
                cnt[o.eng] += 1
                o.cnt = cnt[o.eng]
        engs = ["pe", "act", "dve", "pool", "sp"]
        per = {e: [o for o in self.ops if o.eng == e] for e in engs}
        final_waits = [(r.sem[e], 16 * r.dma_cnt[e]) for r in self.dma_res.values() for e in r.sem]

        def run(e, eng):
            waited = {}
            for o in per[e]:
                for p in o.deps:
                    if p.is_dma:
                        s, v = p.dres.sem[p.eng], p.dval
                    else:
                        s, v = esem[p.eng], p.cnt
                    if waited.get(s.num, -1) >= v:
                        continue
                    waited[s.num] = v
                    eng.wait_ge(s, v)
                ins = o.fn(eng)
                if o.is_dma:
                    ins.then_inc(o.dres.sem[o.eng], 16)
                elif o.signal:
                    ins.then_inc(esem[o.eng], 1)
            if e == "sp":
                for s, v in final_waits:
                    eng.wait_ge(s, v)

        with nc.Block() as block:
            @block.tensor
            def _(eng):
                run("pe", eng)

            @block.scalar
            def _(eng):
                run("act", eng)

            @block.vector
            def _(eng):
                run("dve", eng)

            @block.gpsimd
            def _(eng):
                run("pool", eng)

            @block.sync
            def _(eng):
                run("sp", eng)
        return {e: len(per[e]) for e in engs}


T = 8192
D = 1024
NCHUNK = 16
LG = 4208
LG3 = 768
LGT = LG + LG3
NEGB = -30000.0
NFM = 1280
NTM = 396


def bucket_np(n):
    n = np.maximum(n, 0)
    nf = np.maximum(n, 1).astype(np.float32)
    large = 16 + (np.log(nf / np.float32(16)) / np.float32(math.log(8.0)) * np.float32(16)).astype(np.int32)
    large = np.minimum(large, 31)
    return np.where(n < 16, n, large)


def host_consts():
    c = {}
    c["cf"] = np.eye(128, dtype=np.float32)
    J = np.eye(128, dtype=np.float32)[::-1].copy()
    I4 = np.tile(np.eye(128, dtype=np.float32), (1, 4))
    ov = np.zeros((512, 128), np.float32)
    for i in range(511):
        for j in range(128):
            o = min(16 * i + 32, 64 * j + 64) - max(16 * i, 64 * j)
            if o > 0:
                ov[i, j] = o / 16.0
    ovc = ov.reshape(4, 128, 128).transpose(1, 0, 2).reshape(128, 512)
    E = np.zeros((128, 8192), np.float32)
    for jt in range(64):
        for kk in range(128):
            j = 2 * jt + (1 if kk >= 64 else 0)
            E[64 + (j % 64), jt * 128 + kk] = 1.0
    c["cb"] = np.concatenate([J, I4, ovc, E], axis=1).astype(ml_dtypes.bfloat16)
    oh = np.zeros((33, LGT), np.float32)
    i = np.arange(LG)
    n = i - 2063
    b = bucket_np(n)
    oh[np.where(n < 0, 32, b), i] = 1.0
    i3 = np.arange(LG3)
    n3 = i3 - 127
    b3 = bucket_np(n3)
    oh[np.where((n3 < 0) | (n3 >= 512), 32, b3), LG + i3] = 1.0
    c["oh"] = oh
    hc = np.zeros((128, 4), np.float32)
    hc[64:, 0] = 1.0
    hc[:64, 1] = 3e9
    hc[:64, 2] = -1e30
    hc[64:, 2] = 4e9
    c["halfc"] = hc
    return c


POOL_WINDOWS = (2, 4, 8, 16)


def host_layer_inputs(inp, l, g):
    w_in = inp["w_in"][l]
    offs = np.cumsum([0, 256, 256, 256, 256, 256, 256, 512, 128, 128, 128, 128, 128, 128, 24, 512])
    (o_pv, o_pz, o_cb, o_cc, o_cx, o_cz, o_q, o_kc, o_vc, o_ks, o_vs, o_kw, o_vw, o_g, o_z) = offs[:15]
    s = slice
    cols = []
    for o in (o_pv, o_pz, o_cb, o_cc, o_cx, o_cz):
        cols.append(np.arange(o + 128 * g, o + 128 * g + 128))
    for r in range(4):
        cols.append(np.arange(o_q + 64 * (4 * g + r), o_q + 64 * (4 * g + r) + 64))
    cols.append(np.arange(o_kc + 64 * g, o_kc + 64 * g + 64))
    cols.append(np.arange(o_vc + 64 * g, o_vc + 64 * g + 64))
    cols.append(np.arange(o_ks + 64 * g, o_ks + 64 * g + 64))
    cols.append(np.arange(o_kw + 64 * g, o_kw + 64 * g + 64))
    cf = np.concatenate(cols)
    assert cf.size == NFM
    ct = np.concatenate([
        np.arange(o_vs + 64 * g, o_vs + 64 * g + 64),
        np.arange(o_vw + 64 * g, o_vw + 64 * g + 64),
        np.arange(o_g + 12 * g, o_g + 12 * g + 12),
        np.arange(o_z + 256 * g, o_z + 256 * g + 256),
    ])
    assert ct.size == NTM
    d = {}
    d["wf"] = np.ascontiguousarray(w_in[:, cf])
    d["wt"] = np.ascontiguousarray(w_in[:, ct])
    pw = np.zeros((128, 128), np.float32)
    pw[0:64, 0:64] = inp["pool_w"][l, 2 * g]
    pw[64:128, 64:128] = inp["pool_w"][l, 2 * g + 1]
    d["poolw"] = pw
    pc = np.zeros((128, 8), np.float32)
    for h in range(2):
        w = POOL_WINDOWS[2 * g + h]
        pc[64 * h:64 * h + 64, POOL_WINDOWS.index(w)] = 1.0 / w
    pc[:, 4] = inp["pool_scale"][l, 128 * g:128 * g + 128]
    for k in range(3):
        pc[:, 5 + k] = inp["conv_w"][l, k, 128 * g:128 * g + 128]
    d["pcoef"] = pc
    pcorr = np.ones((128, 16), np.float32)
    for h in range(2):
        w = POOL_WINDOWS[2 * g + h]
        for t in range(16):
            pcorr[64 * h:64 * h + 64, t] = w / min(t + 1, w)
    d["pcorr"] = pcorr
    w1k = inp["cmp_w1_k"][l].reshape(32, 64, 128).transpose(1, 0, 2).reshape(64, 32 * 128)
    w1v = inp["cmp_w1_v"][l].reshape(32, 64, 128).transpose(1, 0, 2).reshape(64, 32 * 128)
    d["w1kv"] = np.ascontiguousarray(np.concatenate([w1k, w1v], axis=0))
    d["pekv"] = np.ascontiguousarray(np.concatenate([inp["cmp_pe_k"][l].T, inp["cmp_pe_v"][l].T], axis=0))
    d["w2kv"] = np.ascontiguousarray(np.concatenate([inp["cmp_w2_k"][l], inp["cmp_w2_v"][l]], axis=1))
    rb = np.full((33, 4), NEGB, np.float32)
    rb[:32] = inp["rel_bias"][:, 4 * g:4 * g + 4]
    d["relb"] = rb
    return d


def build_F(L=2, G=2, nchunks=NCHUNK):
    nc = bass.Bass("TRN2", target_bir_lowering=False)
    k = Emitter(nc)

    def din(name, shape, dt=F32):
        return nc.dram_tensor(name, shape, dt, kind="ExternalInput").ap()

    def dout(name, shape, dt=F32):
        return nc.dram_tensor(name, shape, dt, kind="ExternalOutput").ap()

    TT = 512 * nchunks
    x_d = din("x", [TT, D])
    wf_all = din("wf", [L * G * D, NFM])
    wt_all = din("wt", [L * G * D, NTM])
    poolw_all = din("poolw", [L * G * 128, 128])
    pcoef_all = din("pcoef", [L * G * 128, 8])
    pcorr_all = din("pcorr", [G * 128, 16])
    w1kv_all = din("w1kv", [L * 128, 4096])
    pekv_all = din("pekv", [L * 128, 32])
    w2kv_all = din("w2kv", [L * 128, 128])
    relb_all = din("relb", [G * 33, 4])
    oh_d = din("oh", [33, LGT])
    cf_d = din("cf", [128, 128])
    cb_d = din("cb", [128, 128 + 512 + 512 + 8192], BF16)
    halfc_d = din("halfc", [128, 4])
    wo_all = din("wo", [L * D, D])
    lng_t = nc.dram_tensor("lng", [L, D], F32, kind="ExternalInput")
    lnb_t = nc.dram_tensor("lnb", [L, D], F32, kind="ExternalInput")
    out_d = dout("out", [TT, D])
    gd_t = nc.dram_tensor("gd", [4, LGT], BF16)
    gd = gd_t.ap()
    ys_d = nc.dram_tensor("ys", [D, TT], BF16).ap()
    x1_d = nc.dram_tensor("x1", [TT, D], F32).ap()
    r_gd = k.R("gd")
    r_ys = k.R("ys")
    r_x1 = k.R("x1")
    r_out = k.R("out")

    def sb(name, shape, dt):
        return nc.alloc_sbuf_tensor("s_" + name, list(shape), dt)

    def E(eng, name, kw, reads=(), writes=(), pwrites=()):
        k.op(eng, lambda e: getattr(e, name)(**kw), reads=reads, writes=writes, pwrites=pwrites)

    Wf = sb("Wf", [128, 8, NFM], BF16); r_Wf = k.R("Wf")
    Wt = sb("Wt", [128, 8, NTM], BF16); r_Wt = k.R("Wt")
    ident = sb("ident", [128, 128], F32); r_ident = k.R("ident")
    identb = sb("identb", [128, 128], BF16); r_identb = k.R("identb")
    cbt = sb("cbt", [128, 128 + 512 + 512], BF16); r_cbt = k.R("cbt")
    Jm = cbt[:, 0:128]
    I4 = cbt[:, 128:640]
    ovt = cbt[:, 640:1152]
    KE = sb("KE", [128, T], BF16); r_KEe = k.R("KEe")
    r_KE = [k.R(f"KE{i}") for i in range(NCHUNK)]
    KW = sb("KW", [64, T], BF16); r_KW = [k.R(f"KW{i}") for i in range(NCHUNK)]
    kvT = sb("kvT", [128, T + 32], BF16); r_kvT = [k.R(f"kvT{i}") for i in range(NCHUNK + 1)]
    VS = sb("VS", [128, 64, 65], BF16); r_VS = [k.R(f"VS{i}") for i in range(NCHUNK)]; r_VSo = k.R("VSo")
    VW = sb("VW", [128, 64, 65], BF16); r_VW = [k.R(f"VW{i}") for i in range(NCHUNK)]; r_VWo = k.R("VWo")
    kcT = sb("kcT", [64, 544], BF16); r_kcT = k.R("kcT")
    vcT = sb("vcT", [64, 544], BF16); r_vcT = k.R("vcT")
    vcp = sb("vcp", [128, 4, 65], BF16); r_vcp = k.R("vcp")
    Rres = sb("Rres", [128, 7, 512], BF16); r_Rres = k.R("Rres")
    NR1 = 3
    R1t = [sb(f"R1t{i}", [128, 512], BF16) for i in range(NR1)]; r_R1t = [k.R(f"R1t{i}") for i in range(NR1)]
    poolw = sb("poolw", [128, 128], F32); r_poolw = k.R("poolw")
    poolwb = sb("poolwb", [128, 128], BF16); r_poolwb = k.R("poolwb")
    pcoef = sb("pcoef", [128, 8], F32); r_pcoef = k.R("pcoef")
    pcorr = sb("pcorr", [128, 16], F32); r_pcorr = k.R("pcorr")
    halfc = sb("halfc", [128, 4], F32); r_halfc = k.R("halfc")
    w1kv = sb("w1kv", [128, 32, 128], BF16); r_w1kv = k.R("w1kv")
    pekv = sb("pekv", [128, 32], F32); r_pekv = k.R("pekv")
    pekvb = sb("pekvb", [128, 32], BF16); r_pekvb = k.R("pekvb")
    w2kv = sb("w2kv", [128, 128], F32); r_w2kv = k.R("w2kv")
    w2kvb = sb("w2kvb", [128, 128], BF16); r_w2kvb = k.R("w2kvb")
    hbias = sb("hbias", [128, 2], F32); r_hbias = k.R("hbias")
    relb = sb("relb", [33, 4], F32); r_relb = k.R("relb")
    crt = sb("crt", [4, 1], F32); r_crt = k.R("crt")
    Gc = [sb(f"Gc{i}", [4, 512], BF16) for i in range(2)]; r_Gc = [k.R(f"Gc{i}") for i in range(2)]
    stg = [sb(f"stg{i}", [128, 4096], F32) for i in range(2)]
    r_stg = [k.R(f"stg{i}") for i in range(2)]
    xT = sb("xT", [128, 8, 512], BF16); r_xT = k.R("xT")
    RQA = sb("RQA", [128, 4, 512], BF16); r_RQAq = [k.R(f"RQAq{i}") for i in range(4)]; r_RQAs = [k.R(f"RQAs{i}") for i in range(4)]
    RQB = sb("RQB", [128, 4, 512], BF16); r_RQBq = [k.R(f"RQBq{i}") for i in range(4)]; r_RQBs = [k.R(f"RQBs{i}") for i in range(4)]
    gsig = sb("gsig", [128, 4, 12], F32); r_gsig = [k.R(f"gsig{i}") for i in range(4)]
    zs = sb("zs", [128, 4, 256], F32); r_zs = [k.R(f"zs{i}") for i in range(4)]
    pv = sb("pv", [128, 528], F32); r_pv = k.R("pv")
    sA = sb("sA", [128, 528], F32); r_sA = k.R("sA")
    sB = sb("sB", [128, 528], F32); r_sB = k.R("sB")
    pm = sb("pm", [128, 512], F32); r_pm = k.R("pm")
    pmb = sb("pmb", [128, 512], BF16); r_pmb = k.R("pmb")
    pz = sb("pz", [128, 512], F32); r_pz = k.R("pz")
    cbb = sb("cbb", [128, 512], F32); r_cbb = k.R("cbb")
    ccc = sb("ccc", [128, 512], F32); r_ccc = k.R("ccc")
    uu = sb("uu", [128, 514], F32); r_uu = k.R("uu")
    cz = sb("cz", [128, 512], F32); r_cz = k.R("cz")
    ypool = sb("ypool", [128, 512], BF16); r_ypool = k.R("ypool")
    yconvb = sb("yconvb", [128, 512], BF16); r_yconvb = k.R("yconvb")
    yTn = sb("yTn", [128, 2, 128], BF16); r_yTn = k.R("yTn")
    yconv = sb("yconv", [128, 512], F32); r_yconv = k.R("yconv")
    hkv = sb("hkv", [128, 128], BF16); r_hkv = k.R("hkv")
    Xc = sb("Xc", [128, 16, 66], BF16); r_Xc = k.R("Xc")
    NE = 4
    ET = [sb(f"ET{i}", [128, 1024], BF16) for i in range(2)]
    Et = [ET[0][:, 0:512], ET[0][:, 512:1024], ET[1][:, 0:512], ET[1][:, 512:1024]]; r_E = [k.R(f"E{i}") for i in range(NE)]
    sc = sb("sc", [128, 128], F32); r_sc = k.R("sc")
    sc2 = sb("sc2", [128, 128], F32); r_sc2 = k.R("sc2")
    mx = sb("mx", [128, 16], F32); r_mx = k.R("mx")
    th = sb("th", [128, 1], F32); r_th = k.R("th")
    seln = sb("seln", [128, 192], BF16); r_seln = k.R("seln")
    den = sb("den", [128, 12], F32); r_den = k.R("den")
    coef = sb("coef", [128, 12], F32); r_coef = k.R("coef")
    yacc = sb("yacc", [128, 2, 256], F32); r_yacc = [k.R("yacc0"), k.R("yacc1")]

    SP = [nc.alloc_psum_tensor(f"spb{i}", [128, 1024], F32) for i in range(2)]
    pb = [SP[0][:, 0:512], SP[0][:, 512:1024], SP[1][:, 0:512], SP[1][:, 512:1024]] + [nc.alloc_psum_tensor(f"pb{i}", [128, 512], F32) for i in range(4, 8)]
    r_pb = [k.R(f"pb{i}") for i in range(8)]
    for r_ in r_pb:
        r_.excl = True
    nbs = [0]

    def nb():
        i = nbs[0]
        nbs[0] = (i + 1) % 4
        return i

    cnt2 = [0]

    def evac_eng():
        cnt2[0] += 1
        return "act" if cnt2[0] % 2 else "dve"

    def copy_op(eng, out, in_, reads, writes=(), pwrites=(), scale=None):
        if eng == "act":
            kw = dict(out=out, in_=in_, func=AF.Copy)
            if scale is not None:
                kw["scale"] = scale
            E("act", "activation", kw, reads, writes, pwrites)
        else:
            if scale is None:
                E(eng, "tensor_copy", dict(out=out, in_=in_), reads, writes, pwrites)
            else:
                E(eng, "tensor_scalar_mul", dict(out=out, in0=in_, scalar1=scale), reads, writes, pwrites)

    def dma(eng, out, in_, reads, writes=(), pwrites=(), key=None):
        k.op(eng, lambda e: e.dma_start(out=out, in_=in_), reads=reads, writes=writes, pwrites=pwrites, dma=True, key=key)

    def mm(out, lhsT, rhs, start, stop, reads, writes=(), pwrites=()):
        k.op("pe", lambda e: e.matmul(out, lhsT, rhs, start=start, stop=stop), reads=reads, writes=writes, pwrites=pwrites)

    def tr(out, in_, reads, writes=(), pwrites=()):
        k.op("pe", lambda e: e.transpose(out, in_, ident[:]), reads=list(reads) + [r_ident], writes=writes, pwrites=pwrites)

    si = [0]

    def next_stg():
        s = si[0] % 2
        si[0] += 1
        return s

    def A_body(l, g, xsrc, xreads):
        lg = l * G + g
        dma("sp", ident[:], cf_d[:, :], [], [r_ident])
        dma("sp", cbt[:], cb_d[:, 0:1152], [], [r_cbt])
        dma("sp", KE[64:128, :], cb_d[64:128, 1152:1152 + 8192], [], [r_KEe])
        dma("pool", poolw[:], poolw_all[128 * lg:128 * lg + 128, :], [], [r_poolw])
        dma("pool", pcoef[:], pcoef_all[128 * lg:128 * lg + 128, :], [], [r_pcoef])
        dma("pool", pcorr[:], pcorr_all[128 * g:128 * g + 128, :], [], [r_pcorr])
        dma("pool", halfc[:], halfc_d[:, :], [], [r_halfc])
        dma("pool", pekv[:], pekv_all[128 * l:128 * l + 128, :], [], [r_pekv])
        dma("pool", w2kv[:], w2kv_all[128 * l:128 * l + 128, :], [], [r_w2kv])
        dma("pool", relb[:], relb_all[33 * g:33 * g + 33, :], [], [r_relb])
        copy_op("dve", identb[:], ident[:], [r_ident], [r_identb])
        copy_op("dve", poolwb[:], poolw[:], [r_poolw], [r_poolwb])
        copy_op("dve", pekvb[:], pekv[:], [r_pekv], [r_pekvb])
        copy_op("dve", w2kvb[:], w2kv[:], [r_w2kv], [r_w2kvb])
        E("pool", "memset", dict(ap=kvT[:], constant=0.0), writes=r_kvT)
        E("pool", "memset", dict(ap=kcT[:], constant=0.0), writes=[r_kcT])
        E("pool", "memset", dict(ap=vcT[:], constant=0.0), writes=[r_vcT])
        E("pool", "memset", dict(ap=vcp[:], constant=0.0), writes=[r_vcp])
        E("pool", "memset", dict(ap=vcp[:, :, 64:65], constant=1.0), writes=[r_vcp])
        E("pool", "memset", dict(ap=VS[:, :, 64:65], constant=1.0), writes=[r_VSo])
        E("pool", "memset", dict(ap=VW[:, :, 64:65], constant=1.0), writes=[r_VWo])
        E("pool", "memset", dict(ap=seln[:], constant=0.0), writes=[r_seln])
        E("pool", "memset", dict(ap=pv[:, 0:16], constant=0.0), pwrites=[r_pv])
        E("pool", "memset", dict(ap=uu[:, 0:2], constant=0.0), pwrites=[r_uu])
        E("pool", "memset", dict(ap=KE[0:64, :], constant=0.0), writes=r_KE)
        E("pool", "memset", dict(ap=KW[:], constant=0.0), writes=r_KW)
        E("pool", "memset", dict(ap=VS[:, :, 0:64], constant=0.0), writes=r_VS)
        E("pool", "memset", dict(ap=VW[:, :, 0:64], constant=0.0), writes=r_VW)


        def load_cast(dst_fn, src_fn, ncols, r_dst):
            for j in range(8):
                s = next_stg()
                dma("sp" if j % 2 == 0 else "pool", stg[s][:, 0:ncols], src_fn(j), [], [r_stg[s]])
                copy_op(evac_eng(), dst_fn(j), stg[s][:, 0:ncols], [r_stg[s]], pwrites=[r_dst])

        load_cast(lambda j: Wf[:, j, :], lambda j: wf_all[D * lg + 128 * j:D * lg + 128 * j + 128, :], NFM, r_Wf)
        load_cast(lambda j: Wt[:, j, :], lambda j: wt_all[D * lg + 128 * j:D * lg + 128 * j + 128, :], NTM, r_Wt)
        s = next_stg()
        dma("sp", stg[s][:, 0:4096], w1kv_all[128 * l:128 * l + 128, :], [], [r_stg[s]])
        copy_op("act", w1kv[:].rearrange("p a b -> p (a b)"), stg[s][:, 0:4096], [r_stg[s]], [r_w1kv])

        bk = 7
        for l_ in range(32):
            mm(pb[bk][:, 0:1], w1kv[0:64, l_, :], pekvb[0:64, l_:l_ + 1], l_ == 0, l_ == 31, [r_w1kv, r_pekvb], pwrites=[r_pb[bk]])
        for l_ in range(32):
            mm(pb[6][:, 0:1], w1kv[64:128, l_, :], pekvb[64:128, l_:l_ + 1], l_ == 0, l_ == 31, [r_w1kv, r_pekvb], pwrites=[r_pb[6]])
        copy_op("dve", hbias[:, 0:1], pb[bk][:, 0:1], [r_pb[bk]], pwrites=[r_hbias])
        copy_op("dve", hbias[:, 1:2], pb[6][:, 0:1], [r_pb[6]], pwrites=[r_hbias])

        nchg = (LGT + 511) // 512
        order = [4063 // 512] + [c for c in range(nchg) if c != 4063 // 512]
        for ii, cch in enumerate(order):
            c0 = cch * 512
            w = min(512, LGT - c0)
            s = next_stg()
            dma("sp", stg[s][0:33, 0:w], oh_d[:, c0:c0 + w], [], [r_stg[s]])
            b = nb()
            mm(pb[b][0:4, 0:w], relb[:, :], stg[s][0:33, 0:w], True, True, [r_relb, r_stg[s]], [r_pb[b]])
            if ii == 0:
                copy_op("dve", crt[:], pb[b][0:4, 4063 - c0:4063 - c0 + 1], [r_pb[b]], [r_crt])
            gi = ii % 2
            E("dve", "tensor_scalar", dict(out=Gc[gi][:, 0:w], in0=pb[b][0:4, 0:w], scalar1=crt[:, 0:1], scalar2=None, op0=ALU.subtract),
              [r_pb[b], r_crt], [r_Gc[gi]])
            dma("sp", gd[:, c0:c0 + w], Gc[gi][:, 0:w], [r_Gc[gi]], pwrites=[r_gd], key=r_Gc[gi])

        def hankel(base, step):
            return bass.AP(gd_t, base, [[step, 128], [LGT, 4], [1, 128]])

        bases = [(1936 + 128 * dl, 1) for dl in range(2)] + [(LG + 128 * dl, 1) for dl in range(5)]
        for i, (base, step) in enumerate(bases):
            dma("sp" if i % 2 == 0 else "pool", Rres[:, i, :].rearrange("p (r t) -> p r t", r=4), hankel(base, step), [r_gd], pwrites=[r_Rres])
        r1c = [0]

        for ci in range(nchunks):
            t0 = 512 * ci
            s = next_stg()
            xt = stg[s]
            for a in range(4):
                dma("sp" if a % 2 == 0 else "pool", xt[:, 1024 * a:1024 * a + 1024], xsrc[t0 + 128 * a:t0 + 128 * a + 128, :], xreads, pwrites=[r_stg[s]])
            for j in range(8):
                b = nb()
                for a in range(4):
                    tr(pb[b][:, 128 * a:128 * a + 128], xt[:, 1024 * a + 128 * j:1024 * a + 128 * j + 128], [r_stg[s]], pwrites=[r_pb[b]])
                copy_op(evac_eng(), xT[:, j, :], pb[b][:, :], [r_pb[b]], pwrites=[r_xT])


            def fm_block(col0, M):
                b = nb()
                for j in range(8):
                    mm(pb[b][0:M, :], Wf[:, j, col0:col0 + M], xT[:, j, :], j == 0, j == 7, [r_Wf, r_xT], pwrites=[r_pb[b]])
                return b

            b = fm_block(0, 128)
            copy_op("dve", pv[:, 16:528], pb[b][:, :], [r_pb[b]], pwrites=[r_pv])
            b = fm_block(128, 128)
            E("act", "activation", dict(out=pz[:], in_=pb[b][:, :], func=AF.Silu), [r_pb[b]], [r_pz])
            b = fm_block(256, 128)
            copy_op("dve", cbb[:], pb[b][:, :], [r_pb[b]], [r_cbb])
            b = fm_block(384, 128)
            copy_op("act", ccc[:], pb[b][:, :], [r_pb[b]], [r_ccc])
            b = fm_block(512, 128)
            E("dve", "tensor_tensor", dict(out=uu[:, 2:514], in0=ccc[:], in1=pb[b][:, :], op=ALU.mult), [r_ccc, r_pb[b]], pwrites=[r_uu])
            b = fm_block(640, 128)
            E("act", "activation", dict(out=cz[:], in_=pb[b][:, :], func=AF.Silu), [r_pb[b]], [r_cz])
            for r in range(4):
                b = fm_block(768 + 64 * r, 64)
                src = pb[b][0:64, :].rearrange("p (q t) -> p q t", q=4)
                copy_op("act", RQA[0:64, :, 128 * r:128 * r + 128], src, [r_pb[b]], pwrites=r_RQAq, scale=0.125)
                copy_op("dve", RQB[0:64, :, 128 * r:128 * r + 128], src, [r_pb[b]], pwrites=r_RQBq, scale=0.125)
            b = fm_block(1024, 128)
            copy_op("act", kvT[:, t0:t0 + 512], pb[b][:, :], [r_pb[b]], [r_kvT[ci]])
            b = fm_block(1152, 64)
            copy_op("dve", KE[0:64, t0:t0 + 512], pb[b][0:64, :], [r_pb[b]], [r_KE[ci]])
            b = fm_block(1216, 64)
            copy_op("act", KW[0:64, t0:t0 + 512], pb[b][0:64, :], [r_pb[b]], [r_KW[ci]])
            for a in range(4):
                b = nb()
                jt = 4 * ci + a
                for j in range(8):
                    mm(pb[b][:, 0:NTM], xT[:, j, 128 * a:128 * a + 128], Wt[:, j, :], j == 0, j == 7, [r_Wt, r_xT], pwrites=[r_pb[b]])
                copy_op("dve", VS[:, jt, 0:64], pb[b][:, 0:64], [r_pb[b]], pwrites=[r_VS[ci]])
                copy_op("dve", VW[:, jt, 0:64], pb[b][:, 64:128], [r_pb[b]], pwrites=[r_VW[ci]])
                E("act", "activation", dict(out=gsig[:, a, :], in_=pb[b][:, 128:140], func=AF.Sigmoid), [r_pb[b]], [r_gsig[a]])
                E("act", "activation", dict(out=zs[:, a, :], in_=pb[b][:, 140:396], func=AF.Silu), [r_pb[b]], [r_zs[a]])

            E("dve", "tensor_tensor", dict(out=sA[:, 1:528], in0=pv[:, 1:528], in1=pv[:, 0:527], op=ALU.add), [r_pv], [r_sA])
            E("dve", "tensor_scalar", dict(out=pm[:], in0=sA[:, 16:528], scalar1=pcoef[:, 0:1], scalar2=None, op0=ALU.mult), [r_sA, r_pcoef], [r_pm])
            E("dve", "tensor_tensor", dict(out=sB[:, 3:528], in0=sA[:, 3:528], in1=sA[:, 1:526], op=ALU.add), [r_sA], [r_sB])
            E("dve", "scalar_tensor_tensor", dict(out=pm[:], in0=sB[:, 16:528], scalar=pcoef[:, 1:2], in1=pm[:], op0=ALU.mult, op1=ALU.add), [r_sB, r_pcoef, r_pm], [r_pm])
            E("dve", "tensor_tensor", dict(out=sA[:, 7:528], in0=sB[:, 7:528], in1=sB[:, 3:524], op=ALU.add), [r_sB], [r_sA])
            E("dve", "scalar_tensor_tensor", dict(out=pm[:], in0=sA[:, 16:528], scalar=pcoef[:, 2:3], in1=pm[:], op0=ALU.mult, op1=ALU.add), [r_sA, r_pcoef, r_pm], [r_pm])
            E("dve", "tensor_tensor", dict(out=sB[:, 15:528], in0=sA[:, 15:528], in1=sA[:, 7:520], op=ALU.add), [r_sA], [r_sB])
            E("dve", "scalar_tensor_tensor", dict(out=pm[:], in0=sB[:, 16:528], scalar=pcoef[:, 3:4], in1=pm[:], op0=ALU.mult, op1=ALU.add), [r_sB, r_pcoef, r_pm], [r_pm])
            if ci == 0:
                E("dve", "tensor_tensor", dict(out=pm[:, 0:16], in0=pm[:, 0:16], in1=pcorr[:], op=ALU.mult), [r_pm, r_pcorr], [r_pm])
            E("dve", "tensor_tensor", dict(out=pmb[:], in0=pm[:], in1=pv[:, 16:528], op=ALU.subtract), [r_pm, r_pv], [r_pmb])
            b = nb()
            mm(pb[b][:, :], poolwb[:], pmb[:], True, True, [r_poolwb, r_pmb], [r_pb[b]])
            E("dve", "scalar_tensor_tensor", dict(out=ypool[:], in0=pb[b][:, :], scalar=pcoef[:, 4:5], in1=pz[:], op0=ALU.mult, op1=ALU.mult),
              [r_pb[b], r_pcoef, r_pz], [r_ypool])
            dma("sp", ys_d[512 * g:512 * g + 128, t0:t0 + 512], ypool[:], [r_ypool], pwrites=[r_ys], key=r_ypool)
            E("pool", "tensor_copy", dict(out=pv[:, 0:16], in_=pv[:, 512:528]), [r_pv], [r_pv])
            E("dve", "tensor_scalar", dict(out=yconv[:], in0=uu[:, 0:512], scalar1=pcoef[:, 5:6], scalar2=None, op0=ALU.mult), [r_uu, r_pcoef], [r_yconv])
            E("dve", "scalar_tensor_tensor", dict(out=yconv[:], in0=uu[:, 1:513], scalar=pcoef[:, 6:7], in1=yconv[:], op0=ALU.mult, op1=ALU.add), [r_uu, r_pcoef, r_yconv], [r_yconv])
            E("dve", "scalar_tensor_tensor", dict(out=yconv[:], in0=uu[:, 2:514], scalar=pcoef[:, 7:8], in1=yconv[:], op0=ALU.mult, op1=ALU.add), [r_uu, r_pcoef, r_yconv], [r_yconv])
            E("dve", "tensor_tensor", dict(out=yconv[:], in0=yconv[:], in1=cbb[:], op=ALU.mult), [r_yconv, r_cbb], [r_yconv])
            E("dve", "tensor_tensor", dict(out=yconvb[:], in0=yconv[:], in1=cz[:], op=ALU.mult), [r_yconv, r_cz], [r_yconvb])
            dma("sp", ys_d[512 * g + 128:512 * g + 256, t0:t0 + 512], yconvb[:], [r_yconvb], pwrites=[r_ys], key=r_yconvb)
            E("pool", "tensor_copy", dict(out=uu[:, 0:2], in_=uu[:, 512:514]), [r_uu], [r_uu])

            n0 = max(0, 32 * ci - 32)
            n1 = 32 * ci + 31
            cn = n1 - n0 + 1
            b = nb()
            rk = [r_kvT[max(ci - 1, 0)], r_kvT[ci], r_kvT[min(ci + 1, NCHUNK)]]
            E("dve", "tensor_copy", dict(out=Xc[:, :, 0:cn + 1], in_=kvT[:, 16 * n0:16 * n0 + 16 * (cn + 1)].rearrange("p (m l) -> p l m", l=16)), rk, [r_Xc])
            bv = nb()
            hb = (b, bv)
            for half, p0 in enumerate((0, 64)):
                for l_ in range(32):
                    rhs = Xc[p0:p0 + 64, l_ % 16, (l_ // 16):(l_ // 16) + cn]
                    mm(pb[hb[half]][:, 0:cn], w1kv[p0:p0 + 64, l_, :], rhs, l_ == 0, l_ == 31, [r_w1kv, r_Xc], pwrites=[r_pb[hb[half]]])
            for half in range(2):
                E("act", "activation", dict(out=hkv[:, 64 * half:64 * half + cn], in_=pb[hb[half]][:, 0:cn], func=AF.Silu, bias=hbias[:, half:half + 1]),
                  [r_pb[hb[half]], r_hbias], pwrites=[r_hkv])
            b2 = nb()
            mm(pb[b2][0:64, 0:cn], w2kvb[:, 0:64], hkv[:, 0:cn], True, True, [r_w2kvb, r_hkv], pwrites=[r_pb[b2]])
            mm(pb[b2][0:64, 64:64 + cn], w2kvb[:, 64:128], hkv[:, 64:64 + cn], True, True, [r_w2kvb, r_hkv], pwrites=[r_pb[b2]])
            copy_op("dve", kcT[:, n0:n0 + cn], pb[b2][0:64, 0:cn], [r_pb[b2]], [r_kcT])
            copy_op("dve", vcT[:, n0:n0 + cn], pb[b2][0:64, 64:64 + cn], [r_pb[b2]], [r_vcT])
            for c in sorted(set((n0 // 128, n1 // 128))):
                b3 = nb()
                mm(pb[b3][:, 0:64], vcT[:, 128 * c:128 * c + 128], identb[0:64, 0:64], True, True, [r_vcT, r_identb], [r_pb[b3]])
                copy_op("dve", vcp[:, c, 0:64], pb[b3][:, 0:64], [r_pb[b3]], [r_vcp])

            ecnt = [0]
            pend = []
            LOOKAHEAD = 2

            tlist = []
            pairc = [0]

            def tile_attn(lhsT, lhs_reads, rhs, rhs_reads, bias_ap, bias_reads, Obank, vrhs, v_reads, first, last, extra=None):
                tlist.append((lhsT, lhs_reads, rhs, rhs_reads, bias_ap, bias_reads, Obank, vrhs, v_reads, first, last, extra))

            def run_tiles():
                i = 0
                while i < len(tlist):
                    grp = tlist[i:i + 2]
                    i += len(grp)
                    P = pairc[0] % 2
                    pairc[0] += 1
                    for h, (lhsT, lhs_reads, rhs, rhs_reads, bias_ap, bias_reads, Obank, vrhs, v_reads, first, last, extra) in enumerate(grp):
                        b = 2 * P + h
                        mm(pb[b][:, :], lhsT, rhs, True, bias_ap is None, list(lhs_reads) + list(rhs_reads), [r_pb[b]])
                        if bias_ap is not None:
                            mm(pb[b][:, :], Jm, bias_ap, False, True, [r_cbt] + list(bias_reads), pwrites=[r_pb[b]])
                    wd = 512 * len(grp)
                    E("act", "activation", dict(out=ET[P][:, 0:wd], in_=SP[P][:, 0:wd], func=AF.Exp),
                      [r_pb[2 * P + h] for h in range(len(grp))], [r_E[2 * P + h] for h in range(len(grp))])

                    def stage2(grp=grp, P=P):
                        for h, (lhsT, lhs_reads, rhs, rhs_reads, bias_ap, bias_reads, Obank, vrhs, v_reads, first, last, extra) in enumerate(grp):
                            ei = 2 * P + h
                            for r in range(4):
                                mm(pb[Obank][:, 65 * r:65 * r + 65], Et[ei][:, 128 * r:128 * r + 128], vrhs, first and r == 0, last and r == 3, [r_E[ei]] + list(v_reads), pwrites=[r_pb[Obank]])
                            if extra is not None:
                                extra(ei)

                    pend.append(stage2)
                    while len(pend) > 1:
                        pend.pop(0)()
                del tlist[:]

            def flush():
                run_tiles()
                while pend:
                    pend.pop(0)()

            def fold(bi, bank, ya, ry, qb, first):
                Ob = pb[bank][:, 0:260].rearrange("p (r c) -> p r c", c=65)
                gs = gsig[:, qb, :].rearrange("p (r b) -> p r b", b=3)
                if bi > 0:
                    E("dve", "tensor_scalar_max", dict(out=den[:, 4 * bi:4 * bi + 4], in0=Ob[:, :, 64], scalar1=1e-30), [r_pb[bank]], pwrites=[r_den])
                    E("dve", "reciprocal", dict(out=den[:, 4 * bi:4 * bi + 4], in_=den[:, 4 * bi:4 * bi + 4]), [r_den], pwrites=[r_den])
                E("dve", "tensor_tensor", dict(out=coef[:, 4 * bi:4 * bi + 4], in0=den[:, 4 * bi:4 * bi + 4], in1=gs[:, :, bi], op=ALU.mult),
                  [r_den, r_gsig[qb]], pwrites=[r_coef])
                for r in range(4):
                    if first:
                        E("dve", "tensor_scalar", dict(out=ya[:, 64 * r:64 * r + 64], in0=Ob[:, r, 0:64], scalar1=coef[:, 4 * bi + r:4 * bi + r + 1], scalar2=None, op0=ALU.mult),
                          [r_pb[bank], r_coef], pwrites=[ry])
                    else:
                        E("dve", "scalar_tensor_tensor", dict(out=ya[:, 64 * r:64 * r + 64], in0=Ob[:, r, 0:64], scalar=coef[:, 4 * bi + r:4 * bi + r + 1], in1=ya[:, 64 * r:64 * r + 64], op0=ALU.mult, op1=ALU.add),
                          [r_pb[bank], r_coef, ry], pwrites=[ry])

            def make_deferred(qi, qb, RA, RB, ya, ry):
                def run():
                    for jt in range(qi + 1):
                        if jt < 32:
                            rhs, rr = RA, [r_RQAq[qb], r_RQAs[qb]]
                        else:
                            rhs, rr = RB, [r_RQBq[qb], r_RQBs[qb]]
                        bias_ap = Rres[:, qi - jt, :] if qi - jt <= 1 else None
                        tile_attn(KE[:, 128 * jt:128 * jt + 128], [r_KE[jt // 4], r_KEe], rhs, rr, bias_ap, [r_Rres], 5,
                                  VS[:, jt, :], [r_VS[jt // 4], r_VSo], jt == 0, jt == qi)
                    flush()
                    fold(1, 5, ya, ry, qb, False)
                    E("dve", "tensor_tensor", dict(out=ya, in0=ya, in1=zs[:, qb, :], op=ALU.mult), [ry, r_zs[qb]], [ry])
                    bt_ = nb()
                    tr(pb[bt_][:, 0:128], ya[:, 0:128], [ry], pwrites=[r_pb[bt_]])
                    tr(pb[bt_][:, 128:256], ya[:, 128:256], [ry], pwrites=[r_pb[bt_]])
                    copy_op("act", yTn[:].rearrange("p a b -> p (a b)"), pb[bt_][:, 0:256], [r_pb[bt_]], [r_yTn])
                    q0 = 128 * qi
                    dma("sp", ys_d[512 * g + 256:512 * g + 512, q0:q0 + 128].rearrange("(i c) t -> c i t", c=128), yTn[:], [r_yTn], pwrites=[r_ys], key=r_yTn)
                return run

            deferred = None
            for qb in range(4):
                qi = 4 * ci + qb
                RA = RQA[:, qb, :]
                RB = RQB[:, qb, :]
                ya = yacc[:, qi % 2, :]
                ry = r_yacc[qi % 2]
                cmax = qi // 16
                for c in range(cmax + 1):
                    m = qi - 16 * c
                    bias_ap, bias_reads = None, []
                    if m <= 16:
                        ri = r1c[0] % NR1
                        r1c[0] += 1
                        dma("pool", R1t[ri][:].rearrange("p (r t) -> p r t", r=4), hankel(128 * m, 16), [r_gd], [r_R1t[ri]])
                        bias_ap, bias_reads = R1t[ri][:], [r_R1t[ri]]

                    def extra(ei, c=c, cmax=cmax):
                        for r in range(4):
                            mm(pb[7][:, 128 * r:128 * r + 128], Et[ei][:, 128 * r:128 * r + 128], ovt[:, 128 * c:128 * c + 128], c == 0 and r == 0, c == cmax and r == 3,
                               [r_E[ei], r_cbt], pwrites=[r_pb[7]])

                    tile_attn(kcT[:, 128 * c:128 * c + 128], [r_kcT], RA[0:64, :], [r_RQAq[qb]], bias_ap, bias_reads, 4, vcp[:, c, :], [r_vcp], c == 0, c == cmax, extra)
                flush()
                O1 = pb[4][:, 0:260].rearrange("p (r c) -> p r c", c=65)
                E("dve", "tensor_scalar_max", dict(out=den[:, 0:4], in0=O1[:, :, 64], scalar1=1e-30), [r_pb[4]], pwrites=[r_den])
                E("dve", "reciprocal", dict(out=den[:, 0:4], in_=den[:, 0:4]), [r_den], pwrites=[r_den])
                E("pool", "memset", dict(ap=sc[:], constant=-1e30), writes=[r_sc])
                ncol = 2 * qi + 2
                E("dve", "tensor_scalar", dict(out=sc[:, 0:ncol], in0=pb[7][:, 0:ncol], scalar1=den[:, 0:1], scalar2=None, op0=ALU.mult),
                  [r_pb[7], r_den], pwrites=[r_sc])
                for r in range(1, 4):
                    E("dve", "scalar_tensor_tensor", dict(out=sc[:, 0:ncol], in0=pb[7][:, 128 * r:128 * r + ncol], scalar=den[:, r:r + 1], in1=sc[:, 0:ncol], op0=ALU.mult, op1=ALU.add),
                      [r_pb[7], r_den, r_sc], pwrites=[r_sc])
                if qi >= 1:
                    ja = 2 * qi - 1
                    E("dve", "tensor_scalar", dict(out=sc[:, ja:ja + 1], in0=sc[:, ja:ja + 1], scalar1=halfc[:, 0:1], scalar2=halfc[:, 1:2], op0=ALU.mult, op1=ALU.add),
                      [r_sc, r_halfc], pwrites=[r_sc])
                E("dve", "memset", dict(ap=sc[:, 0:1], constant=1e9), [r_sc], pwrites=[r_sc])
                E("dve", "memset", dict(ap=sc[:, 2 * qi:2 * qi + 1], constant=2e9), [r_sc], pwrites=[r_sc])
                E("dve", "tensor_copy", dict(out=sc[:, 2 * qi + 1:2 * qi + 2], in_=halfc[:, 2:3]), [r_sc, r_halfc], pwrites=[r_sc])
                E("dve", "max", dict(out=mx[:, 0:8], in_=sc[:]), [r_sc], pwrites=[r_mx])
                E("dve", "match_replace", dict(out=sc2[:], in_to_replace=mx[:, 0:8], in_values=sc[:], imm_value=-3e38), [r_sc, r_mx], [r_sc2])
                E("dve", "max", dict(out=mx[:, 8:16], in_=sc2[:]), [r_sc2, r_mx], pwrites=[r_mx])
                E("dve", "tensor_reduce", dict(out=th[:], in_=mx[:, 8:16], axis=AX.X, op=ALU.min), [r_mx], [r_th])
                E("dve", "tensor_scalar", dict(out=seln[:, 64:192], in0=sc[:], scalar1=th[:, 0:1], scalar2=NEGB, op0=ALU.is_lt, op1=ALU.mult),
                  [r_sc, r_th], [r_seln])
                fold(0, 4, ya, ry, qb, True)
                dmax = min(4, qi)
                for dl in range(dmax + 1):
                    jt = qi - dl
                    tile_attn(KW[0:64, 128 * jt:128 * jt + 128], [r_KW[jt // 4]], RA[0:64, :], [r_RQAq[qb]], Rres[:, 2 + dl, :], [r_Rres], 6,
                              VW[:, jt, :], [r_VW[jt // 4], r_VWo], dl == 0, dl == dmax)
                flush()
                fold(2, 6, ya, ry, qb, False)
                if deferred is not None:
                    deferred()
                b = nb()
                mm(pb[b][:, :], seln[:, 0:128], I4, True, True, [r_seln, r_cbt], [r_pb[b]])
                copy_op("dve", RA[64:128, :], pb[b][64:128, :], [r_pb[b]], [r_RQAs[qb]])
                if qi >= 32:
                    b = nb()
                    mm(pb[b][:, :], seln[:, 64:192], I4, True, True, [r_seln, r_cbt], [r_pb[b]])
                    copy_op("dve", RB[64:128, :], pb[b][64:128, :], [r_pb[b]], [r_RQBs[qb]])
                deferred = make_deferred(qi, qb, RA, RB, ya, ry)
            deferred()


    WoV = KE[:].rearrange("p (a b) -> p a b", a=8)
    yTbV = [kvT[:, 4096 * s_:4096 * s_ + 4096].rearrange("p (a b) -> p a b", a=8) for s_ in range(2)]
    WfF = bass.AP(Wf, 0, [[8 * NFM, 128], [1, 8 * NFM]]).bitcast(F32)
    xtV = [WfF[:, 0:1024], WfF[:, 1024:2048]]
    ztV = [WfF[:, 2048:3072], WfF[:, 3072:4096]]
    WtF = bass.AP(Wt, 0, [[8 * NTM, 128], [1, 8 * NTM]]).bitcast(F32)
    sqV2 = [WfF[:, 4096:5120], WtF[:, 0:1024]]
    otV = [bass.AP(Rres, 0, [[3584, 128], [1, 3584]]).bitcast(F32)[:, 0:1024], bass.AP(xT, 0, [[4096, 128], [1, 4096]]).bitcast(F32)[:, 0:1024]]
    w1F = bass.AP(w1kv, 0, [[4096, 128], [1, 4096]]).bitcast(F32)
    GtV = w1F[:, 0:1024]
    BtV = w1F[:, 1024:2048]
    stt2 = sb("stt", [128, 8], F32); r_stt2 = [k.R("stt0"), k.R("stt1")]
    r_WoV = k.R("WoV"); r_yTbV = [k.R("yTbV0"), k.R("yTbV1")]; r_xtV = [k.R("xtV0"), k.R("xtV1")]
    r_ztV = [k.R("ztV0"), k.R("ztV1")]; r_sqV2 = [k.R("sqV0"), k.R("sqV1")]; r_otV = [k.R("otV0"), k.R("otV1")]
    r_GtV = k.R("GtV"); r_BtV = k.R("BtV")
    ALPHA_ = (2 * 2) ** 0.25

    def B_body(l, xsrc, xreads, dst, r_dst):
        dma("sp", GtV, bass.AP(lng_t, D * l, [[0, 128], [1, D]]), [], [r_GtV])
        dma("sp", BtV, bass.AP(lnb_t, D * l, [[0, 128], [1, D]]), [], [r_BtV])
        for j in range(8):
            s = next_stg()
            dma("sp" if j % 2 == 0 else "pool", stg[s][:, 0:1024], wo_all[D * l + 128 * j:D * l + 128 * j + 128, :], [], [r_stg[s]])
            copy_op(evac_eng(), WoV[:, j, :], stg[s][:, 0:1024], [r_stg[s]], pwrites=[r_WoV])
        tiles = [(ch, a) for ch in range(TT // 512) for a in range(4)]
        NTL = len(tiles)

        def P1(t):
            ch, a = tiles[t]
            s = ch % 2
            u = t % 2
            t0 = 512 * ch
            r0 = t0 + 128 * a
            if a == 0:
                dma("sp" if ch % 2 == 0 else "pool", yTbV[s], ys_d[:, t0:t0 + 512].rearrange("(j p) t -> p j t", p=128), [r_ys], [r_yTbV[s]])
            dma("sp", xtV[u], xsrc[r0:r0 + 128, :], xreads, [r_xtV[u]])
            stt = stt2[:, 4 * u:4 * u + 4]
            for h in range(2):
                b = nb()
                for j in range(8):
                    mm(pb[b][:, :], yTbV[s][:, j, 128 * a:128 * a + 128], WoV[:, j, 512 * h:512 * h + 512], j == 0, j == 7, [r_yTbV[s], r_WoV], pwrites=[r_pb[b]])
                E("dve", "scalar_tensor_tensor", dict(out=ztV[u][:, 512 * h:512 * h + 512], in0=xtV[u][:, 512 * h:512 * h + 512], scalar=ALPHA_, in1=pb[b][:, :], op0=ALU.mult, op1=ALU.add),
                  [r_xtV[u], r_pb[b]], pwrites=[r_ztV[u]])
            E("dve", "tensor_reduce", dict(out=stt[:, 0:1], in_=ztV[u], axis=AX.X, op=ALU.add), [r_ztV[u]], pwrites=[r_stt2[u]])
            E("dve", "tensor_scalar_mul", dict(out=stt[:, 0:1], in0=stt[:, 0:1], scalar1=1.0 / 1024), [r_stt2[u]], pwrites=[r_stt2[u]])
            E("dve", "tensor_scalar", dict(out=ztV[u], in0=ztV[u], scalar1=stt[:, 0:1], scalar2=None, op0=ALU.subtract), [r_ztV[u], r_stt2[u]], [r_ztV[u]])

        def P2a(t):
            u = t % 2
            E("pool", "tensor_tensor", dict(out=sqV2[u], in0=ztV[u], in1=ztV[u], op=ALU.mult), [r_ztV[u]], [r_sqV2[u]])

        def P2b(t):
            u = t % 2
            stt = stt2[:, 4 * u:4 * u + 4]
            E("dve", "tensor_reduce", dict(out=stt[:, 1:2], in_=sqV2[u], axis=AX.X, op=ALU.add), [r_sqV2[u]], pwrites=[r_stt2[u]])
            E("dve", "tensor_scalar", dict(out=stt[:, 1:2], in0=stt[:, 1:2], scalar1=1.0 / 1024, scalar2=1e-5, op0=ALU.mult, op1=ALU.add), [r_stt2[u]], pwrites=[r_stt2[u]])
            E("act", "activation", dict(out=stt[:, 2:3], in_=stt[:, 1:2], func=AF.Sqrt), [r_stt2[u]], pwrites=[r_stt2[u]])
            E("dve", "reciprocal", dict(out=stt[:, 3:4], in_=stt[:, 2:3]), [r_stt2[u]], pwrites=[r_stt2[u]])

        def P3(t):
            ch, a = tiles[t]
            u = t % 2
            r0 = 512 * ch + 128 * a
            stt = stt2[:, 4 * u:4 * u + 4]
            E("dve", "scalar_tensor_tensor", dict(out=otV[u], in0=ztV[u], scalar=stt[:, 3:4], in1=GtV, op0=ALU.mult, op1=ALU.mult), [r_ztV[u], r_stt2[u], r_GtV], [r_otV[u]])
            E("pool", "tensor_tensor", dict(out=otV[u], in0=otV[u], in1=BtV, op=ALU.add), [r_otV[u], r_BtV], [r_otV[u]])
            dma("sp", dst[r0:r0 + 128, :], otV[u], [r_otV[u]], pwrites=[r_dst], key=r_otV[u])

        for t in range(NTL + 2):
            if 0 <= t - 1 < NTL:
                P2a(t - 1)
            if 0 <= t - 2 < NTL:
                P3(t - 2)
            if t < NTL:
                P1(t)
            if 0 <= t - 1 < NTL:
                P2b(t - 1)

    xsrc, xreads = x_d, []
    for l in range(L):
        for g in range(G):
            A_body(l, g, xsrc, xreads)
        k.barrier()
        last = l == L - 1
        dst, r_dst = (out_d, r_out) if last else (x1_d, r_x1)
        B_body(l, xsrc, xreads, dst, r_dst)
        k.barrier()
        xsrc, xreads = x1_d, [r_x1]
    stats = k.emit()
    return nc, stats


def host_fused_inputs(inp, L=2, G=2):
    d = dict(host_consts())
    per = [[host_layer_inputs(inp, l, g) for g in range(G)] for l in range(L)]
    d["wf"] = np.concatenate([per[l][g]["wf"] for l in range(L) for g in range(G)], axis=0)
    d["wt"] = np.concatenate([per[l][g]["wt"] for l in range(L) for g in range(G)], axis=0)
    d["poolw"] = np.concatenate([per[l][g]["poolw"] for l in range(L) for g in range(G)], axis=0)
    d["pcoef"] = np.concatenate([per[l][g]["pcoef"] for l in range(L) for g in range(G)], axis=0)
    d["pcorr"] = np.concatenate([per[0][g]["pcorr"] for g in range(G)], axis=0)
    d["w1kv"] = np.concatenate([per[l][0]["w1kv"] for l in range(L)], axis=0)
    d["pekv"] = np.concatenate([per[l][0]["pekv"] for l in range(L)], axis=0)
    d["w2kv"] = np.concatenate([per[l][0]["w2kv"] for l in range(L)], axis=0)
    d["relb"] = np.concatenate([per[0][g]["relb"] for g in range(G)], axis=0)
    perm = []
    for g in range(G):
        perm += list(range(128 * g, 128 * g + 128)) + list(range(256 + 128 * g, 256 + 128 * g + 128)) + list(range(512 + 256 * g, 512 + 256 * g + 256))
    perm = np.asarray(perm)
    d["wo"] = np.concatenate([inp["w_out"][l][perm] for l in range(L)], axis=0)
    d["lng"] = np.ascontiguousarray(inp["ln_g"][:L])
    d["lnb"] = np.ascontiguousarray(inp["ln_b"][:L])
    return {k_: np.ascontiguousarray(v) for k_, v in d.items()}


_CACHE = {}


def kernel(x, w_in, w_out, pool_w, pool_scale, conv_w, cmp_pe_k, cmp_w1_k, cmp_w2_k,
           cmp_pe_v, cmp_w1_v, cmp_w2_v, rel_bias, ln_g, ln_b):
    from concourse.bass_utils import run_bass_kernel_spmd
    inp = dict(x=x, w_in=w_in, w_out=w_out, pool_w=pool_w, pool_scale=pool_scale, conv_w=conv_w,
               cmp_pe_k=cmp_pe_k, cmp_w1_k=cmp_w1_k, cmp_w2_k=cmp_w2_k, cmp_pe_v=cmp_pe_v,
               cmp_w1_v=cmp_w1_v, cmp_w2_v=cmp_w2_v, rel_bias=rel_bias, ln_g=ln_g, ln_b=ln_b)
    inp = {k_: np.asarray(v, dtype=np.float32) for k_, v in inp.items()}
    if "F" not in _CACHE:
        _CACHE["F"] = build_F(2, 2, NCHUNK)[0]
    nc = _CACHE["F"]
    base = host_fused_inputs(inp, 2, 2)
    in_maps = []
    for core in range(8):
        m = dict(base)
        m["x"] = np.ascontiguousarray(inp["x"][core // 2])
        in_maps.append(m)
    res = run_bass_kernel_spmd(nc, in_maps, core_ids=list(range(8))).results
    return np.stack([res[2 * b]["out"] for b in range(inp["x"].shape[0])], axis=0)
```

```python
import math
import ml_dtypes
import numpy as np
import concourse.bass as bass
import concourse.mybir as mybir

F32 = mybir.dt.float32
BF16 = mybir.dt.bfloat16
AF = mybir.ActivationFunctionType
ALU = mybir.AluOpType
AX = mybir.AxisListType

COMPUTE = ("pe", "act", "dve", "pool")


class Res:
    __slots__ = ("name", "writers", "readers", "sem", "dma_cnt", "gen_deps", "excl")

    def __init__(self, name):
        self.name = name
        self.writers = []
        self.readers = []
        self.gen_deps = []
        self.excl = False
        self.sem = {}
        self.dma_cnt = {}


class Op:
    __slots__ = ("eng", "fn", "deps", "is_dma", "dres", "dval", "idx", "cnt", "signal")

    def __init__(self, eng, fn, is_dma):
        self.eng = eng
        self.fn = fn
        self.deps = []
        self.is_dma = is_dma
        self.dres = None
        self.dval = 0
        self.cnt = 0
        self.signal = False


class Emitter:
    def __init__(self, nc):
        self.nc = nc
        self.ops = []
        self.res = {}
        self.nres = 0
        self.dma_res = {}
        self.phase = Res("phase")
        self.bar_tile = None

    def R(self, name=None):
        self.nres += 1
        r = Res(name or f"r{self.nres}")
        return r

    def sb(self, name, shape, dtype):
        t = self.nc.alloc_sbuf_tensor(name, list(shape), dtype)
        t.res = None
        return t

    def op(self, eng, fn, reads=(), writes=(), pwrites=(), dma=False, key=None):
        o = Op(eng, fn, dma)
        o.idx = len(self.ops)
        if self.phase not in writes:
            reads = list(reads) + [self.phase]
        xr = [r for r in reads if r.excl]
        if xr:
            reads = [r for r in reads if not r.excl]
            writes = list(writes) + [r for r in xr if r not in writes]
        deps = []
        for r in reads:
            deps.extend((p, "raw") for p in r.writers)
        for r in writes:
            deps.extend((p, "war") for p in r.readers)
            deps.extend((p, "waw") for p in r.writers)
        for r in pwrites:
            if r.readers:
                deps.extend((p, "war") for p in r.readers)
                deps.extend((p, "waw") for p in r.writers if not (p.is_dma and dma))
            else:
                deps.extend((p, "war") for p in r.gen_deps)
                deps.extend((p, "waw") for p in r.writers if not (p.is_dma and dma))
        for r in reads:
            r.readers.append(o)
        for r in writes:
            r.gen_deps = r.readers + r.writers
            r.writers = [o]
            r.readers = []
        for r in pwrites:
            if r.readers:
                r.gen_deps = r.readers + r.writers
                r.writers = [o]
                r.readers = []
            else:
                r.writers.append(o)
        if dma:
            allw = list(writes) + list(pwrites)
            d = key if key is not None else allw[0]
            self.dma_res[id(d)] = d
            d.dma_cnt[eng] = d.dma_cnt.get(eng, 0) + 1
            o.dres = d
            o.dval = 16 * d.dma_cnt[eng]
        best = {}
        for p, kind in deps:
            if p is o:
                continue
            if p.is_dma:
                key = ("d", id(p.dres), p.eng)
                if key not in best or best[key].dval < p.dval:
                    best[key] = p
            else:
                if p.eng == o.eng and not dma:
                    if p.eng == "pe":
                        continue
                key = ("c", p.eng)
                if key not in best or best[key].idx < p.idx:
                    best[key] = p
        o.deps = list(best.values())
        for p in o.deps:
            p.signal = True
        self.ops.append(o)
        return o

    def barrier(self):
        if self.bar_tile is None:
            self.bar_tile = self.nc.alloc_sbuf_tensor("s_bar_tile", [128, 2], F32)
        t = self.bar_tile
        self.op("dve", lambda e: e.memset(t[:], 0.0), writes=[self.phase])

    def emit(self, final_res=()):
        nc = self.nc
        esem = {e: nc.alloc_semaphore(f"s_{e}") for e in COMPUTE}
        cnt = {e: 0 for e in COMPUTE}
        for o in self.ops:
            if o.is_dma:
                if o.eng not in o.dres.sem:
                    o.dres.sem[o.eng] = nc.alloc_semaphore(f"d_{o.dres.name}_{o.eng}")
            elif o.signal:
                cnt[o.eng] += 1
                o.cnt = cnt[o.eng]
        engs = ["pe", "act", "dve", "pool", "sp"]
        per = {e: [o for o in self.ops if o.eng == e] for e in engs}
        final_waits = [(r.sem[e], 16 * r.dma_cnt[e]) for r in self.dma_res.values() for e in r.sem]

        def run(e, eng):
            waited = {}
            for o in per[e]:
                for p in o.deps:
                    if p.is_dma:
                        s, v = p.dres.sem[p.eng], p.dval
                    else:
                        s, v = esem[p.eng], p.cnt
                    if waited.get(s.num, -1) >= v:
                        continue
                    waited[s.num] = v
                    eng.wait_ge(s, v)
                ins = o.fn(eng)
                if o.is_dma:
                    ins.then_inc(o.dres.sem[o.eng], 16)
                elif o.signal:
                    ins.then_inc(esem[o.eng], 1)
            if e == "sp":
                for s, v in final_waits:
                    eng.wait_ge(s, v)

        with nc.Block() as block:
            @block.tensor
            def _(eng):
                run("pe", eng)

            @block.scalar
            def _(eng):
                run("act", eng)

            @block.vector
            def _(eng):
                run("dve", eng)

            @block.gpsimd
            def _(eng):
                run("pool", eng)

            @block.sync
            def _(eng):
                run("sp", eng)
        return {e: len(per[e]) for e in engs}


T = 8192
D = 1024
NCHUNK = 16
LG = 4208
LG3 = 768
LGT = LG + LG3
NEGB = -30000.0
NFM = 1280
NTM = 396


def bucket_np(n):
    n = np.maximum(n, 0)
    nf = np.maximum(n, 1).astype(np.float32)
    large = 16 + (np.log(nf / np.float32(16)) / np.float32(math.log(8.0)) * np.float32(16)).astype(np.int32)
    large = np.minimum(large, 31)
    return np.where(n < 16, n, large)


def host_consts():
    c = {}
    c["cf"] = np.eye(128, dtype=np.float32)
    J = np.eye(128, dtype=np.float32)[::-1].copy()
    I4 = np.tile(np.eye(128, dtype=np.float32), (1, 4))
    ov = np.zeros((512, 128), np.float32)
    for i in range(511):
        for j in range(128):
            o = min(16 * i + 32, 64 * j + 64) - max(16 * i, 64 * j)
            if o > 0:
                ov[i, j] = o / 16.0
    ovc = ov.reshape(4, 128, 128).transpose(1, 0, 2).reshape(128, 512)
    E = np.zeros((128, 8192), np.float32)
    for jt in range(64):
        for kk in range(128):
            j = 2 * jt + (1 if kk >= 64 else 0)
            E[64 + (j % 64), jt * 128 + kk] = 1.0
    c["cb"] = np.concatenate([J, I4, ovc, E], axis=1).astype(ml_dtypes.bfloat16)
    oh = np.zeros((33, LGT), np.float32)
    i = np.arange(LG)
    n = i - 2063
    b = bucket_np(n)
    oh[np.where(n < 0, 32, b), i] = 1.0
    i3 = np.arange(LG3)
    n3 = i3 - 127
    b3 = bucket_np(n3)
    oh[np.where((n3 < 0) | (n3 >= 512), 32, b3), LG + i3] = 1.0
    c["oh"] = oh
    hc = np.zeros((128, 4), np.float32)
    hc[64:, 0] = 1.0
    hc[:64, 1] = 3e9
    hc[:64, 2] = -1e30
    hc[64:, 2] = 4e9
    c["halfc"] = hc
    return c


POOL_WINDOWS = (2, 4, 8, 16)


def host_layer_inputs(inp, l, g):
    w_in = inp["w_in"][l]
    offs = np.cumsum([0, 256, 256, 256, 256, 256, 256, 512, 128, 128, 128, 128, 128, 128, 24, 512])
    (o_pv, o_pz, o_cb, o_cc, o_cx, o_cz, o_q, o_kc, o_vc, o_ks, o_vs, o_kw, o_vw, o_g, o_z) = offs[:15]
    s = slice
    cols = []
    for o in (o_pv, o_pz, o_cb, o_cc, o_cx, o_cz):
        cols.append(np.arange(o + 128 * g, o + 128 * g + 128))
    for r in range(4):
        cols.append(np.arange(o_q + 64 * (4 * g + r), o_q + 64 * (4 * g + r) + 64))
    cols.append(np.arange(o_kc + 64 * g, o_kc + 64 * g + 64))
    cols.append(np.arange(o_vc + 64 * g, o_vc + 64 * g + 64))
    cols.append(np.arange(o_ks + 64 * g, o_ks + 64 * g + 64))
    cols.append(np.arange(o_kw + 64 * g, o_kw + 64 * g + 64))
    cf = np.concatenate(cols)
    assert cf.size == NFM
    ct = np.concatenate([
        np.arange(o_vs + 64 * g, o_vs + 64 * g + 64),
        np.arange(o_vw + 64 * g, o_vw + 64 * g + 64),
        np.arange(o_g + 12 * g, o_g + 12 * g + 12),
        np.arange(o_z + 256 * g, o_z + 256 * g + 256),
    ])
    assert ct.size == NTM
    d = {}
    d["wf"] = np.ascontiguousarray(w_in[:, cf])
    d["wt"] = np.ascontiguousarray(w_in[:, ct])
    pw = np.zeros((128, 128), np.float32)
    pw[0:64, 0:64] = inp["pool_w"][l, 2 * g]
    pw[64:128, 64:128] = inp["pool_w"][l, 2 * g + 1]
    d["poolw"] = pw
    pc = np.zeros((128, 8), np.float32)
    for h in range(2):
        w = POOL_WINDOWS[2 * g + h]
        pc[64 * h:64 * h + 64, POOL_WINDOWS.index(w)] = 1.0 / w
    pc[:, 4] = inp["pool_scale"][l, 128 * g:128 * g + 128]
    for k in range(3):
        pc[:, 5 + k] = inp["conv_w"][l, k, 128 * g:128 * g + 128]
    d["pcoef"] = pc
    pcorr = np.ones((128, 16), np.float32)
    for h in range(2):
        w = POOL_WINDOWS[2 * g + h]
        for t in range(16):
            pcorr[64 * h:64 * h + 64, t] = w / min(t + 1, w)
    d["pcorr"] = pcorr
    w1k = inp["cmp_w1_k"][l].reshape(32, 64, 128).transpose(1, 0, 2).reshape(64, 32 * 128)
    w1v = inp["cmp_w1_v"][l].reshape(32, 64, 128).transpose(1, 0, 2).reshape(64, 32 * 128)
    d["w1kv"] = np.ascontiguousarray(np.concatenate([w1k, w1v], axis=0))
    d["pekv"] = np.ascontiguousarray(np.concatenate([inp["cmp_pe_k"][l].T, inp["cmp_pe_v"][l].T], axis=0))
    d["w2kv"] = np.ascontiguousarray(np.concatenate([inp["cmp_w2_k"][l], inp["cmp_w2_v"][l]], axis=1))
    rb = np.full((33, 4), NEGB, np.float32)
    rb[:32] = inp["rel_bias"][:, 4 * g:4 * g + 4]
    d["relb"] = rb
    return d


def build_F(L=2, G=2, nchunks=NCHUNK):
    nc = bass.Bass("TRN2", target_bir_lowering=False)
    k = Emitter(nc)

    def din(name, shape, dt=F32):
        return nc.dram_tensor(name, shape, dt, kind="ExternalInput").ap()

    def dout(name, shape, dt=F32):
        return nc.dram_tensor(name, shape, dt, kind="ExternalOutput").ap()

    TT = 512 * nchunks
    x_d = din("x", [TT, D])
    wf_all = din("wf", [L * G * D, NFM])
    wt_all = din("wt", [L * G * D, NTM])
    poolw_all = din("poolw", [L * G * 128, 128])
    pcoef_all = din("pcoef", [L * G * 128, 8])
    pcorr_all = din("pcorr", [G * 128, 16])
    w1kv_all = din("w1kv", [L * 128, 4096])
    pekv_all = din("pekv", [L * 128, 32])
    w2kv_all = din("w2kv", [L * 128, 128])
    relb_all = din("relb", [G * 33, 4])
    oh_d = din("oh", [33, LGT])
    cf_d = din("cf", [128, 128])
    cb_d = din("cb", [128, 128 + 512 + 512 + 8192], BF16)
    halfc_d = din("halfc", [128, 4])
    wo_all = din("wo", [L * D, D])
    lng_t = nc.dram_tensor("lng", [L, D], F32, kind="ExternalInput")
    lnb_t = nc.dram_tensor("lnb", [L, D], F32, kind="ExternalInput")
    out_d = dout("out", [TT, D])
    gd_t = nc.dram_tensor("gd", [4, LGT], BF16)
    gd = gd_t.ap()
    ys_d = nc.dram_tensor("ys", [D, TT], BF16).ap()
    x1_d = nc.dram_tensor("x1", [TT, D], F32).ap()
    r_gd = k.R("gd")
    r_ys = k.R("ys")
    r_x1 = k.R("x1")
    r_out = k.R("out")

    def sb(name, shape, dt):
        return nc.alloc_sbuf_tensor("s_" + name, list(shape), dt)

    def E(eng, name, kw, reads=(), writes=(), pwrites=()):
        k.op(eng, lambda e: getattr(e, name)(**kw), reads=reads, writes=writes, pwrites=pwrites)

    Wf = sb("Wf", [128, 8, NFM], BF16); r_Wf = k.R("Wf")
    Wt = sb("Wt", [128, 8, NTM], BF16); r_Wt = k.R("Wt")
    ident = sb("ident", [128, 128], F32); r_ident = k.R("ident")
    identb = sb("identb", [128, 128], BF16); r_identb = k.R("identb")
    cbt = sb("cbt", [128, 128 + 512 + 512], BF16); r_cbt = k.R("cbt")
    Jm = cbt[:, 0:128]
    I4 = cbt[:, 128:640]
    ovt = cbt[:, 640:1152]
    KE = sb("KE", [128, T], BF16); r_KEe = k.R("KEe")
    r_KE = [k.R(f"KE{i}") for i in range(NCHUNK)]
    KW = sb("KW", [64, T], BF16); r_KW = [k.R(f"KW{i}") for i in range(NCHUNK)]
    kvT = sb("kvT", [128, T + 32], BF16); r_kvT = [k.R(f"kvT{i}") for i in range(NCHUNK + 1)]
    VS = sb("VS", [128, 64, 65], BF16); r_VS = [k.R(f"VS{i}") for i in range(NCHUNK)]; r_VSo = k.R("VSo")
    VW = sb("VW", [128, 64, 65], BF16); r_VW = [k.R(f"VW{i}") for i in range(NCHUNK)]; r_VWo = k.R("VWo")
    kcT = sb("kcT", [64, 544], BF16); r_kcT = k.R("kcT")
    vcT = sb("vcT", [64, 544], BF16); r_vcT = k.R("vcT")
    vcp = sb("vcp", [128, 4, 65], BF16); r_vcp = k.R("vcp")
    Rres = sb("Rres", [128, 7, 512], BF16); r_Rres = k.R("Rres")
    NR1 = 4
    R1t = [sb(f"R1t{i}", [128, 512], BF16) for i in range(NR1)]; r_R1t = [k.R(f"R1t{i}") for i in range(NR1)]
    poolw = sb("poolw", [128, 128], F32); r_poolw = k.R("poolw")
    poolwb = sb("poolwb", [128, 128], BF16); r_poolwb = k.R("poolwb")
    pcoef = sb("pcoef", [128, 8], F32); r_pcoef = k.R("pcoef")
    pcorr = sb("pcorr", [128, 16], F32); r_pcorr = k.R("pcorr")
    halfc = sb("halfc", [128, 4], F32); r_halfc = k.R("halfc")
    w1kv = sb("w1kv", [128, 32, 128], BF16); r_w1kv = k.R("w1kv")
    pekv = sb("pekv", [128, 32], F32); r_pekv = k.R("pekv")
    pekvb = sb("pekvb", [128, 32], BF16); r_pekvb = k.R("pekvb")
    w2kv = sb("w2kv", [128, 128], F32); r_w2kv = k.R("w2kv")
    w2kvb = sb("w2kvb", [128, 128], BF16); r_w2kvb = k.R("w2kvb")
    hbias = sb("hbias", [128, 2], F32); r_hbias = k.R("hbias")
    relb = sb("relb", [33, 4], F32); r_relb = k.R("relb")
    crt = sb("crt", [4, 1], F32); r_crt = k.R("crt")
    Gc = [sb(f"Gc{i}", [4, 512], BF16) for i in range(2)]; r_Gc = [k.R(f"Gc{i}") for i in range(2)]
    stg = [sb(f"stg{i}", [128, 4096], F32) for i in range(2)]
    r_stg = [k.R(f"stg{i}") for i in range(2)]
    xT = sb("xT", [128, 8, 512], BF16); r_xT = k.R("xT")
    RQA = sb("RQA", [128, 4, 512], BF16); r_RQAq = [k.R(f"RQAq{i}") for i in range(4)]; r_RQAs = [k.R(f"RQAs{i}") for i in range(4)]
    RQB = sb("RQB", [128, 4, 512], BF16); r_RQBq = [k.R(f"RQBq{i}") for i in range(4)]; r_RQBs = [k.R(f"RQBs{i}") for i in range(4)]
    gsig = sb("gsig", [128, 4, 12], F32); r_gsig = [k.R(f"gsig{i}") for i in range(4)]
    zs = sb("zs", [128, 4, 256], F32); r_zs = [k.R(f"zs{i}") for i in range(4)]
    pv = sb("pv", [128, 528], F32); r_pv = k.R("pv")
    sA = sb("sA", [128, 528], F32); r_sA = k.R("sA")
    sB = sb("sB", [128, 528], F32); r_sB = k.R("sB")
    pm = sb("pm", [128, 512], F32); r_pm = k.R("pm")
    pmb = sb("pmb", [128, 512], BF16); r_pmb = k.R("pmb")
    pz = sb("pz", [128, 512], F32); r_pz = k.R("pz")
    cbb = sb("cbb", [128, 512], F32); r_cbb = k.R("cbb")
    ccc = sb("ccc", [128, 512], F32); r_ccc = k.R("ccc")
    uu = sb("uu", [128, 514], F32); r_uu = k.R("uu")
    cz = sb("cz", [128, 512], F32); r_cz = k.R("cz")
    ypool = sb("ypool", [128, 512], BF16); r_ypool = k.R("ypool")
    yconvb = sb("yconvb", [128, 512], BF16); r_yconvb = k.R("yconvb")
    yTn = sb("yTn", [128, 2, 128], BF16); r_yTn = k.R("yTn")
    yconv = sb("yconv", [128, 512], F32); r_yconv = k.R("yconv")
    hkv = sb("hkv", [128, 128], BF16); r_hkv = k.R("hkv")
    Xc = sb("Xc", [128, 16, 66], BF16); r_Xc = k.R("Xc")
    NE = 4
    ET = [sb(f"ET{i}", [128, 1024], BF16) for i in range(2)]
    Et = [ET[0][:, 0:512], ET[0][:, 512:1024], ET[1][:, 0:512], ET[1][:, 512:1024]]; r_E = [k.R(f"E{i}") for i in range(NE)]
    sc = sb("sc", [128, 128], F32); r_sc = k.R("sc")
    sc2 = sb("sc2", [128, 128], F32); r_sc2 = k.R("sc2")
    mx = sb("mx", [128, 16], F32); r_mx = k.R("mx")
    th = sb("th", [128, 1], F32); r_th = k.R("th")
    seln = sb("seln", [128, 192], BF16); r_seln = k.R("seln")
    den = sb("den", [128, 12], F32); r_den = k.R("den")
    coef = sb("coef", [128, 12], F32); r_coef = k.R("coef")
    yacc = sb("yacc", [128, 2, 256], F32); r_yacc = [k.R("yacc0"), k.R("yacc1")]

    SP = [nc.alloc_psum_tensor(f"spb{i}", [128, 1024], F32) for i in range(2)]
    pb = [SP[0][:, 0:512], SP[0][:, 512:1024], SP[1][:, 0:512], SP[1][:, 512:1024]] + [nc.alloc_psum_tensor(f"pb{i}", [128, 512], F32) for i in range(4, 8)]
    r_pb = [k.R(f"pb{i}") for i in range(8)]
    for r_ in r_pb:
        r_.excl = True
    nbs = [0]

    def nb():
        i = nbs[0]
        nbs[0] = (i + 1) % 4
        return i

    cnt2 = [0]

    def evac_eng():
        cnt2[0] += 1
        return "act" if cnt2[0] % 2 else "dve"

    def copy_op(eng, out, in_, reads, writes=(), pwrites=(), scale=None):
        if eng == "act":
            kw = dict(out=out, in_=in_, func=AF.Copy)
            if scale is not None:
                kw["scale"] = scale
            E("act", "activation", kw, reads, writes, pwrites)
        else:
            if scale is None:
                E(eng, "tensor_copy", dict(out=out, in_=in_), reads, writes, pwrites)
            else:
                E(eng, "tensor_scalar_mul", dict(out=out, in0=in_, scalar1=scale), reads, writes, pwrites)

    def dma(eng, out, in_, reads, writes=(), pwrites=(), key=None):
        k.op(eng, lambda e: e.dma_start(out=out, in_=in_), reads=reads, writes=writes, pwrites=pwrites, dma=True, key=key)

    def mm(out, lhsT, rhs, start, stop, reads, writes=(), pwrites=()):
        k.op("pe", lambda e: e.matmul(out, lhsT, rhs, start=start, stop=stop), reads=reads, writes=writes, pwrites=pwrites)

    def tr(out, in_, reads, writes=(), pwrites=()):
        k.op("pe", lambda e: e.transpose(out, in_, ident[:]), reads=list(reads) + [r_ident], writes=writes, pwrites=pwrites)

    si = [0]

    def next_stg():
        s = si[0] % 2
        si[0] += 1
        return s

    def A_body(l, g, xsrc, xreads):
        lg = l * G + g
        dma("sp", ident[:], cf_d[:, :], [], [r_ident])
        dma("sp", cbt[:], cb_d[:, 0:1152], [], [r_cbt])
        dma("sp", KE[64:128, :], cb_d[64:128, 1152:1152 + 8192], [], [r_KEe])
        dma("pool", poolw[:], poolw_all[128 * lg:128 * lg + 128, :], [], [r_poolw])
        dma("pool", pcoef[:], pcoef_all[128 * lg:128 * lg + 128, :], [], [r_pcoef])
        dma("pool", pcorr[:], pcorr_all[128 * g:128 * g + 128, :], [], [r_pcorr])
        dma("pool", halfc[:], halfc_d[:, :], [], [r_halfc])
        dma("pool", pekv[:], pekv_all[128 * l:128 * l + 128, :], [], [r_pekv])
        dma("pool", w2kv[:], w2kv_all[128 * l:128 * l + 128, :], [], [r_w2kv])
        dma("pool", relb[:], relb_all[33 * g:33 * g + 33, :], [], [r_relb])
        copy_op("dve", identb[:], ident[:], [r_ident], [r_identb])
        copy_op("dve", poolwb[:], poolw[:], [r_poolw], [r_poolwb])
        copy_op("dve", pekvb[:], pekv[:], [r_pekv], [r_pekvb])
        copy_op("dve", w2kvb[:], w2kv[:], [r_w2kv], [r_w2kvb])
        E("pool", "memset", dict(ap=kvT[:], constant=0.0), writes=r_kvT)
        E("pool", "memset", dict(ap=kcT[:], constant=0.0), writes=[r_kcT])
        E("pool", "memset", dict(ap=vcT[:], constant=0.0), writes=[r_vcT])
        E("pool", "memset", dict(ap=vcp[:], constant=0.0), writes=[r_vcp])
        E("pool", "memset", dict(ap=vcp[:, :, 64:65], constant=1.0), writes=[r_vcp])
        E("pool", "memset", dict(ap=VS[:, :, 64:65], constant=1.0), writes=[r_VSo])
        E("pool", "memset", dict(ap=VW[:, :, 64:65], constant=1.0), writes=[r_VWo])
        E("pool", "memset", dict(ap=seln[:], constant=0.0), writes=[r_seln])
        E("pool", "memset", dict(ap=pv[:, 0:16], constant=0.0), pwrites=[r_pv])
        E("pool", "memset", dict(ap=uu[:, 0:2], constant=0.0), pwrites=[r_uu])
        E("pool", "memset", dict(ap=KE[0:64, :], constant=0.0), writes=r_KE)
        E("pool", "memset", dict(ap=KW[:], constant=0.0), writes=r_KW)
        E("pool", "memset", dict(ap=VS[:, :, 0:64], constant=0.0), writes=r_VS)
        E("pool", "memset", dict(ap=VW[:, :, 0:64], constant=0.0), writes=r_VW)


        def load_cast(dst_fn, src_fn, ncols, r_dst):
            for j in range(8):
                s = next_stg()
                dma("sp" if j % 2 == 0 else "pool", stg[s][:, 0:ncols], src_fn(j), [], [r_stg[s]])
                copy_op(evac_eng(), dst_fn(j), stg[s][:, 0:ncols], [r_stg[s]], pwrites=[r_dst])

        load_cast(lambda j: Wf[:, j, :], lambda j: wf_all[D * lg + 128 * j:D * lg + 128 * j + 128, :], NFM, r_Wf)
        load_cast(lambda j: Wt[:, j, :], lambda j: wt_all[D * lg + 128 * j:D * lg + 128 * j + 128, :], NTM, r_Wt)
        s = next_stg()
        dma("sp", stg[s][:, 0:4096], w1kv_all[128 * l:128 * l + 128, :], [], [r_stg[s]])
        copy_op("act", w1kv[:].rearrange("p a b -> p (a b)"), stg[s][:, 0:4096], [r_stg[s]], [r_w1kv])

        bk = 7
        for l_ in range(32):
            mm(pb[bk][:, 0:1], w1kv[0:64, l_, :], pekvb[0:64, l_:l_ + 1], l_ == 0, l_ == 31, [r_w1kv, r_pekvb], pwrites=[r_pb[bk]])
        for l_ in range(32):
            mm(pb[6][:, 0:1], w1kv[64:128, l_, :], pekvb[64:128, l_:l_ + 1], l_ == 0, l_ == 31, [r_w1kv, r_pekvb], pwrites=[r_pb[6]])
        copy_op("dve", hbias[:, 0:1], pb[bk][:, 0:1], [r_pb[bk]], pwrites=[r_hbias])
        copy_op("dve", hbias[:, 1:2], pb[6][:, 0:1], [r_pb[6]], pwrites=[r_hbias])

        nchg = (LGT + 511) // 512
        order = [4063 // 512] + [c for c in range(nchg) if c != 4063 // 512]
        for ii, cch in enumerate(order):
            c0 = cch * 512
            w = min(512, LGT - c0)
            s = next_stg()
            dma("sp", stg[s][0:33, 0:w], oh_d[:, c0:c0 + w], [], [r_stg[s]])
            b = nb()
            mm(pb[b][0:4, 0:w], relb[:, :], stg[s][0:33, 0:w], True, True, [r_relb, r_stg[s]], [r_pb[b]])
            if ii == 0:
                copy_op("dve", crt[:], pb[b][0:4, 4063 - c0:4063 - c0 + 1], [r_pb[b]], [r_crt])
            gi = ii % 2
            E("dve", "tensor_scalar", dict(out=Gc[gi][:, 0:w], in0=pb[b][0:4, 0:w], scalar1=crt[:, 0:1], scalar2=None, op0=ALU.subtract),
              [r_pb[b], r_crt], [r_Gc[gi]])
            dma("sp", gd[:, c0:c0 + w], Gc[gi][:, 0:w], [r_Gc[gi]], pwrites=[r_gd], key=r_Gc[gi])

        def hankel(base, step):
            return bass.AP(gd_t, base, [[step, 128], [LGT, 4], [1, 128]])

        bases = [(1936 + 128 * dl, 1) for dl in range(2)] + [(LG + 128 * dl, 1) for dl in range(5)]
        for i, (base, step) in enumerate(bases):
            dma("sp" if i % 2 == 0 else "pool", Rres[:, i, :].rearrange("p (r t) -> p r t", r=4), hankel(base, step), [r_gd], pwrites=[r_Rres])
        r1c = [0]

        def load_x(ci_):
            s_ = next_stg()
            for a in range(4):
                dma("sp" if a % 2 == 0 else "pool", stg[s_][:, 1024 * a:1024 * a + 1024], xsrc[512 * ci_ + 128 * a:512 * ci_ + 128 * a + 128, :], xreads, pwrites=[r_stg[s_]])
            return s_

        r1map = {}

        def prefetch_r1(qi_):
            for c_ in range(qi_ // 16 + 1):
                m_ = qi_ - 16 * c_
                if m_ <= 16 and (qi_, c_) not in r1map:
                    ri = r1c[0] % NR1
                    r1c[0] += 1
                    r1map[(qi_, c_)] = ri
                    dma("pool", R1t[ri][:].rearrange("p (r t) -> p r t", r=4), hankel(128 * m_, 16), [r_gd], [r_R1t[ri]])

        xs_next = load_x(0)
        for ci in range(nchunks):
            t0 = 512 * ci
            s = xs_next
            xt = stg[s]
            for j in range(8):
                b = nb()
                for a in range(4):
                    tr(pb[b][:, 128 * a:128 * a + 128], xt[:, 1024 * a + 128 * j:1024 * a + 128 * j + 128], [r_stg[s]], pwrites=[r_pb[b]])
                copy_op(evac_eng(), xT[:, j, :], pb[b][:, :], [r_pb[b]], pwrites=[r_xT])
            if ci + 1 < nchunks:
                xs_next = load_x(ci + 1)


            def fm_block(col0, M):
                b = nb()
                for j in range(8):
                    mm(pb[b][0:M, :], Wf[:, j, col0:col0 + M], xT[:, j, :], j == 0, j == 7, [r_Wf, r_xT], pwrites=[r_pb[b]])
                return b

            b = fm_block(0, 128)
            copy_op("dve", pv[:, 16:528], pb[b][:, :], [r_pb[b]], pwrites=[r_pv])
            b = fm_block(128, 128)
            E("act", "activation", dict(out=pz[:], in_=pb[b][:, :], func=AF.Silu), [r_pb[b]], [r_pz])
            b = fm_block(256, 128)
            copy_op("dve", cbb[:], pb[b][:, :], [r_pb[b]], [r_cbb])
            b = fm_block(384, 128)
            copy_op("act", ccc[:], pb[b][:, :], [r_pb[b]], [r_ccc])
            b = fm_block(512, 128)
            E("dve", "tensor_tensor", dict(out=uu[:, 2:514], in0=ccc[:], in1=pb[b][:, :], op=ALU.mult), [r_ccc, r_pb[b]], pwrites=[r_uu])
            b = fm_block(640, 128)
            E("act", "activation", dict(out=cz[:], in_=pb[b][:, :], func=AF.Silu), [r_pb[b]], [r_cz])
            for r in range(4):
                b = fm_block(768 + 64 * r, 64)
                src = pb[b][0:64, :].rearrange("p (q t) -> p q t", q=4)
                copy_op("act", RQA[0:64, :, 128 * r:128 * r + 128], src, [r_pb[b]], pwrites=r_RQAq, scale=0.125)
                copy_op("dve", RQB[0:64, :, 128 * r:128 * r + 128], src, [r_pb[b]], pwrites=r_RQBq, scale=0.125)
            b = fm_block(1024, 128)
            copy_op("act", kvT[:, t0:t0 + 512], pb[b][:, :], [r_pb[b]], [r_kvT[ci]])
            b = fm_block(1152, 64)
            copy_op("dve", KE[0:64, t0:t0 + 512], pb[b][0:64, :], [r_pb[b]], [r_KE[ci]])
            b = fm_block(1216, 64)
            copy_op("act", KW[0:64, t0:t0 + 512], pb[b][0:64, :], [r_pb[b]], [r_KW[ci]])
            for a in range(4):
                b = nb()
                jt = 4 * ci + a
                for j in range(8):
                    mm(pb[b][:, 0:NTM], xT[:, j, 128 * a:128 * a + 128], Wt[:, j, :], j == 0, j == 7, [r_Wt, r_xT], pwrites=[r_pb[b]])
                copy_op("dve", VS[:, jt, 0:64], pb[b][:, 0:64], [r_pb[b]], pwrites=[r_VS[ci]])
                copy_op("dve", VW[:, jt, 0:64], pb[b][:, 64:128], [r_pb[b]], pwrites=[r_VW[ci]])
                E("act", "activation", dict(out=gsig[:, a, :], in_=pb[b][:, 128:140], func=AF.Sigmoid), [r_pb[b]], [r_gsig[a]])
                E("act", "activation", dict(out=zs[:, a, :], in_=pb[b][:, 140:396], func=AF.Silu), [r_pb[b]], [r_zs[a]])

            E("dve", "tensor_tensor", dict(out=sA[:, 1:528], in0=pv[:, 1:528], in1=pv[:, 0:527], op=ALU.add), [r_pv], [r_sA])
            E("dve", "tensor_scalar", dict(out=pm[:], in0=sA[:, 16:528], scalar1=pcoef[:, 0:1], scalar2=None, op0=ALU.mult), [r_sA, r_pcoef], [r_pm])
            E("dve", "tensor_tensor", dict(out=sB[:, 3:528], in0=sA[:, 3:528], in1=sA[:, 1:526], op=ALU.add), [r_sA], [r_sB])
            E("dve", "scalar_tensor_tensor", dict(out=pm[:], in0=sB[:, 16:528], scalar=pcoef[:, 1:2], in1=pm[:], op0=ALU.mult, op1=ALU.add), [r_sB, r_pcoef, r_pm], [r_pm])
            E("dve", "tensor_tensor", dict(out=sA[:, 7:528], in0=sB[:, 7:528], in1=sB[:, 3:524], op=ALU.add), [r_sB], [r_sA])
            E("dve", "scalar_tensor_tensor", dict(out=pm[:], in0=sA[:, 16:528], scalar=pcoef[:, 2:3], in1=pm[:], op0=ALU.mult, op1=ALU.add), [r_sA, r_pcoef, r_pm], [r_pm])
            E("dve", "tensor_tensor", dict(out=sB[:, 15:528], in0=sA[:, 15:528], in1=sA[:, 7:520], op=ALU.add), [r_sA], [r_sB])
            E("dve", "scalar_tensor_tensor", dict(out=pm[:], in0=sB[:, 16:528], scalar=pcoef[:, 3:4], in1=pm[:], op0=ALU.mult, op1=ALU.add), [r_sB, r_pcoef, r_pm], [r_pm])
            if ci == 0:
                E("dve", "tensor_tensor", dict(out=pm[:, 0:16], in0=pm[:, 0:16], in1=pcorr[:], op=ALU.mult), [r_pm, r_pcorr], [r_pm])
            E("dve", "tensor_tensor", dict(out=pmb[:], in0=pm[:], in1=pv[:, 16:528], op=ALU.subtract), [r_pm, r_pv], [r_pmb])
            b = nb()
            mm(pb[b][:, :], poolwb[:], pmb[:], True, True, [r_poolwb, r_pmb], [r_pb[b]])
            E("dve", "scalar_tensor_tensor", dict(out=ypool[:], in0=pb[b][:, :], scalar=pcoef[:, 4:5], in1=pz[:], op0=ALU.mult, op1=ALU.mult),
              [r_pb[b], r_pcoef, r_pz], [r_ypool])
            dma("sp", ys_d[512 * g:512 * g + 128, t0:t0 + 512], ypool[:], [r_ypool], pwrites=[r_ys], key=r_ypool)
            E("pool", "tensor_copy", dict(out=pv[:, 0:16], in_=pv[:, 512:528]), [r_pv], [r_pv])
            E("dve", "tensor_scalar", dict(out=yconv[:], in0=uu[:, 0:512], scalar1=pcoef[:, 5:6], scalar2=None, op0=ALU.mult), [r_uu, r_pcoef], [r_yconv])
            E("dve", "scalar_tensor_tensor", dict(out=yconv[:], in0=uu[:, 1:513], scalar=pcoef[:, 6:7], in1=yconv[:], op0=ALU.mult, op1=ALU.add), [r_uu, r_pcoef, r_yconv], [r_yconv])
            E("dve", "scalar_tensor_tensor", dict(out=yconv[:], in0=uu[:, 2:514], scalar=pcoef[:, 7:8], in1=yconv[:], op0=ALU.mult, op1=ALU.add), [r_uu, r_pcoef, r_yconv], [r_yconv])
            E("dve", "tensor_tensor", dict(out=yconv[:], in0=yconv[:], in1=cbb[:], op=ALU.mult), [r_yconv, r_cbb], [r_yconv])
            E("dve", "tensor_tensor", dict(out=yconvb[:], in0=yconv[:], in1=cz[:], op=ALU.mult), [r_yconv, r_cz], [r_yconvb])
            dma("sp", ys_d[512 * g + 128:512 * g + 256, t0:t0 + 512], yconvb[:], [r_yconvb], pwrites=[r_ys], key=r_yconvb)
            E("pool", "tensor_copy", dict(out=uu[:, 0:2], in_=uu[:, 512:514]), [r_uu], [r_uu])

            n0 = max(0, 32 * ci - 32)
            n1 = 32 * ci + 31
            cn = n1 - n0 + 1
            b = nb()
            rk = [r_kvT[max(ci - 1, 0)], r_kvT[ci], r_kvT[min(ci + 1, NCHUNK)]]
            E("dve", "tensor_copy", dict(out=Xc[:, :, 0:cn + 1], in_=kvT[:, 16 * n0:16 * n0 + 16 * (cn + 1)].rearrange("p (m l) -> p l m", l=16)), rk, [r_Xc])
            bv = nb()
            hb = (b, bv)
            for half, p0 in enumerate((0, 64)):
                for l_ in range(32):
                    rhs = Xc[p0:p0 + 64, l_ % 16, (l_ // 16):(l_ // 16) + cn]
                    mm(pb[hb[half]][:, 0:cn], w1kv[p0:p0 + 64, l_, :], rhs, l_ == 0, l_ == 31, [r_w1kv, r_Xc], pwrites=[r_pb[hb[half]]])
            for half in range(2):
                E("act", "activation", dict(out=hkv[:, 64 * half:64 * half + cn], in_=pb[hb[half]][:, 0:cn], func=AF.Silu, bias=hbias[:, half:half + 1]),
                  [r_pb[hb[half]], r_hbias], pwrites=[r_hkv])
            b2 = nb()
            mm(pb[b2][0:64, 0:cn], w2kvb[:, 0:64], hkv[:, 0:cn], True, True, [r_w2kvb, r_hkv], pwrites=[r_pb[b2]])
            mm(pb[b2][0:64, 64:64 + cn], w2kvb[:, 64:128], hkv[:, 64:64 + cn], True, True, [r_w2kvb, r_hkv], pwrites=[r_pb[b2]])
            copy_op("dve", kcT[:, n0:n0 + cn], pb[b2][0:64, 0:cn], [r_pb[b2]], [r_kcT])
            copy_op("dve", vcT[:, n0:n0 + cn], pb[b2][0:64, 64:64 + cn], [r_pb[b2]], [r_vcT])
            for c in sorted(set((n0 // 128, n1 // 128))):
                b3 = nb()
                mm(pb[b3][:, 0:64], vcT[:, 128 * c:128 * c + 128], identb[0:64, 0:64], True, True, [r_vcT, r_identb], [r_pb[b3]])
                copy_op("dve", vcp[:, c, 0:64], pb[b3][:, 0:64], [r_pb[b3]], [r_vcp])

            ecnt = [0]
            pend = []
            LOOKAHEAD = 2

            tlist = []
            pairc = [0]

            def tile_attn(lhsT, lhs_reads, rhs, rhs_reads, bias_ap, bias_reads, Obank, vrhs, v_reads, first, last, extra=None):
                tlist.append((lhsT, lhs_reads, rhs, rhs_reads, bias_ap, bias_reads, Obank, vrhs, v_reads, first, last, extra))

            def run_tiles():
                i = 0
                while i < len(tlist):
                    grp = tlist[i:i + 2]
                    i += len(grp)
                    P = pairc[0] % 2
                    pairc[0] += 1
                    for h, (lhsT, lhs_reads, rhs, rhs_reads, bias_ap, bias_reads, Obank, vrhs, v_reads, first, last, extra) in enumerate(grp):
                        b = 2 * P + h
                        mm(pb[b][:, :], lhsT, rhs, True, bias_ap is None, list(lhs_reads) + list(rhs_reads), [r_pb[b]])
                        if bias_ap is not None:
                            mm(pb[b][:, :], Jm, bias_ap, False, True, [r_cbt] + list(bias_reads), pwrites=[r_pb[b]])
                    wd = 512 * len(grp)
                    E("act", "activation", dict(out=ET[P][:, 0:wd], in_=SP[P][:, 0:wd], func=AF.Exp),
                      [r_pb[2 * P + h] for h in range(len(grp))], [r_E[2 * P + h] for h in range(len(grp))])

                    def stage2(grp=grp, P=P):
                        for h, (lhsT, lhs_reads, rhs, rhs_reads, bias_ap, bias_reads, Obank, vrhs, v_reads, first, last, extra) in enumerate(grp):
                            ei = 2 * P + h
                            for r in range(4):
                                mm(pb[Obank][:, 65 * r:65 * r + 65], Et[ei][:, 128 * r:128 * r + 128], vrhs, first and r == 0, last and r == 3, [r_E[ei]] + list(v_reads), pwrites=[r_pb[Obank]])
                            if extra is not None:
                                extra(ei)

                    pend.append(stage2)
                    while len(pend) > 1:
                        pend.pop(0)()
                del tlist[:]

            def flush():
                run_tiles()
                while pend:
                    pend.pop(0)()

            def fold(bi, bank, ya, ry, qb, first):
                Ob = pb[bank][:, 0:260].rearrange("p (r c) -> p r c", c=65)
                gs = gsig[:, qb, :].rearrange("p (r b) -> p r b", b=3)
                if bi > 0:
                    E("dve", "tensor_scalar_max", dict(out=den[:, 4 * bi:4 * bi + 4], in0=Ob[:, :, 64], scalar1=1e-30), [r_pb[bank]], pwrites=[r_den])
                    E("dve", "reciprocal", dict(out=den[:, 4 * bi:4 * bi + 4], in_=den[:, 4 * bi:4 * bi + 4]), [r_den], pwrites=[r_den])
                E("dve", "tensor_tensor", dict(out=coef[:, 4 * bi:4 * bi + 4], in0=den[:, 4 * bi:4 * bi + 4], in1=gs[:, :, bi], op=ALU.mult),
                  [r_den, r_gsig[qb]], pwrites=[r_coef])
                for r in range(4):
                    if first:
                        E("dve", "tensor_scalar", dict(out=ya[:, 64 * r:64 * r + 64], in0=Ob[:, r, 0:64], scalar1=coef[:, 4 * bi + r:4 * bi + r + 1], scalar2=None, op0=ALU.mult),
                          [r_pb[bank], r_coef], pwrites=[ry])
                    else:
                        E("dve", "scalar_tensor_tensor", dict(out=ya[:, 64 * r:64 * r + 64], in0=Ob[:, r, 0:64], scalar=coef[:, 4 * bi + r:4 * bi + r + 1], in1=ya[:, 64 * r:64 * r + 64], op0=ALU.mult, op1=ALU.add),
                          [r_pb[bank], r_coef, ry], pwrites=[ry])

            def make_deferred(qi, qb, RA, RB, ya, ry):
                def run():
                    for jt in range(qi + 1):
                        if jt < 32:
                            rhs, rr = RA, [r_RQAq[qb], r_RQAs[qb]]
                        else:
                            rhs, rr = RB, [r_RQBq[qb], r_RQBs[qb]]
                        bias_ap = Rres[:, qi - jt, :] if qi - jt <= 1 else None
                        tile_attn(KE[:, 128 * jt:128 * jt + 128], [r_KE[jt // 4], r_KEe], rhs, rr, bias_ap, [r_Rres], 5,
                                  VS[:, jt, :], [r_VS[jt // 4], r_VSo], jt == 0, jt == qi)
                    flush()
                    fold(1, 5, ya, ry, qb, False)
                    E("dve", "tensor_tensor", dict(out=ya, in0=ya, in1=zs[:, qb, :], op=ALU.mult), [ry, r_zs[qb]], [ry])
                    bt_ = nb()
                    tr(pb[bt_][:, 0:128], ya[:, 0:128], [ry], pwrites=[r_pb[bt_]])
                    tr(pb[bt_][:, 128:256], ya[:, 128:256], [ry], pwrites=[r_pb[bt_]])
                    copy_op("act", yTn[:].rearrange("p a b -> p (a b)"), pb[bt_][:, 0:256], [r_pb[bt_]], [r_yTn])
                    q0 = 128 * qi
                    dma("sp", ys_d[512 * g + 256:512 * g + 512, q0:q0 + 128].rearrange("(i c) t -> c i t", c=128), yTn[:], [r_yTn], pwrites=[r_ys], key=r_yTn)
                return run

            deferred = None
            for qb in range(4):
                qi = 4 * ci + qb
                RA = RQA[:, qb, :]
                RB = RQB[:, qb, :]
                ya = yacc[:, qi % 2, :]
                ry = r_yacc[qi % 2]
                cmax = qi // 16
                prefetch_r1(qi)
                if qi + 1 < 4 * nchunks:
                    prefetch_r1(qi + 1)
                for c in range(cmax + 1):
                    m = qi - 16 * c
                    bias_ap, bias_reads = None, []
                    if m <= 16:
                        ri = r1map[(qi, c)]
                        bias_ap, bias_reads = R1t[ri][:], [r_R1t[ri]]

                    def extra(ei, c=c, cmax=cmax):
                        for r in range(4):
                            mm(pb[7][:, 128 * r:128 * r + 128], Et[ei][:, 128 * r:128 * r + 128], ovt[:, 128 * c:128 * c + 128], c == 0 and r == 0, c == cmax and r == 3,
                               [r_E[ei], r_cbt], pwrites=[r_pb[7]])

                    tile_attn(kcT[:, 128 * c:128 * c + 128], [r_kcT], RA[0:64, :], [r_RQAq[qb]], bias_ap, bias_reads, 4, vcp[:, c, :], [r_vcp], c == 0, c == cmax, extra)
                flush()
                O1 = pb[4][:, 0:260].rearrange("p (r c) -> p r c", c=65)
                E("dve", "tensor_scalar_max", dict(out=den[:, 0:4], in0=O1[:, :, 64], scalar1=1e-30), [r_pb[4]], pwrites=[r_den])
                E("dve", "reciprocal", dict(out=den[:, 0:4], in_=den[:, 0:4]), [r_den], pwrites=[r_den])
                E("pool", "memset", dict(ap=sc[:], constant=-1e30), writes=[r_sc])
                ncol = 2 * qi + 2
                E("dve", "tensor_scalar", dict(out=sc[:, 0:ncol], in0=pb[7][:, 0:ncol], scalar1=den[:, 0:1], scalar2=None, op0=ALU.mult),
                  [r_pb[7], r_den], pwrites=[r_sc])
                for r in range(1, 4):
                    E("dve", "scalar_tensor_tensor", dict(out=sc[:, 0:ncol], in0=pb[7][:, 128 * r:128 * r + ncol], scalar=den[:, r:r + 1], in1=sc[:, 0:ncol], op0=ALU.mult, op1=ALU.add),
                      [r_pb[7], r_den, r_sc], pwrites=[r_sc])
                if qi >= 1:
                    ja = 2 * qi - 1
                    E("dve", "tensor_scalar", dict(out=sc[:, ja:ja + 1], in0=sc[:, ja:ja + 1], scalar1=halfc[:, 0:1], scalar2=halfc[:, 1:2], op0=ALU.mult, op1=ALU.add),
                      [r_sc, r_halfc], pwrites=[r_sc])
                E("dve", "memset", dict(ap=sc[:, 0:1], constant=1e9), [r_sc], pwrites=[r_sc])
                E("dve", "memset", dict(ap=sc[:, 2 * qi:2 * qi + 1], constant=2e9), [r_sc], pwrites=[r_sc])
                E("dve", "tensor_copy", dict(out=sc[:, 2 * qi + 1:2 * qi + 2], in_=halfc[:, 2:3]), [r_sc, r_halfc], pwrites=[r_sc])
                E("dve", "max", dict(out=mx[:, 0:8], in_=sc[:]), [r_sc], pwrites=[r_mx])
                E("dve", "match_replace", dict(out=sc2[:], in_to_replace=mx[:, 0:8], in_values=sc[:], imm_value=-3e38), [r_sc, r_mx], [r_sc2])
                E("dve", "max", dict(out=mx[:, 8:16], in_=sc2[:]), [r_sc2, r_mx], pwrites=[r_mx])
                E("dve", "tensor_reduce", dict(out=th[:], in_=mx[:, 8:16], axis=AX.X, op=ALU.min), [r_mx], [r_th])
                E("dve", "tensor_scalar", dict(out=seln[:, 64:192], in0=sc[:], scalar1=th[:, 0:1], scalar2=NEGB, op0=ALU.is_lt, op1=ALU.mult),
                  [r_sc, r_th], [r_seln])
                fold(0, 4, ya, ry, qb, True)
                dmax = min(4, qi)
                for dl in range(dmax + 1):
                    jt = qi - dl
                    tile_attn(KW[0:64, 128 * jt:128 * jt + 128], [r_KW[jt // 4]], RA[0:64, :], [r_RQAq[qb]], Rres[:, 2 + dl, :], [r_Rres], 6,
                              VW[:, jt, :], [r_VW[jt // 4], r_VWo], dl == 0, dl == dmax)
                flush()
                fold(2, 6, ya, ry, qb, False)
                if deferred is not None:
                    deferred()
                b = nb()
                mm(pb[b][:, :], seln[:, 0:128], I4, True, True, [r_seln, r_cbt], [r_pb[b]])
                copy_op("dve", RA[64:128, :], pb[b][64:128, :], [r_pb[b]], [r_RQAs[qb]])
                if qi >= 32:
                    b = nb()
                    mm(pb[b][:, :], seln[:, 64:192], I4, True, True, [r_seln, r_cbt], [r_pb[b]])
                    copy_op("dve", RB[64:128, :], pb[b][64:128, :], [r_pb[b]], [r_RQBs[qb]])
                deferred = make_deferred(qi, qb, RA, RB, ya, ry)
            deferred()


    WoV = KE[:].rearrange("p (a b) -> p a b", a=8)
    yTbV = [kvT[:, 4096 * s_:4096 * s_ + 4096].rearrange("p (a b) -> p a b", a=8) for s_ in range(2)]
    WfF = bass.AP(Wf, 0, [[8 * NFM, 128], [1, 8 * NFM]]).bitcast(F32)
    xtV = [WfF[:, 0:1024], WfF[:, 1024:2048]]
    ztV = [WfF[:, 2048:3072], WfF[:, 3072:4096]]
    WtF = bass.AP(Wt, 0, [[8 * NTM, 128], [1, 8 * NTM]]).bitcast(F32)
    sqV2 = [WfF[:, 4096:5120], WtF[:, 0:1024]]
    otV = [bass.AP(Rres, 0, [[3584, 128], [1, 3584]]).bitcast(F32)[:, 0:1024], bass.AP(xT, 0, [[4096, 128], [1, 4096]]).bitcast(F32)[:, 0:1024]]
    w1F = bass.AP(w1kv, 0, [[4096, 128], [1, 4096]]).bitcast(F32)
    GtV = w1F[:, 0:1024]
    BtV = w1F[:, 1024:2048]
    stt2 = sb("stt", [128, 8], F32); r_stt2 = [k.R("stt0"), k.R("stt1")]
    r_WoV = k.R("WoV"); r_yTbV = [k.R("yTbV0"), k.R("yTbV1")]; r_xtV = [k.R("xtV0"), k.R("xtV1")]
    r_ztV = [k.R("ztV0"), k.R("ztV1")]; r_sqV2 = [k.R("sqV0"), k.R("sqV1")]; r_otV = [k.R("otV0"), k.R("otV1")]
    r_GtV = k.R("GtV"); r_BtV = k.R("BtV")
    ALPHA_ = (2 * 2) ** 0.25

    def B_body(l, xsrc, xreads, dst, r_dst):
        dma("sp", GtV, bass.AP(lng_t, D * l, [[0, 128], [1, D]]), [], [r_GtV])
        dma("sp", BtV, bass.AP(lnb_t, D * l, [[0, 128], [1, D]]), [], [r_BtV])
        for j in range(8):
            s = next_stg()
            dma("sp" if j % 2 == 0 else "pool", stg[s][:, 0:1024], wo_all[D * l + 128 * j:D * l + 128 * j + 128, :], [], [r_stg[s]])
            copy_op(evac_eng(), WoV[:, j, :], stg[s][:, 0:1024], [r_stg[s]], pwrites=[r_WoV])
        tiles = [(ch, a) for ch in range(TT // 512) for a in range(4)]
        NTL = len(tiles)

        def P1(t):
            ch, a = tiles[t]
            s = ch % 2
            u = t % 2
            t0 = 512 * ch
            r0 = t0 + 128 * a
            if a == 0:
                dma("sp" if ch % 2 == 0 else "pool", yTbV[s], ys_d[:, t0:t0 + 512].rearrange("(j p) t -> p j t", p=128), [r_ys], [r_yTbV[s]])
            stt = stt2[:, 4 * u:4 * u + 4]
            for h in range(2):
                b = nb()
                for j in range(8):
                    mm(pb[b][:, :], yTbV[s][:, j, 128 * a:128 * a + 128], WoV[:, j, 512 * h:512 * h + 512], j == 0, j == 7, [r_yTbV[s], r_WoV], pwrites=[r_pb[b]])
                E("dve", "scalar_tensor_tensor", dict(out=ztV[u][:, 512 * h:512 * h + 512], in0=xtV[u][:, 512 * h:512 * h + 512], scalar=ALPHA_, in1=pb[b][:, :], op0=ALU.mult, op1=ALU.add),
                  [r_xtV[u], r_pb[b]], pwrites=[r_ztV[u]])
            E("dve", "tensor_reduce", dict(out=stt[:, 0:1], in_=ztV[u], axis=AX.X, op=ALU.add), [r_ztV[u]], pwrites=[r_stt2[u]])
            E("dve", "tensor_scalar_mul", dict(out=stt[:, 0:1], in0=stt[:, 0:1], scalar1=1.0 / 1024), [r_stt2[u]], pwrites=[r_stt2[u]])
            E("dve", "tensor_scalar", dict(out=ztV[u], in0=ztV[u], scalar1=stt[:, 0:1], scalar2=None, op0=ALU.subtract), [r_ztV[u], r_stt2[u]], [r_ztV[u]])

        def P2a(t):
            u = t % 2
            E("pool", "tensor_tensor", dict(out=sqV2[u], in0=ztV[u], in1=ztV[u], op=ALU.mult), [r_ztV[u]], [r_sqV2[u]])

        def P2b(t):
            u = t % 2
            stt = stt2[:, 4 * u:4 * u + 4]
            E("dve", "tensor_reduce", dict(out=stt[:, 1:2], in_=sqV2[u], axis=AX.X, op=ALU.add), [r_sqV2[u]], pwrites=[r_stt2[u]])
            E("dve", "tensor_scalar", dict(out=stt[:, 1:2], in0=stt[:, 1:2], scalar1=1.0 / 1024, scalar2=1e-5, op0=ALU.mult, op1=ALU.add), [r_stt2[u]], pwrites=[r_stt2[u]])
            E("act", "activation", dict(out=stt[:, 2:3], in_=stt[:, 1:2], func=AF.Sqrt), [r_stt2[u]], pwrites=[r_stt2[u]])
            E("dve", "reciprocal", dict(out=stt[:, 3:4], in_=stt[:, 2:3]), [r_stt2[u]], pwrites=[r_stt2[u]])

        def P3(t):
            ch, a = tiles[t]
            u = t % 2
            r0 = 512 * ch + 128 * a
            stt = stt2[:, 4 * u:4 * u + 4]
            E("dve", "scalar_tensor_tensor", dict(out=otV[u], in0=ztV[u], scalar=stt[:, 3:4], in1=GtV, op0=ALU.mult, op1=ALU.mult), [r_ztV[u], r_stt2[u], r_GtV], [r_otV[u]])
            E("pool", "tensor_tensor", dict(out=otV[u], in0=otV[u], in1=BtV, op=ALU.add), [r_otV[u], r_BtV], [r_otV[u]])
            dma("sp", dst[r0:r0 + 128, :], otV[u], [r_otV[u]], pwrites=[r_dst], key=r_otV[u])

        def LX(t):
            ch, a = tiles[t]
            r0 = 512 * ch + 128 * a
            dma("act", xtV[t % 2], xsrc[r0:r0 + 128, :], xreads, [r_xtV[t % 2]])

        LX(0)
        for t in range(NTL + 2):
            if t + 1 < NTL:
                LX(t + 1)
            if 0 <= t - 1 < NTL:
                P2a(t - 1)
            if 0 <= t - 2 < NTL:
                P3(t - 2)
            if t < NTL:
                P1(t)
            if 0 <= t - 1 < NTL:
                P2b(t - 1)

    xsrc, xreads = x_d, []
    for l in range(L):
        for g in range(G):
            A_body(l, g, xsrc, xreads)
        k.barrier()
        last = l == L - 1
        dst, r_dst = (out_d, r_out) if last else (x1_d, r_x1)
        B_body(l, xsrc, xreads, dst, r_dst)
        k.barrier()
        xsrc, xreads = x1_d, [r_x1]
    stats = k.emit()
    return nc, stats


def host_fused_inputs(inp, L=2, G=2):
    d = dict(host_consts())
    per = [[host_layer_inputs(inp, l, g) for g in range(G)] for l in range(L)]
    d["wf"] = np.concatenate([per[l][g]["wf"] for l in range(L) for g in range(G)], axis=0)
    d["wt"] = np.concatenate([per[l][g]["wt"] for l in range(L) for g in range(G)], axis=0)
    d["poolw"] = np.concatenate([per[l][g]["poolw"] for l in range(L) for g in range(G)], axis=0)
    d["pcoef"] = np.concatenate([per[l][g]["pcoef"] for l in range(L) for g in range(G)], axis=0)
    d["pcorr"] = np.concatenate([per[0][g]["pcorr"] for g in range(G)], axis=0)
    d["w1kv"] = np.concatenate([per[l][0]["w1kv"] for l in range(L)], axis=0)
    d["pekv"] = np.concatenate([per[l][0]["pekv"] for l in range(L)], axis=0)
    d["w2kv"] = np.concatenate([per[l][0]["w2kv"] for l in range(L)], axis=0)
    d["relb"] = np.concatenate([per[0][g]["relb"] for g in range(G)], axis=0)
    perm = []
    for g in range(G):
        perm += list(range(128 * g, 128 * g + 128)) + list(range(256 + 128 * g, 256 + 128 * g + 128)) + list(range(512 + 256 * g, 512 + 256 * g + 256))
    perm = np.asarray(perm)
    d["wo"] = np.concatenate([inp["w_out"][l][perm] for l in range(L)], axis=0)
    d["lng"] = np.ascontiguousarray(inp["ln_g"][:L])
    d["lnb"] = np.ascontiguousarray(inp["ln_b"][:L])
    return {k_: np.ascontiguousarray(v) for k_, v in d.items()}


_CACHE = {}


def kernel(x, w_in, w_out, pool_w, pool_scale, conv_w, cmp_pe_k, cmp_w1_k, cmp_w2_k,
           cmp_pe_v, cmp_w1_v, cmp_w2_v, rel_bias, ln_g, ln_b):
    from concourse.bass_utils import run_bass_kernel_spmd
    inp = dict(x=x, w_in=w_in, w_out=w_out, pool_w=pool_w, pool_scale=pool_scale, conv_w=conv_w,
               cmp_pe_k=cmp_pe_k, cmp_w1_k=cmp_w1_k, cmp_w2_k=cmp_w2_k, cmp_pe_v=cmp_pe_v,
               cmp_w1_v=cmp_w1_v, cmp_w2_v=cmp_w2_v, rel_bias=rel_bias, ln_g=ln_g, ln_b=ln_b)
    inp = {k_: np.asarray(v, dtype=np.float32) for k_, v in inp.items()}
    if "F" not in _CACHE:
        _CACHE["F"] = build_F(2, 2, NCHUNK)[0]
    nc = _CACHE["F"]
    base = host_fused_inputs(inp, 2, 2)
    in_maps = []
    for core in range(8):
        m = dict(base)
        m["x"] = np.ascontiguousarray(inp["x"][core // 2])
        in_maps.append(m)
    res = run_bass_kernel_spmd(nc, in_maps, core_ids=list(range(8))).results
    return np.stack([res[2 * b]["out"] for b in range(inp["x"].shape[0])], axis=0)
```

```python
import math
import ml_dtypes
import numpy as np
import concourse.bass as bass
import concourse.mybir as mybir

F32 = mybir.dt.float32
BF16 = mybir.dt.bfloat16
AF = mybir.ActivationFunctionType
ALU = mybir.AluOpType
AX = mybir.AxisListType

COMPUTE = ("pe", "act", "dve", "pool")


class Res:
    __slots__ = ("name", "writers", "readers", "sem", "dma_cnt", "gen_deps", "excl")

    def __init__(self, name):
        self.name = name
        self.writers = []
        self.readers = []
        self.gen_deps = []
        self.excl = False
        self.sem = {}
        self.dma_cnt = {}


class Op:
    __slots__ = ("eng", "fn", "deps", "is_dma", "dres", "dval", "idx", "cnt", "signal")

    def __init__(self, eng, fn, is_dma):
        self.eng = eng
        self.fn = fn
        self.deps = []
        self.is_dma = is_dma
        self.dres = None
        self.dval = 0
        self.cnt = 0
        self.signal = False


class Emitter:
    def __init__(self, nc):
        self.nc = nc
        self.ops = []
        self.res = {}
        self.nres = 0
        self.dma_res = {}
        self.phase = Res("phase")
        self.bar_tile = None

    def R(self, name=None):
        self.nres += 1
        r = Res(name or f"r{self.nres}")
        return r

    def sb(self, name, shape, dtype):
        t = self.nc.alloc_sbuf_tensor(name, list(shape), dtype)
        t.res = None
        return t

    def op(self, eng, fn, reads=(), writes=(), pwrites=(), dma=False, key=None):
        o = Op(eng, fn, dma)
        o.idx = len(self.ops)
        if self.phase not in writes:
            reads = list(reads) + [self.phase]
        xr = [r for r in reads if r.excl]
        if xr:
            reads = [r for r in reads if not r.excl]
            writes = list(writes) + [r for r in xr if r not in writes]
        deps = []
        for r in reads:
            deps.extend((p, "raw") for p in r.writers)
        for r in writes:
            deps.extend((p, "war") for p in r.readers)
            deps.extend((p, "waw") for p in r.writers)
        for r in pwrites:
            if r.readers:
                deps.extend((p, "war") for p in r.readers)
                deps.extend((p, "waw") for p in r.writers if not (p.is_dma and dma))
            else:
                deps.extend((p, "war") for p in r.gen_deps)
                deps.extend((p, "waw") for p in r.writers if not (p.is_dma and dma))
        for r in reads:
            r.readers.append(o)
        for r in writes:
            r.gen_deps = r.readers + r.writers
            r.writers = [o]
            r.readers = []
        for r in pwrites:
            if r.readers:
                r.gen_deps = r.readers + r.writers
                r.writers = [o]
                r.readers = []
            else:
                r.writers.append(o)
        if dma:
            allw = list(writes) + list(pwrites)
            d = key if key is not None else allw[0]
            self.dma_res[id(d)] = d
            d.dma_cnt[eng] = d.dma_cnt.get(eng, 0) + 1
            o.dres = d
            o.dval = 16 * d.dma_cnt[eng]
        best = {}
        for p, kind in deps:
            if p is o:
                continue
            if p.is_dma:
                key = ("d", id(p.dres), p.eng)
                if key not in best or best[key].dval < p.dval:
                    best[key] = p
            else:
                if p.eng == o.eng and not dma:
                    if p.eng == "pe":
                        continue
                key = ("c", p.eng)
                if key not in best or best[key].idx < p.idx:
                    best[key] = p
        o.deps = list(best.values())
        for p in o.deps:
            p.signal = True
        self.ops.append(o)
        return o

    def barrier(self):
        if self.bar_tile is None:
            self.bar_tile = self.nc.alloc_sbuf_tensor("s_bar_tile", [128, 2], F32)
        t = self.bar_tile
        self.op("dve", lambda e: e.memset(t[:], 0.0), writes=[self.phase])

    def emit(self, final_res=()):
        nc = self.nc
        esem = {e: nc.alloc_semaphore(f"s_{e}") for e in COMPUTE}
        cnt = {e: 0 for e in COMPUTE}
        for o in self.ops:
            if o.is_dma:
                if o.eng not in o.dres.sem:
                    o.dres.sem[o.eng] = nc.alloc_semaphore(f"d_{o.dres.name}_{o.eng}")
            elif o.signal:
                cnt[o.eng] += 1
                o.cnt = cnt[o.eng]
        engs = ["pe", "act", "dve", "pool", "sp"]
        per = {e: [o for o in self.ops if o.eng == e] for e in engs}
        final_waits = [(r.sem[e], 16 * r.dma_cnt[e]) for r in self.dma_res.values() for e in r.sem]

        def run(e, eng):
            waited = {}
            for o in per[e]:
                for p in o.deps:
                    if p.is_dma:
                        s, v = p.dres.sem[p.eng], p.dval
                    else:
                        s, v = esem[p.eng], p.cnt
                    if waited.get(s.num, -1) >= v:
                        continue
                    waited[s.num] = v
                    eng.wait_ge(s, v)
                ins = o.fn(eng)
                if o.is_dma:
                    ins.then_inc(o.dres.sem[o.eng], 16)
                elif o.signal:
                    ins.then_inc(esem[o.eng], 1)
            if e == "sp":
                for s, v in final_waits:
                    eng.wait_ge(s, v)

        with nc.Block() as block:
            @block.tensor
            def _(eng):
                run("pe", eng)

            @block.scalar
            def _(eng):
                run("act", eng)

            @block.vector
            def _(eng):
                run("dve", eng)

            @block.gpsimd
            def _(eng):
                run("pool", eng)

            @block.sync
            def _(eng):
                run("sp", eng)
        return {e: len(per[e]) for e in engs}


T = 8192
D = 1024
NCHUNK = 16
LG = 4208
LG3 = 768
LGT = LG + LG3
NEGB = -30000.0
NFM = 1280
NTM = 396


def bucket_np(n):
    n = np.maximum(n, 0)
    nf = np.maximum(n, 1).astype(np.float32)
    large = 16 + (np.log(nf / np.float32(16)) / np.float32(math.log(8.0)) * np.float32(16)).astype(np.int32)
    large = np.minimum(large, 31)
    return np.where(n < 16, n, large)


def host_consts():
    c = {}
    c["cf"] = np.eye(128, dtype=np.float32)
    J = np.eye(128, dtype=np.float32)[::-1].copy()
    I4 = np.tile(np.eye(128, dtype=np.float32), (1, 4))
    ov = np.zeros((512, 128), np.float32)
    for i in range(511):
        for j in range(128):
            o = min(16 * i + 32, 64 * j + 64) - max(16 * i, 64 * j)
            if o > 0:
                ov[i, j] = o / 16.0
    ovc = ov.reshape(4, 128, 128).transpose(1, 0, 2).reshape(128, 512)
    E = np.zeros((128, 8192), np.float32)
    for jt in range(64):
        for kk in range(128):
            j = 2 * jt + (1 if kk >= 64 else 0)
            E[64 + (j % 64), jt * 128 + kk] = 1.0
    c["cb"] = np.concatenate([J, I4, ovc, E], axis=1).astype(ml_dtypes.bfloat16)
    oh = np.zeros((33, LGT), np.float32)
    i = np.arange(LG)
    n = i - 2063
    b = bucket_np(n)
    oh[np.where(n < 0, 32, b), i] = 1.0
    i3 = np.arange(LG3)
    n3 = i3 - 127
    b3 = bucket_np(n3)
    oh[np.where((n3 < 0) | (n3 >= 512), 32, b3), LG + i3] = 1.0
    c["oh"] = oh
    hc = np.zeros((128, 4), np.float32)
    hc[64:, 0] = 1.0
    hc[:64, 1] = 3e9
    hc[:64, 2] = -1e30
    hc[64:, 2] = 4e9
    c["halfc"] = hc
    return c


POOL_WINDOWS = (2, 4, 8, 16)


def host_layer_inputs(inp, l, g):
    w_in = inp["w_in"][l]
    offs = np.cumsum([0, 256, 256, 256, 256, 256, 256, 512, 128, 128, 128, 128, 128, 128, 24, 512])
    (o_pv, o_pz, o_cb, o_cc, o_cx, o_cz, o_q, o_kc, o_vc, o_ks, o_vs, o_kw, o_vw, o_g, o_z) = offs[:15]
    s = slice
    cols = []
    for o in (o_pv, o_pz, o_cb, o_cc, o_cx, o_cz):
        cols.append(np.arange(o + 128 * g, o + 128 * g + 128))
    for r in range(4):
        cols.append(np.arange(o_q + 64 * (4 * g + r), o_q + 64 * (4 * g + r) + 64))
    cols.append(np.arange(o_kc + 64 * g, o_kc + 64 * g + 64))
    cols.append(np.arange(o_vc + 64 * g, o_vc + 64 * g + 64))
    cols.append(np.arange(o_ks + 64 * g, o_ks + 64 * g + 64))
    cols.append(np.arange(o_kw + 64 * g, o_kw + 64 * g + 64))
    cf = np.concatenate(cols)
    assert cf.size == NFM
    ct = np.concatenate([
        np.arange(o_vs + 64 * g, o_vs + 64 * g + 64),
        np.arange(o_vw + 64 * g, o_vw + 64 * g + 64),
        np.arange(o_g + 12 * g, o_g + 12 * g + 12),
        np.arange(o_z + 256 * g, o_z + 256 * g + 256),
    ])
    assert ct.size == NTM
    d = {}
    d["wf"] = np.ascontiguousarray(w_in[:, cf])
    d["wt"] = np.ascontiguousarray(w_in[:, ct])
    pw = np.zeros((128, 128), np.float32)
    pw[0:64, 0:64] = inp["pool_w"][l, 2 * g]
    pw[64:128, 64:128] = inp["pool_w"][l, 2 * g + 1]
    d["poolw"] = pw
    pc = np.zeros((128, 8), np.float32)
    for h in range(2):
        w = POOL_WINDOWS[2 * g + h]
        pc[64 * h:64 * h + 64, POOL_WINDOWS.index(w)] = 1.0 / w
    pc[:, 4] = inp["pool_scale"][l, 128 * g:128 * g + 128]
    for k in range(3):
        pc[:, 5 + k] = inp["conv_w"][l, k, 128 * g:128 * g + 128]
    d["pcoef"] = pc
    pcorr = np.ones((128, 16), np.float32)
    for h in range(2):
        w = POOL_WINDOWS[2 * g + h]
        for t in range(16):
            pcorr[64 * h:64 * h + 64, t] = w / min(t + 1, w)
    d["pcorr"] = pcorr
    w1k = inp["cmp_w1_k"][l].reshape(32, 64, 128).transpose(1, 0, 2).reshape(64, 32 * 128)
    w1v = inp["cmp_w1_v"][l].reshape(32, 64, 128).transpose(1, 0, 2).reshape(64, 32 * 128)
    d["w1kv"] = np.ascontiguousarray(np.concatenate([w1k, w1v], axis=0))
    d["pekv"] = np.ascontiguousarray(np.concatenate([inp["cmp_pe_k"][l].T, inp["cmp_pe_v"][l].T], axis=0))
    d["w2kv"] = np.ascontiguousarray(np.concatenate([inp["cmp_w2_k"][l], inp["cmp_w2_v"][l]], axis=1))
    rb = np.full((33, 4), NEGB, np.float32)
    rb[:32] = inp["rel_bias"][:, 4 * g:4 * g + 4]
    d["relb"] = rb
    return d


def build_F(L=2, G=2, nchunks=NCHUNK):
    nc = bass.Bass("TRN2", target_bir_lowering=False)
    k = Emitter(nc)

    def din(name, shape, dt=F32):
        return nc.dram_tensor(name, shape, dt, kind="ExternalInput").ap()

    def dout(name, shape, dt=F32):
        return nc.dram_tensor(name, shape, dt, kind="ExternalOutput").ap()

    TT = 512 * nchunks
    x_d = din("x", [TT, D])
    wf_all = din("wf", [L * G * D, NFM])
    wt_all = din("wt", [L * G * D, NTM])
    poolw_all = din("poolw", [L * G * 128, 128])
    pcoef_all = din("pcoef", [L * G * 128, 8])
    pcorr_all = din("pcorr", [G * 128, 16])
    w1kv_all = din("w1kv", [L * 128, 4096])
    pekv_all = din("pekv", [L * 128, 32])
    w2kv_all = din("w2kv", [L * 128, 128])
    relb_all = din("relb", [G * 33, 4])
    oh_d = din("oh", [33, LGT])
    cf_d = din("cf", [128, 128])
    cb_d = din("cb", [128, 128 + 512 + 512 + 8192], BF16)
    halfc_d = din("halfc", [128, 4])
    wo_all = din("wo", [L * D, D])
    lng_t = nc.dram_tensor("lng", [L, D], F32, kind="ExternalInput")
    lnb_t = nc.dram_tensor("lnb", [L, D], F32, kind="ExternalInput")
    out_d = dout("out", [TT, D])
    gd_t = nc.dram_tensor("gd", [4, LGT], BF16)
    gd = gd_t.ap()
    ys_d = nc.dram_tensor("ys", [D, TT], BF16).ap()
    x1_d = nc.dram_tensor("x1", [TT, D], F32).ap()
    r_gd = k.R("gd")
    r_ys = k.R("ys")
    r_x1 = k.R("x1")
    r_out = k.R("out")

    def sb(name, shape, dt):
        return nc.alloc_sbuf_tensor("s_" + name, list(shape), dt)

    def E(eng, name, kw, reads=(), writes=(), pwrites=()):
        k.op(eng, lambda e: getattr(e, name)(**kw), reads=reads, writes=writes, pwrites=pwrites)

    Wf = sb("Wf", [128, 8, NFM], BF16); r_Wf = k.R("Wf")
    Wt = sb("Wt", [128, 8, NTM], BF16); r_Wt = k.R("Wt")
    ident = sb("ident", [128, 128], F32); r_ident = k.R("ident")
    identb = sb("identb", [128, 128], BF16); r_identb = k.R("identb")
    cbt = sb("cbt", [128, 128 + 512 + 512], BF16); r_cbt = k.R("cbt")
    Jm = cbt[:, 0:128]
    I4 = cbt[:, 128:640]
    ovt = cbt[:, 640:1152]
    KE = sb("KE", [128, T], BF16); r_KEe = k.R("KEe")
    r_KE = [k.R(f"KE{i}") for i in range(NCHUNK)]
    KW = sb("KW", [64, T], BF16); r_KW = [k.R(f"KW{i}") for i in range(NCHUNK)]
    kvT = sb("kvT", [128, T + 32], BF16); r_kvT = [k.R(f"kvT{i}") for i in range(NCHUNK + 1)]
    VS = sb("VS", [128, 64, 65], BF16); r_VS = [k.R(f"VS{i}") for i in range(NCHUNK)]; r_VSo = k.R("VSo")
    VW = sb("VW", [128, 64, 65], BF16); r_VW = [k.R(f"VW{i}") for i in range(NCHUNK)]; r_VWo = k.R("VWo")
    kcT = sb("kcT", [64, 544], BF16); r_kcT = k.R("kcT")
    vcT = sb("vcT", [64, 544], BF16); r_vcT = k.R("vcT")
    vcp = sb("vcp", [128, 4, 65], BF16); r_vcp = k.R("vcp")
    Rres = sb("Rres", [128, 7, 512], BF16); r_Rres = k.R("Rres")
    NR1 = 4
    R1t = [sb(f"R1t{i}", [128, 512], BF16) for i in range(NR1)]; r_R1t = [k.R(f"R1t{i}") for i in range(NR1)]
    poolw = sb("poolw", [128, 128], F32); r_poolw = k.R("poolw")
    poolwb = sb("poolwb", [128, 128], BF16); r_poolwb = k.R("poolwb")
    pcoef = sb("pcoef", [128, 8], F32); r_pcoef = k.R("pcoef")
    pcorr = sb("pcorr", [128, 16], F32); r_pcorr = k.R("pcorr")
    halfc = sb("halfc", [128, 4], F32); r_halfc = k.R("halfc")
    w1kv = sb("w1kv", [128, 32, 128], BF16); r_w1kv = k.R("w1kv")
    pekv = sb("pekv", [128, 32], F32); r_pekv = k.R("pekv")
    pekvb = sb("pekvb", [128, 32], BF16); r_pekvb = k.R("pekvb")
    w2kv = sb("w2kv", [128, 128], F32); r_w2kv = k.R("w2kv")
    w2kvb = sb("w2kvb", [128, 128], BF16); r_w2kvb = k.R("w2kvb")
    hbias = sb("hbias", [128, 2], F32); r_hbias = k.R("hbias")
    relb = sb("relb", [33, 4], F32); r_relb = k.R("relb")
    crt = sb("crt", [4, 1], F32); r_crt = k.R("crt")
    Gc = [sb(f"Gc{i}", [4, 512], BF16) for i in range(2)]; r_Gc = [k.R(f"Gc{i}") for i in range(2)]
    stg = [sb(f"stg{i}", [128, 4096], F32) for i in range(2)]
    r_stg = [k.R(f"stg{i}") for i in range(2)]
    xT = sb("xT", [128, 8, 512], BF16); r_xT = k.R("xT")
    RQA = sb("RQA", [128, 4, 512], BF16); r_RQAq = [k.R(f"RQAq{i}") for i in range(4)]; r_RQAs = [k.R(f"RQAs{i}") for i in range(4)]
    RQB = sb("RQB", [128, 4, 512], BF16); r_RQBq = [k.R(f"RQBq{i}") for i in range(4)]; r_RQBs = [k.R(f"RQBs{i}") for i in range(4)]
    gsig = sb("gsig", [128, 4, 12], F32); r_gsig = [k.R(f"gsig{i}") for i in range(4)]
    zs = sb("zs", [128, 4, 256], F32); r_zs = [k.R(f"zs{i}") for i in range(4)]
    pv = sb("pv", [128, 528], F32); r_pv = k.R("pv")
    sA = sb("sA", [128, 528], F32); r_sA = k.R("sA")
    sB = sb("sB", [128, 528], F32); r_sB = k.R("sB")
    pm = sb("pm", [128, 512], F32); r_pm = k.R("pm")
    pmb = sb("pmb", [128, 512], BF16); r_pmb = k.R("pmb")
    pz = sb("pz", [128, 512], F32); r_pz = k.R("pz")
    cbb = sb("cbb", [128, 512], F32); r_cbb = k.R("cbb")
    ccc = sb("ccc", [128, 512], F32); r_ccc = k.R("ccc")
    uu = sb("uu", [128, 514], F32); r_uu = k.R("uu")
    cz = sb("cz", [128, 512], F32); r_cz = k.R("cz")
    ypool = sb("ypool", [128, 512], BF16); r_ypool = k.R("ypool")
    yconvb = sb("yconvb", [128, 512], BF16); r_yconvb = k.R("yconvb")
    yTn = sb("yTn", [128, 2, 128], BF16); r_yTn = k.R("yTn")
    yconv = sb("yconv", [128, 512], F32); r_yconv = k.R("yconv")
    hkv = sb("hkv", [128, 128], BF16); r_hkv = k.R("hkv")
    Xc = sb("Xc", [128, 16, 66], BF16); r_Xc = k.R("Xc")
    NE = 4
    ET = [sb(f"ET{i}", [128, 1024], BF16) for i in range(2)]
    Et = [ET[0][:, 0:512], ET[0][:, 512:1024], ET[1][:, 0:512], ET[1][:, 512:1024]]; r_E = [k.R(f"E{i}") for i in range(NE)]
    sc = sb("sc", [128, 128], F32); r_sc = k.R("sc")
    sc2 = sb("sc2", [128, 128], F32); r_sc2 = k.R("sc2")
    mx = sb("mx", [128, 16], F32); r_mx = k.R("mx")
    th = sb("th", [128, 1], F32); r_th = k.R("th")
    seln = sb("seln", [128, 192], BF16); r_seln = k.R("seln")
    den = sb("den", [128, 12], F32); r_den = k.R("den")
    coef = sb("coef", [128, 12], F32); r_coef = k.R("coef")
    yacc = sb("yacc", [128, 2, 256], F32); r_yacc = [k.R("yacc0"), k.R("yacc1")]

    SP = [nc.alloc_psum_tensor(f"spb{i}", [128, 1024], F32) for i in range(2)]
    pb = [SP[0][:, 0:512], SP[0][:, 512:1024], SP[1][:, 0:512], SP[1][:, 512:1024]] + [nc.alloc_psum_tensor(f"pb{i}", [128, 512], F32) for i in range(4, 8)]
    r_pb = [k.R(f"pb{i}") for i in range(8)]
    for r_ in r_pb:
        r_.excl = True
    nbs = [0]

    def nb():
        i = nbs[0]
        nbs[0] = (i + 1) % 4
        return i

    cnt2 = [0]

    def evac_eng():
        cnt2[0] += 1
        return "act" if cnt2[0] % 2 else "dve"

    def copy_op(eng, out, in_, reads, writes=(), pwrites=(), scale=None):
        if eng == "act":
            kw = dict(out=out, in_=in_, func=AF.Copy)
            if scale is not None:
                kw["scale"] = scale
            E("act", "activation", kw, reads, writes, pwrites)
        else:
            if scale is None:
                E(eng, "tensor_copy", dict(out=out, in_=in_), reads, writes, pwrites)
            else:
                E(eng, "tensor_scalar_mul", dict(out=out, in0=in_, scalar1=scale), reads, writes, pwrites)

    def dma(eng, out, in_, reads, writes=(), pwrites=(), key=None):
        k.op(eng, lambda e: e.dma_start(out=out, in_=in_), reads=reads, writes=writes, pwrites=pwrites, dma=True, key=key)

    def mm(out, lhsT, rhs, start, stop, reads, writes=(), pwrites=()):
        k.op("pe", lambda e: e.matmul(out, lhsT, rhs, start=start, stop=stop), reads=reads, writes=writes, pwrites=pwrites)

    def tr(out, in_, reads, writes=(), pwrites=()):
        k.op("pe", lambda e: e.transpose(out, in_, ident[:]), reads=list(reads) + [r_ident], writes=writes, pwrites=pwrites)

    si = [0]

    def next_stg():
        s = si[0] % 2
        si[0] += 1
        return s

    def A_body(l, g, xsrc, xreads):
        lg = l * G + g
        dma("sp", ident[:], cf_d[:, :], [], [r_ident])
        dma("sp", cbt[:], cb_d[:, 0:1152], [], [r_cbt])
        dma("sp", KE[64:128, :], cb_d[64:128, 1152:1152 + 8192], [], [r_KEe])
        dma("pool", poolw[:], poolw_all[128 * lg:128 * lg + 128, :], [], [r_poolw])
        dma("pool", pcoef[:], pcoef_all[128 * lg:128 * lg + 128, :], [], [r_pcoef])
        dma("pool", pcorr[:], pcorr_all[128 * g:128 * g + 128, :], [], [r_pcorr])
        dma("pool", halfc[:], halfc_d[:, :], [], [r_halfc])
        dma("pool", pekv[:], pekv_all[128 * l:128 * l + 128, :], [], [r_pekv])
        dma("pool", w2kv[:], w2kv_all[128 * l:128 * l + 128, :], [], [r_w2kv])
        dma("pool", relb[:], relb_all[33 * g:33 * g + 33, :], [], [r_relb])
        copy_op("dve", identb[:], ident[:], [r_ident], [r_identb])
        copy_op("dve", poolwb[:], poolw[:], [r_poolw], [r_poolwb])
        copy_op("dve", pekvb[:], pekv[:], [r_pekv], [r_pekvb])
        copy_op("dve", w2kvb[:], w2kv[:], [r_w2kv], [r_w2kvb])
        E("pool", "memset", dict(ap=kvT[:], constant=0.0), writes=r_kvT)
        E("pool", "memset", dict(ap=kcT[:], constant=0.0), writes=[r_kcT])
        E("pool", "memset", dict(ap=vcT[:], constant=0.0), writes=[r_vcT])
        E("pool", "memset", dict(ap=vcp[:], constant=0.0), writes=[r_vcp])
        E("pool", "memset", dict(ap=vcp[:, :, 64:65], constant=1.0), writes=[r_vcp])
        E("pool", "memset", dict(ap=VS[:, :, 64:65], constant=1.0), writes=[r_VSo])
        E("pool", "memset", dict(ap=VW[:, :, 64:65], constant=1.0), writes=[r_VWo])
        E("pool", "memset", dict(ap=seln[:], constant=0.0), writes=[r_seln])
        E("pool", "memset", dict(ap=pv[:, 0:16], constant=0.0), pwrites=[r_pv])
        E("pool", "memset", dict(ap=uu[:, 0:2], constant=0.0), pwrites=[r_uu])
        E("pool", "memset", dict(ap=KE[0:64, :], constant=0.0), writes=r_KE)
        E("pool", "memset", dict(ap=KW[:], constant=0.0), writes=r_KW)
        E("pool", "memset", dict(ap=VS[:, :, 0:64], constant=0.0), writes=r_VS)
        E("pool", "memset", dict(ap=VW[:, :, 0:64], constant=0.0), writes=r_VW)


        def load_cast(dst_fn, src_fn, ncols, r_dst):
            for j in range(8):
                s = next_stg()
                dma("sp" if j % 2 == 0 else "pool", stg[s][:, 0:ncols], src_fn(j), [], [r_stg[s]])
                copy_op(evac_eng(), dst_fn(j), stg[s][:, 0:ncols], [r_stg[s]], pwrites=[r_dst])

        load_cast(lambda j: Wf[:, j, :], lambda j: wf_all[D * lg + 128 * j:D * lg + 128 * j + 128, :], NFM, r_Wf)
        load_cast(lambda j: Wt[:, j, :], lambda j: wt_all[D * lg + 128 * j:D * lg + 128 * j + 128, :], NTM, r_Wt)
        s = next_stg()
        dma("sp", stg[s][:, 0:4096], w1kv_all[128 * l:128 * l + 128, :], [], [r_stg[s]])
        copy_op("act", w1kv[:].rearrange("p a b -> p (a b)"), stg[s][:, 0:4096], [r_stg[s]], [r_w1kv])

        bk = 7
        for l_ in range(32):
            mm(pb[bk][:, 0:1], w1kv[0:64, l_, :], pekvb[0:64, l_:l_ + 1], l_ == 0, l_ == 31, [r_w1kv, r_pekvb], pwrites=[r_pb[bk]])
        for l_ in range(32):
            mm(pb[6][:, 0:1], w1kv[64:128, l_, :], pekvb[64:128, l_:l_ + 1], l_ == 0, l_ == 31, [r_w1kv, r_pekvb], pwrites=[r_pb[6]])
        copy_op("dve", hbias[:, 0:1], pb[bk][:, 0:1], [r_pb[bk]], pwrites=[r_hbias])
        copy_op("dve", hbias[:, 1:2], pb[6][:, 0:1], [r_pb[6]], pwrites=[r_hbias])

        nchg = (LGT + 511) // 512
        order = [4063 // 512] + [c for c in range(nchg) if c != 4063 // 512]
        for ii, cch in enumerate(order):
            c0 = cch * 512
            w = min(512, LGT - c0)
            s = next_stg()
            dma("sp", stg[s][0:33, 0:w], oh_d[:, c0:c0 + w], [], [r_stg[s]])
            b = nb()
            mm(pb[b][0:4, 0:w], relb[:, :], stg[s][0:33, 0:w], True, True, [r_relb, r_stg[s]], [r_pb[b]])
            if ii == 0:
                copy_op("dve", crt[:], pb[b][0:4, 4063 - c0:4063 - c0 + 1], [r_pb[b]], [r_crt])
            gi = ii % 2
            E("dve", "tensor_scalar", dict(out=Gc[gi][:, 0:w], in0=pb[b][0:4, 0:w], scalar1=crt[:, 0:1], scalar2=None, op0=ALU.subtract),
              [r_pb[b], r_crt], [r_Gc[gi]])
            dma("sp", gd[:, c0:c0 + w], Gc[gi][:, 0:w], [r_Gc[gi]], pwrites=[r_gd], key=r_Gc[gi])

        def hankel(base, step):
            return bass.AP(gd_t, base, [[step, 128], [LGT, 4], [1, 128]])

        bases = [(1936 + 128 * dl, 1) for dl in range(2)] + [(LG + 128 * dl, 1) for dl in range(5)]
        for i, (base, step) in enumerate(bases):
            dma("sp" if i % 2 == 0 else "pool", Rres[:, i, :].rearrange("p (r t) -> p r t", r=4), hankel(base, step), [r_gd], pwrites=[r_Rres])
        r1c = [0]

        def load_x(ci_):
            s_ = next_stg()
            for a in range(4):
                dma("sp" if a % 2 == 0 else "pool", stg[s_][:, 1024 * a:1024 * a + 1024], xsrc[512 * ci_ + 128 * a:512 * ci_ + 128 * a + 128, :], xreads, pwrites=[r_stg[s_]])
            return s_

        r1map = {}

        def prefetch_r1(qi_):
            for c_ in range(qi_ // 16 + 1):
                m_ = qi_ - 16 * c_
                if m_ <= 16 and (qi_, c_) not in r1map:
                    ri = r1c[0] % NR1
                    r1c[0] += 1
                    r1map[(qi_, c_)] = ri
                    dma("pool", R1t[ri][:].rearrange("p (r t) -> p r t", r=4), hankel(128 * m_, 16), [r_gd], [r_R1t[ri]])

        xs_next = load_x(0)
        for ci in range(nchunks):
            t0 = 512 * ci
            s = xs_next
            xt = stg[s]
            for j in range(8):
                b = nb()
                for a in range(4):
                    tr(pb[b][:, 128 * a:128 * a + 128], xt[:, 1024 * a + 128 * j:1024 * a + 128 * j + 128], [r_stg[s]], pwrites=[r_pb[b]])
                copy_op(evac_eng(), xT[:, j, :], pb[b][:, :], [r_pb[b]], pwrites=[r_xT])
            if ci + 1 < nchunks:
                xs_next = load_x(ci + 1)


            def fm_block(col0, M):
                b = nb()
                for j in range(8):
                    mm(pb[b][0:M, :], Wf[:, j, col0:col0 + M], xT[:, j, :], j == 0, j == 7, [r_Wf, r_xT], pwrites=[r_pb[b]])
                return b

            b = fm_block(0, 128)
            copy_op("dve", pv[:, 16:528], pb[b][:, :], [r_pb[b]], pwrites=[r_pv])
            b = fm_block(128, 128)
            E("act", "activation", dict(out=pz[:], in_=pb[b][:, :], func=AF.Silu), [r_pb[b]], [r_pz])
            b = fm_block(256, 128)
            copy_op("dve", cbb[:], pb[b][:, :], [r_pb[b]], [r_cbb])
            b = fm_block(384, 128)
            copy_op("act", ccc[:], pb[b][:, :], [r_pb[b]], [r_ccc])
            b = fm_block(512, 128)
            E("dve", "tensor_tensor", dict(out=uu[:, 2:514], in0=ccc[:], in1=pb[b][:, :], op=ALU.mult), [r_ccc, r_pb[b]], pwrites=[r_uu])
            b = fm_block(640, 128)
            E("act", "activation", dict(out=cz[:], in_=pb[b][:, :], func=AF.Silu), [r_pb[b]], [r_cz])
            for r in range(4):
                b = fm_block(768 + 64 * r, 64)
                src = pb[b][0:64, :].rearrange("p (q t) -> p q t", q=4)
                copy_op("act", RQA[0:64, :, 128 * r:128 * r + 128], src, [r_pb[b]], pwrites=r_RQAq, scale=0.125)
                copy_op("dve", RQB[0:64, :, 128 * r:128 * r + 128], src, [r_pb[b]], pwrites=r_RQBq, scale=0.125)
            b = fm_block(1024, 128)
            copy_op("act", kvT[:, t0:t0 + 512], pb[b][:, :], [r_pb[b]], [r_kvT[ci]])
            b = fm_block(1152, 64)
            copy_op("dve", KE[0:64, t0:t0 + 512], pb[b][0:64, :], [r_pb[b]], [r_KE[ci]])
            b = fm_block(1216, 64)
            copy_op("act", KW[0:64, t0:t0 + 512], pb[b][0:64, :], [r_pb[b]], [r_KW[ci]])
            for a in range(4):
                b = nb()
                jt = 4 * ci + a
                for j in range(8):
                    mm(pb[b][:, 0:NTM], xT[:, j, 128 * a:128 * a + 128], Wt[:, j, :], j == 0, j == 7, [r_Wt, r_xT], pwrites=[r_pb[b]])
                copy_op("dve", VS[:, jt, 0:64], pb[b][:, 0:64], [r_pb[b]], pwrites=[r_VS[ci]])
                copy_op("dve", VW[:, jt, 0:64], pb[b][:, 64:128], [r_pb[b]], pwrites=[r_VW[ci]])
                E("act", "activation", dict(out=gsig[:, a, :], in_=pb[b][:, 128:140], func=AF.Sigmoid), [r_pb[b]], [r_gsig[a]])
                E("act", "activation", dict(out=zs[:, a, :], in_=pb[b][:, 140:396], func=AF.Silu), [r_pb[b]], [r_zs[a]])

            n0 = max(0, 32 * ci - 32)
            n1 = 32 * ci + 31
            cn = n1 - n0 + 1
            b = nb()
            rk = [r_kvT[max(ci - 1, 0)], r_kvT[ci], r_kvT[min(ci + 1, NCHUNK)]]
            E("dve", "tensor_copy", dict(out=Xc[:, :, 0:cn + 1], in_=kvT[:, 16 * n0:16 * n0 + 16 * (cn + 1)].rearrange("p (m l) -> p l m", l=16)), rk, [r_Xc])
            bv = nb()
            hb = (b, bv)
            for half, p0 in enumerate((0, 64)):
                for l_ in range(32):
                    rhs = Xc[p0:p0 + 64, l_ % 16, (l_ // 16):(l_ // 16) + cn]
                    mm(pb[hb[half]][:, 0:cn], w1kv[p0:p0 + 64, l_, :], rhs, l_ == 0, l_ == 31, [r_w1kv, r_Xc], pwrites=[r_pb[hb[half]]])
            for half in range(2):
                E("act", "activation", dict(out=hkv[:, 64 * half:64 * half + cn], in_=pb[hb[half]][:, 0:cn], func=AF.Silu, bias=hbias[:, half:half + 1]),
                  [r_pb[hb[half]], r_hbias], pwrites=[r_hkv])
            b2 = nb()
            mm(pb[b2][0:64, 0:cn], w2kvb[:, 0:64], hkv[:, 0:cn], True, True, [r_w2kvb, r_hkv], pwrites=[r_pb[b2]])
            mm(pb[b2][0:64, 64:64 + cn], w2kvb[:, 64:128], hkv[:, 64:64 + cn], True, True, [r_w2kvb, r_hkv], pwrites=[r_pb[b2]])
            E("dve", "tensor_tensor", dict(out=sA[:, 1:528], in0=pv[:, 1:528], in1=pv[:, 0:527], op=ALU.add), [r_pv], [r_sA])
            E("dve", "tensor_scalar", dict(out=pm[:], in0=sA[:, 16:528], scalar1=pcoef[:, 0:1], scalar2=None, op0=ALU.mult), [r_sA, r_pcoef], [r_pm])
            E("dve", "tensor_tensor", dict(out=sB[:, 3:528], in0=sA[:, 3:528], in1=sA[:, 1:526], op=ALU.add), [r_sA], [r_sB])
            E("dve", "scalar_tensor_tensor", dict(out=pm[:], in0=sB[:, 16:528], scalar=pcoef[:, 1:2], in1=pm[:], op0=ALU.mult, op1=ALU.add), [r_sB, r_pcoef, r_pm], [r_pm])
            E("dve", "tensor_tensor", dict(out=sA[:, 7:528], in0=sB[:, 7:528], in1=sB[:, 3:524], op=ALU.add), [r_sB], [r_sA])
            E("dve", "scalar_tensor_tensor", dict(out=pm[:], in0=sA[:, 16:528], scalar=pcoef[:, 2:3], in1=pm[:], op0=ALU.mult, op1=ALU.add), [r_sA, r_pcoef, r_pm], [r_pm])
            E("dve", "tensor_tensor", dict(out=sB[:, 15:528], in0=sA[:, 15:528], in1=sA[:, 7:520], op=ALU.add), [r_sA], [r_sB])
            E("dve", "scalar_tensor_tensor", dict(out=pm[:], in0=sB[:, 16:528], scalar=pcoef[:, 3:4], in1=pm[:], op0=ALU.mult, op1=ALU.add), [r_sB, r_pcoef, r_pm], [r_pm])
            if ci == 0:
                E("dve", "tensor_tensor", dict(out=pm[:, 0:16], in0=pm[:, 0:16], in1=pcorr[:], op=ALU.mult), [r_pm, r_pcorr], [r_pm])
            E("dve", "tensor_tensor", dict(out=pmb[:], in0=pm[:], in1=pv[:, 16:528], op=ALU.subtract), [r_pm, r_pv], [r_pmb])
            copy_op("dve", kcT[:, n0:n0 + cn], pb[b2][0:64, 0:cn], [r_pb[b2]], [r_kcT])
            copy_op("dve", vcT[:, n0:n0 + cn], pb[b2][0:64, 64:64 + cn], [r_pb[b2]], [r_vcT])
            for c in sorted(set((n0 // 128, n1 // 128))):
                b3 = nb()
                mm(pb[b3][:, 0:64], vcT[:, 128 * c:128 * c + 128], identb[0:64, 0:64], True, True, [r_vcT, r_identb], [r_pb[b3]])
                copy_op("dve", vcp[:, c, 0:64], pb[b3][:, 0:64], [r_pb[b3]], [r_vcp])

            b = nb()
            mm(pb[b][:, :], poolwb[:], pmb[:], True, True, [r_poolwb, r_pmb], [r_pb[b]])
            E("dve", "scalar_tensor_tensor", dict(out=ypool[:], in0=pb[b][:, :], scalar=pcoef[:, 4:5], in1=pz[:], op0=ALU.mult, op1=ALU.mult),
              [r_pb[b], r_pcoef, r_pz], [r_ypool])
            dma("sp", ys_d[512 * g:512 * g + 128, t0:t0 + 512], ypool[:], [r_ypool], pwrites=[r_ys], key=r_ypool)
            E("pool", "tensor_copy", dict(out=pv[:, 0:16], in_=pv[:, 512:528]), [r_pv], [r_pv])
            E("dve", "tensor_scalar", dict(out=yconv[:], in0=uu[:, 0:512], scalar1=pcoef[:, 5:6], scalar2=None, op0=ALU.mult), [r_uu, r_pcoef], [r_yconv])
            E("dve", "scalar_tensor_tensor", dict(out=yconv[:], in0=uu[:, 1:513], scalar=pcoef[:, 6:7], in1=yconv[:], op0=ALU.mult, op1=ALU.add), [r_uu, r_pcoef, r_yconv], [r_yconv])
            E("dve", "scalar_tensor_tensor", dict(out=yconv[:], in0=uu[:, 2:514], scalar=pcoef[:, 7:8], in1=yconv[:], op0=ALU.mult, op1=ALU.add), [r_uu, r_pcoef, r_yconv], [r_yconv])
            E("dve", "tensor_tensor", dict(out=yconv[:], in0=yconv[:], in1=cbb[:], op=ALU.mult), [r_yconv, r_cbb], [r_yconv])
            E("dve", "tensor_tensor", dict(out=yconvb[:], in0=yconv[:], in1=cz[:], op=ALU.mult), [r_yconv, r_cz], [r_yconvb])
            dma("sp", ys_d[512 * g + 128:512 * g + 256, t0:t0 + 512], yconvb[:], [r_yconvb], pwrites=[r_ys], key=r_yconvb)
            E("pool", "tensor_copy", dict(out=uu[:, 0:2], in_=uu[:, 512:514]), [r_uu], [r_uu])

            ecnt = [0]
            pend = []
            LOOKAHEAD = 2

            tlist = []
            pairc = [0]

            def tile_attn(lhsT, lhs_reads, rhs, rhs_reads, bias_ap, bias_reads, Obank, vrhs, v_reads, first, last, extra=None):
                tlist.append((lhsT, lhs_reads, rhs, rhs_reads, bias_ap, bias_reads, Obank, vrhs, v_reads, first, last, extra))

            def run_tiles():
                i = 0
                while i < len(tlist):
                    grp = tlist[i:i + 2]
                    i += len(grp)
                    P = pairc[0] % 2
                    pairc[0] += 1
                    for h, (lhsT, lhs_reads, rhs, rhs_reads, bias_ap, bias_reads, Obank, vrhs, v_reads, first, last, extra) in enumerate(grp):
                        b = 2 * P + h
                        mm(pb[b][:, :], lhsT, rhs, True, bias_ap is None, list(lhs_reads) + list(rhs_reads), [r_pb[b]])
                        if bias_ap is not None:
                            mm(pb[b][:, :], Jm, bias_ap, False, True, [r_cbt] + list(bias_reads), pwrites=[r_pb[b]])
                    wd = 512 * len(grp)
                    E("act", "activation", dict(out=ET[P][:, 0:wd], in_=SP[P][:, 0:wd], func=AF.Exp),
                      [r_pb[2 * P + h] for h in range(len(grp))], [r_E[2 * P + h] for h in range(len(grp))])

                    def stage2(grp=grp, P=P):
                        for h, (lhsT, lhs_reads, rhs, rhs_reads, bias_ap, bias_reads, Obank, vrhs, v_reads, first, last, extra) in enumerate(grp):
                            ei = 2 * P + h
                            for r in range(4):
                                mm(pb[Obank][:, 65 * r:65 * r + 65], Et[ei][:, 128 * r:128 * r + 128], vrhs, first and r == 0, last and r == 3, [r_E[ei]] + list(v_reads), pwrites=[r_pb[Obank]])
                            if extra is not None:
                                extra(ei)

                    pend.append(stage2)
                    while len(pend) > 1:
                        pend.pop(0)()
                del tlist[:]

            def flush():
                run_tiles()
                while pend:
                    pend.pop(0)()

            def fold(bi, bank, ya, ry, qb, first):
                Ob = pb[bank][:, 0:260].rearrange("p (r c) -> p r c", c=65)
                gs = gsig[:, qb, :].rearrange("p (r b) -> p r b", b=3)
                if bi > 0:
                    E("dve", "tensor_scalar_max", dict(out=den[:, 4 * bi:4 * bi + 4], in0=Ob[:, :, 64], scalar1=1e-30), [r_pb[bank]], pwrites=[r_den])
                    E("dve", "reciprocal", dict(out=den[:, 4 * bi:4 * bi + 4], in_=den[:, 4 * bi:4 * bi + 4]), [r_den], pwrites=[r_den])
                E("dve", "tensor_tensor", dict(out=coef[:, 4 * bi:4 * bi + 4], in0=den[:, 4 * bi:4 * bi + 4], in1=gs[:, :, bi], op=ALU.mult),
                  [r_den, r_gsig[qb]], pwrites=[r_coef])
                for r in range(4):
                    if first:
                        E("dve", "tensor_scalar", dict(out=ya[:, 64 * r:64 * r + 64], in0=Ob[:, r, 0:64], scalar1=coef[:, 4 * bi + r:4 * bi + r + 1], scalar2=None, op0=ALU.mult),
                          [r_pb[bank], r_coef], pwrites=[ry])
                    else:
                        E("dve", "scalar_tensor_tensor", dict(out=ya[:, 64 * r:64 * r + 64], in0=Ob[:, r, 0:64], scalar=coef[:, 4 * bi + r:4 * bi + r + 1], in1=ya[:, 64 * r:64 * r + 64], op0=ALU.mult, op1=ALU.add),
                          [r_pb[bank], r_coef, ry], pwrites=[ry])

            def make_deferred(qi, qb, RA, RB, ya, ry):
                def run():
                    for jt in range(qi + 1):
                        if jt < 32:
                            rhs, rr = RA, [r_RQAq[qb], r_RQAs[qb]]
                        else:
                            rhs, rr = RB, [r_RQBq[qb], r_RQBs[qb]]
                        bias_ap = Rres[:, qi - jt, :] if qi - jt <= 1 else None
                        tile_attn(KE[:, 128 * jt:128 * jt + 128], [r_KE[jt // 4], r_KEe], rhs, rr, bias_ap, [r_Rres], 5,
                                  VS[:, jt, :], [r_VS[jt // 4], r_VSo], jt == 0, jt == qi)
                    flush()
                    fold(1, 5, ya, ry, qb, False)
                    E("dve", "tensor_tensor", dict(out=ya, in0=ya, in1=zs[:, qb, :], op=ALU.mult), [ry, r_zs[qb]], [ry])

                    def out_fn():
                        bt_ = nb()
                        tr(pb[bt_][:, 0:128], ya[:, 0:128], [ry], pwrites=[r_pb[bt_]])
                        tr(pb[bt_][:, 128:256], ya[:, 128:256], [ry], pwrites=[r_pb[bt_]])
                        copy_op("act", yTn[:].rearrange("p a b -> p (a b)"), pb[bt_][:, 0:256], [r_pb[bt_]], [r_yTn])
                        q0 = 128 * qi
                        dma("sp", ys_d[512 * g + 256:512 * g + 512, q0:q0 + 128].rearrange("(i c) t -> c i t", c=128), yTn[:], [r_yTn], pwrites=[r_ys], key=r_yTn)

                    pout.append(out_fn)
                return run

            deferred = None
            pout = []
            for qb in range(4):
                qi = 4 * ci + qb
                RA = RQA[:, qb, :]
                RB = RQB[:, qb, :]
                ya = yacc[:, qi % 2, :]
                ry = r_yacc[qi % 2]
                cmax = qi // 16
                prefetch_r1(qi)
                if qi + 1 < 4 * nchunks:
                    prefetch_r1(qi + 1)
                for c in range(cmax + 1):
                    m = qi - 16 * c
                    bias_ap, bias_reads = None, []
                    if m <= 16:
                        ri = r1map[(qi, c)]
                        bias_ap, bias_reads = R1t[ri][:], [r_R1t[ri]]

                    def extra(ei, c=c, cmax=cmax):
                        for r in range(4):
                            mm(pb[7][:, 128 * r:128 * r + 128], Et[ei][:, 128 * r:128 * r + 128], ovt[:, 128 * c:128 * c + 128], c == 0 and r == 0, c == cmax and r == 3,
                               [r_E[ei], r_cbt], pwrites=[r_pb[7]])

                    tile_attn(kcT[:, 128 * c:128 * c + 128], [r_kcT], RA[0:64, :], [r_RQAq[qb]], bias_ap, bias_reads, 4, vcp[:, c, :], [r_vcp], c == 0, c == cmax, extra)
                flush()
                while pout:
                    pout.pop(0)()
                O1 = pb[4][:, 0:260].rearrange("p (r c) -> p r c", c=65)
                E("dve", "tensor_scalar_max", dict(out=den[:, 0:4], in0=O1[:, :, 64], scalar1=1e-30), [r_pb[4]], pwrites=[r_den])
                E("dve", "reciprocal", dict(out=den[:, 0:4], in_=den[:, 0:4]), [r_den], pwrites=[r_den])
                E("pool", "memset", dict(ap=sc[:], constant=-1e30), writes=[r_sc])
                ncol = 2 * qi + 2
                E("dve", "tensor_scalar", dict(out=sc[:, 0:ncol], in0=pb[7][:, 0:ncol], scalar1=den[:, 0:1], scalar2=None, op0=ALU.mult),
                  [r_pb[7], r_den], pwrites=[r_sc])
                for r in range(1, 4):
                    E("dve", "scalar_tensor_tensor", dict(out=sc[:, 0:ncol], in0=pb[7][:, 128 * r:128 * r + ncol], scalar=den[:, r:r + 1], in1=sc[:, 0:ncol], op0=ALU.mult, op1=ALU.add),
                      [r_pb[7], r_den, r_sc], pwrites=[r_sc])
                if qi >= 1:
                    ja = 2 * qi - 1
                    E("dve", "tensor_scalar", dict(out=sc[:, ja:ja + 1], in0=sc[:, ja:ja + 1], scalar1=halfc[:, 0:1], scalar2=halfc[:, 1:2], op0=ALU.mult, op1=ALU.add),
                      [r_sc, r_halfc], pwrites=[r_sc])
                E("dve", "memset", dict(ap=sc[:, 0:1], constant=1e9), [r_sc], pwrites=[r_sc])
                E("dve", "memset", dict(ap=sc[:, 2 * qi:2 * qi + 1], constant=2e9), [r_sc], pwrites=[r_sc])
                E("dve", "tensor_copy", dict(out=sc[:, 2 * qi + 1:2 * qi + 2], in_=halfc[:, 2:3]), [r_sc, r_halfc], pwrites=[r_sc])
                E("dve", "max", dict(out=mx[:, 0:8], in_=sc[:]), [r_sc], pwrites=[r_mx])
                E("dve", "match_replace", dict(out=sc2[:], in_to_replace=mx[:, 0:8], in_values=sc[:], imm_value=-3e38), [r_sc, r_mx], [r_sc2])
                E("dve", "max", dict(out=mx[:, 8:16], in_=sc2[:]), [r_sc2, r_mx], pwrites=[r_mx])
                E("dve", "tensor_reduce", dict(out=th[:], in_=mx[:, 8:16], axis=AX.X, op=ALU.min), [r_mx], [r_th])
                E("dve", "tensor_scalar", dict(out=seln[:, 64:192], in0=sc[:], scalar1=th[:, 0:1], scalar2=NEGB, op0=ALU.is_lt, op1=ALU.mult),
                  [r_sc, r_th], [r_seln])
                fold(0, 4, ya, ry, qb, True)
                dmax = min(4, qi)
                for dl in range(dmax + 1):
                    jt = qi - dl
                    tile_attn(KW[0:64, 128 * jt:128 * jt + 128], [r_KW[jt // 4]], RA[0:64, :], [r_RQAq[qb]], Rres[:, 2 + dl, :], [r_Rres], 6,
                              VW[:, jt, :], [r_VW[jt // 4], r_VWo], dl == 0, dl == dmax)
                flush()
                fold(2, 6, ya, ry, qb, False)
                if deferred is not None:
                    deferred()
                b = nb()
                mm(pb[b][:, :], seln[:, 0:128], I4, True, True, [r_seln, r_cbt], [r_pb[b]])
                copy_op("dve", RA[64:128, :], pb[b][64:128, :], [r_pb[b]], [r_RQAs[qb]])
                if qi >= 32:
                    b = nb()
                    mm(pb[b][:, :], seln[:, 64:192], I4, True, True, [r_seln, r_cbt], [r_pb[b]])
                    copy_op("dve", RB[64:128, :], pb[b][64:128, :], [r_pb[b]], [r_RQBs[qb]])
                deferred = make_deferred(qi, qb, RA, RB, ya, ry)
            deferred()
            while pout:
                pout.pop(0)()


    WoV = KE[:].rearrange("p (a b) -> p a b", a=8)
    yTbV = [kvT[:, 4096 * s_:4096 * s_ + 4096].rearrange("p (a b) -> p a b", a=8) for s_ in range(2)]
    WfF = bass.AP(Wf, 0, [[8 * NFM, 128], [1, 8 * NFM]]).bitcast(F32)
    xtV = [WfF[:, 0:1024], WfF[:, 1024:2048]]
    ztV = [WfF[:, 2048:3072], WfF[:, 3072:4096]]
    WtF = bass.AP(Wt, 0, [[8 * NTM, 128], [1, 8 * NTM]]).bitcast(F32)
    sqV2 = [WfF[:, 4096:5120], WtF[:, 0:1024]]
    otV = [bass.AP(Rres, 0, [[3584, 128], [1, 3584]]).bitcast(F32)[:, 0:1024], bass.AP(xT, 0, [[4096, 128], [1, 4096]]).bitcast(F32)[:, 0:1024]]
    w1F = bass.AP(w1kv, 0, [[4096, 128], [1, 4096]]).bitcast(F32)
    GtV = w1F[:, 0:1024]
    BtV = w1F[:, 1024:2048]
    stt2 = sb("stt", [128, 8], F32); r_stt2 = [k.R("stt0"), k.R("stt1")]
    r_WoV = k.R("WoV"); r_yTbV = [k.R("yTbV0"), k.R("yTbV1")]; r_xtV = [k.R("xtV0"), k.R("xtV1")]
    r_ztV = [k.R("ztV0"), k.R("ztV1")]; r_sqV2 = [k.R("sqV0"), k.R("sqV1")]; r_otV = [k.R("otV0"), k.R("otV1")]
    r_GtV = k.R("GtV"); r_BtV = k.R("BtV")
    ALPHA_ = (2 * 2) ** 0.25

    def B_body(l, xsrc, xreads, dst, r_dst):
        dma("sp", GtV, bass.AP(lng_t, D * l, [[0, 128], [1, D]]), [], [r_GtV])
        dma("sp", BtV, bass.AP(lnb_t, D * l, [[0, 128], [1, D]]), [], [r_BtV])
        for j in range(8):
            s = next_stg()
            dma("sp" if j % 2 == 0 else "pool", stg[s][:, 0:1024], wo_all[D * l + 128 * j:D * l + 128 * j + 128, :], [], [r_stg[s]])
            copy_op(evac_eng(), WoV[:, j, :], stg[s][:, 0:1024], [r_stg[s]], pwrites=[r_WoV])
        tiles = [(ch, a) for ch in range(TT // 512) for a in range(4)]
        NTL = len(tiles)

        def P1(t):
            ch, a = tiles[t]
            s = ch % 2
            u = t % 2
            t0 = 512 * ch
            r0 = t0 + 128 * a
            if a == 0:
                dma("sp" if ch % 2 == 0 else "pool", yTbV[s], ys_d[:, t0:t0 + 512].rearrange("(j p) t -> p j t", p=128), [r_ys], [r_yTbV[s]])
            stt = stt2[:, 4 * u:4 * u + 4]
            for h in range(2):
                b = nb()
                for j in range(8):
                    mm(pb[b][:, :], yTbV[s][:, j, 128 * a:128 * a + 128], WoV[:, j, 512 * h:512 * h + 512], j == 0, j == 7, [r_yTbV[s], r_WoV], pwrites=[r_pb[b]])
                E("dve", "scalar_tensor_tensor", dict(out=ztV[u][:, 512 * h:512 * h + 512], in0=xtV[u][:, 512 * h:512 * h + 512], scalar=ALPHA_, in1=pb[b][:, :], op0=ALU.mult, op1=ALU.add),
                  [r_xtV[u], r_pb[b]], pwrites=[r_ztV[u]])
            E("dve", "tensor_reduce", dict(out=stt[:, 0:1], in_=ztV[u], axis=AX.X, op=ALU.add), [r_ztV[u]], pwrites=[r_stt2[u]])
            E("dve", "tensor_scalar_mul", dict(out=stt[:, 0:1], in0=stt[:, 0:1], scalar1=1.0 / 1024), [r_stt2[u]], pwrites=[r_stt2[u]])
            E("dve", "tensor_scalar", dict(out=ztV[u], in0=ztV[u], scalar1=stt[:, 0:1], scalar2=None, op0=ALU.subtract), [r_ztV[u], r_stt2[u]], [r_ztV[u]])

        def P2a(t):
            u = t % 2
            E("pool", "tensor_tensor", dict(out=sqV2[u], in0=ztV[u], in1=ztV[u], op=ALU.mult), [r_ztV[u]], [r_sqV2[u]])

        def P2b(t):
            u = t % 2
            stt = stt2[:, 4 * u:4 * u + 4]
            E("dve", "tensor_reduce", dict(out=stt[:, 1:2], in_=sqV2[u], axis=AX.X, op=ALU.add), [r_sqV2[u]], pwrites=[r_stt2[u]])
            E("dve", "tensor_scalar", dict(out=stt[:, 1:2], in0=stt[:, 1:2], scalar1=1.0 / 1024, scalar2=1e-5, op0=ALU.mult, op1=ALU.add), [r_stt2[u]], pwrites=[r_stt2[u]])
            E("act", "activation", dict(out=stt[:, 2:3], in_=stt[:, 1:2], func=AF.Sqrt), [r_stt2[u]], pwrites=[r_stt2[u]])
            E("dve", "reciprocal", dict(out=stt[:, 3:4], in_=stt[:, 2:3]), [r_stt2[u]], pwrites=[r_stt2[u]])

        def P3(t):
            ch, a = tiles[t]
            u = t % 2
            r0 = 512 * ch + 128 * a
            stt = stt2[:, 4 * u:4 * u + 4]
            E("dve", "scalar_tensor_tensor", dict(out=otV[u], in0=ztV[u], scalar=stt[:, 3:4], in1=GtV, op0=ALU.mult, op1=ALU.mult), [r_ztV[u], r_stt2[u], r_GtV], [r_otV[u]])
            E("pool", "tensor_tensor", dict(out=otV[u], in0=otV[u], in1=BtV, op=ALU.add), [r_otV[u], r_BtV], [r_otV[u]])
            dma("sp", dst[r0:r0 + 128, :], otV[u], [r_otV[u]], pwrites=[r_dst], key=r_otV[u])

        def LX(t):
            ch, a = tiles[t]
            r0 = 512 * ch + 128 * a
            dma("act", xtV[t % 2], xsrc[r0:r0 + 128, :], xreads, [r_xtV[t % 2]])

        LX(0)
        for t in range(NTL + 2):
            if t + 1 < NTL:
                LX(t + 1)
            if 0 <= t - 1 < NTL:
                P2a(t - 1)
            if 0 <= t - 2 < NTL:
                P3(t - 2)
            if t < NTL:
                P1(t)
            if 0 <= t - 1 < NTL:
                P2b(t - 1)

    xsrc, xreads = x_d, []
    for l in range(L):
        for g in range(G):
            A_body(l, g, xsrc, xreads)
        k.barrier()
        last = l == L - 1
        dst, r_dst = (out_d, r_out) if last else (x1_d, r_x1)
        B_body(l, xsrc, xreads, dst, r_dst)
        k.barrier()
        xsrc, xreads = x1_d, [r_x1]
    stats = k.emit()
    return nc, stats


def host_fused_inputs(inp, L=2, G=2):
    d = dict(host_consts())
    per = [[host_layer_inputs(inp, l, g) for g in range(G)] for l in range(L)]
    d["wf"] = np.concatenate([per[l][g]["wf"] for l in range(L) for g in range(G)], axis=0)
    d["wt"] = np.concatenate([per[l][g]["wt"] for l in range(L) for g in range(G)], axis=0)
    d["poolw"] = np.concatenate([per[l][g]["poolw"] for l in range(L) for g in range(G)], axis=0)
    d["pcoef"] = np.concatenate([per[l][g]["pcoef"] for l in range(L) for g in range(G)], axis=0)
    d["pcorr"] = np.concatenate([per[0][g]["pcorr"] for g in range(G)], axis=0)
    d["w1kv"] = np.concatenate([per[l][0]["w1kv"] for l in range(L)], axis=0)
    d["pekv"] = np.concatenate([per[l][0]["pekv"] for l in range(L)], axis=0)
    d["w2kv"] = np.concatenate([per[l][0]["w2kv"] for l in range(L)], axis=0)
    d["relb"] = np.concatenate([per[0][g]["relb"] for g in range(G)], axis=0)
    perm = []
    for g in range(G):
        perm += list(range(128 * g, 128 * g + 128)) + list(range(256 + 128 * g, 256 + 128 * g + 128)) + list(range(512 + 256 * g, 512 + 256 * g + 256))
    perm = np.asarray(perm)
    d["wo"] = np.concatenate([inp["w_out"][l][perm] for l in range(L)], axis=0)
    d["lng"] = np.ascontiguousarray(inp["ln_g"][:L])
    d["lnb"] = np.ascontiguousarray(inp["ln_b"][:L])
    return {k_: np.ascontiguousarray(v) for k_, v in d.items()}


_CACHE = {}


def kernel(x, w_in, w_out, pool_w, pool_scale, conv_w, cmp_pe_k, cmp_w1_k, cmp_w2_k,
           cmp_pe_v, cmp_w1_v, cmp_w2_v, rel_bias, ln_g, ln_b):
    from concourse.bass_utils import run_bass_kernel_spmd
    inp = dict(x=x, w_in=w_in, w_out=w_out, pool_w=pool_w, pool_scale=pool_scale, conv_w=conv_w,
               cmp_pe_k=cmp_pe_k, cmp_w1_k=cmp_w1_k, cmp_w2_k=cmp_w2_k, cmp_pe_v=cmp_pe_v,
               cmp_w1_v=cmp_w1_v, cmp_w2_v=cmp_w2_v, rel_bias=rel_bias, ln_g=ln_g, ln_b=ln_b)
    inp = {k_: np.asarray(v, dtype=np.float32) for k_, v in inp.items()}
    if "F" not in _CACHE:
        _CACHE["F"] = build_F(2, 2, NCHUNK)[0]
    nc = _CACHE["F"]
    base = host_fused_inputs(inp, 2, 2)
    in_maps = []
    for core in range(8):
        m = dict(base)
        m["x"] = np.ascontiguousarray(inp["x"][core // 2])
        in_maps.append(m)
    res = run_bass_kernel_spmd(nc, in_maps, core_ids=list(range(8))).results
    return np.stack([res[2 * b]["out"] for b in range(inp["x"].shape[0])], axis=0)
```

```python
import math
import ml_dtypes
import numpy as np
import concourse.bass as bass
import concourse.mybir as mybir

F32 = mybir.dt.float32
BF16 = mybir.dt.bfloat16
AF = mybir.ActivationFunctionType
ALU = mybir.AluOpType
AX = mybir.AxisListType

COMPUTE = ("pe", "act", "dve", "pool")


class Res:
    __slots__ = ("name", "writers", "readers", "sem", "dma_cnt", "gen_deps", "excl")

    def __init__(self, name):
        self.name = name
        self.writers = []
        self.readers = []
        self.gen_deps = []
        self.excl = False
        self.sem = {}
        self.dma_cnt = {}


class Op:
    __slots__ = ("eng", "fn", "deps", "is_dma", "dres", "dval", "idx", "cnt", "signal")

    def __init__(self, eng, fn, is_dma):
        self.eng = eng
        self.fn = fn
        self.deps = []
        self.is_dma = is_dma
        self.dres = None
        self.dval = 0
        self.cnt = 0
        self.signal = False


class Emitter:
    def __init__(self, nc):
        self.nc = nc
        self.ops = []
        self.res = {}
        self.nres = 0
        self.dma_res = {}
        self.phase = Res("phase")
        self.bar_tile = None

    def R(self, name=None):
        self.nres += 1
        r = Res(name or f"r{self.nres}")
        return r

    def sb(self, name, shape, dtype):
        t = self.nc.alloc_sbuf_tensor(name, list(shape), dtype)
        t.res = None
        return t

    def op(self, eng, fn, reads=(), writes=(), pwrites=(), dma=False, key=None):
        o = Op(eng, fn, dma)
        o.idx = len(self.ops)
        if self.phase not in writes:
            reads = list(reads) + [self.phase]
        xr = [r for r in reads if r.excl]
        if xr:
            reads = [r for r in reads if not r.excl]
            writes = list(writes) + [r for r in xr if r not in writes]
        deps = []
        for r in reads:
            deps.extend((p, "raw") for p in r.writers)
        for r in writes:
            deps.extend((p, "war") for p in r.readers)
            deps.extend((p, "waw") for p in r.writers)
        for r in pwrites:
            if r.readers:
                deps.extend((p, "war") for p in r.readers)
                deps.extend((p, "waw") for p in r.writers if not (p.is_dma and dma))
            else:
                deps.extend((p, "war") for p in r.gen_deps)
                deps.extend((p, "waw") for p in r.writers if not (p.is_dma and dma))
        for r in reads:
            r.readers.append(o)
        for r in writes:
            r.gen_deps = r.readers + r.writers
            r.writers = [o]
            r.readers = []
        for r in pwrites:
            if r.readers:
                r.gen_deps = r.readers + r.writers
                r.writers = [o]
                r.readers = []
            else:
                r.writers.append(o)
        if dma:
            allw = list(writes) + list(pwrites)
            d = key if key is not None else allw[0]
            self.dma_res[id(d)] = d
            d.dma_cnt[eng] = d.dma_cnt.get(eng, 0) + 1
            o.dres = d
            o.dval = 16 * d.dma_cnt[eng]
        best = {}
        for p, kind in deps:
            if p is o:
                continue
            if p.is_dma:
                key = ("d", id(p.dres), p.eng)
                if key not in best or best[key].dval < p.dval:
                    best[key] = p
            else:
                if p.eng == o.eng and not dma:
                    if p.eng == "pe":
                        continue
                key = ("c", p.eng)
                if key not in best or best[key].idx < p.idx:
                    best[key] = p
        o.deps = list(best.values())
        for p in o.deps:
            p.signal = True
        self.ops.append(o)
        return o

    def barrier(self):
        if self.bar_tile is None:
            self.bar_tile = self.nc.alloc_sbuf_tensor("s_bar_tile", [128, 2], F32)
        t = self.bar_tile
        self.op("dve", lambda e: e.memset(t[:], 0.0), writes=[self.phase])

    def emit(self, final_res=()):
        nc = self.nc
        esem = {e: nc.alloc_semaphore(f"s_{e}") for e in COMPUTE}
        cnt = {e: 0 for e in COMPUTE}
        for o in self.ops:
            if o.is_dma:
                if o.eng not in o.dres.sem:
                    o.dres.sem[o.eng] = nc.alloc_semaphore(f"d_{o.dres.name}_{o.eng}")
            elif o.signal:
                cnt[o.eng] += 1
                o.cnt = cnt[o.eng]
        engs = ["pe", "act", "dve", "pool", "sp"]
        per = {e: [o for o in self.ops if o.eng == e] for e in engs}
        final_waits = [(r.sem[e], 16 * r.dma_cnt[e]) for r in self.dma_res.values() for e in r.sem]

        def run(e, eng):
            waited = {}
            for o in per[e]:
                for p in o.deps:
                    if p.is_dma:
                        s, v = p.dres.sem[p.eng], p.dval
                    else:
                        s, v = esem[p.eng], p.cnt
                    if waited.get(s.num, -1) >= v:
                        continue
                    waited[s.num] = v
                    eng.wait_ge(s, v)
                ins = o.fn(eng)
                if o.is_dma:
                    ins.then_inc(o.dres.sem[o.eng], 16)
                elif o.signal:
                    ins.then_inc(esem[o.eng], 1)
            if e == "sp":
                for s, v in final_waits:
                    eng.wait_ge(s, v)

        with nc.Block() as block:
            @block.tensor
            def _(eng):
                run("pe", eng)

            @block.scalar
            def _(eng):
                run("act", eng)

            @block.vector
            def _(eng):
                run("dve", eng)

            @block.gpsimd
            def _(eng):
                run("pool", eng)

            @block.sync
            def _(eng):
                run("sp", eng)
        return {e: len(per[e]) for e in engs}


T = 8192
D = 1024
NCHUNK = 16
LG = 4208
LG3 = 768
LGT = LG + LG3
NEGB = -30000.0
NFM = 1280
NTM = 396


def bucket_np(n):
    n = np.maximum(n, 0)
    nf = np.maximum(n, 1).astype(np.float32)
    large = 16 + (np.log(nf / np.float32(16)) / np.float32(math.log(8.0)) * np.float32(16)).astype(np.int32)
    large = np.minimum(large, 31)
    return np.where(n < 16, n, large)


def host_consts():
    c = {}
    c["cf"] = np.eye(128, dtype=np.float32)
    J = np.eye(128, dtype=np.float32)[::-1].copy()
    I4 = np.tile(np.eye(128, dtype=np.float32), (1, 4))
    ov = np.zeros((512, 128), np.float32)
    for i in range(511):
        for j in range(128):
            o = min(16 * i + 32, 64 * j + 64) - max(16 * i, 64 * j)
            if o > 0:
                ov[i, j] = o / 16.0
    ovc = ov.reshape(4, 128, 128).transpose(1, 0, 2).reshape(128, 512)
    E = np.zeros((128, 8192), np.float32)
    for jt in range(64):
        for kk in range(128):
            j = 2 * jt + (1 if kk >= 64 else 0)
            E[64 + (j % 64), jt * 128 + kk] = 1.0
    c["cb"] = np.concatenate([J, I4, ovc, E], axis=1).astype(ml_dtypes.bfloat16)
    oh = np.zeros((33, LGT), np.float32)
    i = np.arange(LG)
    n = i - 2063
    b = bucket_np(n)
    oh[np.where(n < 0, 32, b), i] = 1.0
    i3 = np.arange(LG3)
    n3 = i3 - 127
    b3 = bucket_np(n3)
    oh[np.where((n3 < 0) | (n3 >= 512), 32, b3), LG + i3] = 1.0
    c["oh"] = oh
    hc = np.zeros((128, 4), np.float32)
    hc[64:, 0] = 1.0
    hc[:64, 1] = 3e9
    hc[:64, 2] = -1e30
    hc[64:, 2] = 4e9
    c["halfc"] = hc
    return c


POOL_WINDOWS = (2, 4, 8, 16)


def host_layer_inputs(inp, l, g):
    w_in = inp["w_in"][l]
    offs = np.cumsum([0, 256, 256, 256, 256, 256, 256, 512, 128, 128, 128, 128, 128, 128, 24, 512])
    (o_pv, o_pz, o_cb, o_cc, o_cx, o_cz, o_q, o_kc, o_vc, o_ks, o_vs, o_kw, o_vw, o_g, o_z) = offs[:15]
    s = slice
    cols = []
    for o in (o_pv, o_pz, o_cb, o_cc, o_cx, o_cz):
        cols.append(np.arange(o + 128 * g, o + 128 * g + 128))
    for r in range(4):
        cols.append(np.arange(o_q + 64 * (4 * g + r), o_q + 64 * (4 * g + r) + 64))
    cols.append(np.arange(o_kc + 64 * g, o_kc + 64 * g + 64))
    cols.append(np.arange(o_vc + 64 * g, o_vc + 64 * g + 64))
    cols.append(np.arange(o_ks + 64 * g, o_ks + 64 * g + 64))
    cols.append(np.arange(o_kw + 64 * g, o_kw + 64 * g + 64))
    cf = np.concatenate(cols)
    assert cf.size == NFM
    ct = np.concatenate([
        np.arange(o_vs + 64 * g, o_vs + 64 * g + 64),
        np.arange(o_vw + 64 * g, o_vw + 64 * g + 64),
        np.arange(o_g + 12 * g, o_g + 12 * g + 12),
        np.arange(o_z + 256 * g, o_z + 256 * g + 256),
    ])
    assert ct.size == NTM
    d = {}
    d["wf"] = np.ascontiguousarray(w_in[:, cf])
    d["wt"] = np.ascontiguousarray(w_in[:, ct])
    pw = np.zeros((128, 128), np.float32)
    pw[0:64, 0:64] = inp["pool_w"][l, 2 * g]
    pw[64:128, 64:128] = inp["pool_w"][l, 2 * g + 1]
    d["poolw"] = pw
    pc = np.zeros((128, 8), np.float32)
    for h in range(2):
        w = POOL_WINDOWS[2 * g + h]
        pc[64 * h:64 * h + 64, POOL_WINDOWS.index(w)] = 1.0 / w
    pc[:, 4] = inp["pool_scale"][l, 128 * g:128 * g + 128]
    for k in range(3):
        pc[:, 5 + k] = inp["conv_w"][l, k, 128 * g:128 * g + 128]
    d["pcoef"] = pc
    pcorr = np.ones((128, 16), np.float32)
    for h in range(2):
        w = POOL_WINDOWS[2 * g + h]
        for t in range(16):
            pcorr[64 * h:64 * h + 64, t] = w / min(t + 1, w)
    d["pcorr"] = pcorr
    w1k = inp["cmp_w1_k"][l].reshape(32, 64, 128).transpose(1, 0, 2).reshape(64, 32 * 128)
    w1v = inp["cmp_w1_v"][l].reshape(32, 64, 128).transpose(1, 0, 2).reshape(64, 32 * 128)
    d["w1kv"] = np.ascontiguousarray(np.concatenate([w1k, w1v], axis=0))
    d["pekv"] = np.ascontiguousarray(np.concatenate([inp["cmp_pe_k"][l].T, inp["cmp_pe_v"][l].T], axis=0))
    d["w2kv"] = np.ascontiguousarray(np.concatenate([inp["cmp_w2_k"][l], inp["cmp_w2_v"][l]], axis=1))
    rb = np.full((33, 4), NEGB, np.float32)
    rb[:32] = inp["rel_bias"][:, 4 * g:4 * g + 4]
    d["relb"] = rb
    return d


def build_F(L=2, G=2, nchunks=NCHUNK):
    nc = bass.Bass("TRN2", target_bir_lowering=False)
    k = Emitter(nc)

    def din(name, shape, dt=F32):
        return nc.dram_tensor(name, shape, dt, kind="ExternalInput").ap()

    def dout(name, shape, dt=F32):
        return nc.dram_tensor(name, shape, dt, kind="ExternalOutput").ap()

    TT = 512 * nchunks
    x_d = din("x", [TT, D])
    wf_all = din("wf", [L * G * D, NFM])
    wt_all = din("wt", [L * G * D, NTM])
    poolw_all = din("poolw", [L * G * 128, 128])
    pcoef_all = din("pcoef", [L * G * 128, 8])
    pcorr_all = din("pcorr", [G * 128, 16])
    w1kv_all = din("w1kv", [L * 128, 4096])
    pekv_all = din("pekv", [L * 128, 32])
    w2kv_all = din("w2kv", [L * 128, 128])
    relb_all = din("relb", [G * 33, 4])
    oh_d = din("oh", [33, LGT])
    cf_d = din("cf", [128, 128])
    cb_d = din("cb", [128, 128 + 512 + 512 + 8192], BF16)
    halfc_d = din("halfc", [128, 4])
    wo_all = din("wo", [L * D, D])
    lng_t = nc.dram_tensor("lng", [L, D], F32, kind="ExternalInput")
    lnb_t = nc.dram_tensor("lnb", [L, D], F32, kind="ExternalInput")
    out_d = dout("out", [TT, D])
    gd_t = nc.dram_tensor("gd", [4, LGT], BF16)
    gd = gd_t.ap()
    ys_d = nc.dram_tensor("ys", [D, TT], BF16).ap()
    x1_d = nc.dram_tensor("x1", [TT, D], F32).ap()
    r_gd = k.R("gd")
    r_ys = k.R("ys")
    r_x1 = k.R("x1")
    r_out = k.R("out")

    def sb(name, shape, dt):
        return nc.alloc_sbuf_tensor("s_" + name, list(shape), dt)

    def E(eng, name, kw, reads=(), writes=(), pwrites=()):
        k.op(eng, lambda e: getattr(e, name)(**kw), reads=reads, writes=writes, pwrites=pwrites)

    Wf = sb("Wf", [128, 8, NFM], BF16); r_Wf = k.R("Wf")
    Wt = sb("Wt", [128, 8, NTM], BF16); r_Wt = k.R("Wt")
    ident = sb("ident", [128, 128], F32); r_ident = k.R("ident")
    identb = sb("identb", [128, 128], BF16); r_identb = k.R("identb")
    cbt = sb("cbt", [128, 128 + 512 + 512], BF16); r_cbt = k.R("cbt")
    Jm = cbt[:, 0:128]
    I4 = cbt[:, 128:640]
    ovt = cbt[:, 640:1152]
    KE = sb("KE", [128, T], BF16); r_KEe = k.R("KEe")
    r_KE = [k.R(f"KE{i}") for i in range(NCHUNK)]
    KW = sb("KW", [64, T], BF16); r_KW = [k.R(f"KW{i}") for i in range(NCHUNK)]
    kvT = sb("kvT", [128, T + 32], BF16); r_kvT = [k.R(f"kvT{i}") for i in range(NCHUNK + 1)]
    VS = sb("VS", [128, 64, 65], BF16); r_VS = [k.R(f"VS{i}") for i in range(NCHUNK)]; r_VSo = k.R("VSo")
    VW = sb("VW", [128, 64, 65], BF16); r_VW = [k.R(f"VW{i}") for i in range(NCHUNK)]; r_VWo = k.R("VWo")
    kcT = sb("kcT", [64, 544], BF16); r_kcT = k.R("kcT")
    vcT = sb("vcT", [64, 544], BF16); r_vcT = k.R("vcT")
    vcp = sb("vcp", [128, 4, 65], BF16); r_vcp = k.R("vcp")
    Rres = sb("Rres", [128, 7, 512], BF16); r_Rres = k.R("Rres")
    NR1 = 4
    R1t = [sb(f"R1t{i}", [128, 512], BF16) for i in range(NR1)]; r_R1t = [k.R(f"R1t{i}") for i in range(NR1)]
    poolw = sb("poolw", [128, 128], F32); r_poolw = k.R("poolw")
    poolwb = sb("poolwb", [128, 128], BF16); r_poolwb = k.R("poolwb")
    pcoef = sb("pcoef", [128, 8], F32); r_pcoef = k.R("pcoef")
    pcorr = sb("pcorr", [128, 16], F32); r_pcorr = k.R("pcorr")
    halfc = sb("halfc", [128, 4], F32); r_halfc = k.R("halfc")
    w1kv = sb("w1kv", [128, 32, 128], BF16); r_w1kv = k.R("w1kv")
    pekv = sb("pekv", [128, 32], F32); r_pekv = k.R("pekv")
    pekvb = sb("pekvb", [128, 32], BF16); r_pekvb = k.R("pekvb")
    w2kv = sb("w2kv", [128, 128], F32); r_w2kv = k.R("w2kv")
    w2kvb = sb("w2kvb", [128, 128], BF16); r_w2kvb = k.R("w2kvb")
    hbias = sb("hbias", [128, 2], F32); r_hbias = k.R("hbias")
    relb = sb("relb", [33, 4], F32); r_relb = k.R("relb")
    crt = sb("crt", [4, 1], F32); r_crt = k.R("crt")
    Gc = [sb(f"Gc{i}", [4, 512], BF16) for i in range(2)]; r_Gc = [k.R(f"Gc{i}") for i in range(2)]
    stg = [sb(f"stg{i}", [128, 4096], F32) for i in range(2)]
    r_stg = [k.R(f"stg{i}") for i in range(2)]
    xT = sb("xT", [128, 8, 512], BF16); r_xT = k.R("xT")
    RQA = sb("RQA", [128, 4, 512], BF16); r_RQAq = [k.R(f"RQAq{i}") for i in range(4)]; r_RQAs = [k.R(f"RQAs{i}") for i in range(4)]
    RQB = sb("RQB", [128, 4, 512], BF16); r_RQBq = [k.R(f"RQBq{i}") for i in range(4)]; r_RQBs = [k.R(f"RQBs{i}") for i in range(4)]
    gsig = sb("gsig", [128, 4, 12], F32); r_gsig = [k.R(f"gsig{i}") for i in range(4)]
    zs = sb("zs", [128, 4, 256], F32); r_zs = [k.R(f"zs{i}") for i in range(4)]
    pv = sb("pv", [128, 528], F32); r_pv = k.R("pv")
    sA = sb("sA", [128, 528], F32); r_sA = k.R("sA")
    sB = sb("sB", [128, 528], F32); r_sB = k.R("sB")
    pm = sb("pm", [128, 512], F32); r_pm = k.R("pm")
    pmb = sb("pmb", [128, 512], BF16); r_pmb = k.R("pmb")
    pz = sb("pz", [128, 512], F32); r_pz = k.R("pz")
    cbb = sb("cbb", [128, 512], F32); r_cbb = k.R("cbb")
    ccc = sb("ccc", [128, 512], F32); r_ccc = k.R("ccc")
    uu = sb("uu", [128, 514], F32); r_uu = k.R("uu")
    cz = sb("cz", [128, 512], F32); r_cz = k.R("cz")
    ypool = sb("ypool", [128, 512], BF16); r_ypool = k.R("ypool")
    yconvb = sb("yconvb", [128, 512], BF16); r_yconvb = k.R("yconvb")
    yTn = sb("yTn", [128, 2, 128], BF16); r_yTn = k.R("yTn")
    yconv = sb("yconv", [128, 512], F32); r_yconv = k.R("yconv")
    hkv = sb("hkv", [128, 128], BF16); r_hkv = k.R("hkv")
    Xc = sb("Xc", [128, 16, 66], BF16); r_Xc = k.R("Xc")
    NE = 4
    ET = [sb(f"ET{i}", [128, 1024], BF16) for i in range(2)]
    Et = [ET[0][:, 0:512], ET[0][:, 512:1024], ET[1][:, 0:512], ET[1][:, 512:1024]]; r_E = [k.R(f"E{i}") for i in range(NE)]
    sc = sb("sc", [128, 128], F32); r_sc = k.R("sc")
    sc2 = sb("sc2", [128, 128], F32); r_sc2 = k.R("sc2")
    mx = sb("mx", [128, 16], F32); r_mx = k.R("mx")
    th = sb("th", [128, 1], F32); r_th = k.R("th")
    seln = sb("seln", [128, 192], BF16); r_seln = k.R("seln")
    den = sb("den", [128, 12], F32); r_den = k.R("den")
    coef = sb("coef", [128, 12], F32); r_coef = k.R("coef")
    yacc = sb("yacc", [128, 2, 256], F32); r_yacc = [k.R("yacc0"), k.R("yacc1")]

    SP = [nc.alloc_psum_tensor(f"spb{i}", [128, 1024], F32) for i in range(2)]
    pb = [SP[0][:, 0:512], SP[0][:, 512:1024], SP[1][:, 0:512], SP[1][:, 512:1024]] + [nc.alloc_psum_tensor(f"pb{i}", [128, 512], F32) for i in range(4, 8)]
    r_pb = [k.R(f"pb{i}") for i in range(8)]
    for r_ in r_pb:
        r_.excl = True
    nbs = [0]

    def nb():
        i = nbs[0]
        nbs[0] = (i + 1) % 4
        return i

    cnt2 = [0]

    def evac_eng():
        cnt2[0] += 1
        return "act" if cnt2[0] % 2 else "dve"

    def copy_op(eng, out, in_, reads, writes=(), pwrites=(), scale=None):
        if eng == "act":
            kw = dict(out=out, in_=in_, func=AF.Copy)
            if scale is not None:
                kw["scale"] = scale
            E("act", "activation", kw, reads, writes, pwrites)
        else:
            if scale is None:
                E(eng, "tensor_copy", dict(out=out, in_=in_), reads, writes, pwrites)
            else:
                E(eng, "tensor_scalar_mul", dict(out=out, in0=in_, scalar1=scale), reads, writes, pwrites)

    def dma(eng, out, in_, reads, writes=(), pwrites=(), key=None):
        k.op(eng, lambda e: e.dma_start(out=out, in_=in_), reads=reads, writes=writes, pwrites=pwrites, dma=True, key=key)

    def mm(out, lhsT, rhs, start, stop, reads, writes=(), pwrites=()):
        k.op("pe", lambda e: e.matmul(out, lhsT, rhs, start=start, stop=stop), reads=reads, writes=writes, pwrites=pwrites)

    def tr(out, in_, reads, writes=(), pwrites=()):
        k.op("pe", lambda e: e.transpose(out, in_, ident[:]), reads=list(reads) + [r_ident], writes=writes, pwrites=pwrites)

    si = [0]

    def next_stg():
        s = si[0] % 2
        si[0] += 1
        return s

    def A_body(l, g, xsrc, xreads):
        lg = l * G + g
        dma("sp", ident[:], cf_d[:, :], [], [r_ident])
        dma("sp", cbt[:], cb_d[:, 0:1152], [], [r_cbt])
        dma("sp", KE[64:128, :], cb_d[64:128, 1152:1152 + 8192], [], [r_KEe])
        dma("pool", poolw[:], poolw_all[128 * lg:128 * lg + 128, :], [], [r_poolw])
        dma("pool", pcoef[:], pcoef_all[128 * lg:128 * lg + 128, :], [], [r_pcoef])
        dma("pool", pcorr[:], pcorr_all[128 * g:128 * g + 128, :], [], [r_pcorr])
        dma("pool", halfc[:], halfc_d[:, :], [], [r_halfc])
        dma("pool", pekv[:], pekv_all[128 * l:128 * l + 128, :], [], [r_pekv])
        dma("pool", w2kv[:], w2kv_all[128 * l:128 * l + 128, :], [], [r_w2kv])
        dma("pool", relb[:], relb_all[33 * g:33 * g + 33, :], [], [r_relb])
        copy_op("dve", identb[:], ident[:], [r_ident], [r_identb])
        copy_op("dve", poolwb[:], poolw[:], [r_poolw], [r_poolwb])
        copy_op("dve", pekvb[:], pekv[:], [r_pekv], [r_pekvb])
        copy_op("dve", w2kvb[:], w2kv[:], [r_w2kv], [r_w2kvb])
        E("pool", "memset", dict(ap=kvT[:], constant=0.0), writes=r_kvT)
        E("pool", "memset", dict(ap=kcT[:], constant=0.0), writes=[r_kcT])
        E("pool", "memset", dict(ap=vcT[:], constant=0.0), writes=[r_vcT])
        E("pool", "memset", dict(ap=vcp[:], constant=0.0), writes=[r_vcp])
        E("pool", "memset", dict(ap=vcp[:, :, 64:65], constant=1.0), writes=[r_vcp])
        E("pool", "memset", dict(ap=VS[:, :, 64:65], constant=1.0), writes=[r_VSo])
        E("pool", "memset", dict(ap=VW[:, :, 64:65], constant=1.0), writes=[r_VWo])
        E("pool", "memset", dict(ap=seln[:], constant=0.0), writes=[r_seln])
        E("pool", "memset", dict(ap=pv[:, 0:16], constant=0.0), pwrites=[r_pv])
        E("pool", "memset", dict(ap=uu[:, 0:2], constant=0.0), pwrites=[r_uu])
        E("pool", "memset", dict(ap=KE[0:64, :], constant=0.0), writes=r_KE)
        E("pool", "memset", dict(ap=KW[:], constant=0.0), writes=r_KW)
        E("pool", "memset", dict(ap=VS[:, :, 0:64], constant=0.0), writes=r_VS)
        E("pool", "memset", dict(ap=VW[:, :, 0:64], constant=0.0), writes=r_VW)


        def load_cast(dst_fn, src_fn, ncols, r_dst):
            for j in range(8):
                s = next_stg()
                dma("sp" if j % 2 == 0 else "pool", stg[s][:, 0:ncols], src_fn(j), [], [r_stg[s]])
                copy_op(evac_eng(), dst_fn(j), stg[s][:, 0:ncols], [r_stg[s]], pwrites=[r_dst])

        load_cast(lambda j: Wf[:, j, :], lambda j: wf_all[D * lg + 128 * j:D * lg + 128 * j + 128, :], NFM, r_Wf)
        load_cast(lambda j: Wt[:, j, :], lambda j: wt_all[D * lg + 128 * j:D * lg + 128 * j + 128, :], NTM, r_Wt)
        s = next_stg()
        dma("sp", stg[s][:, 0:4096], w1kv_all[128 * l:128 * l + 128, :], [], [r_stg[s]])
        copy_op("act", w1kv[:].rearrange("p a b -> p (a b)"), stg[s][:, 0:4096], [r_stg[s]], [r_w1kv])

        bk = 7
        for l_ in range(32):
            mm(pb[bk][:, 0:1], w1kv[0:64, l_, :], pekvb[0:64, l_:l_ + 1], l_ == 0, l_ == 31, [r_w1kv, r_pekvb], pwrites=[r_pb[bk]])
        for l_ in range(32):
            mm(pb[6][:, 0:1], w1kv[64:128, l_, :], pekvb[64:128, l_:l_ + 1], l_ == 0, l_ == 31, [r_w1kv, r_pekvb], pwrites=[r_pb[6]])
        copy_op("dve", hbias[:, 0:1], pb[bk][:, 0:1], [r_pb[bk]], pwrites=[r_hbias])
        copy_op("dve", hbias[:, 1:2], pb[6][:, 0:1], [r_pb[6]], pwrites=[r_hbias])

        nchg = (LGT + 511) // 512
        order = [4063 // 512] + [c for c in range(nchg) if c != 4063 // 512]
        for ii, cch in enumerate(order):
            c0 = cch * 512
            w = min(512, LGT - c0)
            s = next_stg()
            dma("sp", stg[s][0:33, 0:w], oh_d[:, c0:c0 + w], [], [r_stg[s]])
            b = nb()
            mm(pb[b][0:4, 0:w], relb[:, :], stg[s][0:33, 0:w], True, True, [r_relb, r_stg[s]], [r_pb[b]])
            if ii == 0:
                copy_op("dve", crt[:], pb[b][0:4, 4063 - c0:4063 - c0 + 1], [r_pb[b]], [r_crt])
            gi = ii % 2
            E("dve", "tensor_scalar", dict(out=Gc[gi][:, 0:w], in0=pb[b][0:4, 0:w], scalar1=crt[:, 0:1], scalar2=None, op0=ALU.subtract),
              [r_pb[b], r_crt], [r_Gc[gi]])
            dma("sp", gd[:, c0:c0 + w], Gc[gi][:, 0:w], [r_Gc[gi]], pwrites=[r_gd], key=r_Gc[gi])

        def hankel(base, step):
            return bass.AP(gd_t, base, [[step, 128], [LGT, 4], [1, 128]])

        bases = [(1936 + 128 * dl, 1) for dl in range(2)] + [(LG + 128 * dl, 1) for dl in range(5)]
        for i, (base, step) in enumerate(bases):
            dma("sp" if i % 2 == 0 else "pool", Rres[:, i, :].rearrange("p (r t) -> p r t", r=4), hankel(base, step), [r_gd], pwrites=[r_Rres])
        r1c = [0]

        def load_x(ci_):
            s_ = next_stg()
            for a in range(4):
                dma("sp" if a % 2 == 0 else "pool", stg[s_][:, 1024 * a:1024 * a + 1024], xsrc[512 * ci_ + 128 * a:512 * ci_ + 128 * a + 128, :], xreads, pwrites=[r_stg[s_]])
            return s_

        r1map = {}

        def prefetch_r1(qi_):
            for c_ in range(qi_ // 16 + 1):
                m_ = qi_ - 16 * c_
                if m_ <= 16 and (qi_, c_) not in r1map:
                    ri = r1c[0] % NR1
                    r1c[0] += 1
                    r1map[(qi_, c_)] = ri
                    dma("pool", R1t[ri][:].rearrange("p (r t) -> p r t", r=4), hankel(128 * m_, 16), [r_gd], [r_R1t[ri]])

        xs_next = load_x(0)
        for ci in range(nchunks):
            t0 = 512 * ci
            s = xs_next
            xt = stg[s]
            for j in range(8):
                b = nb()
                for a in range(4):
                    tr(pb[b][:, 128 * a:128 * a + 128], xt[:, 1024 * a + 128 * j:1024 * a + 128 * j + 128], [r_stg[s]], pwrites=[r_pb[b]])
                copy_op(evac_eng(), xT[:, j, :], pb[b][:, :], [r_pb[b]], pwrites=[r_xT])
            if ci + 1 < nchunks:
                xs_next = load_x(ci + 1)


            def fm_block(col0, M):
                b = nb()
                for j in range(8):
                    mm(pb[b][0:M, :], Wf[:, j, col0:col0 + M], xT[:, j, :], j == 0, j == 7, [r_Wf, r_xT], pwrites=[r_pb[b]])
                return b

            b = fm_block(0, 128)
            copy_op("dve", pv[:, 16:528], pb[b][:, :], [r_pb[b]], pwrites=[r_pv])
            b = fm_block(128, 128)
            E("act", "activation", dict(out=pz[:], in_=pb[b][:, :], func=AF.Silu), [r_pb[b]], [r_pz])
            b = fm_block(256, 128)
            copy_op("dve", cbb[:], pb[b][:, :], [r_pb[b]], [r_cbb])
            b = fm_block(384, 128)
            copy_op("act", ccc[:], pb[b][:, :], [r_pb[b]], [r_ccc])
            b = fm_block(512, 128)
            E("dve", "tensor_tensor", dict(out=uu[:, 2:514], in0=ccc[:], in1=pb[b][:, :], op=ALU.mult), [r_ccc, r_pb[b]], pwrites=[r_uu])
            b = fm_block(640, 128)
            E("act", "activation", dict(out=cz[:], in_=pb[b][:, :], func=AF.Silu), [r_pb[b]], [r_cz])
            for r in range(4):
                b = fm_block(768 + 64 * r, 64)
                src = pb[b][0:64, :].rearrange("p (q t) -> p q t", q=4)
                copy_op("act", RQA[0:64, :, 128 * r:128 * r + 128], src, [r_pb[b]], pwrites=r_RQAq, scale=0.125)
                copy_op("dve", RQB[0:64, :, 128 * r:128 * r + 128], src, [r_pb[b]], pwrites=r_RQBq, scale=0.125)
            b = fm_block(1024, 128)
            copy_op("act", kvT[:, t0:t0 + 512], pb[b][:, :], [r_pb[b]], [r_kvT[ci]])
            b = fm_block(1152, 64)
            copy_op("dve", KE[0:64, t0:t0 + 512], pb[b][0:64, :], [r_pb[b]], [r_KE[ci]])
            b = fm_block(1216, 64)
            copy_op("act", KW[0:64, t0:t0 + 512], pb[b][0:64, :], [r_pb[b]], [r_KW[ci]])
            for a in range(4):
                b = nb()
                jt = 4 * ci + a
                for j in range(8):
                    mm(pb[b][:, 0:NTM], xT[:, j, 128 * a:128 * a + 128], Wt[:, j, :], j == 0, j == 7, [r_Wt, r_xT], pwrites=[r_pb[b]])
                copy_op("dve", VS[:, jt, 0:64], pb[b][:, 0:64], [r_pb[b]], pwrites=[r_VS[ci]])
                copy_op("dve", VW[:, jt, 0:64], pb[b][:, 64:128], [r_pb[b]], pwrites=[r_VW[ci]])
                E("act", "activation", dict(out=gsig[:, a, :], in_=pb[b][:, 128:140], func=AF.Sigmoid), [r_pb[b]], [r_gsig[a]])
                E("act", "activation", dict(out=zs[:, a, :], in_=pb[b][:, 140:396], func=AF.Silu), [r_pb[b]], [r_zs[a]])

            n0 = max(0, 32 * ci - 32)
            n1 = 32 * ci + 31
            cn = n1 - n0 + 1
            b = nb()
            rk = [r_kvT[max(ci - 1, 0)], r_kvT[ci], r_kvT[min(ci + 1, NCHUNK)]]
            E("dve", "tensor_copy", dict(out=Xc[:, :, 0:cn + 1], in_=kvT[:, 16 * n0:16 * n0 + 16 * (cn + 1)].rearrange("p (m l) -> p l m", l=16)), rk, [r_Xc])
            bv = nb()
            hb = (b, bv)
            for half, p0 in enumerate((0, 64)):
                for l_ in range(32):
                    rhs = Xc[p0:p0 + 64, l_ % 16, (l_ // 16):(l_ // 16) + cn]
                    mm(pb[hb[half]][:, 0:cn], w1kv[p0:p0 + 64, l_, :], rhs, l_ == 0, l_ == 31, [r_w1kv, r_Xc], pwrites=[r_pb[hb[half]]])
            for half in range(2):
                E("act", "activation", dict(out=hkv[:, 64 * half:64 * half + cn], in_=pb[hb[half]][:, 0:cn], func=AF.Silu, bias=hbias[:, half:half + 1]),
                  [r_pb[hb[half]], r_hbias], pwrites=[r_hkv])
            E("dve", "tensor_tensor", dict(out=sA[:, 1:528], in0=pv[:, 1:528], in1=pv[:, 0:527], op=ALU.add), [r_pv], [r_sA])
            E("dve", "tensor_scalar", dict(out=pm[:], in0=sA[:, 16:528], scalar1=pcoef[:, 0:1], scalar2=None, op0=ALU.mult), [r_sA, r_pcoef], [r_pm])
            E("dve", "tensor_tensor", dict(out=sB[:, 3:528], in0=sA[:, 3:528], in1=sA[:, 1:526], op=ALU.add), [r_sA], [r_sB])
            E("dve", "scalar_tensor_tensor", dict(out=pm[:], in0=sB[:, 16:528], scalar=pcoef[:, 1:2], in1=pm[:], op0=ALU.mult, op1=ALU.add), [r_sB, r_pcoef, r_pm], [r_pm])
            E("dve", "tensor_tensor", dict(out=sA[:, 7:528], in0=sB[:, 7:528], in1=sB[:, 3:524], op=ALU.add), [r_sB], [r_sA])
            E("dve", "scalar_tensor_tensor", dict(out=pm[:], in0=sA[:, 16:528], scalar=pcoef[:, 2:3], in1=pm[:], op0=ALU.mult, op1=ALU.add), [r_sA, r_pcoef, r_pm], [r_pm])
            E("dve", "tensor_tensor", dict(out=sB[:, 15:528], in0=sA[:, 15:528], in1=sA[:, 7:520], op=ALU.add), [r_sA], [r_sB])
            E("dve", "scalar_tensor_tensor", dict(out=pm[:], in0=sB[:, 16:528], scalar=pcoef[:, 3:4], in1=pm[:], op0=ALU.mult, op1=ALU.add), [r_sB, r_pcoef, r_pm], [r_pm])
            if ci == 0:
                E("dve", "tensor_tensor", dict(out=pm[:, 0:16], in0=pm[:, 0:16], in1=pcorr[:], op=ALU.mult), [r_pm, r_pcorr], [r_pm])
            E("dve", "tensor_tensor", dict(out=pmb[:], in0=pm[:], in1=pv[:, 16:528], op=ALU.subtract), [r_pm, r_pv], [r_pmb])
            b = nb()
            mm(pb[b][:, :], poolwb[:], pmb[:], True, True, [r_poolwb, r_pmb], [r_pb[b]])
            E("dve", "scalar_tensor_tensor", dict(out=ypool[:], in0=pb[b][:, :], scalar=pcoef[:, 4:5], in1=pz[:], op0=ALU.mult, op1=ALU.mult),
              [r_pb[b], r_pcoef, r_pz], [r_ypool])
            dma("sp", ys_d[512 * g:512 * g + 128, t0:t0 + 512], ypool[:], [r_ypool], pwrites=[r_ys], key=r_ypool)
            E("pool", "tensor_copy", dict(out=pv[:, 0:16], in_=pv[:, 512:528]), [r_pv], [r_pv])
            b2 = nb()
            mm(pb[b2][0:64, 0:cn], w2kvb[:, 0:64], hkv[:, 0:cn], True, True, [r_w2kvb, r_hkv], pwrites=[r_pb[b2]])
            mm(pb[b2][0:64, 64:64 + cn], w2kvb[:, 64:128], hkv[:, 64:64 + cn], True, True, [r_w2kvb, r_hkv], pwrites=[r_pb[b2]])
            copy_op("dve", kcT[:, n0:n0 + cn], pb[b2][0:64, 0:cn], [r_pb[b2]], [r_kcT])
            copy_op("dve", vcT[:, n0:n0 + cn], pb[b2][0:64, 64:64 + cn], [r_pb[b2]], [r_vcT])
            for c in sorted(set((n0 // 128, n1 // 128))):
                b3 = nb()
                mm(pb[b3][:, 0:64], vcT[:, 128 * c:128 * c + 128], identb[0:64, 0:64], True, True, [r_vcT, r_identb], [r_pb[b3]])
                copy_op("dve", vcp[:, c, 0:64], pb[b3][:, 0:64], [r_pb[b3]], [r_vcp])

            E("dve", "tensor_scalar", dict(out=yconv[:], in0=uu[:, 0:512], scalar1=pcoef[:, 5:6], scalar2=None, op0=ALU.mult), [r_uu, r_pcoef], [r_yconv])
            E("dve", "scalar_tensor_tensor", dict(out=yconv[:], in0=uu[:, 1:513], scalar=pcoef[:, 6:7], in1=yconv[:], op0=ALU.mult, op1=ALU.add), [r_uu, r_pcoef, r_yconv], [r_yconv])
            E("dve", "scalar_tensor_tensor", dict(out=yconv[:], in0=uu[:, 2:514], scalar=pcoef[:, 7:8], in1=yconv[:], op0=ALU.mult, op1=ALU.add), [r_uu, r_pcoef, r_yconv], [r_yconv])
            E("dve", "tensor_tensor", dict(out=yconv[:], in0=yconv[:], in1=cbb[:], op=ALU.mult), [r_yconv, r_cbb], [r_yconv])
            E("dve", "tensor_tensor", dict(out=yconvb[:], in0=yconv[:], in1=cz[:], op=ALU.mult), [r_yconv, r_cz], [r_yconvb])
            dma("sp", ys_d[512 * g + 128:512 * g + 256, t0:t0 + 512], yconvb[:], [r_yconvb], pwrites=[r_ys], key=r_yconvb)
            E("pool", "tensor_copy", dict(out=uu[:, 0:2], in_=uu[:, 512:514]), [r_uu], [r_uu])

            ecnt = [0]
            pend = []
            LOOKAHEAD = 2

            tlist = []
            pairc = [0]

            def tile_attn(lhsT, lhs_reads, rhs, rhs_reads, bias_ap, bias_reads, Obank, vrhs, v_reads, first, last, extra=None):
                tlist.append((lhsT, lhs_reads, rhs, rhs_reads, bias_ap, bias_reads, Obank, vrhs, v_reads, first, last, extra))

            def run_tiles():
                i = 0
                while i < len(tlist):
                    grp = tlist[i:i + 2]
                    i += len(grp)
                    P = pairc[0] % 2
                    pairc[0] += 1
                    for h, (lhsT, lhs_reads, rhs, rhs_reads, bias_ap, bias_reads, Obank, vrhs, v_reads, first, last, extra) in enumerate(grp):
                        b = 2 * P + h
                        mm(pb[b][:, :], lhsT, rhs, True, bias_ap is None, list(lhs_reads) + list(rhs_reads), [r_pb[b]])
                        if bias_ap is not None:
                            mm(pb[b][:, :], Jm, bias_ap, False, True, [r_cbt] + list(bias_reads), pwrites=[r_pb[b]])
                    wd = 512 * len(grp)
                    E("act", "activation", dict(out=ET[P][:, 0:wd], in_=SP[P][:, 0:wd], func=AF.Exp),
                      [r_pb[2 * P + h] for h in range(len(grp))], [r_E[2 * P + h] for h in range(len(grp))])

                    def stage2(grp=grp, P=P):
                        for h, (lhsT, lhs_reads, rhs, rhs_reads, bias_ap, bias_reads, Obank, vrhs, v_reads, first, last, extra) in enumerate(grp):
                            ei = 2 * P + h
                            for r in range(4):
                                mm(pb[Obank][:, 65 * r:65 * r + 65], Et[ei][:, 128 * r:128 * r + 128], vrhs, first and r == 0, last and r == 3, [r_E[ei]] + list(v_reads), pwrites=[r_pb[Obank]])
                            if extra is not None:
                                extra(ei)

                    pend.append(stage2)
                    while len(pend) > 1:
                        pend.pop(0)()
                del tlist[:]

            def flush():
                run_tiles()
                while pend:
                    pend.pop(0)()

            def fold(bi, bank, ya, ry, qb, first):
                Ob = pb[bank][:, 0:260].rearrange("p (r c) -> p r c", c=65)
                gs = gsig[:, qb, :].rearrange("p (r b) -> p r b", b=3)
                if bi > 0:
                    E("dve", "tensor_scalar_max", dict(out=den[:, 4 * bi:4 * bi + 4], in0=Ob[:, :, 64], scalar1=1e-30), [r_pb[bank]], pwrites=[r_den])
                    E("dve", "reciprocal", dict(out=den[:, 4 * bi:4 * bi + 4], in_=den[:, 4 * bi:4 * bi + 4]), [r_den], pwrites=[r_den])
                E("dve", "tensor_tensor", dict(out=coef[:, 4 * bi:4 * bi + 4], in0=den[:, 4 * bi:4 * bi + 4], in1=gs[:, :, bi], op=ALU.mult),
                  [r_den, r_gsig[qb]], pwrites=[r_coef])
                for r in range(4):
                    if first:
                        E("dve", "tensor_scalar", dict(out=ya[:, 64 * r:64 * r + 64], in0=Ob[:, r, 0:64], scalar1=coef[:, 4 * bi + r:4 * bi + r + 1], scalar2=None, op0=ALU.mult),
                          [r_pb[bank], r_coef], pwrites=[ry])
                    else:
                        E("dve", "scalar_tensor_tensor", dict(out=ya[:, 64 * r:64 * r + 64], in0=Ob[:, r, 0:64], scalar=coef[:, 4 * bi + r:4 * bi + r + 1], in1=ya[:, 64 * r:64 * r + 64], op0=ALU.mult, op1=ALU.add),
                          [r_pb[bank], r_coef, ry], pwrites=[ry])

            def make_deferred(qi, qb, RA, RB, ya, ry):
                def run():
                    for jt in range(qi + 1):
                        if jt < 32:
                            rhs, rr = RA, [r_RQAq[qb], r_RQAs[qb]]
                        else:
                            rhs, rr = RB, [r_RQBq[qb], r_RQBs[qb]]
                        bias_ap = Rres[:, qi - jt, :] if qi - jt <= 1 else None
                        tile_attn(KE[:, 128 * jt:128 * jt + 128], [r_KE[jt // 4], r_KEe], rhs, rr, bias_ap, [r_Rres], 5,
                                  VS[:, jt, :], [r_VS[jt // 4], r_VSo], jt == 0, jt == qi)
                    flush()
                    fold(1, 5, ya, ry, qb, False)
                    E("dve", "tensor_tensor", dict(out=ya, in0=ya, in1=zs[:, qb, :], op=ALU.mult), [ry, r_zs[qb]], [ry])

                    def out_fn():
                        bt_ = nb()
                        tr(pb[bt_][:, 0:128], ya[:, 0:128], [ry], pwrites=[r_pb[bt_]])
                        tr(pb[bt_][:, 128:256], ya[:, 128:256], [ry], pwrites=[r_pb[bt_]])
                        copy_op("act", yTn[:].rearrange("p a b -> p (a b)"), pb[bt_][:, 0:256], [r_pb[bt_]], [r_yTn])
                        q0 = 128 * qi
                        dma("sp", ys_d[512 * g + 256:512 * g + 512, q0:q0 + 128].rearrange("(i c) t -> c i t", c=128), yTn[:], [r_yTn], pwrites=[r_ys], key=r_yTn)

                    pout.append(out_fn)
                return run

            deferred = None
            pout = []
            for qb in range(4):
                qi = 4 * ci + qb
                RA = RQA[:, qb, :]
                RB = RQB[:, qb, :]
                ya = yacc[:, qi % 2, :]
                ry = r_yacc[qi % 2]
                cmax = qi // 16
                prefetch_r1(qi)
                if qi + 1 < 4 * nchunks:
                    prefetch_r1(qi + 1)
                for c in range(cmax + 1):
                    m = qi - 16 * c
                    bias_ap, bias_reads = None, []
                    if m <= 16:
                        ri = r1map[(qi, c)]
                        bias_ap, bias_reads = R1t[ri][:], [r_R1t[ri]]

                    def extra(ei, c=c, cmax=cmax):
                        for r in range(4):
                            mm(pb[7][:, 128 * r:128 * r + 128], Et[ei][:, 128 * r:128 * r + 128], ovt[:, 128 * c:128 * c + 128], c == 0 and r == 0, c == cmax and r == 3,
                               [r_E[ei], r_cbt], pwrites=[r_pb[7]])

                    tile_attn(kcT[:, 128 * c:128 * c + 128], [r_kcT], RA[0:64, :], [r_RQAq[qb]], bias_ap, bias_reads, 4, vcp[:, c, :], [r_vcp], c == 0, c == cmax, extra)
                flush()
                while pout:
                    pout.pop(0)()
                O1 = pb[4][:, 0:260].rearrange("p (r c) -> p r c", c=65)
                E("dve", "tensor_scalar_max", dict(out=den[:, 0:4], in0=O1[:, :, 64], scalar1=1e-30), [r_pb[4]], pwrites=[r_den])
                E("dve", "reciprocal", dict(out=den[:, 0:4], in_=den[:, 0:4]), [r_den], pwrites=[r_den])
                E("pool", "memset", dict(ap=sc[:], constant=-1e30), writes=[r_sc])
                ncol = 2 * qi + 2
                E("dve", "tensor_scalar", dict(out=sc[:, 0:ncol], in0=pb[7][:, 0:ncol], scalar1=den[:, 0:1], scalar2=None, op0=ALU.mult),
                  [r_pb[7], r_den], pwrites=[r_sc])
                for r in range(1, 4):
                    E("dve", "scalar_tensor_tensor", dict(out=sc[:, 0:ncol], in0=pb[7][:, 128 * r:128 * r + ncol], scalar=den[:, r:r + 1], in1=sc[:, 0:ncol], op0=ALU.mult, op1=ALU.add),
                      [r_pb[7], r_den, r_sc], pwrites=[r_sc])
                if qi >= 1:
                    ja = 2 * qi - 1
                    E("dve", "tensor_scalar", dict(out=sc[:, ja:ja + 1], in0=sc[:, ja:ja + 1], scalar1=halfc[:, 0:1], scalar2=halfc[:, 1:2], op0=ALU.mult, op1=ALU.add),
                      [r_sc, r_halfc], pwrites=[r_sc])
                E("dve", "memset", dict(ap=sc[:, 0:1], constant=1e9), [r_sc], pwrites=[r_sc])
                E("dve", "memset", dict(ap=sc[:, 2 * qi:2 * qi + 1], constant=2e9), [r_sc], pwrites=[r_sc])
                E("dve", "tensor_copy", dict(out=sc[:, 2 * qi + 1:2 * qi + 2], in_=halfc[:, 2:3]), [r_sc, r_halfc], pwrites=[r_sc])
                E("dve", "max", dict(out=mx[:, 0:8], in_=sc[:]), [r_sc], pwrites=[r_mx])
                E("dve", "match_replace", dict(out=sc2[:], in_to_replace=mx[:, 0:8], in_values=sc[:], imm_value=-3e38), [r_sc, r_mx], [r_sc2])
                E("dve", "max", dict(out=mx[:, 8:16], in_=sc2[:]), [r_sc2, r_mx], pwrites=[r_mx])
                E("dve", "tensor_reduce", dict(out=th[:], in_=mx[:, 8:16], axis=AX.X, op=ALU.min), [r_mx], [r_th])
                E("dve", "tensor_scalar", dict(out=seln[:, 64:192], in0=sc[:], scalar1=th[:, 0:1], scalar2=NEGB, op0=ALU.is_lt, op1=ALU.mult),
                  [r_sc, r_th], [r_seln])
                fold(0, 4, ya, ry, qb, True)
                dmax = min(4, qi)
                for dl in range(dmax + 1):
                    jt = qi - dl
                    tile_attn(KW[0:64, 128 * jt:128 * jt + 128], [r_KW[jt // 4]], RA[0:64, :], [r_RQAq[qb]], Rres[:, 2 + dl, :], [r_Rres], 6,
                              VW[:, jt, :], [r_VW[jt // 4], r_VWo], dl == 0, dl == dmax)
                flush()
                fold(2, 6, ya, ry, qb, False)
                if deferred is not None:
                    deferred()
                b = 6
                mm(pb[b][:, :], seln[:, 0:128], I4, True, True, [r_seln, r_cbt], [r_pb[b]])
                copy_op("dve", RA[64:128, :], pb[b][64:128, :], [r_pb[b]], [r_RQAs[qb]])
                if qi >= 32:
                    b = 5
                    mm(pb[b][:, :], seln[:, 64:192], I4, True, True, [r_seln, r_cbt], [r_pb[b]])
                    copy_op("dve", RB[64:128, :], pb[b][64:128, :], [r_pb[b]], [r_RQBs[qb]])
                deferred = make_deferred(qi, qb, RA, RB, ya, ry)
            deferred()
            while pout:
                pout.pop(0)()


    WoV = KE[:].rearrange("p (a b) -> p a b", a=8)
    yTbV = [kvT[:, 4096 * s_:4096 * s_ + 4096].rearrange("p (a b) -> p a b", a=8) for s_ in range(2)]
    WfF = bass.AP(Wf, 0, [[8 * NFM, 128], [1, 8 * NFM]]).bitcast(F32)
    xtV = [WfF[:, 0:1024], WfF[:, 1024:2048]]
    ztV = [WfF[:, 2048:3072], WfF[:, 3072:4096]]
    WtF = bass.AP(Wt, 0, [[8 * NTM, 128], [1, 8 * NTM]]).bitcast(F32)
    sqV2 = [WfF[:, 4096:5120], WtF[:, 0:1024]]
    otV = [bass.AP(Rres, 0, [[3584, 128], [1, 3584]]).bitcast(F32)[:, 0:1024], bass.AP(xT, 0, [[4096, 128], [1, 4096]]).bitcast(F32)[:, 0:1024]]
    w1F = bass.AP(w1kv, 0, [[4096, 128], [1, 4096]]).bitcast(F32)
    GtV = w1F[:, 0:1024]
    BtV = w1F[:, 1024:2048]
    stt2 = sb("stt", [128, 8], F32); r_stt2 = [k.R("stt0"), k.R("stt1")]
    r_WoV = k.R("WoV"); r_yTbV = [k.R("yTbV0"), k.R("yTbV1")]; r_xtV = [k.R("xtV0"), k.R("xtV1")]
    r_ztV = [k.R("ztV0"), k.R("ztV1")]; r_sqV2 = [k.R("sqV0"), k.R("sqV1")]; r_otV = [k.R("otV0"), k.R("otV1")]
    r_GtV = k.R("GtV"); r_BtV = k.R("BtV")
    ALPHA_ = (2 * 2) ** 0.25

    def B_body(l, xsrc, xreads, dst, r_dst):
        dma("sp", GtV, bass.AP(lng_t, D * l, [[0, 128], [1, D]]), [], [r_GtV])
        dma("sp", BtV, bass.AP(lnb_t, D * l, [[0, 128], [1, D]]), [], [r_BtV])
        for j in range(8):
            s = next_stg()
            dma("sp" if j % 2 == 0 else "pool", stg[s][:, 0:1024], wo_all[D * l + 128 * j:D * l + 128 * j + 128, :], [], [r_stg[s]])
            copy_op(evac_eng(), WoV[:, j, :], stg[s][:, 0:1024], [r_stg[s]], pwrites=[r_WoV])
        tiles = [(ch, a) for ch in range(TT // 512) for a in range(4)]
        NTL = len(tiles)

        def P1(t):
            ch, a = tiles[t]
            s = ch % 2
            u = t % 2
            t0 = 512 * ch
            r0 = t0 + 128 * a
            if a == 0:
                dma("sp" if ch % 2 == 0 else "pool", yTbV[s], ys_d[:, t0:t0 + 512].rearrange("(j p) t -> p j t", p=128), [r_ys], [r_yTbV[s]])
            stt = stt2[:, 4 * u:4 * u + 4]
            for h in range(2):
                b = nb()
                for j in range(8):
                    mm(pb[b][:, :], yTbV[s][:, j, 128 * a:128 * a + 128], WoV[:, j, 512 * h:512 * h + 512], j == 0, j == 7, [r_yTbV[s], r_WoV], pwrites=[r_pb[b]])
                E("dve", "scalar_tensor_tensor", dict(out=ztV[u][:, 512 * h:512 * h + 512], in0=xtV[u][:, 512 * h:512 * h + 512], scalar=ALPHA_, in1=pb[b][:, :], op0=ALU.mult, op1=ALU.add),
                  [r_xtV[u], r_pb[b]], pwrites=[r_ztV[u]])
            E("dve", "tensor_reduce", dict(out=stt[:, 0:1], in_=ztV[u], axis=AX.X, op=ALU.add), [r_ztV[u]], pwrites=[r_stt2[u]])
            E("dve", "tensor_scalar_mul", dict(out=stt[:, 0:1], in0=stt[:, 0:1], scalar1=1.0 / 1024), [r_stt2[u]], pwrites=[r_stt2[u]])
            E("dve", "tensor_scalar", dict(out=ztV[u], in0=ztV[u], scalar1=stt[:, 0:1], scalar2=None, op0=ALU.subtract), [r_ztV[u], r_stt2[u]], [r_ztV[u]])

        def P2a(t):
            u = t % 2
            E("pool", "tensor_tensor", dict(out=sqV2[u], in0=ztV[u], in1=ztV[u], op=ALU.mult), [r_ztV[u]], [r_sqV2[u]])

        def P2b(t):
            u = t % 2
            stt = stt2[:, 4 * u:4 * u + 4]
            E("dve", "tensor_reduce", dict(out=stt[:, 1:2], in_=sqV2[u], axis=AX.X, op=ALU.add), [r_sqV2[u]], pwrites=[r_stt2[u]])
            E("dve", "tensor_scalar", dict(out=stt[:, 1:2], in0=stt[:, 1:2], scalar1=1.0 / 1024, scalar2=1e-5, op0=ALU.mult, op1=ALU.add), [r_stt2[u]], pwrites=[r_stt2[u]])
            E("act", "activation", dict(out=stt[:, 2:3], in_=stt[:, 1:2], func=AF.Sqrt), [r_stt2[u]], pwrites=[r_stt2[u]])
            E("dve", "reciprocal", dict(out=stt[:, 3:4], in_=stt[:, 2:3]), [r_stt2[u]], pwrites=[r_stt2[u]])

        def P3(t):
            ch, a = tiles[t]
            u = t % 2
            r0 = 512 * ch + 128 * a
            stt = stt2[:, 4 * u:4 * u + 4]
            E("dve", "scalar_tensor_tensor", dict(out=otV[u], in0=ztV[u], scalar=stt[:, 3:4], in1=GtV, op0=ALU.mult, op1=ALU.mult), [r_ztV[u], r_stt2[u], r_GtV], [r_otV[u]])
            E("pool", "tensor_tensor", dict(out=otV[u], in0=otV[u], in1=BtV, op=ALU.add), [r_otV[u], r_BtV], [r_otV[u]])
            dma("sp", dst[r0:r0 + 128, :], otV[u], [r_otV[u]], pwrites=[r_dst], key=r_otV[u])

        def LX(t):
            ch, a = tiles[t]
            r0 = 512 * ch + 128 * a
            dma("act", xtV[t % 2], xsrc[r0:r0 + 128, :], xreads, [r_xtV[t % 2]])

        LX(0)
        for t in range(NTL + 2):
            if t + 1 < NTL:
                LX(t + 1)
            if 0 <= t - 1 < NTL:
                P2a(t - 1)
            if 0 <= t - 2 < NTL:
                P3(t - 2)
            if t < NTL:
                P1(t)
            if 0 <= t - 1 < NTL:
                P2b(t - 1)

    xsrc, xreads = x_d, []
    for l in range(L):
        for g in range(G):
            A_body(l, g, xsrc, xreads)
        k.barrier()
        last = l == L - 1
        dst, r_dst = (out_d, r_out) if last else (x1_d, r_x1)
        B_body(l, xsrc, xreads, dst, r_dst)
        k.barrier()
        xsrc, xreads = x1_d, [r_x1]
    stats = k.emit()
    return nc, stats


def host_fused_inputs(inp, L=2, G=2):
    d = dict(host_consts())
    per = [[host_layer_inputs(inp, l, g) for g in range(G)] for l in range(L)]
    d["wf"] = np.concatenate([per[l][g]["wf"] for l in range(L) for g in range(G)], axis=0)
    d["wt"] = np.concatenate([per[l][g]["wt"] for l in range(L) for g in range(G)], axis=0)
    d["poolw"] = np.concatenate([per[l][g]["poolw"] for l in range(L) for g in range(G)], axis=0)
    d["pcoef"] = np.concatenate([per[l][g]["pcoef"] for l in range(L) for g in range(G)], axis=0)
    d["pcorr"] = np.concatenate([per[0][g]["pcorr"] for g in range(G)], axis=0)
    d["w1kv"] = np.concatenate([per[l][0]["w1kv"] for l in range(L)], axis=0)
    d["pekv"] = np.concatenate([per[l][0]["pekv"] for l in range(L)], axis=0)
    d["w2kv"] = np.concatenate([per[l][0]["w2kv"] for l in range(L)], axis=0)
    d["relb"] = np.concatenate([per[0][g]["relb"] for g in range(G)], axis=0)
    perm = []
    for g in range(G):
        perm += list(range(128 * g, 128 * g + 128)) + list(range(256 + 128 * g, 256 + 128 * g + 128)) + list(range(512 + 256 * g, 512 + 256 * g + 256))
    perm = np.asarray(perm)
    d["wo"] = np.concatenate([inp["w_out"][l][perm] for l in range(L)], axis=0)
    d["lng"] = np.ascontiguousarray(inp["ln_g"][:L])
    d["lnb"] = np.ascontiguousarray(inp["ln_b"][:L])
    return {k_: np.ascontiguousarray(v) for k_, v in d.items()}


_CACHE = {}


def kernel(x, w_in, w_out, pool_w, pool_scale, conv_w, cmp_pe_k, cmp_w1_k, cmp_w2_k,
           cmp_pe_v, cmp_w1_v, cmp_w2_v, rel_bias, ln_g, ln_b):
    from concourse.bass_utils import run_bass_kernel_spmd
    inp = dict(x=x, w_in=w_in, w_out=w_out, pool_w=pool_w, pool_scale=pool_scale, conv_w=conv_w,
               cmp_pe_k=cmp_pe_k, cmp_w1_k=cmp_w1_k, cmp_w2_k=cmp_w2_k, cmp_pe_v=cmp_pe_v,
               cmp_w1_v=cmp_w1_v, cmp_w2_v=cmp_w2_v, rel_bias=rel_bias, ln_g=ln_g, ln_b=ln_b)
    inp = {k_: np.asarray(v, dtype=np.float32) for k_, v in inp.items()}
    if "F" not in _CACHE:
        _CACHE["F"] = build_F(2, 2, NCHUNK)[0]
    nc = _CACHE["F"]
    base = host_fused_inputs(inp, 2, 2)
    in_maps = []
    for core in range(8):
        m = dict(base)
        m["x"] = np.ascontiguousarray(inp["x"][core // 2])
        in_maps.append(m)
    res = run_bass_kernel_spmd(nc, in_maps, core_ids=list(range(8))).results
    return np.stack([res[2 * b]["out"] for b in range(inp["x"].shape[0])], axis=0)
```

```python
import math
import ml_dtypes
import numpy as np
import concourse.bass as bass
import concourse.mybir as mybir

F32 = mybir.dt.float32
BF16 = mybir.dt.bfloat16
AF = mybir.ActivationFunctionType
ALU = mybir.AluOpType
AX = mybir.AxisListType

COMPUTE = ("pe", "act", "dve", "pool")


class Res:
    __slots__ = ("name", "writers", "readers", "sem", "dma_cnt", "gen_deps", "excl")

    def __init__(self, name):
        self.name = name
        self.writers = []
        self.readers = []
        self.gen_deps = []
        self.excl = False
        self.sem = {}
        self.dma_cnt = {}


class Op:
    __slots__ = ("eng", "fn", "deps", "is_dma", "dres", "dval", "idx", "cnt", "signal")

    def __init__(self, eng, fn, is_dma):
        self.eng = eng
        self.fn = fn
        self.deps = []
        self.is_dma = is_dma
        self.dres = None
        self.dval = 0
        self.cnt = 0
        self.signal = False


class Emitter:
    def __init__(self, nc):
        self.nc = nc
        self.ops = []
        self.res = {}
        self.nres = 0
        self.dma_res = {}
        self.phase = Res("phase")
        self.bar_tile = None

    def R(self, name=None):
        self.nres += 1
        r = Res(name or f"r{self.nres}")
        return r

    def sb(self, name, shape, dtype):
        t = self.nc.alloc_sbuf_tensor(name, list(shape), dtype)
        t.res = None
        return t

    def op(self, eng, fn, reads=(), writes=(), pwrites=(), dma=False, key=None):
        o = Op(eng, fn, dma)
        o.idx = len(self.ops)
        if self.phase not in writes:
            reads = list(reads) + [self.phase]
        xr = [r for r in reads if r.excl]
        if xr:
            reads = [r for r in reads if not r.excl]
            writes = list(writes) + [r for r in xr if r not in writes]
        deps = []
        for r in reads:
            deps.extend((p, "raw") for p in r.writers)
        for r in writes:
            deps.extend((p, "war") for p in r.readers)
            deps.extend((p, "waw") for p in r.writers)
        for r in pwrites:
            if r.readers:
                deps.extend((p, "war") for p in r.readers)
                deps.extend((p, "waw") for p in r.writers if not (p.is_dma and dma))
            else:
                deps.extend((p, "war") for p in r.gen_deps)
                deps.extend((p, "waw") for p in r.writers if not (p.is_dma and dma))
        for r in reads:
            r.readers.append(o)
        for r in writes:
            r.gen_deps = r.readers + r.writers
            r.writers = [o]
            r.readers = []
        for r in pwrites:
            if r.readers:
                r.gen_deps = r.readers + r.writers
                r.writers = [o]
                r.readers = []
            else:
                r.writers.append(o)
        if dma:
            allw = list(writes) + list(pwrites)
            d = key if key is not None else allw[0]
            self.dma_res[id(d)] = d
            d.dma_cnt[eng] = d.dma_cnt.get(eng, 0) + 1
            o.dres = d
            o.dval = 16 * d.dma_cnt[eng]
        best = {}
        for p, kind in deps:
            if p is o:
                continue
            if p.is_dma:
                key = ("d", id(p.dres), p.eng)
                if key not in best or best[key].dval < p.dval:
                    best[key] = p
            else:
                if p.eng == o.eng and not dma:
                    if p.eng == "pe":
                        continue
                key = ("c", p.eng)
                if key not in best or best[key].idx < p.idx:
                    best[key] = p
        o.deps = list(best.values())
        for p in o.deps:
            p.signal = True
        self.ops.append(o)
        return o

    def barrier(self):
        if self.bar_tile is None:
            self.bar_tile = self.nc.alloc_sbuf_tensor("s_bar_tile", [128, 2], F32)
        t = self.bar_tile
        self.op("dve", lambda e: e.memset(t[:], 0.0), writes=[self.phase])

    def emit(self, final_res=()):
        nc = self.nc
        esem = {e: nc.alloc_semaphore(f"s_{e}") for e in COMPUTE}
        cnt = {e: 0 for e in COMPUTE}
        for o in self.ops:
            if o.is_dma:
                if o.eng not in o.dres.sem:
                    o.dres.sem[o.eng] = nc.alloc_semaphore(f"d_{o.dres.name}_{o.eng}")
            elif o.signal:
                cnt[o.eng] += 1
                o.cnt = cnt[o.eng]
        engs = ["pe", "act", "dve", "pool", "sp"]
        per = {e: [o for o in self.ops if o.eng == e] for e in engs}
        final_waits = [(r.sem[e], 16 * r.dma_cnt[e]) for r in self.dma_res.values() for e in r.sem]

        def run(e, eng):
            waited = {}
            for o in per[e]:
                for p in o.deps:
                    if p.is_dma:
                        s, v = p.dres.sem[p.eng], p.dval
                    else:
                        s, v = esem[p.eng], p.cnt
                    if waited.get(s.num, -1) >= v:
                        continue
                    waited[s.num] = v
                    eng.wait_ge(s, v)
                ins = o.fn(eng)
                if o.is_dma:
                    ins.then_inc(o.dres.sem[o.eng], 16)
                elif o.signal:
                    ins.then_inc(esem[o.eng], 1)
            if e == "sp":
                for s, v in final_waits:
                    eng.wait_ge(s, v)

        with nc.Block() as block:
            @block.tensor
            def _(eng):
                run("pe", eng)

            @block.scalar
            def _(eng):
                run("act", eng)

            @block.vector
            def _(eng):
                run("dve", eng)

            @block.gpsimd
            def _(eng):
                run("pool", eng)

            @block.sync
            def _(eng):
                run("sp", eng)
        return {e: len(per[e]) for e in engs}


T = 8192
D = 1024
NCHUNK = 16
LG = 4208
LG3 = 768
LGT = LG + LG3
NEGB = -30000.0
NFM = 1280
NTM = 396


def bucket_np(n):
    n = np.maximum(n, 0)
    nf = np.maximum(n, 1).astype(np.float32)
    large = 16 + (np.log(nf / np.float32(16)) / np.float32(math.log(8.0)) * np.float32(16)).astype(np.int32)
    large = np.minimum(large, 31)
    return np.where(n < 16, n, large)


def host_consts():
    c = {}
    c["cf"] = np.eye(128, dtype=np.float32)
    J = np.eye(128, dtype=np.float32)[::-1].copy()
    I4 = np.tile(np.eye(128, dtype=np.float32), (1, 4))
    ov = np.zeros((512, 128), np.float32)
    for i in range(511):
        for j in range(128):
            o = min(16 * i + 32, 64 * j + 64) - max(16 * i, 64 * j)
            if o > 0:
                ov[i, j] = o / 16.0
    ovc = ov.reshape(4, 128, 128).transpose(1, 0, 2).reshape(128, 512)
    E = np.zeros((128, 8192), np.float32)
    for jt in range(64):
        for kk in range(128):
            j = 2 * jt + (1 if kk >= 64 else 0)
            E[64 + (j % 64), jt * 128 + kk] = 1.0
    c["cb"] = np.concatenate([J, I4, ovc, E], axis=1).astype(ml_dtypes.bfloat16)
    oh = np.zeros((33, LGT), np.float32)
    i = np.arange(LG)
    n = i - 2063
    b = bucket_np(n)
    oh[np.where(n < 0, 32, b), i] = 1.0
    i3 = np.arange(LG3)
    n3 = i3 - 127
    b3 = bucket_np(n3)
    oh[np.where((n3 < 0) | (n3 >= 512), 32, b3), LG + i3] = 1.0
    c["oh"] = oh
    hc = np.zeros((128, 4), np.float32)
    hc[64:, 0] = 1.0
    hc[:64, 1] = 3e9
    hc[:64, 2] = -1e30
    hc[64:, 2] = 4e9
    c["halfc"] = hc
    return c


POOL_WINDOWS = (2, 4, 8, 16)


def host_layer_inputs(inp, l, g):
    w_in = inp["w_in"][l]
    offs = np.cumsum([0, 256, 256, 256, 256, 256, 256, 512, 128, 128, 128, 128, 128, 128, 24, 512])
    (o_pv, o_pz, o_cb, o_cc, o_cx, o_cz, o_q, o_kc, o_vc, o_ks, o_vs, o_kw, o_vw, o_g, o_z) = offs[:15]
    s = slice
    cols = []
    for o in (o_pv, o_pz, o_cb, o_cc, o_cx, o_cz):
        cols.append(np.arange(o + 128 * g, o + 128 * g + 128))
    for r in range(4):
        cols.append(np.arange(o_q + 64 * (4 * g + r), o_q + 64 * (4 * g + r) + 64))
    cols.append(np.arange(o_kc + 64 * g, o_kc + 64 * g + 64))
    cols.append(np.arange(o_vc + 64 * g, o_vc + 64 * g + 64))
    cols.append(np.arange(o_ks + 64 * g, o_ks + 64 * g + 64))
    cols.append(np.arange(o_kw + 64 * g, o_kw + 64 * g + 64))
    cf = np.concatenate(cols)
    assert cf.size == NFM
    ct = np.concatenate([
        np.arange(o_vs + 64 * g, o_vs + 64 * g + 64),
        np.arange(o_vw + 64 * g, o_vw + 64 * g + 64),
        np.arange(o_g + 12 * g, o_g + 12 * g + 12),
        np.arange(o_z + 256 * g, o_z + 256 * g + 256),
    ])
    assert ct.size == NTM
    d = {}
    d["wf"] = np.ascontiguousarray(w_in[:, cf])
    d["wt"] = np.ascontiguousarray(w_in[:, ct])
    pw = np.zeros((128, 128), np.float32)
    pw[0:64, 0:64] = inp["pool_w"][l, 2 * g]
    pw[64:128, 64:128] = inp["pool_w"][l, 2 * g + 1]
    d["poolw"] = pw
    pc = np.zeros((128, 8), np.float32)
    for h in range(2):
        w = POOL_WINDOWS[2 * g + h]
        pc[64 * h:64 * h + 64, POOL_WINDOWS.index(w)] = 1.0 / w
    pc[:, 4] = inp["pool_scale"][l, 128 * g:128 * g + 128]
    for k in range(3):
        pc[:, 5 + k] = inp["conv_w"][l, k, 128 * g:128 * g + 128]
    d["pcoef"] = pc
    pcorr = np.ones((128, 16), np.float32)
    for h in range(2):
        w = POOL_WINDOWS[2 * g + h]
        for t in range(16):
            pcorr[64 * h:64 * h + 64, t] = w / min(t + 1, w)
    d["pcorr"] = pcorr
    w1k = inp["cmp_w1_k"][l].reshape(32, 64, 128).transpose(1, 0, 2).reshape(64, 32 * 128)
    w1v = inp["cmp_w1_v"][l].reshape(32, 64, 128).transpose(1, 0, 2).reshape(64, 32 * 128)
    d["w1kv"] = np.ascontiguousarray(np.concatenate([w1k, w1v], axis=0))
    d["pekv"] = np.ascontiguousarray(np.concatenate([inp["cmp_pe_k"][l].T, inp["cmp_pe_v"][l].T], axis=0))
    d["w2kv"] = np.ascontiguousarray(np.concatenate([inp["cmp_w2_k"][l], inp["cmp_w2_v"][l]], axis=1))
    rb = np.full((33, 4), NEGB, np.float32)
    rb[:32] = inp["rel_bias"][:, 4 * g:4 * g + 4]
    d["relb"] = rb
    return d


def build_F(L=2, G=2, nchunks=NCHUNK):
    nc = bass.Bass("TRN2", target_bir_lowering=False)
    k = Emitter(nc)

    def din(name, shape, dt=F32):
        return nc.dram_tensor(name, shape, dt, kind="ExternalInput").ap()

    def dout(name, shape, dt=F32):
        return nc.dram_tensor(name, shape, dt, kind="ExternalOutput").ap()

    TT = 512 * nchunks
    x_d = din("x", [TT, D])
    wf_all = din("wf", [L * G * D, NFM])
    wt_all = din("wt", [L * G * D, NTM])
    poolw_all = din("poolw", [L * G * 128, 128])
    pcoef_all = din("pcoef", [L * G * 128, 8])
    pcorr_all = din("pcorr", [G * 128, 16])
    w1kv_all = din("w1kv", [L * 128, 4096])
    pekv_all = din("pekv", [L * 128, 32])
    w2kv_all = din("w2kv", [L * 128, 128])
    relb_all = din("relb", [G * 33, 4])
    oh_d = din("oh", [33, LGT])
    cf_d = din("cf", [128, 128])
    cb_d = din("cb", [128, 128 + 512 + 512 + 8192], BF16)
    halfc_d = din("halfc", [128, 4])
    wo_all = din("wo", [L * D, D])
    lng_t = nc.dram_tensor("lng", [L, D], F32, kind="ExternalInput")
    lnb_t = nc.dram_tensor("lnb", [L, D], F32, kind="ExternalInput")
    out_d = dout("out", [TT, D])
    gd_t = nc.dram_tensor("gd", [4, LGT], BF16)
    gd = gd_t.ap()
    ys_d = nc.dram_tensor("ys", [D, TT], BF16).ap()
    x1_d = nc.dram_tensor("x1", [TT, D], F32).ap()
    r_gd = k.R("gd")
    r_ys = k.R("ys")
    r_x1 = k.R("x1")
    r_out = k.R("out")

    def sb(name, shape, dt):
        return nc.alloc_sbuf_tensor("s_" + name, list(shape), dt)

    def E(eng, name, kw, reads=(), writes=(), pwrites=()):
        k.op(eng, lambda e: getattr(e, name)(**kw), reads=reads, writes=writes, pwrites=pwrites)

    Wf = sb("Wf", [128, 8, NFM], BF16); r_Wf = k.R("Wf")
    Wt = sb("Wt", [128, 8, NTM], BF16); r_Wt = k.R("Wt")
    ident = sb("ident", [128, 128], F32); r_ident = k.R("ident")
    identb = sb("identb", [128, 128], BF16); r_identb = k.R("identb")
    cbt = sb("cbt", [128, 128 + 512 + 512], BF16); r_cbt = k.R("cbt")
    Jm = cbt[:, 0:128]
    I4 = cbt[:, 128:640]
    ovt = cbt[:, 640:1152]
    KE = sb("KE", [128, T], BF16); r_KEe = k.R("KEe")
    r_KE = [k.R(f"KE{i}") for i in range(NCHUNK)]
    KW = sb("KW", [64, T], BF16); r_KW = [k.R(f"KW{i}") for i in range(NCHUNK)]
    kvT = sb("kvT", [128, T + 32], BF16); r_kvT = [k.R(f"kvT{i}") for i in range(NCHUNK + 1)]
    VS = sb("VS", [128, 64, 65], BF16); r_VS = [k.R(f"VS{i}") for i in range(NCHUNK)]; r_VSo = k.R("VSo")
    VW = sb("VW", [128, 64, 65], BF16); r_VW = [k.R(f"VW{i}") for i in range(NCHUNK)]; r_VWo = k.R("VWo")
    kcT = sb("kcT", [64, 544], BF16); r_kcT = k.R("kcT")
    vcT = sb("vcT", [64, 544], BF16); r_vcT = k.R("vcT")
    vcp = sb("vcp", [128, 4, 65], BF16); r_vcp = k.R("vcp")
    Rres = sb("Rres", [128, 7, 512], BF16); r_Rres = k.R("Rres")
    NR1 = 4
    R1t = [sb(f"R1t{i}", [128, 512], BF16) for i in range(NR1)]; r_R1t = [k.R(f"R1t{i}") for i in range(NR1)]
    poolw = sb("poolw", [128, 128], F32); r_poolw = k.R("poolw")
    poolwb = sb("poolwb", [128, 128], BF16); r_poolwb = k.R("poolwb")
    pcoef = sb("pcoef", [128, 8], F32); r_pcoef = k.R("pcoef")
    pcorr = sb("pcorr", [128, 16], F32); r_pcorr = k.R("pcorr")
    halfc = sb("halfc", [128, 4], F32); r_halfc = k.R("halfc")
    w1kv = sb("w1kv", [128, 32, 128], BF16); r_w1kv = k.R("w1kv")
    pekv = sb("pekv", [128, 32], F32); r_pekv = k.R("pekv")
    pekvb = sb("pekvb", [128, 32], BF16); r_pekvb = k.R("pekvb")
    w2kv = sb("w2kv", [128, 128], F32); r_w2kv = k.R("w2kv")
    w2kvb = sb("w2kvb", [128, 128], BF16); r_w2kvb = k.R("w2kvb")
    hbias = sb("hbias", [128, 2], F32); r_hbias = k.R("hbias")
    relb = sb("relb", [33, 4], F32); r_relb = k.R("relb")
    crt = sb("crt", [4, 1], F32); r_crt = k.R("crt")
    Gc = [sb(f"Gc{i}", [4, 512], BF16) for i in range(2)]; r_Gc = [k.R(f"Gc{i}") for i in range(2)]
    stg = [sb(f"stg{i}", [128, 4096], F32) for i in range(2)]
    r_stg = [k.R(f"stg{i}") for i in range(2)]
    xT = sb("xT", [128, 8, 512], BF16); r_xT = k.R("xT")
    RQA = sb("RQA", [128, 4, 512], BF16); r_RQAq = [k.R(f"RQAq{i}") for i in range(4)]; r_RQAs = [k.R(f"RQAs{i}") for i in range(4)]
    RQB = sb("RQB", [128, 4, 512], BF16); r_RQBq = [k.R(f"RQBq{i}") for i in range(4)]; r_RQBs = [k.R(f"RQBs{i}") for i in range(4)]
    gsig = sb("gsig", [128, 4, 12], F32); r_gsig = [k.R(f"gsig{i}") for i in range(4)]
    zs = sb("zs", [128, 4, 256], F32); r_zs = [k.R(f"zs{i}") for i in range(4)]
    pv = sb("pv", [128, 528], F32); r_pv = k.R("pv")
    sA = sb("sA", [128, 528], F32); r_sA = k.R("sA")
    sB = sb("sB", [128, 528], F32); r_sB = k.R("sB")
    pm = sb("pm", [128, 512], F32); r_pm = k.R("pm")
    pmb = sb("pmb", [128, 512], BF16); r_pmb = k.R("pmb")
    pz = sb("pz", [128, 512], F32); r_pz = k.R("pz")
    cbb = sb("cbb", [128, 512], F32); r_cbb = k.R("cbb")
    ccc = sb("ccc", [128, 512], F32); r_ccc = k.R("ccc")
    uu = sb("uu", [128, 514], F32); r_uu = k.R("uu")
    cz = sb("cz", [128, 512], F32); r_cz = k.R("cz")
    ypool = sb("ypool", [128, 512], BF16); r_ypool = k.R("ypool")
    yconvb = sb("yconvb", [128, 512], BF16); r_yconvb = k.R("yconvb")
    yTn = sb("yTn", [128, 2, 128], BF16); r_yTn = k.R("yTn")
    yconv = sb("yconv", [128, 512], F32); r_yconv = k.R("yconv")
    hkv = sb("hkv", [128, 128], BF16); r_hkv = k.R("hkv")
    Xc = sb("Xc", [128, 16, 66], BF16); r_Xc = k.R("Xc")
    NE = 4
    ET = [sb(f"ET{i}", [128, 1024], BF16) for i in range(2)]
    Et = [ET[0][:, 0:512], ET[0][:, 512:1024], ET[1][:, 0:512], ET[1][:, 512:1024]]; r_E = [k.R(f"E{i}") for i in range(NE)]
    sc = sb("sc", [128, 128], F32); r_sc = k.R("sc")
    sc2 = sb("sc2", [128, 128], F32); r_sc2 = k.R("sc2")
    mx = sb("mx", [128, 16], F32); r_mx = k.R("mx")
    th = sb("th", [128, 1], F32); r_th = k.R("th")
    seln = sb("seln", [128, 192], BF16); r_seln = k.R("seln")
    den = sb("den", [128, 12], F32); r_den = k.R("den")
    coef = sb("coef", [128, 12], F32); r_coef = k.R("coef")
    yacc = sb("yacc", [128, 2, 256], F32); r_yacc = [k.R("yacc0"), k.R("yacc1")]

    SP = [nc.alloc_psum_tensor(f"spb{i}", [128, 1024], F32) for i in range(2)]
    pb = [SP[0][:, 0:512], SP[0][:, 512:1024], SP[1][:, 0:512], SP[1][:, 512:1024]] + [nc.alloc_psum_tensor(f"pb{i}", [128, 512], F32) for i in range(4, 8)]
    r_pb = [k.R(f"pb{i}") for i in range(8)]
    for r_ in r_pb:
        r_.excl = True
    nbs = [0]

    def nb():
        i = nbs[0]
        nbs[0] = (i + 1) % 4
        return i

    cnt2 = [0]

    def evac_eng():
        cnt2[0] += 1
        return "act" if cnt2[0] % 2 else "dve"

    def copy_op(eng, out, in_, reads, writes=(), pwrites=(), scale=None):
        if eng == "act":
            kw = dict(out=out, in_=in_, func=AF.Copy)
            if scale is not None:
                kw["scale"] = scale
            E("act", "activation", kw, reads, writes, pwrites)
        else:
            if scale is None:
                E(eng, "tensor_copy", dict(out=out, in_=in_), reads, writes, pwrites)
            else:
                E(eng, "tensor_scalar_mul", dict(out=out, in0=in_, scalar1=scale), reads, writes, pwrites)

    def dma(eng, out, in_, reads, writes=(), pwrites=(), key=None):
        k.op(eng, lambda e: e.dma_start(out=out, in_=in_), reads=reads, writes=writes, pwrites=pwrites, dma=True, key=key)

    def mm(out, lhsT, rhs, start, stop, reads, writes=(), pwrites=()):
        k.op("pe", lambda e: e.matmul(out, lhsT, rhs, start=start, stop=stop), reads=reads, writes=writes, pwrites=pwrites)

    def tr(out, in_, reads, writes=(), pwrites=()):
        k.op("pe", lambda e: e.transpose(out, in_, ident[:]), reads=list(reads) + [r_ident], writes=writes, pwrites=pwrites)

    si = [0]

    def next_stg():
        s = si[0] % 2
        si[0] += 1
        return s

    def A_body(l, g, xsrc, xreads):
        lg = l * G + g
        dma("sp", ident[:], cf_d[:, :], [], [r_ident])
        dma("sp", cbt[:], cb_d[:, 0:1152], [], [r_cbt])
        dma("sp", KE[64:128, :], cb_d[64:128, 1152:1152 + 8192], [], [r_KEe])
        dma("pool", poolw[:], poolw_all[128 * lg:128 * lg + 128, :], [], [r_poolw])
        dma("pool", pcoef[:], pcoef_all[128 * lg:128 * lg + 128, :], [], [r_pcoef])
        dma("pool", pcorr[:], pcorr_all[128 * g:128 * g + 128, :], [], [r_pcorr])
        dma("pool", halfc[:], halfc_d[:, :], [], [r_halfc])
        dma("pool", pekv[:], pekv_all[128 * l:128 * l + 128, :], [], [r_pekv])
        dma("pool", w2kv[:], w2kv_all[128 * l:128 * l + 128, :], [], [r_w2kv])
        dma("pool", relb[:], relb_all[33 * g:33 * g + 33, :], [], [r_relb])
        copy_op("dve", identb[:], ident[:], [r_ident], [r_identb])
        copy_op("dve", poolwb[:], poolw[:], [r_poolw], [r_poolwb])
        copy_op("dve", pekvb[:], pekv[:], [r_pekv], [r_pekvb])
        copy_op("dve", w2kvb[:], w2kv[:], [r_w2kv], [r_w2kvb])
        E("pool", "memset", dict(ap=kvT[:], constant=0.0), writes=r_kvT)
        E("pool", "memset", dict(ap=kcT[:], constant=0.0), writes=[r_kcT])
        E("pool", "memset", dict(ap=vcT[:], constant=0.0), writes=[r_vcT])
        E("pool", "memset", dict(ap=vcp[:], constant=0.0), writes=[r_vcp])
        E("pool", "memset", dict(ap=vcp[:, :, 64:65], constant=1.0), writes=[r_vcp])
        E("pool", "memset", dict(ap=VS[:, :, 64:65], constant=1.0), writes=[r_VSo])
        E("pool", "memset", dict(ap=VW[:, :, 64:65], constant=1.0), writes=[r_VWo])
        E("pool", "memset", dict(ap=seln[:], constant=0.0), writes=[r_seln])
        E("pool", "memset", dict(ap=pv[:, 0:16], constant=0.0), pwrites=[r_pv])
        E("pool", "memset", dict(ap=uu[:, 0:2], constant=0.0), pwrites=[r_uu])
        E("pool", "memset", dict(ap=KE[0:64, :], constant=0.0), writes=r_KE)
        E("pool", "memset", dict(ap=KW[:], constant=0.0), writes=r_KW)
        E("pool", "memset", dict(ap=VS[:, :, 0:64], constant=0.0), writes=r_VS)
        E("pool", "memset", dict(ap=VW[:, :, 0:64], constant=0.0), writes=r_VW)


        def load_cast(dst_fn, src_fn, ncols, r_dst):
            for j in range(8):
                s = next_stg()
                dma("sp" if j % 2 == 0 else "pool", stg[s][:, 0:ncols], src_fn(j), [], [r_stg[s]])
                copy_op(evac_eng(), dst_fn(j), stg[s][:, 0:ncols], [r_stg[s]], pwrites=[r_dst])

        load_cast(lambda j: Wf[:, j, :], lambda j: wf_all[D * lg + 128 * j:D * lg + 128 * j + 128, :], NFM, r_Wf)
        load_cast(lambda j: Wt[:, j, :], lambda j: wt_all[D * lg + 128 * j:D * lg + 128 * j + 128, :], NTM, r_Wt)
        s = next_stg()
        dma("sp", stg[s][:, 0:4096], w1kv_all[128 * l:128 * l + 128, :], [], [r_stg[s]])
        copy_op("act", w1kv[:].rearrange("p a b -> p (a b)"), stg[s][:, 0:4096], [r_stg[s]], [r_w1kv])

        bk = 7
        for l_ in range(32):
            mm(pb[bk][:, 0:1], w1kv[0:64, l_, :], pekvb[0:64, l_:l_ + 1], l_ == 0, l_ == 31, [r_w1kv, r_pekvb], pwrites=[r_pb[bk]])
        for l_ in range(32):
            mm(pb[6][:, 0:1], w1kv[64:128, l_, :], pekvb[64:128, l_:l_ + 1], l_ == 0, l_ == 31, [r_w1kv, r_pekvb], pwrites=[r_pb[6]])
        copy_op("dve", hbias[:, 0:1], pb[bk][:, 0:1], [r_pb[bk]], pwrites=[r_hbias])
        copy_op("dve", hbias[:, 1:2], pb[6][:, 0:1], [r_pb[6]], pwrites=[r_hbias])

        nchg = (LGT + 511) // 512
        order = [4063 // 512] + [c for c in range(nchg) if c != 4063 // 512]
        for ii, cch in enumerate(order):
            c0 = cch * 512
            w = min(512, LGT - c0)
            s = next_stg()
            dma("sp", stg[s][0:33, 0:w], oh_d[:, c0:c0 + w], [], [r_stg[s]])
            b = nb()
            mm(pb[b][0:4, 0:w], relb[:, :], stg[s][0:33, 0:w], True, True, [r_relb, r_stg[s]], [r_pb[b]])
            if ii == 0:
                copy_op("dve", crt[:], pb[b][0:4, 4063 - c0:4063 - c0 + 1], [r_pb[b]], [r_crt])
            gi = ii % 2
            E("dve", "tensor_scalar", dict(out=Gc[gi][:, 0:w], in0=pb[b][0:4, 0:w], scalar1=crt[:, 0:1], scalar2=None, op0=ALU.subtract),
              [r_pb[b], r_crt], [r_Gc[gi]])
            dma("sp", gd[:, c0:c0 + w], Gc[gi][:, 0:w], [r_Gc[gi]], pwrites=[r_gd], key=r_Gc[gi])

        def hankel(base, step):
            return bass.AP(gd_t, base, [[step, 128], [LGT, 4], [1, 128]])

        bases = [(1936 + 128 * dl, 1) for dl in range(2)] + [(LG + 128 * dl, 1) for dl in range(5)]
        for i, (base, step) in enumerate(bases):
            dma("sp" if i % 2 == 0 else "pool", Rres[:, i, :].rearrange("p (r t) -> p r t", r=4), hankel(base, step), [r_gd], pwrites=[r_Rres])
        r1c = [0]

        def load_x(ci_):
            s_ = next_stg()
            for a in range(4):
                dma("sp" if a % 2 == 0 else "pool", stg[s_][:, 1024 * a:1024 * a + 1024], xsrc[512 * ci_ + 128 * a:512 * ci_ + 128 * a + 128, :], xreads, pwrites=[r_stg[s_]])
            return s_

        r1map = {}

        def prefetch_r1(qi_):
            for c_ in range(qi_ // 16 + 1):
                m_ = qi_ - 16 * c_
                if m_ <= 16 and (qi_, c_) not in r1map:
                    ri = r1c[0] % NR1
                    r1c[0] += 1
                    r1map[(qi_, c_)] = ri
                    dma("pool", R1t[ri][:].rearrange("p (r t) -> p r t", r=4), hankel(128 * m_, 16), [r_gd], [r_R1t[ri]])

        xs_next = load_x(0)
        for ci in range(nchunks):
            t0 = 512 * ci
            s = xs_next
            xt = stg[s]
            for j in range(8):
                b = nb()
                for a in range(4):
                    tr(pb[b][:, 128 * a:128 * a + 128], xt[:, 1024 * a + 128 * j:1024 * a + 128 * j + 128], [r_stg[s]], pwrites=[r_pb[b]])
                copy_op(evac_eng(), xT[:, j, :], pb[b][:, :], [r_pb[b]], pwrites=[r_xT])
            if ci + 1 < nchunks:
                xs_next = load_x(ci + 1)


            def fm_block(col0, M):
                b = nb()
                for j in range(8):
                    mm(pb[b][0:M, :], Wf[:, j, col0:col0 + M], xT[:, j, :], j == 0, j == 7, [r_Wf, r_xT], pwrites=[r_pb[b]])
                return b

            b = fm_block(0, 128)
            copy_op("dve", pv[:, 16:528], pb[b][:, :], [r_pb[b]], pwrites=[r_pv])
            b = fm_block(128, 128)
            E("act", "activation", dict(out=pz[:], in_=pb[b][:, :], func=AF.Silu), [r_pb[b]], [r_pz])
            b = fm_block(256, 128)
            copy_op("dve", cbb[:], pb[b][:, :], [r_pb[b]], [r_cbb])
            b = fm_block(384, 128)
            copy_op("act", ccc[:], pb[b][:, :], [r_pb[b]], [r_ccc])
            b = fm_block(512, 128)
            E("dve", "tensor_tensor", dict(out=uu[:, 2:514], in0=ccc[:], in1=pb[b][:, :], op=ALU.mult), [r_ccc, r_pb[b]], pwrites=[r_uu])
            b = fm_block(640, 128)
            E("act", "activation", dict(out=cz[:], in_=pb[b][:, :], func=AF.Silu), [r_pb[b]], [r_cz])
            for r in range(4):
                b = fm_block(768 + 64 * r, 64)
                src = pb[b][0:64, :].rearrange("p (q t) -> p q t", q=4)
                copy_op("act", RQA[0:64, :, 128 * r:128 * r + 128], src, [r_pb[b]], pwrites=r_RQAq, scale=0.125)
                copy_op("dve", RQB[0:64, :, 128 * r:128 * r + 128], src, [r_pb[b]], pwrites=r_RQBq, scale=0.125)
            b = fm_block(1024, 128)
            copy_op("act", kvT[:, t0:t0 + 512], pb[b][:, :], [r_pb[b]], [r_kvT[ci]])
            b = fm_block(1152, 64)
            copy_op("dve", KE[0:64, t0:t0 + 512], pb[b][0:64, :], [r_pb[b]], [r_KE[ci]])
            b = fm_block(1216, 64)
            copy_op("act", KW[0:64, t0:t0 + 512], pb[b][0:64, :], [r_pb[b]], [r_KW[ci]])
            for a in range(4):
                b = nb()
                jt = 4 * ci + a
                for j in range(8):
                    mm(pb[b][:, 0:NTM], xT[:, j, 128 * a:128 * a + 128], Wt[:, j, :], j == 0, j == 7, [r_Wt, r_xT], pwrites=[r_pb[b]])
                copy_op("dve", VS[:, jt, 0:64], pb[b][:, 0:64], [r_pb[b]], pwrites=[r_VS[ci]])
                copy_op("dve", VW[:, jt, 0:64], pb[b][:, 64:128], [r_pb[b]], pwrites=[r_VW[ci]])
                E("act", "activation", dict(out=gsig[:, a, :], in_=pb[b][:, 128:140], func=AF.Sigmoid), [r_pb[b]], [r_gsig[a]])
                E("act", "activation", dict(out=zs[:, a, :], in_=pb[b][:, 140:396], func=AF.Silu), [r_pb[b]], [r_zs[a]])

            n0 = max(0, 32 * ci - 32)
            n1 = 32 * ci + 31
            cn = n1 - n0 + 1
            b = nb()
            rk = [r_kvT[max(ci - 1, 0)], r_kvT[ci], r_kvT[min(ci + 1, NCHUNK)]]
            E("dve", "tensor_copy", dict(out=Xc[:, :, 0:cn + 1], in_=kvT[:, 16 * n0:16 * n0 + 16 * (cn + 1)].rearrange("p (m l) -> p l m", l=16)), rk, [r_Xc])
            bv = nb()
            hb = (b, bv)
            for half, p0 in enumerate((0, 64)):
                for l_ in range(32):
                    rhs = Xc[p0:p0 + 64, l_ % 16, (l_ // 16):(l_ // 16) + cn]
                    mm(pb[hb[half]][:, 0:cn], w1kv[p0:p0 + 64, l_, :], rhs, l_ == 0, l_ == 31, [r_w1kv, r_Xc], pwrites=[r_pb[hb[half]]])
            for half in range(2):
                E("act", "activation", dict(out=hkv[:, 64 * half:64 * half + cn], in_=pb[hb[half]][:, 0:cn], func=AF.Silu, bias=hbias[:, half:half + 1]),
                  [r_pb[hb[half]], r_hbias], pwrites=[r_hkv])
            E("dve", "tensor_tensor", dict(out=sA[:, 1:528], in0=pv[:, 1:528], in1=pv[:, 0:527], op=ALU.add), [r_pv], [r_sA])
            E("dve", "tensor_scalar", dict(out=pm[:], in0=sA[:, 16:528], scalar1=pcoef[:, 0:1], scalar2=None, op0=ALU.mult), [r_sA, r_pcoef], [r_pm])
            E("dve", "tensor_tensor", dict(out=sB[:, 3:528], in0=sA[:, 3:528], in1=sA[:, 1:526], op=ALU.add), [r_sA], [r_sB])
            E("dve", "scalar_tensor_tensor", dict(out=pm[:], in0=sB[:, 16:528], scalar=pcoef[:, 1:2], in1=pm[:], op0=ALU.mult, op1=ALU.add), [r_sB, r_pcoef, r_pm], [r_pm])
            E("dve", "tensor_tensor", dict(out=sA[:, 7:528], in0=sB[:, 7:528], in1=sB[:, 3:524], op=ALU.add), [r_sB], [r_sA])
            E("dve", "scalar_tensor_tensor", dict(out=pm[:], in0=sA[:, 16:528], scalar=pcoef[:, 2:3], in1=pm[:], op0=ALU.mult, op1=ALU.add), [r_sA, r_pcoef, r_pm], [r_pm])
            E("dve", "tensor_tensor", dict(out=sB[:, 15:528], in0=sA[:, 15:528], in1=sA[:, 7:520], op=ALU.add), [r_sA], [r_sB])
            E("dve", "scalar_tensor_tensor", dict(out=pm[:], in0=sB[:, 16:528], scalar=pcoef[:, 3:4], in1=pm[:], op0=ALU.mult, op1=ALU.add), [r_sB, r_pcoef, r_pm], [r_pm])
            if ci == 0:
                E("dve", "tensor_tensor", dict(out=pm[:, 0:16], in0=pm[:, 0:16], in1=pcorr[:], op=ALU.mult), [r_pm, r_pcorr], [r_pm])
            E("dve", "tensor_tensor", dict(out=pmb[:], in0=pm[:], in1=pv[:, 16:528], op=ALU.subtract), [r_pm, r_pv], [r_pmb])
            b = nb()
            mm(pb[b][:, :], poolwb[:], pmb[:], True, True, [r_poolwb, r_pmb], [r_pb[b]])
            E("dve", "scalar_tensor_tensor", dict(out=ypool[:], in0=pb[b][:, :], scalar=pcoef[:, 4:5], in1=pz[:], op0=ALU.mult, op1=ALU.mult),
              [r_pb[b], r_pcoef, r_pz], [r_ypool])
            dma("sp", ys_d[512 * g:512 * g + 128, t0:t0 + 512], ypool[:], [r_ypool], pwrites=[r_ys], key=r_ypool)
            E("pool", "tensor_copy", dict(out=pv[:, 0:16], in_=pv[:, 512:528]), [r_pv], [r_pv])
            b2 = nb()
            mm(pb[b2][0:64, 0:cn], w2kvb[:, 0:64], hkv[:, 0:cn], True, True, [r_w2kvb, r_hkv], pwrites=[r_pb[b2]])
            mm(pb[b2][0:64, 64:64 + cn], w2kvb[:, 64:128], hkv[:, 64:64 + cn], True, True, [r_w2kvb, r_hkv], pwrites=[r_pb[b2]])
            copy_op("dve", kcT[:, n0:n0 + cn], pb[b2][0:64, 0:cn], [r_pb[b2]], [r_kcT])
            copy_op("dve", vcT[:, n0:n0 + cn], pb[b2][0:64, 64:64 + cn], [r_pb[b2]], [r_vcT])
            for c in sorted(set((n0 // 128, n1 // 128))):
                b3 = nb()
                mm(pb[b3][:, 0:64], vcT[:, 128 * c:128 * c + 128], identb[0:64, 0:64], True, True, [r_vcT, r_identb], [r_pb[b3]])
                copy_op("dve", vcp[:, c, 0:64], pb[b3][:, 0:64], [r_pb[b3]], [r_vcp])

            E("dve", "tensor_scalar", dict(out=yconv[:], in0=uu[:, 0:512], scalar1=pcoef[:, 5:6], scalar2=None, op0=ALU.mult), [r_uu, r_pcoef], [r_yconv])
            E("dve", "scalar_tensor_tensor", dict(out=yconv[:], in0=uu[:, 1:513], scalar=pcoef[:, 6:7], in1=yconv[:], op0=ALU.mult, op1=ALU.add), [r_uu, r_pcoef, r_yconv], [r_yconv])
            E("dve", "scalar_tensor_tensor", dict(out=yconv[:], in0=uu[:, 2:514], scalar=pcoef[:, 7:8], in1=yconv[:], op0=ALU.mult, op1=ALU.add), [r_uu, r_pcoef, r_yconv], [r_yconv])
            E("dve", "tensor_tensor", dict(out=yconv[:], in0=yconv[:], in1=cbb[:], op=ALU.mult), [r_yconv, r_cbb], [r_yconv])
            E("dve", "tensor_tensor", dict(out=yconvb[:], in0=yconv[:], in1=cz[:], op=ALU.mult), [r_yconv, r_cz], [r_yconvb])
            dma("sp", ys_d[512 * g + 128:512 * g + 256, t0:t0 + 512], yconvb[:], [r_yconvb], pwrites=[r_ys], key=r_yconvb)
            E("pool", "tensor_copy", dict(out=uu[:, 0:2], in_=uu[:, 512:514]), [r_uu], [r_uu])

            ecnt = [0]
            pend = []
            LOOKAHEAD = 2

            tlist = []
            pairc = [0]

            def tile_attn(lhsT, lhs_reads, rhs, rhs_reads, bias_ap, bias_reads, Obank, vrhs, v_reads, first, last, extra=None):
                tlist.append((lhsT, lhs_reads, rhs, rhs_reads, bias_ap, bias_reads, Obank, vrhs, v_reads, first, last, extra))

            def run_tiles():
                i = 0
                while i < len(tlist):
                    grp = tlist[i:i + 2]
                    i += len(grp)
                    P = pairc[0] % 2
                    pairc[0] += 1
                    for h, (lhsT, lhs_reads, rhs, rhs_reads, bias_ap, bias_reads, Obank, vrhs, v_reads, first, last, extra) in enumerate(grp):
                        b = 2 * P + h
                        mm(pb[b][:, :], lhsT, rhs, True, bias_ap is None, list(lhs_reads) + list(rhs_reads), [r_pb[b]])
                        if bias_ap is not None:
                            mm(pb[b][:, :], Jm, bias_ap, False, True, [r_cbt] + list(bias_reads), pwrites=[r_pb[b]])
                    wd = 512 * len(grp)
                    E("act", "activation", dict(out=ET[P][:, 0:wd], in_=SP[P][:, 0:wd], func=AF.Exp),
                      [r_pb[2 * P + h] for h in range(len(grp))], [r_E[2 * P + h] for h in range(len(grp))])

                    def stage2(grp=grp, P=P):
                        for h, (lhsT, lhs_reads, rhs, rhs_reads, bias_ap, bias_reads, Obank, vrhs, v_reads, first, last, extra) in enumerate(grp):
                            ei = 2 * P + h
                            for r in range(4):
                                mm(pb[Obank][:, 65 * r:65 * r + 65], Et[ei][:, 128 * r:128 * r + 128], vrhs, first and r == 0, last and r == 3, [r_E[ei]] + list(v_reads), pwrites=[r_pb[Obank]])
                            if extra is not None:
                                extra(ei)

                    pend.append(stage2)
                    while len(pend) > 1:
                        pend.pop(0)()
                del tlist[:]

            def flush():
                run_tiles()
                while pend:
                    pend.pop(0)()

            def fold(bi, bank, ya, ry, qb, first):
                Ob = pb[bank][:, 0:260].rearrange("p (r c) -> p r c", c=65)
                gs = gsig[:, qb, :].rearrange("p (r b) -> p r b", b=3)
                if bi > 0:
                    E("dve", "tensor_scalar_max", dict(out=den[:, 4 * bi:4 * bi + 4], in0=Ob[:, :, 64], scalar1=1e-30), [r_pb[bank]], pwrites=[r_den])
                    E("dve", "reciprocal", dict(out=den[:, 4 * bi:4 * bi + 4], in_=den[:, 4 * bi:4 * bi + 4]), [r_den], pwrites=[r_den])
                E("dve", "tensor_tensor", dict(out=coef[:, 4 * bi:4 * bi + 4], in0=den[:, 4 * bi:4 * bi + 4], in1=gs[:, :, bi], op=ALU.mult),
                  [r_den, r_gsig[qb]], pwrites=[r_coef])
                for r in range(4):
                    if first:
                        E("dve", "tensor_scalar", dict(out=ya[:, 64 * r:64 * r + 64], in0=Ob[:, r, 0:64], scalar1=coef[:, 4 * bi + r:4 * bi + r + 1], scalar2=None, op0=ALU.mult),
                          [r_pb[bank], r_coef], pwrites=[ry])
                    else:
                        E("dve", "scalar_tensor_tensor", dict(out=ya[:, 64 * r:64 * r + 64], in0=Ob[:, r, 0:64], scalar=coef[:, 4 * bi + r:4 * bi + r + 1], in1=ya[:, 64 * r:64 * r + 64], op0=ALU.mult, op1=ALU.add),
                          [r_pb[bank], r_coef, ry], pwrites=[ry])

            def make_deferred(qi, qb, RA, RB, ya, ry):
                def run():
                    for jt in range(qi + 1):
                        if jt < 32:
                            rhs, rr = RA, [r_RQAq[qb], r_RQAs[qb]]
                        else:
                            rhs, rr = RB, [r_RQBq[qb], r_RQBs[qb]]
                        bias_ap = Rres[:, qi - jt, :] if qi - jt <= 1 else None
                        tile_attn(KE[:, 128 * jt:128 * jt + 128], [r_KE[jt // 4], r_KEe], rhs, rr, bias_ap, [r_Rres], 5,
                                  VS[:, jt, :], [r_VS[jt // 4], r_VSo], jt == 0, jt == qi)
                    flush()
                    fold(1, 5, ya, ry, qb, False)
                    E("dve", "tensor_tensor", dict(out=ya, in0=ya, in1=zs[:, qb, :], op=ALU.mult), [ry, r_zs[qb]], [ry])

                    def out_fn():
                        bt_ = nb()
                        tr(pb[bt_][:, 0:128], ya[:, 0:128], [ry], pwrites=[r_pb[bt_]])
                        tr(pb[bt_][:, 128:256], ya[:, 128:256], [ry], pwrites=[r_pb[bt_]])
                        copy_op("act", yTn[:].rearrange("p a b -> p (a b)"), pb[bt_][:, 0:256], [r_pb[bt_]], [r_yTn])
                        q0 = 128 * qi
                        dma("sp", ys_d[512 * g + 256:512 * g + 512, q0:q0 + 128].rearrange("(i c) t -> c i t", c=128), yTn[:], [r_yTn], pwrites=[r_ys], key=r_yTn)

                    pout.append(out_fn)
                return run

            deferred = None
            pout = []
            for qb in range(4):
                qi = 4 * ci + qb
                RA = RQA[:, qb, :]
                RB = RQB[:, qb, :]
                ya = yacc[:, qi % 2, :]
                ry = r_yacc[qi % 2]
                cmax = qi // 16
                prefetch_r1(qi)
                if qi + 1 < 4 * nchunks:
                    prefetch_r1(qi + 1)
                for c in range(cmax + 1):
                    m = qi - 16 * c
                    bias_ap, bias_reads = None, []
                    if m <= 16:
                        ri = r1map[(qi, c)]
                        bias_ap, bias_reads = R1t[ri][:], [r_R1t[ri]]

                    def extra(ei, c=c, cmax=cmax):
                        for r in range(4):
                            mm(pb[7][:, 128 * r:128 * r + 128], Et[ei][:, 128 * r:128 * r + 128], ovt[:, 128 * c:128 * c + 128], c == 0 and r == 0, c == cmax and r == 3,
                               [r_E[ei], r_cbt], pwrites=[r_pb[7]])

                    tile_attn(kcT[:, 128 * c:128 * c + 128], [r_kcT], RA[0:64, :], [r_RQAq[qb]], bias_ap, bias_reads, 4, vcp[:, c, :], [r_vcp], c == 0, c == cmax, extra)
                flush()
                while pout:
                    pout.pop(0)()
                O1 = pb[4][:, 0:260].rearrange("p (r c) -> p r c", c=65)
                E("dve", "tensor_scalar_max", dict(out=den[:, 0:4], in0=O1[:, :, 64], scalar1=1e-30), [r_pb[4]], pwrites=[r_den])
                E("dve", "reciprocal", dict(out=den[:, 0:4], in_=den[:, 0:4]), [r_den], pwrites=[r_den])
                E("pool", "memset", dict(ap=sc[:], constant=-1e30), writes=[r_sc])
                ncol = 2 * qi + 2
                E("dve", "tensor_scalar", dict(out=sc[:, 0:ncol], in0=pb[7][:, 0:ncol], scalar1=den[:, 0:1], scalar2=None, op0=ALU.mult),
                  [r_pb[7], r_den], pwrites=[r_sc])
                for r in range(1, 4):
                    E("dve", "scalar_tensor_tensor", dict(out=sc[:, 0:ncol], in0=pb[7][:, 128 * r:128 * r + ncol], scalar=den[:, r:r + 1], in1=sc[:, 0:ncol], op0=ALU.mult, op1=ALU.add),
                      [r_pb[7], r_den, r_sc], pwrites=[r_sc])
                if qi >= 1:
                    ja = 2 * qi - 1
                    E("dve", "tensor_scalar", dict(out=sc[:, ja:ja + 1], in0=sc[:, ja:ja + 1], scalar1=halfc[:, 0:1], scalar2=halfc[:, 1:2], op0=ALU.mult, op1=ALU.add),
                      [r_sc, r_halfc], pwrites=[r_sc])
                E("dve", "memset", dict(ap=sc[:, 0:1], constant=1e9), [r_sc], pwrites=[r_sc])
                E("dve", "memset", dict(ap=sc[:, 2 * qi:2 * qi + 1], constant=2e9), [r_sc], pwrites=[r_sc])
                E("dve", "tensor_copy", dict(out=sc[:, 2 * qi + 1:2 * qi + 2], in_=halfc[:, 2:3]), [r_sc, r_halfc], pwrites=[r_sc])
                E("dve", "max", dict(out=mx[:, 0:8], in_=sc[:]), [r_sc], pwrites=[r_mx])
                E("dve", "match_replace", dict(out=sc2[:], in_to_replace=mx[:, 0:8], in_values=sc[:], imm_value=-3e38), [r_sc, r_mx], [r_sc2])
                E("dve", "max", dict(out=mx[:, 8:16], in_=sc2[:]), [r_sc2, r_mx], pwrites=[r_mx])
                E("dve", "tensor_reduce", dict(out=th[:], in_=mx[:, 8:16], axis=AX.X, op=ALU.min), [r_mx], [r_th])
                E("dve", "tensor_scalar", dict(out=seln[:, 64:192], in0=sc[:], scalar1=th[:, 0:1], scalar2=NEGB, op0=ALU.is_lt, op1=ALU.mult),
                  [r_sc, r_th], [r_seln])
                fold(0, 4, ya, ry, qb, True)
                dmax = min(4, qi)
                for dl in range(dmax + 1):
                    jt = qi - dl
                    tile_attn(KW[0:64, 128 * jt:128 * jt + 128], [r_KW[jt // 4]], RA[0:64, :], [r_RQAq[qb]], Rres[:, 2 + dl, :], [r_Rres], 6,
                              VW[:, jt, :], [r_VW[jt // 4], r_VWo], dl == 0, dl == dmax)
                flush()
                fold(2, 6, ya, ry, qb, False)
                if deferred is not None:
                    deferred()
                b = 6
                mm(pb[b][:, :], seln[:, 0:128], I4, True, True, [r_seln, r_cbt], [r_pb[b]])
                copy_op("dve", RA[64:128, :], pb[b][64:128, :], [r_pb[b]], [r_RQAs[qb]])
                if qi >= 32:
                    b = 5
                    mm(pb[b][:, :], seln[:, 64:192], I4, True, True, [r_seln, r_cbt], [r_pb[b]])
                    copy_op("dve", RB[64:128, :], pb[b][64:128, :], [r_pb[b]], [r_RQBs[qb]])
                deferred = make_deferred(qi, qb, RA, RB, ya, ry)
            deferred()
            while pout:
                pout.pop(0)()


    WoV = KE[:].rearrange("p (a b) -> p a b", a=8)
    yTbV = [kvT[:, 4096 * s_:4096 * s_ + 4096].rearrange("p (a b) -> p a b", a=8) for s_ in range(2)]
    WfF = bass.AP(Wf, 0, [[8 * NFM, 128], [1, 8 * NFM]]).bitcast(F32)
    xtV = [WfF[:, 0:1024], WfF[:, 1024:2048]]
    ztV = [WfF[:, 2048:3072], WfF[:, 3072:4096]]
    WtF = bass.AP(Wt, 0, [[8 * NTM, 128], [1, 8 * NTM]]).bitcast(F32)
    sqV2 = [WfF[:, 4096:5120], WtF[:, 0:1024]]
    otV = [bass.AP(Rres, 0, [[3584, 128], [1, 3584]]).bitcast(F32)[:, 0:1024], bass.AP(xT, 0, [[4096, 128], [1, 4096]]).bitcast(F32)[:, 0:1024]]
    w1F = bass.AP(w1kv, 0, [[4096, 128], [1, 4096]]).bitcast(F32)
    GtV = w1F[:, 0:1024]
    BtV = w1F[:, 1024:2048]
    stt2 = sb("stt", [128, 8], F32); r_stt2 = [k.R("stt0"), k.R("stt1")]
    r_WoV = k.R("WoV"); r_yTbV = [k.R("yTbV0"), k.R("yTbV1")]; r_xtV = [k.R("xtV0"), k.R("xtV1")]
    r_ztV = [k.R("ztV0"), k.R("ztV1")]; r_sqV2 = [k.R("sqV0"), k.R("sqV1")]; r_otV = [k.R("otV0"), k.R("otV1")]
    r_GtV = k.R("GtV"); r_BtV = k.R("BtV")
    ALPHA_ = (2 * 2) ** 0.25

    def B_body(l, xsrc, xreads, dst, r_dst):
        dma("sp", GtV, bass.AP(lng_t, D * l, [[0, 128], [1, D]]), [], [r_GtV])
        dma("sp", BtV, bass.AP(lnb_t, D * l, [[0, 128], [1, D]]), [], [r_BtV])
        for j in range(8):
            s = next_stg()
            dma("sp" if j % 2 == 0 else "pool", stg[s][:, 0:1024], wo_all[D * l + 128 * j:D * l + 128 * j + 128, :], [], [r_stg[s]])
            copy_op(evac_eng(), WoV[:, j, :], stg[s][:, 0:1024], [r_stg[s]], pwrites=[r_WoV])
        tiles = [(ch, a) for ch in range(TT // 512) for a in range(4)]
        NTL = len(tiles)

        def P1(t):
            ch, a = tiles[t]
            s = ch % 2
            u = t % 2
            t0 = 512 * ch
            r0 = t0 + 128 * a
            if a == 0:
                dma("sp" if ch % 2 == 0 else "pool", yTbV[s], ys_d[:, t0:t0 + 512].rearrange("(j p) t -> p j t", p=128), [r_ys], [r_yTbV[s]])
            stt = stt2[:, 4 * u:4 * u + 4]
            for h in range(2):
                b = nb()
                for j in range(8):
                    mm(pb[b][:, :], yTbV[s][:, j, 128 * a:128 * a + 128], WoV[:, j, 512 * h:512 * h + 512], j == 0, j == 7, [r_yTbV[s], r_WoV], pwrites=[r_pb[b]])
                E("dve", "scalar_tensor_tensor", dict(out=ztV[u][:, 512 * h:512 * h + 512], in0=xtV[u][:, 512 * h:512 * h + 512], scalar=ALPHA_, in1=pb[b][:, :], op0=ALU.mult, op1=ALU.add),
                  [r_xtV[u], r_pb[b]], pwrites=[r_ztV[u]])
            E("act", "activation", dict(out=sqV2[u], in_=ztV[u], func=AF.Copy, accum_out=stt[:, 0:1]), [r_ztV[u]], [r_sqV2[u], r_stt2[u]])
            E("dve", "tensor_scalar_mul", dict(out=stt[:, 0:1], in0=stt[:, 0:1], scalar1=1.0 / 1024), [r_stt2[u]], pwrites=[r_stt2[u]])
            E("dve", "tensor_scalar", dict(out=ztV[u], in0=ztV[u], scalar1=stt[:, 0:1], scalar2=None, op0=ALU.subtract), [r_ztV[u], r_stt2[u]], [r_ztV[u]])

        def P2a(t):
            u = t % 2
            stt_ = stt2[:, 4 * u:4 * u + 4]
            E("act", "activation", dict(out=sqV2[u], in_=ztV[u], func=AF.Square, accum_out=stt_[:, 1:2]), [r_ztV[u]], [r_sqV2[u], r_stt2[u]])

        def P2b(t):
            u = t % 2
            stt = stt2[:, 4 * u:4 * u + 4]
            E("dve", "tensor_scalar", dict(out=stt[:, 1:2], in0=stt[:, 1:2], scalar1=1.0 / 1024, scalar2=1e-5, op0=ALU.mult, op1=ALU.add), [r_stt2[u]], pwrites=[r_stt2[u]])
            E("act", "activation", dict(out=stt[:, 2:3], in_=stt[:, 1:2], func=AF.Sqrt), [r_stt2[u]], pwrites=[r_stt2[u]])
            E("dve", "reciprocal", dict(out=stt[:, 3:4], in_=stt[:, 2:3]), [r_stt2[u]], pwrites=[r_stt2[u]])

        def P3(t):
            ch, a = tiles[t]
            u = t % 2
            r0 = 512 * ch + 128 * a
            stt = stt2[:, 4 * u:4 * u + 4]
            E("dve", "scalar_tensor_tensor", dict(out=otV[u], in0=ztV[u], scalar=stt[:, 3:4], in1=GtV, op0=ALU.mult, op1=ALU.mult), [r_ztV[u], r_stt2[u], r_GtV], [r_otV[u]])
            E("pool", "tensor_tensor", dict(out=otV[u], in0=otV[u], in1=BtV, op=ALU.add), [r_otV[u], r_BtV], [r_otV[u]])
            dma("sp", dst[r0:r0 + 128, :], otV[u], [r_otV[u]], pwrites=[r_dst], key=r_otV[u])

        def LX(t):
            ch, a = tiles[t]
            r0 = 512 * ch + 128 * a
            dma("act", xtV[t % 2], xsrc[r0:r0 + 128, :], xreads, [r_xtV[t % 2]])

        LX(0)
        for t in range(NTL + 2):
            if t + 1 < NTL:
                LX(t + 1)
            if 0 <= t - 1 < NTL:
                P2a(t - 1)
            if 0 <= t - 2 < NTL:
                P3(t - 2)
            if t < NTL:
                P1(t)
            if 0 <= t - 1 < NTL:
                P2b(t - 1)

    xsrc, xreads = x_d, []
    for l in range(L):
        for g in range(G):
            A_body(l, g, xsrc, xreads)
        k.barrier()
        last = l == L - 1
        dst, r_dst = (out_d, r_out) if last else (x1_d, r_x1)
        B_body(l, xsrc, xreads, dst, r_dst)
        k.barrier()
        xsrc, xreads = x1_d, [r_x1]
    stats = k.emit()
    return nc, stats


def host_fused_inputs(inp, L=2, G=2):
    d = dict(host_consts())
    per = [[host_layer_inputs(inp, l, g) for g in range(G)] for l in range(L)]
    d["wf"] = np.concatenate([per[l][g]["wf"] for l in range(L) for g in range(G)], axis=0)
    d["wt"] = np.concatenate([per[l][g]["wt"] for l in range(L) for g in range(G)], axis=0)
    d["poolw"] = np.concatenate([per[l][g]["poolw"] for l in range(L) for g in range(G)], axis=0)
    d["pcoef"] = np.concatenate([per[l][g]["pcoef"] for l in range(L) for g in range(G)], axis=0)
    d["pcorr"] = np.concatenate([per[0][g]["pcorr"] for g in range(G)], axis=0)
    d["w1kv"] = np.concatenate([per[l][0]["w1kv"] for l in range(L)], axis=0)
    d["pekv"] = np.concatenate([per[l][0]["pekv"] for l in range(L)], axis=0)
    d["w2kv"] = np.concatenate([per[l][0]["w2kv"] for l in range(L)], axis=0)
    d["relb"] = np.concatenate([per[0][g]["relb"] for g in range(G)], axis=0)
    perm = []
    for g in range(G):
        perm += list(range(128 * g, 128 * g + 128)) + list(range(256 + 128 * g, 256 + 128 * g + 128)) + list(range(512 + 256 * g, 512 + 256 * g + 256))
    perm = np.asarray(perm)
    d["wo"] = np.concatenate([inp["w_out"][l][perm] for l in range(L)], axis=0)
    d["lng"] = np.ascontiguousarray(inp["ln_g"][:L])
    d["lnb"] = np.ascontiguousarray(inp["ln_b"][:L])
    return {k_: np.ascontiguousarray(v) for k_, v in d.items()}


_CACHE = {}


def kernel(x, w_in, w_out, pool_w, pool_scale, conv_w, cmp_pe_k, cmp_w1_k, cmp_w2_k,
           cmp_pe_v, cmp_w1_v, cmp_w2_v, rel_bias, ln_g, ln_b):
    from concourse.bass_utils import run_bass_kernel_spmd
    inp = dict(x=x, w_in=w_in, w_out=w_out, pool_w=pool_w, pool_scale=pool_scale, conv_w=conv_w,
               cmp_pe_k=cmp_pe_k, cmp_w1_k=cmp_w1_k, cmp_w2_k=cmp_w2_k, cmp_pe_v=cmp_pe_v,
               cmp_w1_v=cmp_w1_v, cmp_w2_v=cmp_w2_v, rel_bias=rel_bias, ln_g=ln_g, ln_b=ln_b)
    inp = {k_: np.asarray(v, dtype=np.float32) for k_, v in inp.items()}
    if "F" not in _CACHE:
        _CACHE["F"] = build_F(2, 2, NCHUNK)[0]
    nc = _CACHE["F"]
    base = host_fused_inputs(inp, 2, 2)
    in_maps = []
    for core in range(8):
        m = dict(base)
        m["x"] = np.ascontiguousarray(inp["x"][core // 2])
        in_maps.append(m)
    res = run_bass_kernel_spmd(nc, in_maps, core_ids=list(range(8))).results
    return np.stack([res[2 * b]["out"] for b in range(inp["x"].shape[0])], axis=0)
```

```python
import math
import ml_dtypes
import numpy as np
import concourse.bass as bass
import concourse.mybir as mybir

F32 = mybir.dt.float32
BF16 = mybir.dt.bfloat16
AF = mybir.ActivationFunctionType
ALU = mybir.AluOpType
AX = mybir.AxisListType

COMPUTE = ("pe", "act", "dve", "pool")


class Res:
    __slots__ = ("name", "writers", "readers", "sem", "dma_cnt", "gen_deps", "excl")

    def __init__(self, name):
        self.name = name
        self.writers = []
        self.readers = []
        self.gen_deps = []
        self.excl = False
        self.sem = {}
        self.dma_cnt = {}


class Op:
    __slots__ = ("eng", "fn", "deps", "is_dma", "dres", "dval", "idx", "cnt", "signal")

    def __init__(self, eng, fn, is_dma):
        self.eng = eng
        self.fn = fn
        self.deps = []
        self.is_dma = is_dma
        self.dres = None
        self.dval = 0
        self.cnt = 0
        self.signal = False


class Emitter:
    def __init__(self, nc):
        self.nc = nc
        self.ops = []
        self.res = {}
        self.nres = 0
        self.dma_res = {}
        self.phase = Res("phase")
        self.bar_tile = None

    def R(self, name=None):
        self.nres += 1
        r = Res(name or f"r{self.nres}")
        return r

    def sb(self, name, shape, dtype):
        t = self.nc.alloc_sbuf_tensor(name, list(shape), dtype)
        t.res = None
        return t

    def op(self, eng, fn, reads=(), writes=(), pwrites=(), dma=False, key=None):
        o = Op(eng, fn, dma)
        o.idx = len(self.ops)
        if self.phase not in writes:
            reads = list(reads) + [self.phase]
        xr = [r for r in reads if r.excl]
        if xr:
            reads = [r for r in reads if not r.excl]
            writes = list(writes) + [r for r in xr if r not in writes]
        deps = []
        for r in reads:
            deps.extend((p, "raw") for p in r.writers)
        for r in writes:
            deps.extend((p, "war") for p in r.readers)
            deps.extend((p, "waw") for p in r.writers)
        for r in pwrites:
            if r.readers:
                deps.extend((p, "war") for p in r.readers)
                deps.extend((p, "waw") for p in r.writers if not (p.is_dma and dma))
            else:
                deps.extend((p, "war") for p in r.gen_deps)
                deps.extend((p, "waw") for p in r.writers if not (p.is_dma and dma))
        for r in reads:
            r.readers.append(o)
        for r in writes:
            r.gen_deps = r.readers + r.writers
            r.writers = [o]
            r.readers = []
        for r in pwrites:
            if r.readers:
                r.gen_deps = r.readers + r.writers
                r.writers = [o]
                r.readers = []
            else:
                r.writers.append(o)
        if dma:
            allw = list(writes) + list(pwrites)
            d = key if key is not None else allw[0]
            self.dma_res[id(d)] = d
            d.dma_cnt[eng] = d.dma_cnt.get(eng, 0) + 1
            o.dres = d
            o.dval = 16 * d.dma_cnt[eng]
        best = {}
        for p, kind in deps:
            if p is o:
                continue
            if p.is_dma:
                key = ("d", id(p.dres), p.eng)
                if key not in best or best[key].dval < p.dval:
                    best[key] = p
            else:
                if p.eng == o.eng and not dma:
                    if p.eng == "pe":
                        continue
                key = ("c", p.eng)
                if key not in best or best[key].idx < p.idx:
                    best[key] = p
        o.deps = list(best.values())
        for p in o.deps:
            p.signal = True
        self.ops.append(o)
        return o

    def barrier(self):
        if self.bar_tile is None:
            self.bar_tile = self.nc.alloc_sbuf_tensor("s_bar_tile", [128, 2], F32)
        t = self.bar_tile
        self.op("dve", lambda e: e.memset(t[:], 0.0), writes=[self.phase])

    def emit(self, final_res=()):
        nc = self.nc
        esem = {e: nc.alloc_semaphore(f"s_{e}") for e in COMPUTE}
        cnt = {e: 0 for e in COMPUTE}
        for o in self.ops:
            if o.is_dma:
                if o.eng not in o.dres.sem:
                    o.dres.sem[o.eng] = nc.alloc_semaphore(f"d_{o.dres.name}_{o.eng}")
            elif o.signal:
                cnt[o.eng] += 1
                o.cnt = cnt[o.eng]
        engs = ["pe", "act", "dve", "pool", "sp"]
        per = {e: [o for o in self.ops if o.eng == e] for e in engs}
        final_waits = [(r.sem[e], 16 * r.dma_cnt[e]) for r in self.dma_res.values() for e in r.sem]

        def run(e, eng):
            waited = {}
            for o in per[e]:
                for p in o.deps:
                    if p.is_dma:
                        s, v = p.dres.sem[p.eng], p.dval
                    else:
                        s, v = esem[p.eng], p.cnt
                    if waited.get(s.num, -1) >= v:
                        continue
                    waited[s.num] = v
                    eng.wait_ge(s, v)
                ins = o.fn(eng)
                if o.is_dma:
                    ins.then_inc(o.dres.sem[o.eng], 16)
                elif o.signal:
                    ins.then_inc(esem[o.eng], 1)
            if e == "sp":
                for s, v in final_waits:
                    eng.wait_ge(s, v)

        with nc.Block() as block:
            @block.tensor
            def _(eng):
                run("pe", eng)

            @block.scalar
            def _(eng):
                run("act", eng)

            @block.vector
            def _(eng):
                run("dve", eng)

            @block.gpsimd
            def _(eng):
                run("pool", eng)

            @block.sync
            def _(eng):
                run("sp", eng)
        return {e: len(per[e]) for e in engs}


T = 8192
D = 1024
NCHUNK = 16
LG = 4208
LG3 = 768
LGT = LG + LG3
NEGB = -30000.0
NFM = 1280
NTM = 396


def bucket_np(n):
    n = np.maximum(n, 0)
    nf = np.maximum(n, 1).astype(np.float32)
    large = 16 + (np.log(nf / np.float32(16)) / np.float32(math.log(8.0)) * np.float32(16)).astype(np.int32)
    large = np.minimum(large, 31)
    return np.where(n < 16, n, large)


def host_consts():
    c = {}
    c["cf"] = np.eye(128, dtype=np.float32)
    J = np.eye(128, dtype=np.float32)[::-1].copy()
    I4 = np.tile(np.eye(128, dtype=np.float32), (1, 4))
    ov = np.zeros((512, 128), np.float32)
    for i in range(511):
        for j in range(128):
            o = min(16 * i + 32, 64 * j + 64) - max(16 * i, 64 * j)
            if o > 0:
                ov[i, j] = o / 16.0
    ovc = ov.reshape(4, 128, 128).transpose(1, 0, 2).reshape(128, 512)
    E = np.zeros((128, 8192), np.float32)
    for jt in range(64):
        for kk in range(128):
            j = 2 * jt + (1 if kk >= 64 else 0)
            E[64 + (j % 64), jt * 128 + kk] = 1.0
    c["cb"] = np.concatenate([J, I4, ovc, E], axis=1).astype(ml_dtypes.bfloat16)
    oh = np.zeros((33, LGT), np.float32)
    i = np.arange(LG)
    n = i - 2063
    b = bucket_np(n)
    oh[np.where(n < 0, 32, b), i] = 1.0
    i3 = np.arange(LG3)
    n3 = i3 - 127
    b3 = bucket_np(n3)
    oh[np.where((n3 < 0) | (n3 >= 512), 32, b3), LG + i3] = 1.0
    c["oh"] = oh
    hc = np.zeros((128, 4), np.float32)
    hc[64:, 0] = 1.0
    hc[:64, 1] = 3e9
    hc[:64, 2] = -1e30
    hc[64:, 2] = 4e9
    c["halfc"] = hc
    return c


POOL_WINDOWS = (2, 4, 8, 16)


def host_layer_inputs(inp, l, g):
    w_in = inp["w_in"][l]
    offs = np.cumsum([0, 256, 256, 256, 256, 256, 256, 512, 128, 128, 128, 128, 128, 128, 24, 512])
    (o_pv, o_pz, o_cb, o_cc, o_cx, o_cz, o_q, o_kc, o_vc, o_ks, o_vs, o_kw, o_vw, o_g, o_z) = offs[:15]
    s = slice
    cols = []
    for o in (o_pv, o_pz, o_cb, o_cc, o_cx, o_cz):
        cols.append(np.arange(o + 128 * g, o + 128 * g + 128))
    for r in range(4):
        cols.append(np.arange(o_q + 64 * (4 * g + r), o_q + 64 * (4 * g + r) + 64))
    cols.append(np.arange(o_kc + 64 * g, o_kc + 64 * g + 64))
    cols.append(np.arange(o_vc + 64 * g, o_vc + 64 * g + 64))
    cols.append(np.arange(o_ks + 64 * g, o_ks + 64 * g + 64))
    cols.append(np.arange(o_kw + 64 * g, o_kw + 64 * g + 64))
    cf = np.concatenate(cols)
    assert cf.size == NFM
    ct = np.concatenate([
        np.arange(o_vs + 64 * g, o_vs + 64 * g + 64),
        np.arange(o_vw + 64 * g, o_vw + 64 * g + 64),
        np.arange(o_g + 12 * g, o_g + 12 * g + 12),
        np.arange(o_z + 256 * g, o_z + 256 * g + 256),
    ])
    assert ct.size == NTM
    d = {}
    d["wf"] = np.ascontiguousarray(w_in[:, cf])
    d["wt"] = np.ascontiguousarray(w_in[:, ct])
    pw = np.zeros((128, 128), np.float32)
    pw[0:64, 0:64] = inp["pool_w"][l, 2 * g]
    pw[64:128, 64:128] = inp["pool_w"][l, 2 * g + 1]
    d["poolw"] = pw
    pc = np.zeros((128, 8), np.float32)
    for h in range(2):
        w = POOL_WINDOWS[2 * g + h]
        pc[64 * h:64 * h + 64, POOL_WINDOWS.index(w)] = 1.0 / w
    pc[:, 4] = inp["pool_scale"][l, 128 * g:128 * g + 128]
    for k in range(3):
        pc[:, 5 + k] = inp["conv_w"][l, k, 128 * g:128 * g + 128]
    d["pcoef"] = pc
    pcorr = np.ones((128, 16), np.float32)
    for h in range(2):
        w = POOL_WINDOWS[2 * g + h]
        for t in range(16):
            pcorr[64 * h:64 * h + 64, t] = w / min(t + 1, w)
    d["pcorr"] = pcorr
    w1k = inp["cmp_w1_k"][l].reshape(32, 64, 128).transpose(1, 0, 2).reshape(64, 32 * 128)
    w1v = inp["cmp_w1_v"][l].reshape(32, 64, 128).transpose(1, 0, 2).reshape(64, 32 * 128)
    d["w1kv"] = np.ascontiguousarray(np.concatenate([w1k, w1v], axis=0))
    d["pekv"] = np.ascontiguousarray(np.concatenate([inp["cmp_pe_k"][l].T, inp["cmp_pe_v"][l].T], axis=0))
    d["w2kv"] = np.ascontiguousarray(np.concatenate([inp["cmp_w2_k"][l], inp["cmp_w2_v"][l]], axis=1))
    rb = np.full((33, 4), NEGB, np.float32)
    rb[:32] = inp["rel_bias"][:, 4 * g:4 * g + 4]
    d["relb"] = rb
    return d


def build_F(L=2, G=2, nchunks=NCHUNK):
    nc = bass.Bass("TRN2", target_bir_lowering=False)
    k = Emitter(nc)

    def din(name, shape, dt=F32):
        return nc.dram_tensor(name, shape, dt, kind="ExternalInput").ap()

    def dout(name, shape, dt=F32):
        return nc.dram_tensor(name, shape, dt, kind="ExternalOutput").ap()

    TT = 512 * nchunks
    x_d = din("x", [TT, D])
    wf_all = din("wf", [L * G * D, NFM])
    wt_all = din("wt", [L * G * D, NTM])
    poolw_all = din("poolw", [L * G * 128, 128])
    pcoef_all = din("pcoef", [L * G * 128, 8])
    pcorr_all = din("pcorr", [G * 128, 16])
    w1kv_all = din("w1kv", [L * 128, 4096])
    pekv_all = din("pekv", [L * 128, 32])
    w2kv_all = din("w2kv", [L * 128, 128])
    relb_all = din("relb", [G * 33, 4])
    oh_d = din("oh", [33, LGT])
    cf_d = din("cf", [128, 128])
    cb_d = din("cb", [128, 128 + 512 + 512 + 8192], BF16)
    halfc_d = din("halfc", [128, 4])
    wo_all = din("wo", [L * D, D])
    lng_t = nc.dram_tensor("lng", [L, D], F32, kind="ExternalInput")
    lnb_t = nc.dram_tensor("lnb", [L, D], F32, kind="ExternalInput")
    out_d = dout("out", [TT, D])
    gd_t = nc.dram_tensor("gd", [4, LGT], BF16)
    gd = gd_t.ap()
    ys_d = nc.dram_tensor("ys", [D, TT], BF16).ap()
    x1_d = nc.dram_tensor("x1", [TT, D], F32).ap()
    r_gd = k.R("gd")
    r_ys = k.R("ys")
    r_x1 = k.R("x1")
    r_out = k.R("out")

    def sb(name, shape, dt):
        return nc.alloc_sbuf_tensor("s_" + name, list(shape), dt)

    def E(eng, name, kw, reads=(), writes=(), pwrites=()):
        k.op(eng, lambda e: getattr(e, name)(**kw), reads=reads, writes=writes, pwrites=pwrites)

    Wf = sb("Wf", [128, 8, NFM], BF16); r_Wf = k.R("Wf")
    Wt = sb("Wt", [128, 8, NTM], BF16); r_Wt = k.R("Wt")
    ident = sb("ident", [128, 128], F32); r_ident = k.R("ident")
    identb = sb("identb", [128, 128], BF16); r_identb = k.R("identb")
    cbt = sb("cbt", [128, 128 + 512 + 512], BF16); r_cbt = k.R("cbt")
    Jm = cbt[:, 0:128]
    I4 = cbt[:, 128:640]
    ovt = cbt[:, 640:1152]
    KE = sb("KE", [128, T], BF16); r_KEe = k.R("KEe")
    r_KE = [k.R(f"KE{i}") for i in range(NCHUNK)]
    KW = sb("KW", [64, T], BF16); r_KW = [k.R(f"KW{i}") for i in range(NCHUNK)]
    kvT = sb("kvT", [128, T + 32], BF16); r_kvT = [k.R(f"kvT{i}") for i in range(NCHUNK + 1)]
    VS = sb("VS", [128, 64, 65], BF16); r_VS = [k.R(f"VS{i}") for i in range(NCHUNK)]; r_VSo = k.R("VSo")
    VW = sb("VW", [128, 64, 65], BF16); r_VW = [k.R(f"VW{i}") for i in range(NCHUNK)]; r_VWo = k.R("VWo")
    kcT = sb("kcT", [64, 544], BF16); r_kcT = k.R("kcT")
    vcT = sb("vcT", [64, 544], BF16); r_vcT = k.R("vcT")
    vcp = sb("vcp", [128, 4, 65], BF16); r_vcp = k.R("vcp")
    Rres = sb("Rres", [128, 7, 512], BF16); r_Rres = k.R("Rres")
    NR1 = 4
    R1t = [sb(f"R1t{i}", [128, 512], BF16) for i in range(NR1)]; r_R1t = [k.R(f"R1t{i}") for i in range(NR1)]
    poolw = sb("poolw", [128, 128], F32); r_poolw = k.R("poolw")
    poolwb = sb("poolwb", [128, 128], BF16); r_poolwb = k.R("poolwb")
    pcoef = sb("pcoef", [128, 8], F32); r_pcoef = k.R("pcoef")
    pcorr = sb("pcorr", [128, 16], F32); r_pcorr = k.R("pcorr")
    halfc = sb("halfc", [128, 4], F32); r_halfc = k.R("halfc")
    w1kv = sb("w1kv", [128, 32, 128], BF16); r_w1kv = k.R("w1kv")
    pekv = sb("pekv", [128, 32], F32); r_pekv = k.R("pekv")
    pekvb = sb("pekvb", [128, 32], BF16); r_pekvb = k.R("pekvb")
    w2kv = sb("w2kv", [128, 128], F32); r_w2kv = k.R("w2kv")
    w2kvb = sb("w2kvb", [128, 128], BF16); r_w2kvb = k.R("w2kvb")
    hbias = sb("hbias", [128, 2], F32); r_hbias = k.R("hbias")
    relb = sb("relb", [33, 4], F32); r_relb = k.R("relb")
    crt = sb("crt", [4, 1], F32); r_crt = k.R("crt")
    Gc = [sb(f"Gc{i}", [4, 512], BF16) for i in range(2)]; r_Gc = [k.R(f"Gc{i}") for i in range(2)]
    stg = [sb(f"stg{i}", [128, 4096], F32) for i in range(2)]
    r_stg = [k.R(f"stg{i}") for i in range(2)]
    xT = sb("xT", [128, 8, 512], BF16); r_xT = k.R("xT")
    RQA = sb("RQA", [128, 4, 512], BF16); r_RQAq = [k.R(f"RQAq{i}") for i in range(4)]; r_RQAs = [k.R(f"RQAs{i}") for i in range(4)]
    RQB = sb("RQB", [128, 4, 512], BF16); r_RQBq = [k.R(f"RQBq{i}") for i in range(4)]; r_RQBs = [k.R(f"RQBs{i}") for i in range(4)]
    gsig = sb("gsig", [128, 4, 12], F32); r_gsig = [k.R(f"gsig{i}") for i in range(4)]
    zs = sb("zs", [128, 4, 256], F32); r_zs = [k.R(f"zs{i}") for i in range(4)]
    pv = sb("pv", [128, 528], F32); r_pv = k.R("pv")
    sA = sb("sA", [128, 528], F32); r_sA = k.R("sA")
    sB = sb("sB", [128, 528], F32); r_sB = k.R("sB")
    pm = sb("pm", [128, 512], F32); r_pm = k.R("pm")
    pmb = sb("pmb", [128, 512], BF16); r_pmb = k.R("pmb")
    pz = sb("pz", [128, 512], F32); r_pz = k.R("pz")
    cbb = sb("cbb", [128, 512], F32); r_cbb = k.R("cbb")
    ccc = sb("ccc", [128, 512], F32); r_ccc = k.R("ccc")
    uu = sb("uu", [128, 514], F32); r_uu = k.R("uu")
    cz = sb("cz", [128, 512], F32); r_cz = k.R("cz")
    ypool = sb("ypool", [128, 512], BF16); r_ypool = k.R("ypool")
    yconvb = sb("yconvb", [128, 512], BF16); r_yconvb = k.R("yconvb")
    yTn = sb("yTn", [128, 2, 128], BF16); r_yTn = k.R("yTn")
    yconv = sb("yconv", [128, 512], F32); r_yconv = k.R("yconv")
    hkv = sb("hkv", [128, 128], BF16); r_hkv = k.R("hkv")
    Xc = sb("Xc", [128, 16, 66], BF16); r_Xc = k.R("Xc")
    NE = 4
    ET = [sb(f"ET{i}", [128, 1024], BF16) for i in range(2)]
    Et = [ET[0][:, 0:512], ET[0][:, 512:1024], ET[1][:, 0:512], ET[1][:, 512:1024]]; r_E = [k.R(f"E{i}") for i in range(NE)]
    sc = sb("sc", [128, 128], F32); r_sc = k.R("sc")
    sc2 = sb("sc2", [128, 128], F32); r_sc2 = k.R("sc2")
    mx = sb("mx", [128, 16], F32); r_mx = k.R("mx")
    th = sb("th", [128, 1], F32); r_th = k.R("th")
    seln = sb("seln", [128, 192], BF16); r_seln = k.R("seln")
    den = sb("den", [128, 12], F32); r_den = k.R("den")
    coef = sb("coef", [128, 12], F32); r_coef = k.R("coef")
    yacc = sb("yacc", [128, 2, 256], F32); r_yacc = [k.R("yacc0"), k.R("yacc1")]

    SP = [nc.alloc_psum_tensor(f"spb{i}", [128, 1024], F32) for i in range(2)]
    pb = [SP[0][:, 0:512], SP[0][:, 512:1024], SP[1][:, 0:512], SP[1][:, 512:1024]] + [nc.alloc_psum_tensor(f"pb{i}", [128, 512], F32) for i in range(4, 8)]
    r_pb = [k.R(f"pb{i}") for i in range(8)]
    for r_ in r_pb:
        r_.excl = True
    nbs = [0]

    def nb():
        i = nbs[0]
        nbs[0] = (i + 1) % 4
        return i

    cnt2 = [0]

    def evac_eng():
        cnt2[0] += 1
        return "act" if cnt2[0] % 2 else "dve"

    def copy_op(eng, out, in_, reads, writes=(), pwrites=(), scale=None):
        if eng == "act":
            kw = dict(out=out, in_=in_, func=AF.Copy)
            if scale is not None:
                kw["scale"] = scale
            E("act", "activation", kw, reads, writes, pwrites)
        else:
            if scale is None:
                E(eng, "tensor_copy", dict(out=out, in_=in_), reads, writes, pwrites)
            else:
                E(eng, "tensor_scalar_mul", dict(out=out, in0=in_, scalar1=scale), reads, writes, pwrites)

    def dma(eng, out, in_, reads, writes=(), pwrites=(), key=None):
        k.op(eng, lambda e: e.dma_start(out=out, in_=in_), reads=reads, writes=writes, pwrites=pwrites, dma=True, key=key)

    def mm(out, lhsT, rhs, start, stop, reads, writes=(), pwrites=()):
        k.op("pe", lambda e: e.matmul(out, lhsT, rhs, start=start, stop=stop), reads=reads, writes=writes, pwrites=pwrites)

    def tr(out, in_, reads, writes=(), pwrites=()):
        k.op("pe", lambda e: e.transpose(out, in_, ident[:]), reads=list(reads) + [r_ident], writes=writes, pwrites=pwrites)

    si = [0]

    def next_stg():
        s = si[0] % 2
        si[0] += 1
        return s

    def A_body(l, g, xsrc, xreads):
        lg = l * G + g
        dma("sp", ident[:], cf_d[:, :], [], [r_ident])
        dma("sp", cbt[:], cb_d[:, 0:1152], [], [r_cbt])
        dma("sp", KE[64:128, :], cb_d[64:128, 1152:1152 + 8192], [], [r_KEe])
        dma("pool", poolw[:], poolw_all[128 * lg:128 * lg + 128, :], [], [r_poolw])
        dma("pool", pcoef[:], pcoef_all[128 * lg:128 * lg + 128, :], [], [r_pcoef])
        dma("pool", pcorr[:], pcorr_all[128 * g:128 * g + 128, :], [], [r_pcorr])
        dma("pool", halfc[:], halfc_d[:, :], [], [r_halfc])
        dma("pool", pekv[:], pekv_all[128 * l:128 * l + 128, :], [], [r_pekv])
        dma("pool", w2kv[:], w2kv_all[128 * l:128 * l + 128, :], [], [r_w2kv])
        dma("pool", relb[:], relb_all[33 * g:33 * g + 33, :], [], [r_relb])
        copy_op("dve", identb[:], ident[:], [r_ident], [r_identb])
        copy_op("dve", poolwb[:], poolw[:], [r_poolw], [r_poolwb])
        copy_op("dve", pekvb[:], pekv[:], [r_pekv], [r_pekvb])
        copy_op("dve", w2kvb[:], w2kv[:], [r_w2kv], [r_w2kvb])
        E("pool", "memset", dict(ap=kvT[:], constant=0.0), writes=r_kvT)
        E("pool", "memset", dict(ap=kcT[:], constant=0.0), writes=[r_kcT])
        E("pool", "memset", dict(ap=vcT[:], constant=0.0), writes=[r_vcT])
        E("pool", "memset", dict(ap=vcp[:], constant=0.0), writes=[r_vcp])
        E("pool", "memset", dict(ap=vcp[:, :, 64:65], constant=1.0), writes=[r_vcp])
        E("pool", "memset", dict(ap=VS[:, :, 64:65], constant=1.0), writes=[r_VSo])
        E("pool", "memset", dict(ap=VW[:, :, 64:65], constant=1.0), writes=[r_VWo])
        E("pool", "memset", dict(ap=seln[:], constant=0.0), writes=[r_seln])
        E("pool", "memset", dict(ap=pv[:, 0:16], constant=0.0), pwrites=[r_pv])
        E("pool", "memset", dict(ap=uu[:, 0:2], constant=0.0), pwrites=[r_uu])
        E("pool", "memset", dict(ap=KE[0:64, :], constant=0.0), writes=r_KE)
        E("pool", "memset", dict(ap=KW[:], constant=0.0), writes=r_KW)
        E("pool", "memset", dict(ap=VS[:, :, 0:64], constant=0.0), writes=r_VS)
        E("pool", "memset", dict(ap=VW[:, :, 0:64], constant=0.0), writes=r_VW)


        def load_cast(dst_fn, src_fn, ncols, r_dst):
            for j in range(8):
                s = next_stg()
                dma("sp" if j % 2 == 0 else "pool", stg[s][:, 0:ncols], src_fn(j), [], [r_stg[s]])
                copy_op(evac_eng(), dst_fn(j), stg[s][:, 0:ncols], [r_stg[s]], pwrites=[r_dst])

        load_cast(lambda j: Wf[:, j, :], lambda j: wf_all[D * lg + 128 * j:D * lg + 128 * j + 128, :], NFM, r_Wf)
        load_cast(lambda j: Wt[:, j, :], lambda j: wt_all[D * lg + 128 * j:D * lg + 128 * j + 128, :], NTM, r_Wt)
        s = next_stg()
        dma("sp", stg[s][:, 0:4096], w1kv_all[128 * l:128 * l + 128, :], [], [r_stg[s]])
        copy_op("act", w1kv[:].rearrange("p a b -> p (a b)"), stg[s][:, 0:4096], [r_stg[s]], [r_w1kv])

        bk = 7
        for l_ in range(32):
            mm(pb[bk][:, 0:1], w1kv[0:64, l_, :], pekvb[0:64, l_:l_ + 1], l_ == 0, l_ == 31, [r_w1kv, r_pekvb], pwrites=[r_pb[bk]])
        for l_ in range(32):
            mm(pb[6][:, 0:1], w1kv[64:128, l_, :], pekvb[64:128, l_:l_ + 1], l_ == 0, l_ == 31, [r_w1kv, r_pekvb], pwrites=[r_pb[6]])
        copy_op("dve", hbias[:, 0:1], pb[bk][:, 0:1], [r_pb[bk]], pwrites=[r_hbias])
        copy_op("dve", hbias[:, 1:2], pb[6][:, 0:1], [r_pb[6]], pwrites=[r_hbias])

        nchg = (LGT + 511) // 512
        order = [4063 // 512] + [c for c in range(nchg) if c != 4063 // 512]
        for ii, cch in enumerate(order):
            c0 = cch * 512
            w = min(512, LGT - c0)
            s = next_stg()
            dma("sp", stg[s][0:33, 0:w], oh_d[:, c0:c0 + w], [], [r_stg[s]])
            b = nb()
            mm(pb[b][0:4, 0:w], relb[:, :], stg[s][0:33, 0:w], True, True, [r_relb, r_stg[s]], [r_pb[b]])
            if ii == 0:
                copy_op("dve", crt[:], pb[b][0:4, 4063 - c0:4063 - c0 + 1], [r_pb[b]], [r_crt])
            gi = ii % 2
            E("dve", "tensor_scalar", dict(out=Gc[gi][:, 0:w], in0=pb[b][0:4, 0:w], scalar1=crt[:, 0:1], scalar2=None, op0=ALU.subtract),
              [r_pb[b], r_crt], [r_Gc[gi]])
            dma("sp", gd[:, c0:c0 + w], Gc[gi][:, 0:w], [r_Gc[gi]], pwrites=[r_gd], key=r_Gc[gi])

        def hankel(base, step):
            return bass.AP(gd_t, base, [[step, 128], [LGT, 4], [1, 128]])

        bases = [(1936 + 128 * dl, 1) for dl in range(2)] + [(LG + 128 * dl, 1) for dl in range(5)]
        for i, (base, step) in enumerate(bases):
            dma("sp" if i % 2 == 0 else "pool", Rres[:, i, :].rearrange("p (r t) -> p r t", r=4), hankel(base, step), [r_gd], pwrites=[r_Rres])
        r1c = [0]

        def load_x(ci_):
            s_ = next_stg()
            for a in range(4):
                dma("sp" if a % 2 == 0 else "pool", stg[s_][:, 1024 * a:1024 * a + 1024], xsrc[512 * ci_ + 128 * a:512 * ci_ + 128 * a + 128, :], xreads, pwrites=[r_stg[s_]])
            return s_

        r1map = {}

        def prefetch_r1(qi_):
            for c_ in range(qi_ // 16 + 1):
                m_ = qi_ - 16 * c_
                if m_ <= 16 and (qi_, c_) not in r1map:
                    ri = r1c[0] % NR1
                    r1c[0] += 1
                    r1map[(qi_, c_)] = ri
                    dma("pool", R1t[ri][:].rearrange("p (r t) -> p r t", r=4), hankel(128 * m_, 16), [r_gd], [r_R1t[ri]])

        xs_next = load_x(0)
        for ci in range(nchunks):
            t0 = 512 * ci
            s = xs_next
            xt = stg[s]
            for j in range(8):
                b = nb()
                for a in range(4):
                    tr(pb[b][:, 128 * a:128 * a + 128], xt[:, 1024 * a + 128 * j:1024 * a + 128 * j + 128], [r_stg[s]], pwrites=[r_pb[b]])
                copy_op(evac_eng(), xT[:, j, :], pb[b][:, :], [r_pb[b]], pwrites=[r_xT])
            if ci + 1 < nchunks:
                xs_next = load_x(ci + 1)


            def fm_block(col0, M):
                b = nb()
                for j in range(8):
                    mm(pb[b][0:M, :], Wf[:, j, col0:col0 + M], xT[:, j, :], j == 0, j == 7, [r_Wf, r_xT], pwrites=[r_pb[b]])
                return b

            b = fm_block(0, 128)
            copy_op("dve", pv[:, 16:528], pb[b][:, :], [r_pb[b]], pwrites=[r_pv])
            b = fm_block(128, 128)
            E("act", "activation", dict(out=pz[:], in_=pb[b][:, :], func=AF.Silu), [r_pb[b]], [r_pz])
            b = fm_block(256, 128)
            copy_op("dve", cbb[:], pb[b][:, :], [r_pb[b]], [r_cbb])
            b = fm_block(384, 128)
            copy_op("act", ccc[:], pb[b][:, :], [r_pb[b]], [r_ccc])
            b = fm_block(512, 128)
            E("dve", "tensor_tensor", dict(out=uu[:, 2:514], in0=ccc[:], in1=pb[b][:, :], op=ALU.mult), [r_ccc, r_pb[b]], pwrites=[r_uu])
            b = fm_block(640, 128)
            E("act", "activation", dict(out=cz[:], in_=pb[b][:, :], func=AF.Silu), [r_pb[b]], [r_cz])
            for r in range(4):
                b = fm_block(768 + 64 * r, 64)
                src = pb[b][0:64, :].rearrange("p (q t) -> p q t", q=4)
                copy_op("act", RQA[0:64, :, 128 * r:128 * r + 128], src, [r_pb[b]], pwrites=r_RQAq, scale=0.125)
                copy_op("dve", RQB[0:64, :, 128 * r:128 * r + 128], src, [r_pb[b]], pwrites=r_RQBq, scale=0.125)
            b = fm_block(1024, 128)
            copy_op("act", kvT[:, t0:t0 + 512], pb[b][:, :], [r_pb[b]], [r_kvT[ci]])
            b = fm_block(1152, 64)
            copy_op("dve", KE[0:64, t0:t0 + 512], pb[b][0:64, :], [r_pb[b]], [r_KE[ci]])
            b = fm_block(1216, 64)
            copy_op("act", KW[0:64, t0:t0 + 512], pb[b][0:64, :], [r_pb[b]], [r_KW[ci]])
            tmb = []
            for a in range(4):
                b = nb()
                jt = 4 * ci + a
                for j in range(8):
                    mm(pb[b][:, 0:NTM], xT[:, j, 128 * a:128 * a + 128], Wt[:, j, :], j == 0, j == 7, [r_Wt, r_xT], pwrites=[r_pb[b]])
                copy_op("dve", VS[:, jt, 0:64], pb[b][:, 0:64], [r_pb[b]], pwrites=[r_VS[ci]])
                copy_op("dve", VW[:, jt, 0:64], pb[b][:, 64:128], [r_pb[b]], pwrites=[r_VW[ci]])
                E("act", "activation", dict(out=gsig[:, a, :], in_=pb[b][:, 128:140], func=AF.Sigmoid), [r_pb[b]], [r_gsig[a]])
                tmb.append(b)
            for a in range(4):
                b = tmb[a]
                E("act", "activation", dict(out=zs[:, a, :], in_=pb[b][:, 140:396], func=AF.Silu), [r_pb[b]], [r_zs[a]])

            n0 = max(0, 32 * ci - 32)
            n1 = 32 * ci + 31
            cn = n1 - n0 + 1
            b = nb()
            rk = [r_kvT[max(ci - 1, 0)], r_kvT[ci], r_kvT[min(ci + 1, NCHUNK)]]
            E("dve", "tensor_copy", dict(out=Xc[:, :, 0:cn + 1], in_=kvT[:, 16 * n0:16 * n0 + 16 * (cn + 1)].rearrange("p (m l) -> p l m", l=16)), rk, [r_Xc])
            bv = nb()
            hb = (b, bv)
            for half, p0 in enumerate((0, 64)):
                for l_ in range(32):
                    rhs = Xc[p0:p0 + 64, l_ % 16, (l_ // 16):(l_ // 16) + cn]
                    mm(pb[hb[half]][:, 0:cn], w1kv[p0:p0 + 64, l_, :], rhs, l_ == 0, l_ == 31, [r_w1kv, r_Xc], pwrites=[r_pb[hb[half]]])
            for half in range(2):
                E("act", "activation", dict(out=hkv[:, 64 * half:64 * half + cn], in_=pb[hb[half]][:, 0:cn], func=AF.Silu, bias=hbias[:, half:half + 1]),
                  [r_pb[hb[half]], r_hbias], pwrites=[r_hkv])
            E("dve", "tensor_tensor", dict(out=sA[:, 1:528], in0=pv[:, 1:528], in1=pv[:, 0:527], op=ALU.add), [r_pv], [r_sA])
            E("dve", "tensor_scalar", dict(out=pm[:], in0=sA[:, 16:528], scalar1=pcoef[:, 0:1], scalar2=None, op0=ALU.mult), [r_sA, r_pcoef], [r_pm])
            E("dve", "tensor_tensor", dict(out=sB[:, 3:528], in0=sA[:, 3:528], in1=sA[:, 1:526], op=ALU.add), [r_sA], [r_sB])
            E("dve", "scalar_tensor_tensor", dict(out=pm[:], in0=sB[:, 16:528], scalar=pcoef[:, 1:2], in1=pm[:], op0=ALU.mult, op1=ALU.add), [r_sB, r_pcoef, r_pm], [r_pm])
            E("dve", "tensor_tensor", dict(out=sA[:, 7:528], in0=sB[:, 7:528], in1=sB[:, 3:524], op=ALU.add), [r_sB], [r_sA])
            E("dve", "scalar_tensor_tensor", dict(out=pm[:], in0=sA[:, 16:528], scalar=pcoef[:, 2:3], in1=pm[:], op0=ALU.mult, op1=ALU.add), [r_sA, r_pcoef, r_pm], [r_pm])
            E("dve", "tensor_tensor", dict(out=sB[:, 15:528], in0=sA[:, 15:528], in1=sA[:, 7:520], op=ALU.add), [r_sA], [r_sB])
            E("dve", "scalar_tensor_tensor", dict(out=pm[:], in0=sB[:, 16:528], scalar=pcoef[:, 3:4], in1=pm[:], op0=ALU.mult, op1=ALU.add), [r_sB, r_pcoef, r_pm], [r_pm])
            if ci == 0:
                E("dve", "tensor_tensor", dict(out=pm[:, 0:16], in0=pm[:, 0:16], in1=pcorr[:], op=ALU.mult), [r_pm, r_pcorr], [r_pm])
            E("dve", "tensor_tensor", dict(out=pmb[:], in0=pm[:], in1=pv[:, 16:528], op=ALU.subtract), [r_pm, r_pv], [r_pmb])
            b = nb()
            mm(pb[b][:, :], poolwb[:], pmb[:], True, True, [r_poolwb, r_pmb], [r_pb[b]])
            E("dve", "scalar_tensor_tensor", dict(out=ypool[:], in0=pb[b][:, :], scalar=pcoef[:, 4:5], in1=pz[:], op0=ALU.mult, op1=ALU.mult),
              [r_pb[b], r_pcoef, r_pz], [r_ypool])
            dma("sp", ys_d[512 * g:512 * g + 128, t0:t0 + 512], ypool[:], [r_ypool], pwrites=[r_ys], key=r_ypool)
            E("pool", "tensor_copy", dict(out=pv[:, 0:16], in_=pv[:, 512:528]), [r_pv], [r_pv])
            b2 = nb()
            mm(pb[b2][0:64, 0:cn], w2kvb[:, 0:64], hkv[:, 0:cn], True, True, [r_w2kvb, r_hkv], pwrites=[r_pb[b2]])
            mm(pb[b2][0:64, 64:64 + cn], w2kvb[:, 64:128], hkv[:, 64:64 + cn], True, True, [r_w2kvb, r_hkv], pwrites=[r_pb[b2]])
            copy_op("dve", kcT[:, n0:n0 + cn], pb[b2][0:64, 0:cn], [r_pb[b2]], [r_kcT])
            copy_op("dve", vcT[:, n0:n0 + cn], pb[b2][0:64, 64:64 + cn], [r_pb[b2]], [r_vcT])
            for c in sorted(set((n0 // 128, n1 // 128))):
                b3 = nb()
                mm(pb[b3][:, 0:64], vcT[:, 128 * c:128 * c + 128], identb[0:64, 0:64], True, True, [r_vcT, r_identb], [r_pb[b3]])
                copy_op("dve", vcp[:, c, 0:64], pb[b3][:, 0:64], [r_pb[b3]], [r_vcp])

            E("dve", "tensor_scalar", dict(out=yconv[:], in0=uu[:, 0:512], scalar1=pcoef[:, 5:6], scalar2=None, op0=ALU.mult), [r_uu, r_pcoef], [r_yconv])
            E("dve", "scalar_tensor_tensor", dict(out=yconv[:], in0=uu[:, 1:513], scalar=pcoef[:, 6:7], in1=yconv[:], op0=ALU.mult, op1=ALU.add), [r_uu, r_pcoef, r_yconv], [r_yconv])
            E("dve", "scalar_tensor_tensor", dict(out=yconv[:], in0=uu[:, 2:514], scalar=pcoef[:, 7:8], in1=yconv[:], op0=ALU.mult, op1=ALU.add), [r_uu, r_pcoef, r_yconv], [r_yconv])
            E("dve", "tensor_tensor", dict(out=yconv[:], in0=yconv[:], in1=cbb[:], op=ALU.mult), [r_yconv, r_cbb], [r_yconv])
            E("dve", "tensor_tensor", dict(out=yconvb[:], in0=yconv[:], in1=cz[:], op=ALU.mult), [r_yconv, r_cz], [r_yconvb])
            dma("sp", ys_d[512 * g + 128:512 * g + 256, t0:t0 + 512], yconvb[:], [r_yconvb], pwrites=[r_ys], key=r_yconvb)
            E("pool", "tensor_copy", dict(out=uu[:, 0:2], in_=uu[:, 512:514]), [r_uu], [r_uu])

            ecnt = [0]
            pend = []
            LOOKAHEAD = 2

            tlist = []
            pairc = [0]

            def tile_attn(lhsT, lhs_reads, rhs, rhs_reads, bias_ap, bias_reads, Obank, vrhs, v_reads, first, last, extra=None):
                tlist.append((lhsT, lhs_reads, rhs, rhs_reads, bias_ap, bias_reads, Obank, vrhs, v_reads, first, last, extra))

            def run_tiles():
                i = 0
                while i < len(tlist):
                    grp = tlist[i:i + 2]
                    i += len(grp)
                    P = pairc[0] % 2
                    pairc[0] += 1
                    for h, (lhsT, lhs_reads, rhs, rhs_reads, bias_ap, bias_reads, Obank, vrhs, v_reads, first, last, extra) in enumerate(grp):
                        b = 2 * P + h
                        mm(pb[b][:, :], lhsT, rhs, True, bias_ap is None, list(lhs_reads) + list(rhs_reads), [r_pb[b]])
                        if bias_ap is not None:
                            mm(pb[b][:, :], Jm, bias_ap, False, True, [r_cbt] + list(bias_reads), pwrites=[r_pb[b]])
                    wd = 512 * len(grp)
                    E("act", "activation", dict(out=ET[P][:, 0:wd], in_=SP[P][:, 0:wd], func=AF.Exp),
                      [r_pb[2 * P + h] for h in range(len(grp))], [r_E[2 * P + h] for h in range(len(grp))])

                    def stage2(grp=grp, P=P):
                        for h, (lhsT, lhs_reads, rhs, rhs_reads, bias_ap, bias_reads, Obank, vrhs, v_reads, first, last, extra) in enumerate(grp):
                            ei = 2 * P + h
                            for r in range(4):
                                mm(pb[Obank][:, 65 * r:65 * r + 65], Et[ei][:, 128 * r:128 * r + 128], vrhs, first and r == 0, last and r == 3, [r_E[ei]] + list(v_reads), pwrites=[r_pb[Obank]])
                            if extra is not None:
                                extra(ei)

                    pend.append(stage2)
                    while len(pend) > 1:
                        pend.pop(0)()
                del tlist[:]

            def flush():
                run_tiles()
                while pend:
                    pend.pop(0)()

            def fold(bi, bank, ya, ry, qb, first):
                Ob = pb[bank][:, 0:260].rearrange("p (r c) -> p r c", c=65)
                gs = gsig[:, qb, :].rearrange("p (r b) -> p r b", b=3)
                if bi > 0:
                    E("dve", "tensor_scalar_max", dict(out=den[:, 4 * bi:4 * bi + 4], in0=Ob[:, :, 64], scalar1=1e-30), [r_pb[bank]], pwrites=[r_den])
                    E("dve", "reciprocal", dict(out=den[:, 4 * bi:4 * bi + 4], in_=den[:, 4 * bi:4 * bi + 4]), [r_den], pwrites=[r_den])
                E("dve", "tensor_tensor", dict(out=coef[:, 4 * bi:4 * bi + 4], in0=den[:, 4 * bi:4 * bi + 4], in1=gs[:, :, bi], op=ALU.mult),
                  [r_den, r_gsig[qb]], pwrites=[r_coef])
                for r in range(4):
                    if first:
                        E("dve", "tensor_scalar", dict(out=ya[:, 64 * r:64 * r + 64], in0=Ob[:, r, 0:64], scalar1=coef[:, 4 * bi + r:4 * bi + r + 1], scalar2=None, op0=ALU.mult),
                          [r_pb[bank], r_coef], pwrites=[ry])
                    else:
                        E("dve", "scalar_tensor_tensor", dict(out=ya[:, 64 * r:64 * r + 64], in0=Ob[:, r, 0:64], scalar=coef[:, 4 * bi + r:4 * bi + r + 1], in1=ya[:, 64 * r:64 * r + 64], op0=ALU.mult, op1=ALU.add),
                          [r_pb[bank], r_coef, ry], pwrites=[ry])

            def make_deferred(qi, qb, RA, RB, ya, ry):
                def run():
                    for jt in range(qi + 1):
                        if jt < 32:
                            rhs, rr = RA, [r_RQAq[qb], r_RQAs[qb]]
                        else:
                            rhs, rr = RB, [r_RQBq[qb], r_RQBs[qb]]
                        bias_ap = Rres[:, qi - jt, :] if qi - jt <= 1 else None
                        tile_attn(KE[:, 128 * jt:128 * jt + 128], [r_KE[jt // 4], r_KEe], rhs, rr, bias_ap, [r_Rres], 5,
                                  VS[:, jt, :], [r_VS[jt // 4], r_VSo], jt == 0, jt == qi)
                    flush()
                    fold(1, 5, ya, ry, qb, False)
                    E("dve", "tensor_tensor", dict(out=ya, in0=ya, in1=zs[:, qb, :], op=ALU.mult), [ry, r_zs[qb]], [ry])

                    def out_fn():
                        bt_ = nb()
                        tr(pb[bt_][:, 0:128], ya[:, 0:128], [ry], pwrites=[r_pb[bt_]])
                        tr(pb[bt_][:, 128:256], ya[:, 128:256], [ry], pwrites=[r_pb[bt_]])
                        copy_op("act", yTn[:].rearrange("p a b -> p (a b)"), pb[bt_][:, 0:256], [r_pb[bt_]], [r_yTn])
                        q0 = 128 * qi
                        dma("sp", ys_d[512 * g + 256:512 * g + 512, q0:q0 + 128].rearrange("(i c) t -> c i t", c=128), yTn[:], [r_yTn], pwrites=[r_ys], key=r_yTn)

                    pout.append(out_fn)
                return run

            deferred = None
            pout = []
            for qb in range(4):
                qi = 4 * ci + qb
                RA = RQA[:, qb, :]
                RB = RQB[:, qb, :]
                ya = yacc[:, qi % 2, :]
                ry = r_yacc[qi % 2]
                cmax = qi // 16
                prefetch_r1(qi)
                if qi + 1 < 4 * nchunks:
                    prefetch_r1(qi + 1)
                for c in range(cmax + 1):
                    m = qi - 16 * c
                    bias_ap, bias_reads = None, []
                    if m <= 16:
                        ri = r1map[(qi, c)]
                        bias_ap, bias_reads = R1t[ri][:], [r_R1t[ri]]

                    def extra(ei, c=c, cmax=cmax):
                        for r in range(4):
                            mm(pb[7][:, 128 * r:128 * r + 128], Et[ei][:, 128 * r:128 * r + 128], ovt[:, 128 * c:128 * c + 128], c == 0 and r == 0, c == cmax and r == 3,
                               [r_E[ei], r_cbt], pwrites=[r_pb[7]])

                    tile_attn(kcT[:, 128 * c:128 * c + 128], [r_kcT], RA[0:64, :], [r_RQAq[qb]], bias_ap, bias_reads, 4, vcp[:, c, :], [r_vcp], c == 0, c == cmax, extra)
                flush()
                while pout:
                    pout.pop(0)()
                O1 = pb[4][:, 0:260].rearrange("p (r c) -> p r c", c=65)
                E("dve", "tensor_scalar_max", dict(out=den[:, 0:4], in0=O1[:, :, 64], scalar1=1e-30), [r_pb[4]], pwrites=[r_den])
                E("dve", "reciprocal", dict(out=den[:, 0:4], in_=den[:, 0:4]), [r_den], pwrites=[r_den])
                E("pool", "memset", dict(ap=sc[:], constant=-1e30), writes=[r_sc])
                ncol = 2 * qi + 2
                E("dve", "tensor_scalar", dict(out=sc[:, 0:ncol], in0=pb[7][:, 0:ncol], scalar1=den[:, 0:1], scalar2=None, op0=ALU.mult),
                  [r_pb[7], r_den], pwrites=[r_sc])
                for r in range(1, 4):
                    E("dve", "scalar_tensor_tensor", dict(out=sc[:, 0:ncol], in0=pb[7][:, 128 * r:128 * r + ncol], scalar=den[:, r:r + 1], in1=sc[:, 0:ncol], op0=ALU.mult, op1=ALU.add),
                      [r_pb[7], r_den, r_sc], pwrites=[r_sc])
                if qi >= 1:
                    ja = 2 * qi - 1
                    E("dve", "tensor_scalar", dict(out=sc[:, ja:ja + 1], in0=sc[:, ja:ja + 1], scalar1=halfc[:, 0:1], scalar2=halfc[:, 1:2], op0=ALU.mult, op1=ALU.add),
                      [r_sc, r_halfc], pwrites=[r_sc])
                E("dve", "memset", dict(ap=sc[:, 0:1], constant=1e9), [r_sc], pwrites=[r_sc])
                E("dve", "memset", dict(ap=sc[:, 2 * qi:2 * qi + 1], constant=2e9), [r_sc], pwrites=[r_sc])
                E("dve", "tensor_copy", dict(out=sc[:, 2 * qi + 1:2 * qi + 2], in_=halfc[:, 2:3]), [r_sc, r_halfc], pwrites=[r_sc])
                E("dve", "max", dict(out=mx[:, 0:8], in_=sc[:]), [r_sc], pwrites=[r_mx])
                E("dve", "match_replace", dict(out=sc2[:], in_to_replace=mx[:, 0:8], in_values=sc[:], imm_value=-3e38), [r_sc, r_mx], [r_sc2])
                E("dve", "max", dict(out=mx[:, 8:16], in_=sc2[:]), [r_sc2, r_mx], pwrites=[r_mx])
                E("dve", "tensor_reduce", dict(out=th[:], in_=mx[:, 8:16], axis=AX.X, op=ALU.min), [r_mx], [r_th])
                E("dve", "tensor_scalar", dict(out=seln[:, 64:192], in0=sc[:], scalar1=th[:, 0:1], scalar2=NEGB, op0=ALU.is_lt, op1=ALU.mult),
                  [r_sc, r_th], [r_seln])
                fold(0, 4, ya, ry, qb, True)
                dmax = min(4, qi)
                for dl in range(dmax + 1):
                    jt = qi - dl
                    tile_attn(KW[0:64, 128 * jt:128 * jt + 128], [r_KW[jt // 4]], RA[0:64, :], [r_RQAq[qb]], Rres[:, 2 + dl, :], [r_Rres], 6,
                              VW[:, jt, :], [r_VW[jt // 4], r_VWo], dl == 0, dl == dmax)
                flush()
                fold(2, 6, ya, ry, qb, False)
                if deferred is not None:
                    deferred()
                b = 6
                mm(pb[b][:, :], seln[:, 0:128], I4, True, True, [r_seln, r_cbt], [r_pb[b]])
                copy_op("dve", RA[64:128, :], pb[b][64:128, :], [r_pb[b]], [r_RQAs[qb]])
                if qi >= 32:
                    b = 5
                    mm(pb[b][:, :], seln[:, 64:192], I4, True, True, [r_seln, r_cbt], [r_pb[b]])
                    copy_op("dve", RB[64:128, :], pb[b][64:128, :], [r_pb[b]], [r_RQBs[qb]])
                deferred = make_deferred(qi, qb, RA, RB, ya, ry)
            deferred()
            while pout:
                pout.pop(0)()


    WoV = KE[:].rearrange("p (a b) -> p a b", a=8)
    yTbV = [kvT[:, 4096 * s_:4096 * s_ + 4096].rearrange("p (a b) -> p a b", a=8) for s_ in range(2)]
    WfF = bass.AP(Wf, 0, [[8 * NFM, 128], [1, 8 * NFM]]).bitcast(F32)
    xtV = [WfF[:, 0:1024], WfF[:, 1024:2048]]
    ztV = [WfF[:, 2048:3072], WfF[:, 3072:4096]]
    WtF = bass.AP(Wt, 0, [[8 * NTM, 128], [1, 8 * NTM]]).bitcast(F32)
    sqV2 = [WfF[:, 4096:5120], WtF[:, 0:1024]]
    otV = [bass.AP(Rres, 0, [[3584, 128], [1, 3584]]).bitcast(F32)[:, 0:1024], bass.AP(xT, 0, [[4096, 128], [1, 4096]]).bitcast(F32)[:, 0:1024]]
    w1F = bass.AP(w1kv, 0, [[4096, 128], [1, 4096]]).bitcast(F32)
    GtV = w1F[:, 0:1024]
    BtV = w1F[:, 1024:2048]
    stt2 = sb("stt", [128, 8], F32); r_stt2 = [k.R("stt0"), k.R("stt1")]
    r_WoV = k.R("WoV"); r_yTbV = [k.R("yTbV0"), k.R("yTbV1")]; r_xtV = [k.R("xtV0"), k.R("xtV1")]
    r_ztV = [k.R("ztV0"), k.R("ztV1")]; r_sqV2 = [k.R("sqV0"), k.R("sqV1")]; r_otV = [k.R("otV0"), k.R("otV1")]
    r_GtV = k.R("GtV"); r_BtV = k.R("BtV")
    ALPHA_ = (2 * 2) ** 0.25

    def B_body(l, xsrc, xreads, dst, r_dst):
        dma("sp", GtV, bass.AP(lng_t, D * l, [[0, 128], [1, D]]), [], [r_GtV])
        dma("sp", BtV, bass.AP(lnb_t, D * l, [[0, 128], [1, D]]), [], [r_BtV])
        for j in range(8):
            s = next_stg()
            dma("sp" if j % 2 == 0 else "pool", stg[s][:, 0:1024], wo_all[D * l + 128 * j:D * l + 128 * j + 128, :], [], [r_stg[s]])
            copy_op(evac_eng(), WoV[:, j, :], stg[s][:, 0:1024], [r_stg[s]], pwrites=[r_WoV])
        tiles = [(ch, a) for ch in range(TT // 512) for a in range(4)]
        NTL = len(tiles)

        def P1(t):
            ch, a = tiles[t]
            s = ch % 2
            u = t % 2
            t0 = 512 * ch
            r0 = t0 + 128 * a
            if a == 0:
                dma("sp" if ch % 2 == 0 else "pool", yTbV[s], ys_d[:, t0:t0 + 512].rearrange("(j p) t -> p j t", p=128), [r_ys], [r_yTbV[s]])
            stt = stt2[:, 4 * u:4 * u + 4]
            for h in range(2):
                b = nb()
                for j in range(8):
                    mm(pb[b][:, :], yTbV[s][:, j, 128 * a:128 * a + 128], WoV[:, j, 512 * h:512 * h + 512], j == 0, j == 7, [r_yTbV[s], r_WoV], pwrites=[r_pb[b]])
                E("dve", "scalar_tensor_tensor", dict(out=ztV[u][:, 512 * h:512 * h + 512], in0=xtV[u][:, 512 * h:512 * h + 512], scalar=ALPHA_, in1=pb[b][:, :], op0=ALU.mult, op1=ALU.add),
                  [r_xtV[u], r_pb[b]], pwrites=[r_ztV[u]])
            E("act", "activation", dict(out=sqV2[u], in_=ztV[u], func=AF.Copy, accum_out=stt[:, 0:1]), [r_ztV[u]], [r_sqV2[u], r_stt2[u]])
            E("dve", "tensor_scalar_mul", dict(out=stt[:, 0:1], in0=stt[:, 0:1], scalar1=1.0 / 1024), [r_stt2[u]], pwrites=[r_stt2[u]])
            E("dve", "tensor_scalar", dict(out=ztV[u], in0=ztV[u], scalar1=stt[:, 0:1], scalar2=None, op0=ALU.subtract), [r_ztV[u], r_stt2[u]], [r_ztV[u]])

        def P2a(t):
            u = t % 2
            stt_ = stt2[:, 4 * u:4 * u + 4]
            E("act", "activation", dict(out=sqV2[u], in_=ztV[u], func=AF.Square, accum_out=stt_[:, 1:2]), [r_ztV[u]], [r_sqV2[u], r_stt2[u]])

        def P2b(t):
            u = t % 2
            stt = stt2[:, 4 * u:4 * u + 4]
            E("dve", "tensor_scalar", dict(out=stt[:, 1:2], in0=stt[:, 1:2], scalar1=1.0 / 1024, scalar2=1e-5, op0=ALU.mult, op1=ALU.add), [r_stt2[u]], pwrites=[r_stt2[u]])
            E("act", "activation", dict(out=stt[:, 2:3], in_=stt[:, 1:2], func=AF.Sqrt), [r_stt2[u]], pwrites=[r_stt2[u]])
            E("dve", "reciprocal", dict(out=stt[:, 3:4], in_=stt[:, 2:3]), [r_stt2[u]], pwrites=[r_stt2[u]])

        def P3(t):
            ch, a = tiles[t]
            u = t % 2
            r0 = 512 * ch + 128 * a
            stt = stt2[:, 4 * u:4 * u + 4]
            E("dve", "scalar_tensor_tensor", dict(out=otV[u], in0=ztV[u], scalar=stt[:, 3:4], in1=GtV, op0=ALU.mult, op1=ALU.mult), [r_ztV[u], r_stt2[u], r_GtV], [r_otV[u]])
            E("pool", "tensor_tensor", dict(out=otV[u], in0=otV[u], in1=BtV, op=ALU.add), [r_otV[u], r_BtV], [r_otV[u]])
            dma("sp", dst[r0:r0 + 128, :], otV[u], [r_otV[u]], pwrites=[r_dst], key=r_otV[u])

        def LX(t):
            ch, a = tiles[t]
            r0 = 512 * ch + 128 * a
            dma("act", xtV[t % 2], xsrc[r0:r0 + 128, :], xreads, [r_xtV[t % 2]])

        LX(0)
        for t in range(NTL + 2):
            if t + 1 < NTL:
                LX(t + 1)
            if 0 <= t - 1 < NTL:
                P2a(t - 1)
            if 0 <= t - 2 < NTL:
                P3(t - 2)
            if t < NTL:
                P1(t)
            if 0 <= t - 1 < NTL:
                P2b(t - 1)

    xsrc, xreads = x_d, []
    for l in range(L):
        for g in range(G):
            A_body(l, g, xsrc, xreads)
        k.barrier()
        last = l == L - 1
        dst, r_dst = (out_d, r_out) if last else (x1_d, r_x1)
        B_body(l, xsrc, xreads, dst, r_dst)
        k.barrier()
        xsrc, xreads = x1_d, [r_x1]
    stats = k.emit()
    return nc, stats


def host_fused_inputs(inp, L=2, G=2):
    d = dict(host_consts())
    per = [[host_layer_inputs(inp, l, g) for g in range(G)] for l in range(L)]
    d["wf"] = np.concatenate([per[l][g]["wf"] for l in range(L) for g in range(G)], axis=0)
    d["wt"] = np.concatenate([per[l][g]["wt"] for l in range(L) for g in range(G)], axis=0)
    d["poolw"] = np.concatenate([per[l][g]["poolw"] for l in range(L) for g in range(G)], axis=0)
    d["pcoef"] = np.concatenate([per[l][g]["pcoef"] for l in range(L) for g in range(G)], axis=0)
    d["pcorr"] = np.concatenate([per[0][g]["pcorr"] for g in range(G)], axis=0)
    d["w1kv"] = np.concatenate([per[l][0]["w1kv"] for l in range(L)], axis=0)
    d["pekv"] = np.concatenate([per[l][0]["pekv"] for l in range(L)], axis=0)
    d["w2kv"] = np.concatenate([per[l][0]["w2kv"] for l in range(L)], axis=0)
    d["relb"] = np.concatenate([per[0][g]["relb"] for g in range(G)], axis=0)
    perm = []
    for g in range(G):
        perm += list(range(128 * g, 128 * g + 128)) + list(range(256 + 128 * g, 256 + 128 * g + 128)) + list(range(512 + 256 * g, 512 + 256 * g + 256))
    perm = np.asarray(perm)
    d["wo"] = np.concatenate([inp["w_out"][l][perm] for l in range(L)], axis=0)
    d["lng"] = np.ascontiguousarray(inp["ln_g"][:L])
    d["lnb"] = np.ascontiguousarray(inp["ln_b"][:L])
    return {k_: np.ascontiguousarray(v) for k_, v in d.items()}


_CACHE = {}


def kernel(x, w_in, w_out, pool_w, pool_scale, conv_w, cmp_pe_k, cmp_w1_k, cmp_w2_k,
           cmp_pe_v, cmp_w1_v, cmp_w2_v, rel_bias, ln_g, ln_b):
    from concourse.bass_utils import run_bass_kernel_spmd
    inp = dict(x=x, w_in=w_in, w_out=w_out, pool_w=pool_w, pool_scale=pool_scale, conv_w=conv_w,
               cmp_pe_k=cmp_pe_k, cmp_w1_k=cmp_w1_k, cmp_w2_k=cmp_w2_k, cmp_pe_v=cmp_pe_v,
               cmp_w1_v=cmp_w1_v, cmp_w2_v=cmp_w2_v, rel_bias=rel_bias, ln_g=ln_g, ln_b=ln_b)
    inp = {k_: np.asarray(v, dtype=np.float32) for k_, v in inp.items()}
    if "F" not in _CACHE:
        _CACHE["F"] = build_F(2, 2, NCHUNK)[0]
    nc = _CACHE["F"]
    base = host_fused_inputs(inp, 2, 2)
    in_maps = []
    for core in range(8):
        m = dict(base)
        m["x"] = np.ascontiguousarray(inp["x"][core // 2])
        in_maps.append(m)
    res = run_bass_kernel_spmd(nc, in_maps, core_ids=list(range(8))).results
    return np.stack([res[2 * b]["out"] for b in range(inp["x"].shape[0])], axis=0)
```
